# Optimizing a Trainium2 kernel written in Bass

```python
import math
import jax, jax.numpy as jnp
from jax import lax
import numpy as np


D_MODEL = 1024
BATCH = 8
SEQ = 2048
DEPTH = 1

MEM_LEN = 256
S5_WIDTH = 512
S5_GROUP = 16
S5_GROUPS = S5_WIDTH // S5_GROUP
S5_STATE = 64
DT_MIN = 1e-3
DT_MAX = 1e-1
DIFF_WIDTH = D_MODEL - S5_WIDTH
DIFF_HEAD_DIM = 64
DIFF_V_DIM = 2 * DIFF_HEAD_DIM
N_DIFF_HEADS = DIFF_WIDTH // DIFF_V_DIM
DIFF_QK_WIDTH = 2 * N_DIFF_HEADS * DIFF_HEAD_DIM
MIX_WIDTH = S5_WIDTH + DIFF_WIDTH
MIX_IN = S5_WIDTH + 2 * DIFF_QK_WIDTH + DIFF_WIDTH
Q_BLOCK = 128
NUM_BUCKETS = 32
MAX_DISTANCE = 128
CA_HEADS = 4
CA_HEAD_DIM = D_MODEL // CA_HEADS
FFN_HIDDEN = -(-8 * D_MODEL // (3 * 256)) * 256
DEEPNORM_ALPHA = (2.0 * DEPTH) ** 0.25
DEEPNORM_BETA = (8.0 * DEPTH) ** -0.25
LN_EPS = 1e-5

kernel_name = "hymba_s5_diffattn_deepnorm_layer"


def layer_norm(x, g, b):
    xf = x.astype(jnp.float32)
    mu = jnp.mean(xf, axis=-1, keepdims=True)
    var = jnp.mean(jnp.square(xf - mu), axis=-1, keepdims=True)
    y = (xf - mu) * lax.rsqrt(var + LN_EPS)
    return (y * g.astype(jnp.float32) + b.astype(jnp.float32)).astype(x.dtype)


def rms_norm(x, g):
    xf = x.astype(jnp.float32)
    y = xf * lax.rsqrt(jnp.mean(jnp.square(xf), axis=-1, keepdims=True) + LN_EPS)
    return (y * g.astype(jnp.float32)).astype(x.dtype)


def cmul(ar, ai, br, bi):
    return ar * br - ai * bi, ar * bi + ai * br


def s5_mixer(u, lam_re, lam_im, log_dt, b_re, b_im, c_re, c_im, d, glu_w, glu_b):
    f32 = jnp.float32
    bsz, seq, _ = u.shape
    uf = u.astype(f32).reshape(bsz, seq, S5_GROUPS, S5_GROUP)
    dt = jnp.exp(log_dt.astype(f32))[:, None]
    lr, li = lam_re.astype(f32), lam_im.astype(f32)
    mag = jnp.exp(lr * dt)
    ang = li * dt
    ab_re, ab_im = mag * jnp.cos(ang), mag * jnp.sin(ang)
    den = lr * lr + li * li
    nr, ni = ab_re - 1.0, ab_im
    f_re = (nr * lr + ni * li) / den
    f_im = (ni * lr - nr * li) / den
    bb_re, bb_im = cmul(f_re[..., None], f_im[..., None], b_re.astype(f32), b_im.astype(f32))
    bu_re = jnp.einsum('blgh,gph->blgp', uf, bb_re)
    bu_im = jnp.einsum('blgh,gph->blgp', uf, bb_im)
    a_re = jnp.broadcast_to(ab_re, bu_re.shape)
    a_im = jnp.broadcast_to(ab_im, bu_im.shape)

    def combine(e1, e2):
        a1r, a1i, b1r, b1i = e1
        a2r, a2i, b2r, b2i = e2
        ar, ai = cmul(a2r, a2i, a1r, a1i)
        br, bi = cmul(a2r, a2i, b1r, b1i)
        return ar, ai, br + b2r, bi + b2i

    _, _, xr, xi = lax.associative_scan(combine, (a_re, a_im, bu_re, bu_im), axis=1)
    y = (jnp.einsum('ghp,blgp->blgh', c_re.astype(f32), xr)
         - jnp.einsum('ghp,blgp->blgh', c_im.astype(f32), xi)
         + d.astype(f32) * uf)
    y = jax.nn.gelu(y.reshape(bsz, seq, S5_WIDTH))
    y = y * jax.nn.sigmoid(y @ glu_w.astype(f32) + glu_b.astype(f32))
    return y.astype(u.dtype)


def t5_bucket(dist):
    max_exact = NUM_BUCKETS // 2
    is_small = dist < max_exact
    df = jnp.maximum(dist, 1).astype(jnp.float32)
    large = max_exact + (jnp.log(df / max_exact) / math.log(MAX_DISTANCE / max_exact)
                         * (NUM_BUCKETS - max_exact)).astype(jnp.int32)
    large = jnp.minimum(large, NUM_BUCKETS - 1)
    return jnp.where(is_small, dist, large)


def diff_attention(q, k, v, rel_bias, lq1, lk1, lq2, lk2, subln_g, lambda_init):
    f32 = jnp.float32
    bsz, seq = q.shape[0], q.shape[1]
    lam = (jnp.exp(jnp.sum(lq1.astype(f32) * lk1.astype(f32)))
           - jnp.exp(jnp.sum(lq2.astype(f32) * lk2.astype(f32))) + lambda_init)
    scale = DIFF_HEAD_DIM ** -0.5
    outs = []
    for start in range(0, seq, Q_BLOCK):
        end = start + Q_BLOCK
        qb, kb, vb = q[:, start:end], k[:, :end], v[:, :end]
        s = jnp.einsum('bqhd,bkhd->bhqk', qb, kb).astype(f32) * scale
        s = s.reshape(bsz, N_DIFF_HEADS, 2, Q_BLOCK, end)
        dist = jnp.arange(start, end, dtype=jnp.int32)[:, None] - jnp.arange(end, dtype=jnp.int32)[None, :]
        bias = jnp.transpose(rel_bias.astype(f32)[t5_bucket(jnp.maximum(dist, 0))], (2, 0, 1))
        s = jnp.where(dist >= 0, s + bias[None, :, None], -jnp.inf)
        p = jax.nn.softmax(s, axis=-1)
        attn = p[:, :, 0] - lam * p[:, :, 1]
        outs.append(jnp.einsum('bhqk,bkhd->bqhd', attn.astype(v.dtype), vb))
    o = jnp.concatenate(outs, axis=1)
    o = rms_norm(o, subln_g) * (1.0 - lambda_init)
    return o.reshape(bsz, seq, DIFF_WIDTH)


def memory_cross_attention(h, mem, wq, wkv, wo):
    f32 = jnp.float32
    bsz, seq, _ = h.shape
    q = (h @ wq).reshape(bsz, seq, CA_HEADS, CA_HEAD_DIM)
    kv = mem @ wkv
    k = kv[..., :D_MODEL].reshape(bsz, mem.shape[1], CA_HEADS, CA_HEAD_DIM)
    v = kv[..., D_MODEL:].reshape(bsz, mem.shape[1], CA_HEADS, CA_HEAD_DIM)
    s = jnp.einsum('bqhd,bkhd->bhqk', q, k).astype(f32) * (CA_HEAD_DIM ** -0.5)
    p = jax.nn.softmax(s, axis=-1).astype(v.dtype)
    o = jnp.einsum('bhqk,bkhd->bqhd', p, v).reshape(bsz, seq, D_MODEL)
    return o @ wo


def swiglu_ffn(h, w_gate_up, w_down):
    gu = h @ w_gate_up
    return (jax.nn.silu(gu[..., :FFN_HIDDEN]) * gu[..., FFN_HIDDEN:]) @ w_down


def setup_inputs(seed: int = 0) -> dict:
    key = jax.random.key(seed)
    ks = jax.random.split(key, 40)
    f32 = jnp.float32

    def nrm(k, shape, s):
        return s * jax.random.normal(k, shape, f32)

    L = DEPTH
    return {
        "x": nrm(ks[0], (BATCH, SEQ, D_MODEL), 1.0),
        "mem": nrm(ks[1], (BATCH, MEM_LEN, D_MODEL), 1.0),
        "ln_in_g": 1.0 + nrm(ks[2], (D_MODEL,), 0.02),
        "ln_in_b": nrm(ks[3], (D_MODEL,), 0.02),
        "w_in": nrm(ks[4], (L, D_MODEL, MIX_IN), D_MODEL ** -0.5),
        "s5_lambda_re": -0.5 + nrm(ks[5], (L, S5_GROUPS, S5_STATE), 0.01),
        "s5_lambda_im": math.pi * jnp.arange(S5_STATE, dtype=f32) + nrm(ks[6], (L, S5_GROUPS, S5_STATE), 0.01),
        "s5_log_dt": jax.random.uniform(ks[7], (L, S5_GROUPS), f32, math.log(DT_MIN), math.log(DT_MAX)),
        "s5_b_re": nrm(ks[8], (L, S5_GROUPS, S5_STATE, S5_GROUP), (2.0 * S5_GROUP) ** -0.5),
        "s5_b_im": nrm(ks[9], (L, S5_GROUPS, S5_STATE, S5_GROUP), (2.0 * S5_GROUP) ** -0.5),
        "s5_c_re": nrm(ks[10], (L, S5_GROUPS, S5_GROUP, S5_STATE), S5_STATE ** -0.5),
        "s5_c_im": nrm(ks[11], (L, S5_GROUPS, S5_GROUP, S5_STATE), S5_STATE ** -0.5),
        "s5_d": nrm(ks[12], (L, S5_GROUPS, S5_GROUP), 1.0),
        "s5_glu_w": nrm(ks[13], (L, S5_WIDTH, S5_WIDTH), S5_WIDTH ** -0.5),
        "s5_glu_b": nrm(ks[14], (L, S5_WIDTH), 0.01),
        "diff_lq1": nrm(ks[15], (L, DIFF_HEAD_DIM), 0.1),
        "diff_lk1": nrm(ks[16], (L, DIFF_HEAD_DIM), 0.1),
        "diff_lq2": nrm(ks[17], (L, DIFF_HEAD_DIM), 0.1),
        "diff_lk2": nrm(ks[18], (L, DIFF_HEAD_DIM), 0.1),
        "diff_subln_g": 1.0 + nrm(ks[19], (L, DIFF_V_DIM), 0.02),
        "rel_bias": nrm(ks[20], (NUM_BUCKETS, N_DIFF_HEADS), 0.2),
        "w_out": nrm(ks[21], (L, MIX_WIDTH, D_MODEL), DEEPNORM_BETA * MIX_WIDTH ** -0.5),
        "ln1_g": 1.0 + nrm(ks[22], (L, D_MODEL), 0.02),
        "ln1_b": nrm(ks[23], (L, D_MODEL), 0.02),
        "ca_wq": nrm(ks[24], (L, D_MODEL, D_MODEL), D_MODEL ** -0.5),
        "ca_wkv": nrm(ks[25], (L, D_MODEL, 2 * D_MODEL), D_MODEL ** -0.5),
        "ca_wo": nrm(ks[26], (L, D_MODEL, D_MODEL), DEEPNORM_BETA * D_MODEL ** -0.5),
        "ln2_g": 1.0 + nrm(ks[27], (L, D_MODEL), 0.02),
        "ln2_b": nrm(ks[28], (L, D_MODEL), 0.02),
        "ffn_w_gate_up": nrm(ks[29], (L, D_MODEL, 2 * FFN_HIDDEN), D_MODEL ** -0.5),
        "ffn_w_down": nrm(ks[30], (L, FFN_HIDDEN, D_MODEL), DEEPNORM_BETA * FFN_HIDDEN ** -0.5),
        "ln3_g": 1.0 + nrm(ks[31], (L, D_MODEL), 0.02),
        "ln3_b": nrm(ks[32], (L, D_MODEL), 0.02),
    }


def reference(x, mem, ln_in_g, ln_in_b, w_in, s5_lambda_re, s5_lambda_im, s5_log_dt,
              s5_b_re, s5_b_im, s5_c_re, s5_c_im, s5_d, s5_glu_w, s5_glu_b,
              diff_lq1, diff_lk1, diff_lq2, diff_lk2, diff_subln_g, rel_bias, w_out,
              ln1_g, ln1_b, ca_wq, ca_wkv, ca_wo, ln2_g, ln2_b,
              ffn_w_gate_up, ffn_w_down, ln3_g, ln3_b):
    bsz, seq, _ = x.shape
    h = layer_norm(x, ln_in_g, ln_in_b)
    for l in range(DEPTH):
        lambda_init = 0.8 - 0.6 * math.exp(-0.3 * l)
        proj = h @ w_in[l]
        u = proj[..., :S5_WIDTH]
        q = proj[..., S5_WIDTH:S5_WIDTH + DIFF_QK_WIDTH].reshape(bsz, seq, 2 * N_DIFF_HEADS, DIFF_HEAD_DIM)
        k = proj[..., S5_WIDTH + DIFF_QK_WIDTH:S5_WIDTH + 2 * DIFF_QK_WIDTH].reshape(bsz, seq, 2 * N_DIFF_HEADS, DIFF_HEAD_DIM)
        v = proj[..., S5_WIDTH + 2 * DIFF_QK_WIDTH:].reshape(bsz, seq, N_DIFF_HEADS, DIFF_V_DIM)
        y_s5 = s5_mixer(u, s5_lambda_re[l], s5_lambda_im[l], s5_log_dt[l], s5_b_re[l], s5_b_im[l],
                        s5_c_re[l], s5_c_im[l], s5_d[l], s5_glu_w[l], s5_glu_b[l])
        y_diff = diff_attention(q, k, v, rel_bias, diff_lq1[l], diff_lk1[l], diff_lq2[l], diff_lk2[l],
                                diff_subln_g[l], lambda_init)
        mix = jnp.concatenate([y_s5, y_diff], axis=-1) @ w_out[l]
        h = layer_norm(DEEPNORM_ALPHA * h + mix, ln1_g[l], ln1_b[l])
        h = layer_norm(DEEPNORM_ALPHA * h + memory_cross_attention(h, mem, ca_wq[l], ca_wkv[l], ca_wo[l]),
                       ln2_g[l], ln2_b[l])
        h = layer_norm(DEEPNORM_ALPHA * h + swiglu_ffn(h, ffn_w_gate_up[l], ffn_w_down[l]),
                       ln3_g[l], ln3_b[l])
    return h
```

```python
import numpy as np
import concourse.bass as bass
import concourse.mybir as mybir
from concourse.bass_utils import run_bass_kernel_spmd

F32 = mybir.dt.float32
BF16 = mybir.dt.bfloat16
ALU = mybir.AluOpType
AF = mybir.ActivationFunctionType
AX = mybir.AxisListType

ENGS = ("pe", "act", "dve", "pool", "sp")


class Instr:
    __slots__ = ("eng", "fn", "waits", "signal", "idx", "val", "dma_sem", "dma_val")

    def __init__(self, eng, fn):
        self.eng = eng
        self.fn = fn
        self.waits = []
        self.signal = False
        self.idx = -1
        self.val = 0
        self.dma_sem = None
        self.dma_val = 0


class Buf:
    __slots__ = ("name", "last_w", "readers")

    def __init__(self, name, inherit=()):
        self.name = name
        self.last_w = None
        self.readers = list(inherit)


class Ten:
    def __init__(self, h, name, inherit=()):
        self.h = h
        self.name = name
        self.inherit = list(inherit)
        self._bufs = {}

    def __getitem__(self, k):
        return self.h[k]

    def buf(self, key=0):
        b = self._bufs.get(key)
        if b is None:
            b = Buf(f"{self.name}{key}", self.inherit)
            self._bufs[key] = b
        return b

    def bufs(self, keys):
        return [self.buf(k) for k in keys]

    def all_instrs(self):
        out = list(self.inherit)
        for b in self._bufs.values():
            if b.last_w is not None:
                out.append(b.last_w)
            out.extend(b.readers)
        return out


class Sched:
    def __init__(self, nc, sbuf_base=16512, sbuf_bytes=229312):
        self.nc = nc
        self.q = {e: [] for e in ENGS}
        self.waited = {e: {} for e in ENGS}
        self.dma_count = {}
        self.sbuf_bytes = sbuf_bytes
        self.sbuf_base = sbuf_base
        self.live = {}
        self.hist = []
        self.n_alloc = 0
        self.final_dma = []
        self.ring_pos = {}
        self.ring_last = {}

    def alloc(self, name, free_shape, dtype, nbytes_el):
        size = int(np.prod(free_shape)) * nbytes_el
        size = (size + 63) // 64 * 64
        segs = sorted((o, s) for (o, s, _) in self.live.values())
        off = self.sbuf_base
        for (o, s) in segs:
            if off + size <= o:
                break
            off = max(off, o + s)
        if off + size > self.sbuf_bytes:
            raise RuntimeError(f"SBUF arena overflow allocating {name} ({size} B); live={[(k, v[0], v[1]) for k, v in self.live.items()]}")
        inherit = []
        for (o, s, t) in self.hist:
            if o < off + size and off < o + s:
                inherit.extend(t.all_instrs())
        comp = {}
        for ins in inherit:
            key = ins.dma_sem if ins.dma_sem is not None else ins.eng
            cur = comp.get(key)
            if cur is None or (ins.dma_val if ins.dma_sem is not None else ins.idx) > (cur.dma_val if cur.dma_sem is not None else cur.idx):
                comp[key] = ins
        self.n_alloc += 1
        h = self.nc.alloc_sbuf_tensor_at(f"{name}_{self.n_alloc}", [128] + list(free_shape), dtype, offset=off)
        t = Ten(h, name, list(comp.values()))
        self.live[name] = (off, size, t)
        return t

    def free(self, *names):
        for name in names:
            o, s, t = self.live.pop(name)
            self.hist.append((o, s, t))

    def _need(self, c, p, raw):
        if p is None or p is c:
            return
        E = c.eng
        if p.dma_sem is not None:
            key = "dma:" + p.dma_sem
            if self.waited[E].get(key, 0) >= p.dma_val:
                return
            self.waited[E][key] = p.dma_val
            c.waits.append(p)
            return
        if p.eng == E and c.dma_sem is None:
            if E in ("pe", "sp"):
                return
            if not raw:
                return
        key = p.eng
        if self.waited[E].get(key, -1) >= p.idx:
            return
        self.waited[E][key] = p.idx
        p.signal = True
        c.waits.append(p)

    def _deps(self, c, reads, writes):
        for b in reads:
            self._need(c, b.last_w, True)
        for b in writes:
            self._need(c, b.last_w, False)
            for r in b.readers:
                self._need(c, r, False)
        for b in reads:
            b.readers.append(c)
            if len(b.readers) > 12:
                comp = {}
                for ins in b.readers:
                    key = ins.dma_sem if ins.dma_sem is not None else ins.eng
                    cur = comp.get(key)
                    if cur is None or (ins.dma_val if ins.dma_sem is not None else ins.idx) >= (cur.dma_val if cur.dma_sem is not None else cur.idx):
                        comp[key] = ins
                b.readers = list(comp.values())
        for b in writes:
            b.last_w = c
            b.readers = []

    def op(self, eng, fn, reads=(), writes=()):
        c = Instr(eng, fn)
        c.idx = len(self.q[eng])
        self.q[eng].append(c)
        self._deps(c, reads, writes)
        return c

    NRING = 28

    def dma(self, eng, stream, out, in_, reads=(), writes=(), final=False):
        def fn(e, out=out, in_=in_):
            return e.dma_start(out=out, in_=in_)
        c = Instr(eng, fn)
        c.idx = len(self.q[eng])
        pos = self.ring_pos.get(eng, 0)
        self.ring_pos[eng] = pos + 1
        sem = f"{eng}{pos % self.NRING}"
        n = self.dma_count.get(sem, 0) + 1
        self.dma_count[sem] = n
        c.dma_sem = sem
        c.dma_val = 16 * n
        prev = self.ring_last.get(sem)
        if prev is not None:
            self._need(c, prev, False)
        self.ring_last[sem] = c
        self.q[eng].append(c)
        self._deps(c, reads, writes)
        if final:
            self.final_dma.append(c)
        return c

    def barrier(self):
        lasts = {}
        for e in ENGS:
            for ins in reversed(self.q[e]):
                if ins.dma_sem is None:
                    lasts[e] = ins
                    break
        dmas = []
        for sname, n in self.dma_count.items():
            f = Instr("sp", None)
            f.dma_sem = sname
            f.dma_val = 16 * n
            dmas.append(f)
        for e in ENGS:
            c = Instr(e, lambda eng: eng.nop())
            c.idx = len(self.q[e])
            for e2, p in lasts.items():
                if e2 != e:
                    self._need(c, p, True)
            for f in dmas:
                self._need(c, f, True)
            self.q[e].append(c)

    def emit(self):
        nc = self.nc
        import contextlib
        with contextlib.ExitStack() as st:
            esem = {e: st.enter_context(nc.semaphore(f"s_{e}")) for e in ENGS}
            dsem = {s: st.enter_context(nc.semaphore(f"d_{s}")) for s in self.dma_count}
            for e in ENGS:
                cnt = 0
                for ins in self.q[e]:
                    if ins.signal:
                        cnt += 1
                        ins.val = cnt
            block = st.enter_context(nc.Block())

            def run(eng_name, eobj):
                for ins in self.q[eng_name]:
                    for p in ins.waits:
                        if p.dma_sem is not None:
                            eobj.wait_ge(dsem[p.dma_sem], p.dma_val)
                        else:
                            eobj.wait_ge(esem[p.eng], p.val)
                    r = ins.fn(eobj)
                    if ins.dma_sem is not None:
                        r.then_inc(dsem[ins.dma_sem], 16)
                    elif ins.signal:
                        r.then_inc(esem[eng_name], 1)
                if eng_name == "sp":
                    done = {}
                    for ins in self.final_dma:
                        done[ins.dma_sem] = max(done.get(ins.dma_sem, 0), ins.dma_val)
                    for s, v in done.items():
                        eobj.wait_ge(dsem[s], v)

            @block.tensor
            def _(e):
                run("pe", e)

            @block.scalar
            def _(e):
                run("act", e)

            @block.vector
            def _(e):
                run("dve", e)

            @block.gpsimd
            def _(e):
                run("pool", e)

            @block.sync
            def _(e):
                run("sp", e)


T = 2048
D = 1024
NTT = 16
ALPHA = float(2.0 ** 0.25)
PI = float(np.pi)
LAMBDA_INIT = 0.8 - 0.6 * 1.0
FFN_H = 2816
NJ = 22


def t5_bucket_np(d):
    d = np.asarray(d, dtype=np.int64)
    df = np.maximum(d, 1).astype(np.float32)
    large = 16 + (np.log(df / np.float32(16)) / np.float32(np.log(128 / 16)) * np.float32(16)).astype(np.int32)
    large = np.minimum(large, 31)
    return np.where(d < 16, d, large)


def host_consts():
    c = {}
    c["c_ident"] = np.eye(128, dtype=np.float32)
    c["c_tri"] = np.triu(np.ones((128, 128), dtype=np.float32))
    cst = np.zeros((128, 8), dtype=np.float32)
    tp1 = np.arange(1, 129, dtype=np.float32)
    cst[:, 0] = tp1
    cst[:, 1] = -tp1
    cst[:, 2] = tp1 / np.float32(2 * np.pi)
    cst[:, 3] = 1.5
    cst[:, 4] = 1.75
    cst[:, 5] = -np.pi
    cst[:, 6] = 1e-5
    cst[:, 7] = 1.0
    c["c_cst"] = cst
    c["c_trow"] = np.broadcast_to(tp1[None, :], (128, 128)).astype(np.float32).copy()
    p = np.arange(128)
    mask = np.zeros((128, 4, 128), dtype=np.float32)
    for jm in range(4):
        mask[:, jm, :] = ((p[None, :] // 16) == (2 * jm + p[:, None] // 64)).astype(np.float32)
    c["c_maskC"] = mask
    c["c_maskB"] = np.ascontiguousarray(mask.transpose(2, 1, 0))
    oh = np.zeros((33, 384), dtype=np.float32)
    for m in range(384):
        d = m - 127
        if d < 0:
            oh[32, m] = -30000.0
        else:
            b = int(t5_bucket_np(d))
            oh[b, m] += 8.0
            oh[31, m] -= 8.0
    c["c_oh"] = oh
    return c


def build_program(debug=False, upto=None):
    import os
    nc = bass.Bass("TRN2", target_bir_lowering=False)
    S = Sched(nc)

    def din(name, shape):
        return nc.dram_tensor(name, list(shape), F32, kind="ExternalInput").ap()

    x = din("x", [T, D]); mem = din("mem", [256, D])
    ln_in_g = din("ln_in_g", [D]); ln_in_b = din("ln_in_b", [D])
    w_in = din("w_in", [D, 2048])
    lam_re = din("s5_lambda_re", [2048]); lam_im = din("s5_lambda_im", [2048]); log_dt = din("s5_log_dt", [32])
    b_re = din("s5_b_re", [2048, 16]); b_im = din("s5_b_im", [2048, 16])
    c_re = din("s5_c_re", [512, 64]); c_im = din("s5_c_im", [512, 64])
    s5_d = din("s5_d", [512]); glu_w = din("s5_glu_w", [512, 512]); glu_b = din("s5_glu_b", [512])
    lq1 = din("diff_lq1", [64]); lk1 = din("diff_lk1", [64]); lq2 = din("diff_lq2", [64]); lk2 = din("diff_lk2", [64])
    subln_g = din("diff_subln_g", [128]); rel_bias = din("rel_bias", [32, 4])
    w_out = din("w_out", [D, D]); ln1_g = din("ln1_g", [D]); ln1_b = din("ln1_b", [D])
    ca_wq = din("ca_wq", [D, D]); ca_wkv = din("ca_wkv", [D, 2 * D]); ca_wo = din("ca_wo", [D, D])
    ln2_g = din("ln2_g", [D]); ln2_b = din("ln2_b", [D])
    w_gu = din("ffn_w_gate_up", [D, 2 * FFN_H]); w_dn = din("ffn_w_down", [FFN_H, D])
    ln3_g = din("ln3_g", [D]); ln3_b = din("ln3_b", [D])
    c_ident = din("c_ident", [128, 128]); c_tri = din("c_tri", [128, 128]); c_cst = din("c_cst", [128, 8])
    c_trow = din("c_trow", [128, 128]); c_maskC = din("c_maskC", [128, 4, 128]); c_maskB = din("c_maskB", [128, 4, 128])
    c_oh = din("c_oh", [33, 384])
    out = nc.dram_tensor("out", [T, D], F32, kind="ExternalOutput").ap()
    scr = nc.dram_tensor("bias_scr", [128, 4, 384], F32, kind="Internal").ap()
    dbg = {}
    if debug:
        for nm, shp, dt_ in (("d_hT", [128, 8, T], BF16), ("d_uT", [128, 4, T], BF16), ("d_qT", [128, 4, T], BF16),
                             ("d_kT", [128, 4, T], BF16), ("d_v", [128, 16, 4, 129], BF16), ("d_cat", [128, 8, T], BF16),
                             ("d_h1", [128, 16, D], F32), ("d_h2", [128, 16, D], F32)):
            dbg[nm] = nc.dram_tensor(nm, shp, dt_, kind="ExternalOutput").ap()

    banks = [Ten(nc.alloc_psum_tensor(f"bank{i}", [128, 512], F32), f"bank{i}") for i in range(8)]

    class View:
        def __init__(self, ten):
            self.ten = ten
            self.ap = ten[:, :].bitcast(BF16).rearrange("p (k c) -> p k c", k=8)

        def __getitem__(self, k):
            return self.ap[k]

        def buf(self, key=0):
            return self.ten.buf(key)

    ptb = [View(banks[6]), View(banks[7])]

    idf = S.alloc("idf", [128], F32, 4)
    idb = S.alloc("idb", [128], BF16, 2)
    trib = S.alloc("trib", [128], BF16, 2)
    trif = S.alloc("trif", [128], F32, 4)
    onesb = S.alloc("onesb", [128], BF16, 2)
    onesf = S.alloc("onesf", [128], F32, 4)
    cst = S.alloc("cst", [8], F32, 4)
    stt = S.alloc("ln_st", [8, 12], F32, 4)
    mvt = S.alloc("ln_mv", [8, 2], F32, 4)
    rsd = S.alloc("ln_rs", [8, 1], F32, 4)
    S.dma("sp", "c0", idf[:, :], c_ident, writes=[idf.buf()])
    S.dma("sp", "c0", trif[:, :], c_tri, writes=[trif.buf()])
    S.dma("sp", "c0", cst[:, :], c_cst, writes=[cst.buf()])
    S.op("dve", lambda e: e.tensor_copy(out=idb[:, :], in_=idf[:, :]), reads=[idf.buf()], writes=[idb.buf()])
    S.op("dve", lambda e: e.tensor_copy(out=trib[:, :], in_=trif[:, :]), reads=[trif.buf()], writes=[trib.buf()])
    S.op("pool", lambda e: e.memset(onesb[:, :], 1.0), writes=[onesb.buf()])
    S.op("pool", lambda e: e.memset(onesf[:, :], 1.0), writes=[onesf.buf()])
    EPS = cst[:, 6:7]

    ctr = {"ev": 0, "ln": 0, "pt": 0}

    def evac_eng():
        ctr["ev"] += 1
        return "act" if ctr["ev"] % 2 else "dve"

    def copy_op(eng, out_ap, in_ap, reads, writes):
        if eng == "act":
            S.op("act", lambda e: e.activation(out=out_ap, in_=in_ap, func=AF.Copy), reads, writes)
        else:
            S.op(eng, lambda e: e.tensor_copy(out=out_ap, in_=in_ap), reads, writes)

    def mm(out_ap, lhsT, rhs, start, stop, reads, writes):
        S.op("pe", lambda e: e.matmul(out=out_ap, lhsT=lhsT, rhs=rhs, start=start, stop=stop), reads, writes)

    def load_lnp(name, g, b):
        t = S.alloc(name, [2, D], F32, 4)
        S.dma("sp", "lnp", t[:, 0, :], g.partition_broadcast(128), writes=[t.buf()])
        S.dma("sp", "lnp", t[:, 1, :], b.partition_broadcast(128), writes=[t.buf()])
        return t

    def ln_stats(x_ap, x_buf):
        ctr["ln"] += 1
        sl = ctr["ln"] % 8
        for c in range(2):
            S.op("dve", lambda e, c=c: e.bn_stats(out=stt[:, sl, c * 6:(c + 1) * 6], in_=x_ap[:, c * 512:(c + 1) * 512]),
                 reads=[x_buf], writes=[stt.buf(sl)])
        S.op("dve", lambda e: e.bn_aggr(out=mvt[:, sl, :], in_=stt[:, sl, :]), reads=[stt.buf(sl)], writes=[mvt.buf(sl)])
        S.op("act", lambda e: e.activation(out=rsd[:, sl, :], in_=mvt[:, sl, 1:2], func=AF.Sqrt, bias=EPS, scale=1.0),
             reads=[mvt.buf(sl), cst.buf()], writes=[rsd.buf(sl)])
        S.op("dve", lambda e: e.reciprocal(out=rsd[:, sl, :], in_=rsd[:, sl, :]), reads=[rsd.buf(sl)], writes=[rsd.buf(sl)])
        return sl

    def ln_apply(x_ap, x_buf, lnp, out_ap, out_buf, sl):
        S.op("dve", lambda e: e.scalar_tensor_tensor(out=mvt[:, sl, 1:2], in0=mvt[:, sl, 0:1], scalar=-1.0, in1=rsd[:, sl, :],
                                                      op0=ALU.mult, op1=ALU.mult),
             reads=[mvt.buf(sl), rsd.buf(sl)], writes=[mvt.buf(sl)])
        S.op("act", lambda e: e.activation(out=x_ap, in_=x_ap, func=AF.Identity, bias=mvt[:, sl, 1:2], scale=rsd[:, sl, 0:1]),
             reads=[x_buf, mvt.buf(sl), rsd.buf(sl)], writes=[x_buf])
        S.op("pool", lambda e: e.tensor_tensor(out=x_ap, in0=x_ap, in1=lnp[:, 0, :], op=ALU.mult), reads=[x_buf, lnp.buf()], writes=[x_buf])
        if out_ap is not None:
            ln_apply_b(x_ap, x_buf, lnp, out_ap, out_buf)

    def ln_apply_b(x_ap, x_buf, lnp, out_ap, out_buf):
        S.op("dve", lambda e: e.tensor_tensor(out=out_ap, in0=x_ap, in1=lnp[:, 1, :], op=ALU.add), reads=[x_buf, lnp.buf()],
             writes=[out_buf] if out_buf is not x_buf else [x_buf])

    def transpose_to_hT(lb, lb_buf, hT, tt):
        ctr["pt"] += 1
        pt = ptb[ctr["pt"] % 2]
        for k in range(8):
            S.op("pe", lambda e, k=k: e.transpose(out=pt[:, k, :], in_=lb[:, k * 128:(k + 1) * 128], identity=idb[:, :]),
                 reads=[lb_buf, idb.buf()], writes=[pt.buf()])
        copy_op(evac_eng(), hT[:, :, tt * 128:(tt + 1) * 128], pt[:, :, :], [pt.buf()], [hT.buf(tt)])

    def wload(name, src, kt, ncols, chunk, stream):
        t = S.alloc(name, [kt, ncols], BF16, 2)
        sv = src.rearrange("(kt p) n -> p kt n", p=128)
        for c in range(ncols // chunk):
            S.dma("pool", stream, t[:, :, c * chunk:(c + 1) * chunk], sv[:, :, c * chunk:(c + 1) * chunk], writes=[t.buf(c)])
        return t

    rb = S.alloc("rb", [4], F32, 4)
    S.dma("sp", "c1", rb[0:32, :], rel_bias, writes=[rb.buf()])
    chc = S.alloc("chc", [4], F32, 4)
    S.dma("sp", "c1", chc[:, :], rel_bias[31, :].partition_broadcast(128), writes=[chc.buf()])
    ohs = S.alloc("ohs", [384], F32, 4)
    S.dma("sp", "c1", ohs[0:33, :], c_oh, writes=[ohs.buf()])
    Lh = S.alloc("Lh", [4, 128], F32, 4)
    S.op("pool", lambda e: e.memset(Lh[0:33, :, :], 1.0), writes=[Lh.buf()])
    for h in range(4):
        S.op("dve", lambda e, h=h: e.tensor_scalar(out=Lh[0:32, h, :], in0=onesf[0:32, :], scalar1=rb[0:32, h:h + 1], scalar2=None, op0=ALU.mult),
             reads=[onesf.buf(), rb.buf(), Lh.buf()], writes=[Lh.buf()])
    gsb = S.alloc("gsb", [4, 384], F32, 4)
    for h in range(4):
        bk = banks[h % 4]
        mm(bk[:, 0:384], Lh[0:33, h, :], ohs[0:33, :], True, True, [Lh.buf(), ohs.buf()], [bk.buf()])
        copy_op("dve", gsb[:, h, :], bk[:, 0:384], [bk.buf()], [gsb.buf()])
    scrb = Ten(None, "scr")
    S.dma("pool", "scr", scr, gsb[:, :, :], reads=[gsb.buf()], writes=[scrb.buf()])
    biasf = S.alloc("biasf", [4, 2, 128], F32, 4)
    biasb = S.alloc("biasb", [4, 2, 128], BF16, 2)
    for h in range(4):
        for dsub, off in enumerate((127, 255)):
            S.dma("pool", "scr2", biasf[:, h, dsub, :], bass.AP(scr.tensor, h * 384 + off, [[1535, 128], [1, 128]]),
                  reads=[scrb.buf()], writes=[biasf.buf()])
    S.op("dve", lambda e: e.tensor_copy(out=biasb[:, :, :, :], in_=biasf[:, :, :, :]), reads=[biasf.buf()], writes=[biasb.buf()])
    lqk = S.alloc("lqk", [4, 64], F32, 4)
    for i, v in enumerate((lq1, lk1, lq2, lk2)):
        S.dma("sp", "c1", lqk[:, i, :], v.partition_broadcast(128), writes=[lqk.buf()])
    lsm = S.alloc("lsm", [4], F32, 4)
    S.op("dve", lambda e: e.tensor_mul(out=lqk[:, 0, :], in0=lqk[:, 0, :], in1=lqk[:, 1, :]), reads=[lqk.buf()], writes=[lqk.buf()])
    S.op("dve", lambda e: e.tensor_mul(out=lqk[:, 2, :], in0=lqk[:, 2, :], in1=lqk[:, 3, :]), reads=[lqk.buf()], writes=[lqk.buf()])
    S.op("dve", lambda e: e.reduce_sum(out=lsm[:, 0:1], in_=lqk[:, 0, :], axis=AX.X), reads=[lqk.buf()], writes=[lsm.buf()])
    S.op("dve", lambda e: e.reduce_sum(out=lsm[:, 1:2], in_=lqk[:, 2, :], axis=AX.X), reads=[lqk.buf()], writes=[lsm.buf()])
    S.op("act", lambda e: e.activation(out=lsm[:, 0:2], in_=lsm[:, 0:2], func=AF.Exp), reads=[lsm.buf()], writes=[lsm.buf()])
    S.op("dve", lambda e: e.tensor_sub(out=lsm[:, 2:3], in0=lsm[:, 1:2], in1=lsm[:, 0:1]), reads=[lsm.buf()], writes=[lsm.buf()])
    S.op("dve", lambda e: e.tensor_scalar(out=lsm[:, 2:3], in0=lsm[:, 2:3], scalar1=-LAMBDA_INIT, scalar2=None, op0=ALU.add),
         reads=[lsm.buf()], writes=[lsm.buf()])
    NEGLAM = lsm[:, 2:3]
    gsub = S.alloc("gsub", [128], F32, 4)
    S.dma("sp", "c1", gsub[:, :], subln_g.partition_broadcast(128), writes=[gsub.buf()])
    S.op("dve", lambda e: e.tensor_scalar(out=gsub[:, :], in0=gsub[:, :], scalar1=1.0 - LAMBDA_INIT, scalar2=None, op0=ALU.mult),
         reads=[gsub.buf()], writes=[gsub.buf()])

    hT = S.alloc("hT", [8, T], BF16, 2)
    wi = wload("wi", w_in, 8, 2048, 512, "w_a")
    lnp0 = load_lnp("lnp0", ln_in_g, ln_in_b)
    xin = S.alloc("xin", [4, D], F32, 4)
    lbt = S.alloc("lbt", [2, D], BF16, 2)
    sls = {}
    for tt in range(NTT + 3):
        if tt < NTT:
            s3 = tt % 4
            if tt % 2 == 0:
                S.dma("sp", "xin", xin[:, s3:s3 + 2, :], x[tt * 128:(tt + 2) * 128, :].rearrange("(a p) d -> p a d", p=128),
                      writes=[xin.buf(s3), xin.buf(s3 + 1)])
            sls[tt] = ln_stats(xin[:, s3, :], xin.buf(s3))
        if 1 <= tt <= NTT:
            t1 = tt - 1
            ln_apply(xin[:, t1 % 4, :], xin.buf(t1 % 4), lnp0, None, None, sls[t1])
        if 2 <= tt <= NTT + 1:
            t2 = tt - 2
            ln_apply_b(xin[:, t2 % 4, :], xin.buf(t2 % 4), lnp0, lbt[:, t2 % 2, :], lbt.buf(t2 % 2))
        if tt >= 3:
            t3 = tt - 3
            transpose_to_hT(lbt[:, t3 % 2, :], lbt.buf(t3 % 2), hT, t3)
    S.free("xin", "lnp0")
    if debug:
        S.dma("sp", "dbg", dbg["d_hT"], hT[:, :, :], reads=hT.bufs(range(16)), final=True)
    if upto == "A":
        S.emit()
        return nc


    if str(0) in os.environ.get("BARRIERS", ""):
        S.barrier()
    uT = S.alloc("uT", [4, T], BF16, 2)
    qT = S.alloc("qT", [4, T], BF16, 2)
    kT = S.alloc("kT", [4, T], BF16, 2)
    vaug = S.alloc("vaug", [16, 4, 129], BF16, 2)
    import os
    for tt in range(NTT):
        S.op("dve", lambda e, tt=tt: e.tensor_copy(out=vaug[:, tt, :, 128:129], in_=onesb[:, 0:4].unsqueeze(2)), reads=[onesb.buf()], writes=[vaug.buf(tt)])
    bi = 0
    for grp, dst in enumerate((uT, qT, kT)):
        for ct in range(4):
            col = grp * 512 + ct * 128
            for tb in range(4):
                bk = banks[bi % 4]; bi += 1
                for kt in range(8):
                    mm(bk[:, :], wi[:, kt, col:col + 128], hT[:, kt, tb * 512:(tb + 1) * 512], kt == 0, kt == 7,
                       [wi.buf(grp)] + hT.bufs(range(4 * tb, 4 * tb + 4)), [bk.buf()])
                copy_op(evac_eng(), dst[:, ct, tb * 512:(tb + 1) * 512], bk[:, :], [bk.buf()], dst.bufs([(ct, 4 * tb + i) for i in range(4)]))
    for tt in range(0 if not os.environ.get("NO_V") else NTT, NTT):
        bk = banks[bi % 4]; bi += 1
        for kt in range(8):
            mm(bk[:, :], hT[:, kt, tt * 128:(tt + 1) * 128], wi[:, kt, 1536:2048], kt == 0, kt == 7,
               [wi.buf(3), hT.buf(tt)], [bk.buf()])
        copy_op(evac_eng(), vaug[:, tt, :, 0:128], bk[:, :].rearrange("p (h d) -> p h d", h=4), [bk.buf()], [vaug.buf(tt)])
    S.free("wi", "lbt", "hT")
    if debug:
        S.dma("sp", "dbg", dbg["d_uT"], uT[:, :, :], reads=uT.bufs([(c, t) for c in range(4) for t in range(16)]), final=True)
        S.dma("sp", "dbg", dbg["d_qT"], qT[:, :, :], reads=qT.bufs([(c, t) for c in range(4) for t in range(16)]), final=True)
        S.dma("sp", "dbg", dbg["d_kT"], kT[:, :, :], reads=kT.bufs([(c, t) for c in range(4) for t in range(16)]), final=True)
        S.dma("sp", "dbg", dbg["d_v"], vaug[:, :, :, :], reads=vaug.bufs(range(16)), final=True)
    if upto == "B":
        S.emit()
        return nc


    catT = S.alloc("catT", [8, T], BF16, 2)

    if str(1) in os.environ.get("BARRIERS", ""):
        S.barrier()
    PT = S.alloc("PT", [6, 512], BF16, 2)
    gcol = S.alloc("gcol", [1], F32, 4)
    S.dma("sp", "c1", gcol[:, :], subln_g.rearrange("(p o) -> p o", o=1), writes=[gcol.buf()])
    S.op("dve", lambda e: e.tensor_scalar(out=gcol[:, :], in0=gcol[:, :], scalar1=1.0 - LAMBDA_INIT, scalar2=None, op0=ALU.mult),
         reads=[gcol.buf()], writes=[gcol.buf()])
    rr = S.alloc("rr", [2, 2, 512], F32, 4)
    o1 = S.alloc("o1", [2, 512], F32, 4)
    oo = S.alloc("oo", [2, 512], F32, 4)
    sqb = S.alloc("sqb", [2, 512], BF16, 2)
    rst = S.alloc("rst", [2, 512], F32, 4)
    stb = (banks[0], banks[1])
    Ab = (banks[2], banks[3])
    Sb = (banks[4], banks[5])
    MSb = banks[6]

    def s_stage(it):
        h, I, s, j, st, pt = it
        r0 = s * 64
        qstart = max(512 * I, 128 * j)
        N = 512 * (I + 1) - qstart
        col0 = qstart - 512 * I
        has_diag = j >= 4 * I
        has_sub = (4 * I - 1) <= j <= (4 * I + 2)
        mm(st[:, col0:col0 + N], kT[r0:r0 + 64, h, j * 128:(j + 1) * 128], qT[r0:r0 + 64, h, qstart:qstart + N],
           True, not (has_diag or has_sub),
           [kT.buf((h, j))] + qT.bufs([(h, t) for t in range(qstart // 128, 4 * I + 4)]), [st.buf()])
        if has_diag:
            c = j * 128 - 512 * I
            mm(st[:, c:c + 128], idb[:, :], biasb[:, h, 0, :], False, not has_sub, [idb.buf(), biasb.buf()], [st.buf()])
        if has_sub:
            c = (j + 1) * 128 - 512 * I
            mm(st[:, c:c + 128], idb[:, :], biasb[:, h, 1, :], False, True, [idb.buf(), biasb.buf()], [st.buf()])
        S.op("act", lambda e, st=st, pt=pt, col0=col0, N=N, h=h: e.activation(
            out=PT[:, pt, col0:col0 + N], in_=st[:, col0:col0 + N], func=AF.Exp, bias=chc[:, h:h + 1], scale=0.125),
            reads=[st.buf(), chc.buf()], writes=[PT.buf(pt)])

    def pv_stage(it):
        h, I, s, j, st, pt = it
        qstart = max(512 * I, 128 * j)
        N = 512 * (I + 1) - qstart
        col0 = qstart - 512 * I
        last = (j == 4 * I + 3)
        S.op("pe", lambda e, s=s, pt=pt, col0=col0, N=N, j=j, h=h, last=last: e.matmul(
            out=Ab[s][:, col0:col0 + N], lhsT=vaug[:, j, h, 0:128], rhs=PT[:, pt, col0:col0 + N], start=(j == 0), stop=last, skip_group_check=True),
            [PT.buf(pt), vaug.buf(j)], [Ab[s].buf()])
        S.op("pe", lambda e, s=s, pt=pt, col0=col0, N=N, j=j, last=last: e.matmul(
            out=Sb[s][:, col0:col0 + N], lhsT=onesb[:, :], rhs=PT[:, pt, col0:col0 + N], start=(j == 0), stop=last, skip_group_check=True),
            [PT.buf(pt), onesb.buf()], [Sb[s].buf()])

    def epilogue_a(h, I, rnd):
        r2 = rnd % 2
        for s in range(2):
            S.op("dve", lambda e, s=s, r2=r2: e.reciprocal(out=rr[:, r2, s, :], in_=Sb[s][:, :]), reads=[Sb[s].buf()], writes=[rr.buf((r2, s))])
        tt_op("dve", o1[:, r2, :], Ab[0][:, :], rr[:, r2, 0, :], ALU.mult, [Ab[0].buf(), rr.buf((r2, 0))], [o1.buf(r2)])
        tt_op("dve", oo[:, r2, :], Ab[1][:, :], rr[:, r2, 1, :], ALU.mult, [Ab[1].buf(), rr.buf((r2, 1))], [oo.buf(r2)])
        S.op("dve", lambda e, r2=r2: e.scalar_tensor_tensor(out=oo[:, r2, :], in0=oo[:, r2, :], scalar=NEGLAM, in1=o1[:, r2, :], op0=ALU.mult, op1=ALU.add),
             reads=[oo.buf(r2), o1.buf(r2), lsm.buf()], writes=[oo.buf(r2)])
        tt_op("dve", sqb[:, r2, :], oo[:, r2, :], oo[:, r2, :], ALU.mult, [oo.buf(r2)], [sqb.buf(r2)])

    def epilogue_b(h, I, rnd):
        r2 = rnd % 2
        mm(MSb[:, :], onesb[:, :], sqb[:, r2, :], True, True, [onesb.buf(), sqb.buf(r2)], [MSb.buf()])
        S.op("act", lambda e, r2=r2: e.activation(out=rst[:, r2, :], in_=MSb[:, :], func=AF.Ln, bias=EPS, scale=1.0 / 128.0),
             reads=[MSb.buf(), cst.buf()], writes=[rst.buf(r2)])
        S.op("act", lambda e, r2=r2: e.activation(out=rst[:, r2, :], in_=rst[:, r2, :], func=AF.Exp, scale=-0.5), reads=[rst.buf(r2)], writes=[rst.buf(r2)])
        S.op("dve", lambda e, r2=r2, h=h, I=I: e.scalar_tensor_tensor(out=catT[:, 4 + h, I * 512:(I + 1) * 512], in0=oo[:, r2, :], scalar=gcol[:, 0:1], in1=rst[:, r2, :],
                                                                    op0=ALU.mult, op1=ALU.mult),
             reads=[oo.buf(r2), gcol.buf(), rst.buf(r2)], writes=catT.bufs([(4 + h, 4 * I + i) for i in range(4)]))

    def tt_op(eng, out_ap, a_ap, b_ap, op, reads, writes):
        S.op(eng, lambda e: e.tensor_tensor(out=out_ap, in0=a_ap, in1=b_ap, op=op), reads, writes)

    stb4 = (banks[0], banks[1], banks[7], banks[6])
    iters = []
    k = 0
    for h in range(4):
        for I in range(4):
            for j in range(4 * I + 4):
                for s in range(2):
                    iters.append((h, I, s, j, stb4[k % 4], k % 6))
                    k += 1
    npair = len(iters) // 2
    pending = None
    since = 0
    rnd = 0
    for p in range(npair):
        s_stage(iters[2 * p]); s_stage(iters[2 * p + 1])
        since += 1
        if pending is not None and since >= 2:
            epilogue_b(*pending); pending = None
        if p >= 1:
            pv_stage(iters[2 * p - 2]); pv_stage(iters[2 * p - 1])
            prev, it = iters[2 * p - 1], iters[2 * p]
            if (prev[0], prev[1]) != (it[0], it[1]):
                if pending is not None:
                    epilogue_b(*pending); pending = None
                epilogue_a(prev[0], prev[1], rnd)
                pending = (prev[0], prev[1], rnd); since = 0
                rnd += 1
    pv_stage(iters[-2]); pv_stage(iters[-1])
    if pending is not None:
        epilogue_b(*pending)
    epilogue_a(iters[-1][0], iters[-1][1], rnd)
    epilogue_b(iters[-1][0], iters[-1][1], rnd)
    S.free("rb", "chc", "ohs", "Lh", "gsb", "biasf", "biasb", "lqk", "lsm", "gsub", "PT", "qT", "kT", "vaug", "gcol", "rr", "o1", "oo", "sqb", "rst")
    if upto == "D":
        S.emit()
        return nc


    if str(2) in os.environ.get("BARRIERS", ""):
        S.barrier()
    I32 = mybir.dt.int32
    W2 = 2048

    def A8(name, dt_=F32):
        return S.alloc(name, [W2], dt_, 4)

    def tt_op(eng, out_ap, a_ap, b_ap, op, reads, writes):
        S.op(eng, lambda e: e.tensor_tensor(out=out_ap, in0=a_ap, in1=b_ap, op=op), reads, writes)

    def ts_op(eng, out_ap, a_ap, s1, s2, op0, op1, reads, writes):
        if s2 is None:
            S.op(eng, lambda e: e.tensor_scalar(out=out_ap, in0=a_ap, scalar1=s1, scalar2=None, op0=op0), reads, writes)
        else:
            S.op(eng, lambda e: e.tensor_scalar(out=out_ap, in0=a_ap, scalar1=s1, scalar2=s2, op0=op0, op1=op1), reads, writes)

    ki = A8("ki", I32)
    kf = A8("kf")

    def sin_from_u(u, out):
        S.op("dve", lambda e: e.tensor_copy(out=ki[:, :], in_=u[:, :]), reads=[u.buf()], writes=[ki.buf()])
        S.op("dve", lambda e: e.tensor_copy(out=kf[:, :], in_=ki[:, :]), reads=[ki.buf()], writes=[kf.buf()])
        tt_op("dve", u[:, :], u[:, :], kf[:, :], ALU.subtract, [u.buf(), kf.buf()], [u.buf()])
        S.op("dve", lambda e: e.scalar_tensor_tensor(out=u[:, :], in0=u[:, :], scalar=0.0, in1=u[:, :], op0=ALU.is_lt, op1=ALU.add),
             reads=[u.buf()], writes=[u.buf()])
        S.op("act", lambda e: e.activation(out=out[:, :], in_=u[:, :], func=AF.Sin, bias=cst[:, 5:6], scale=2 * PI),
             reads=[u.buf(), cst.buf()], writes=[out.buf()])

    lr = A8("lr"); li = A8("li"); lrdt = A8("lrdt"); ang = A8("ang")
    dtr = S.alloc("dtr", [32], F32, 4)
    S.dma("sp", "c2", lr[:, :], lam_re.partition_broadcast(128), writes=[lr.buf()])
    S.dma("sp", "c2", li[:, :], lam_im.partition_broadcast(128), writes=[li.buf()])
    S.dma("sp", "c2", dtr[:, :], log_dt.partition_broadcast(128), writes=[dtr.buf()])
    S.op("act", lambda e: e.activation(out=dtr[:, :], in_=dtr[:, :], func=AF.Exp), reads=[dtr.buf()], writes=[dtr.buf()])
    dt_b = dtr[:, :].unsqueeze(2).to_broadcast([128, 32, 64])
    v3 = lambda t: t[:, :].rearrange("p (g s) -> p g s", s=64)
    tt_op("dve", v3(lrdt), v3(lr), dt_b, ALU.mult, [lr.buf(), dtr.buf()], [lrdt.buf()])
    tt_op("dve", v3(ang), v3(li), dt_b, ALU.mult, [li.buf(), dtr.buf()], [ang.buf()])
    mg = A8("mg"); sn = A8("sn"); cs = A8("cs"); ua = A8("ua"); fre = A8("fre"); fim = A8("fim")
    S.op("act", lambda e: e.activation(out=mg[:, :], in_=lrdt[:, :], func=AF.Exp), reads=[lrdt.buf()], writes=[mg.buf()])
    ts_op("dve", ua[:, :], ang[:, :], 1.0 / (2 * PI), 1.5, ALU.mult, ALU.add, [ang.buf()], [ua.buf()])
    sin_from_u(ua, sn)
    ts_op("dve", ua[:, :], ang[:, :], 1.0 / (2 * PI), 1.75, ALU.mult, ALU.add, [ang.buf()], [ua.buf()])
    sin_from_u(ua, cs)
    tt_op("dve", cs[:, :], mg[:, :], cs[:, :], ALU.mult, [mg.buf(), cs.buf()], [cs.buf()])
    ts_op("dve", cs[:, :], cs[:, :], -1.0, None, ALU.add, None, [cs.buf()], [cs.buf()])
    tt_op("dve", sn[:, :], mg[:, :], sn[:, :], ALU.mult, [mg.buf(), sn.buf()], [sn.buf()])
    tt_op("dve", mg[:, :], lr[:, :], lr[:, :], ALU.mult, [lr.buf()], [mg.buf()])
    tt_op("dve", kf[:, :], li[:, :], li[:, :], ALU.mult, [li.buf()], [kf.buf()])
    tt_op("dve", mg[:, :], mg[:, :], kf[:, :], ALU.add, [mg.buf(), kf.buf()], [mg.buf()])
    S.op("dve", lambda e: e.reciprocal(out=mg[:, :], in_=mg[:, :]), reads=[mg.buf()], writes=[mg.buf()])
    tt_op("dve", fre[:, :], cs[:, :], lr[:, :], ALU.mult, [cs.buf(), lr.buf()], [fre.buf()])
    tt_op("dve", kf[:, :], sn[:, :], li[:, :], ALU.mult, [sn.buf(), li.buf()], [kf.buf()])
    tt_op("dve", fre[:, :], fre[:, :], kf[:, :], ALU.add, [fre.buf(), kf.buf()], [fre.buf()])
    tt_op("dve", fre[:, :], fre[:, :], mg[:, :], ALU.mult, [fre.buf(), mg.buf()], [fre.buf()])
    tt_op("dve", fim[:, :], sn[:, :], lr[:, :], ALU.mult, [sn.buf(), lr.buf()], [fim.buf()])
    tt_op("dve", kf[:, :], cs[:, :], li[:, :], ALU.mult, [cs.buf(), li.buf()], [kf.buf()])
    tt_op("dve", fim[:, :], fim[:, :], kf[:, :], ALU.subtract, [fim.buf(), kf.buf()], [fim.buf()])
    tt_op("dve", fim[:, :], fim[:, :], mg[:, :], ALU.mult, [fim.buf(), mg.buf()], [fim.buf()])
    if os.environ.get("S5_STOP") == "1":
        S.emit()
        return nc
    S.free("lr", "li", "dtr")
    Wmr = A8("Wmr"); Wmi = A8("Wmi")
    S.op("act", lambda e: e.activation(out=mg[:, :], in_=lrdt[:, :], func=AF.Exp, scale=cst[:, 1:2]), reads=[lrdt.buf(), cst.buf()], writes=[mg.buf()])
    ts_op("dve", ua[:, :], ang[:, :], cst[:, 2:3], cst[:, 3:4], ALU.mult, ALU.add, [ang.buf(), cst.buf()], [ua.buf()])
    sin_from_u(ua, sn)
    ts_op("dve", ua[:, :], ang[:, :], cst[:, 2:3], cst[:, 4:5], ALU.mult, ALU.add, [ang.buf(), cst.buf()], [ua.buf()])
    sin_from_u(ua, cs)
    tt_op("dve", Wmr[:, :], mg[:, :], cs[:, :], ALU.mult, [mg.buf(), cs.buf()], [Wmr.buf()])
    S.op("dve", lambda e: e.scalar_tensor_tensor(out=Wmi[:, :], in0=mg[:, :], scalar=-1.0, in1=sn[:, :], op0=ALU.mult, op1=ALU.mult),
         reads=[mg.buf(), sn.buf()], writes=[Wmi.buf()])
    if os.environ.get("S5_STOP") == "2":
        S.emit()
        return nc
    trow = S.alloc("trow", [128], F32, 4)
    S.dma("sp", "c2", trow[:, :], c_trow, writes=[trow.buf()])
    lrdtT = A8("lrdtT"); angT = A8("angT")
    bi = 0
    for (src, dst) in ((lrdt, lrdtT), (ang, angT)):
        for q4 in range(4):
            bk = banks[bi % 4]; bi += 1
            for i in range(4):
                j = q4 * 4 + i
                S.op("pe", lambda e, bk=bk, i=i, j=j, src=src: e.transpose(out=bk[:, i * 128:(i + 1) * 128], in_=src[:, j * 128:(j + 1) * 128], identity=idf[:, :]),
                     reads=[src.buf(), idf.buf()], writes=[bk.buf()])
            copy_op("dve", dst[:, q4 * 512:(q4 + 1) * 512], bk[:, :], [bk.buf()], [dst.buf()])
    S.free("lrdt", "ang")
    WpTr = A8("WpTr"); WpTi = A8("WpTi")
    trow_b = trow[:, :].unsqueeze(1).to_broadcast([128, 16, 128])
    v16 = lambda t: t[:, :].rearrange("p (j t) -> p j t", t=128)
    tt_op("dve", v16(lrdtT), v16(lrdtT), trow_b, ALU.mult, [lrdtT.buf(), trow.buf()], [lrdtT.buf()])
    S.op("act", lambda e: e.activation(out=mg[:, :], in_=lrdtT[:, :], func=AF.Exp), reads=[lrdtT.buf()], writes=[mg.buf()])
    tt_op("dve", v16(angT), v16(angT), trow_b, ALU.mult, [angT.buf(), trow.buf()], [angT.buf()])
    ts_op("dve", ua[:, :], angT[:, :], 1.0 / (2 * PI), 1.5, ALU.mult, ALU.add, [angT.buf()], [ua.buf()])
    sin_from_u(ua, sn)
    ts_op("dve", ua[:, :], angT[:, :], 1.0 / (2 * PI), 1.75, ALU.mult, ALU.add, [angT.buf()], [ua.buf()])
    sin_from_u(ua, cs)
    tt_op("dve", WpTr[:, :], mg[:, :], cs[:, :], ALU.mult, [mg.buf(), cs.buf()], [WpTr.buf()])
    tt_op("dve", WpTi[:, :], mg[:, :], sn[:, :], ALU.mult, [mg.buf(), sn.buf()], [WpTi.buf()])
    S.free("lrdtT", "angT", "ua", "ki", "kf", "mg", "trow")
    if os.environ.get("S5_STOP") == "3":
        S.emit()
        return nc
    maskB = S.alloc("maskB", [4, 128], F32, 4)
    maskC = S.alloc("maskC", [4, 128], F32, 4)
    S.dma("sp", "c2", maskB[:, :, :], c_maskB, writes=[maskB.buf()])
    S.dma("sp", "c2", maskC[:, :, :], c_maskC, writes=[maskC.buf()])
    bnat = S.alloc("bnat", [2, 16, 16], F32, 4)
    S.dma("sp", "c2", bnat[:, 0, :, :], b_re.rearrange("(j p) h -> p j h", p=128), writes=[bnat.buf()])
    S.dma("sp", "c2", bnat[:, 1, :, :], b_im.rearrange("(j p) h -> p j h", p=128), writes=[bnat.buf()])
    bn8 = S.alloc("bn8", [2, 16, 8, 16], F32, 4)
    for ri in range(2):
        S.op("dve", lambda e, ri=ri: e.tensor_copy(out=bn8[:, ri, :, :, :], in_=bnat[:, ri, :, :].unsqueeze(2).to_broadcast([128, 16, 8, 16])),
             reads=[bnat.buf()], writes=[bn8.buf()])
    Bmr = S.alloc("Bmr", [4, 512], BF16, 2)
    Bmi = S.alloc("Bmi", [4, 512], BF16, 2)
    tq = S.alloc("tq", [4, 128], F32, 4)
    for j in range(16):
        ctile, jm = j // 4, j % 4
        bk = banks[j % 4]
        for ri in range(2):
            S.op("pe", lambda e, bk=bk, ri=ri, j=j: e.transpose(out=bk[:, ri * 128:(ri + 1) * 128],
                                                             in_=bn8[:, ri, j, :, :].rearrange("p c h -> p (c h)"), identity=idf[:, :]),
                 reads=[bn8.buf(), idf.buf()], writes=[bk.buf()])
        BTr, BTi = bk[:, 0:128], bk[:, 128:256]
        fr, fi = fre[:, j * 128:(j + 1) * 128], fim[:, j * 128:(j + 1) * 128]
        rd = [bk.buf(), fre.buf(), fim.buf()]
        tt_op("dve", tq[:, 0, :], BTr, fr, ALU.mult, rd, [tq.buf()])
        tt_op("dve", tq[:, 1, :], BTi, fi, ALU.mult, rd, [tq.buf()])
        tt_op("dve", tq[:, 2, :], BTr, fi, ALU.mult, rd, [tq.buf()])
        tt_op("dve", tq[:, 3, :], BTi, fr, ALU.mult, rd, [tq.buf()])
        tt_op("dve", tq[:, 0, :], tq[:, 0, :], tq[:, 1, :], ALU.subtract, [tq.buf()], [tq.buf()])
        tt_op("dve", tq[:, 2, :], tq[:, 2, :], tq[:, 3, :], ALU.add, [tq.buf()], [tq.buf()])
        tt_op("dve", Bmr[:, ctile, jm * 128:(jm + 1) * 128], tq[:, 0, :], maskB[:, jm, :], ALU.mult, [tq.buf(), maskB.buf()], [Bmr.buf()])
        tt_op("dve", Bmi[:, ctile, jm * 128:(jm + 1) * 128], tq[:, 2, :], maskB[:, jm, :], ALU.mult, [tq.buf(), maskB.buf()], [Bmi.buf()])
    S.free("bnat", "bn8", "fre", "fim")
    if os.environ.get("S5_STOP") == "4":
        S.emit()
        return nc
    cnat = S.alloc("cnat", [2, 4, 64], F32, 4)
    S.dma("sp", "c2", cnat[:, 0, :, :], c_re.rearrange("(ct p) s -> p ct s", p=128), writes=[cnat.buf()])
    S.dma("sp", "c2", cnat[:, 1, :, :], c_im.rearrange("(ct p) s -> p ct s", p=128), writes=[cnat.buf()])
    cn2 = S.alloc("cn2", [2, 4, 2, 64], F32, 4)
    for ri in range(2):
        S.op("dve", lambda e, ri=ri: e.tensor_copy(out=cn2[:, ri, :, :, :], in_=cnat[:, ri, :, :].unsqueeze(2).to_broadcast([128, 4, 2, 64])),
             reads=[cnat.buf()], writes=[cn2.buf()])
    Cmr = S.alloc("Cmr", [16, 128], BF16, 2)
    Cmi = S.alloc("Cmi", [16, 128], BF16, 2)
    for ctile in range(4):
        bk = banks[ctile % 4]
        for ri in range(2):
            S.op("pe", lambda e, bk=bk, ri=ri, ctile=ctile: e.transpose(out=bk[:, ri * 128:(ri + 1) * 128],
                                                                    in_=cn2[:, ri, ctile, :, :].rearrange("p c s -> p (c s)"), identity=idf[:, :]),
                 reads=[cn2.buf(), idf.buf()], writes=[bk.buf()])
        for jm in range(4):
            j = ctile * 4 + jm
            tt_op("dve", Cmr[:, j, :], bk[:, 0:128], maskC[:, jm, :], ALU.mult, [bk.buf(), maskC.buf()], [Cmr.buf()])
            S.op("dve", lambda e, bk=bk, j=j, jm=jm: e.scalar_tensor_tensor(out=Cmi[:, j, :], in0=bk[:, 128:256], scalar=-1.0, in1=maskC[:, jm, :],
                                                                         op0=ALU.mult, op1=ALU.mult),
                 reads=[bk.buf(), maskC.buf()], writes=[Cmi.buf()])
    S.free("cnat", "cn2", "maskB", "maskC", "tq", "sn", "cs")
    if os.environ.get("S5_STOP") == "5":
        S.emit()
        return nc
    dnat = S.alloc("dnat", [2, 128], F32, 4)
    S.dma("sp", "c2", dnat[0:4, 0, :], s5_d.rearrange("(ct p) -> ct p", p=128), writes=[dnat.buf()])
    S.dma("sp", "c2", dnat[0:4, 1, :], glu_b.rearrange("(ct p) -> ct p", p=128), writes=[dnat.buf()])
    dcol = S.alloc("dcol", [2, 4], F32, 4)
    bk = banks[0]
    for i in range(2):
        S.op("pe", lambda e, i=i, bk=bk: e.transpose(out=bk[:, i * 4:(i + 1) * 4], in_=dnat[0:4, i, :], identity=idf[0:4, 0:4]),
             reads=[dnat.buf(), idf.buf()], writes=[bk.buf()])
    copy_op("dve", dcol[:, 0, :], bk[:, 0:4], [bk.buf()], [dcol.buf()])
    copy_op("dve", dcol[:, 1, :], bk[:, 4:8], [bk.buf()], [dcol.buf()])
    S.free("dnat")
    if os.environ.get("S5_STOP") == "6":
        S.emit()
        return nc
    gw = wload("gw", glu_w, 4, 512, 512, "w_s5")

    if os.environ.get("S5_STOP") == "7":
        S.emit()
        return nc
    if str(3) in os.environ.get("BARRIERS", ""):
        S.barrier()
    zb = S.alloc("zb", [2, 2, W2], BF16, 2)
    tm = S.alloc("tm", [2, 4, 512], F32, 4)
    td = S.alloc("td", [2, 4, 512], F32, 4)
    wc = S.alloc("wc", [2, 2, 512], F32, 4)
    xbf = S.alloc("xbf", [2, 16, 128], BF16, 2)
    car = S.alloc("car", [16, 2], F32, 4)
    ypre = S.alloc("ypre", [2, 4, 512], F32, 4)
    S.op("pool", lambda e: e.memset(car[:, :, :], 0.0), writes=[car.buf(g) for g in range(4)])
    gl = S.alloc("gl", [4, 512], F32, 4)
    glb = S.alloc("glb", [4, 512], BF16, 2)
    g1 = S.alloc("g1", [2, 512], F32, 4)
    bR, bI = banks[0], banks[1]
    wbk = (banks[2], banks[3])
    ybk = banks[4]
    gbk = banks[5]
    mi = 0
    di = 0
    for c in range(int(os.environ.get("S5_CHUNKS", NTT))):
        zs = c % 2
        if os.environ.get("S5_PART") == "1" and c == 0:
            pass
        for ctile in range(4):
            mm(bR[:, :], uT[:, ctile, c * 128:(c + 1) * 128], Bmr[:, ctile, :], True, True, [uT.buf((ctile, c)), Bmr.buf()], [bR.buf()])
            mm(bI[:, :], uT[:, ctile, c * 128:(c + 1) * 128], Bmi[:, ctile, :], True, True, [uT.buf((ctile, c)), Bmi.buf()], [bI.buf()])
            ms = mi % 2; mi += 1
            blk = slice(ctile * 512, (ctile + 1) * 512)
            tt_op("dve", tm[:, ms, 0, :], bR[:, :], Wmr[:, blk], ALU.mult, [bR.buf(), Wmr.buf()], [tm.buf((ms, 0))])
            tt_op("dve", tm[:, ms, 1, :], bI[:, :], Wmi[:, blk], ALU.mult, [bI.buf(), Wmi.buf()], [tm.buf((ms, 1))])
            tt_op("dve", tm[:, ms, 2, :], bR[:, :], Wmi[:, blk], ALU.mult, [bR.buf(), Wmi.buf()], [tm.buf((ms, 2))])
            tt_op("dve", tm[:, ms, 3, :], bI[:, :], Wmr[:, blk], ALU.mult, [bI.buf(), Wmr.buf()], [tm.buf((ms, 3))])
            tt_op("pool", zb[:, zs, 0, blk], tm[:, ms, 0, :], tm[:, ms, 1, :], ALU.subtract, [tm.buf((ms, 0)), tm.buf((ms, 1))], [zb.buf((zs, 0, ctile))])
            tt_op("pool", zb[:, zs, 1, blk], tm[:, ms, 2, :], tm[:, ms, 3, :], ALU.add, [tm.buf((ms, 2)), tm.buf((ms, 3))], [zb.buf((zs, 1, ctile))])
        if os.environ.get("S5_PART") == "1":
            continue
        for g4 in range(4):
            WR, WI = (banks[2], banks[3]) if g4 % 2 == 0 else (banks[6], banks[7])
            for jj in range(4):
                j = 4 * g4 + jj
                mm(WR[:, jj * 128:(jj + 1) * 128], zb[:, zs, 0, j * 128:(j + 1) * 128], trib[:, :], True, True, [zb.buf((zs, 0, j // 4)), trib.buf()], [WR.buf()])
                mm(WI[:, jj * 128:(jj + 1) * 128], zb[:, zs, 1, j * 128:(j + 1) * 128], trib[:, :], True, True, [zb.buf((zs, 1, j // 4)), trib.buf()], [WI.buf()])
            ds = di % 2; di += 1
            for jj in range(4):
                j = 4 * g4 + jj
                S.op("act", lambda e, WR=WR, ds=ds, jj=jj, j=j: e.activation(out=wc[:, ds, 0, jj * 128:(jj + 1) * 128], in_=WR[:, jj * 128:(jj + 1) * 128],
                                                                          func=AF.Identity, bias=car[:, j, 0:1], scale=1.0),
                     reads=[WR.buf(), car.buf(g4)], writes=[wc.buf((ds, 0))])
                S.op("act", lambda e, WI=WI, ds=ds, jj=jj, j=j: e.activation(out=wc[:, ds, 1, jj * 128:(jj + 1) * 128], in_=WI[:, jj * 128:(jj + 1) * 128],
                                                                          func=AF.Identity, bias=car[:, j, 1:2], scale=1.0),
                     reads=[WI.buf(), car.buf(g4)], writes=[wc.buf((ds, 1))])
            gcols = slice(g4 * 512, (g4 + 1) * 512)
            pr, pi_ = WpTr[:, gcols], WpTi[:, gcols]
            wr_, wi_ = wc[:, ds, 0, :], wc[:, ds, 1, :]
            for k, (w_, p_, wk) in enumerate(((wr_, pr, 0), (wi_, pi_, 1), (wr_, pi_, 0), (wi_, pr, 1))):
                tt_op("dve", td[:, ds, k, :], w_, p_, ALU.mult, [wc.buf((ds, wk)), WpTr.buf(), WpTi.buf()], [td.buf((ds, k))])
            xr_out = xbf[:, 0, 4 * g4:4 * g4 + 4, :].rearrange("p j t -> p (j t)")
            xi_out = xbf[:, 1, 4 * g4:4 * g4 + 4, :].rearrange("p j t -> p (j t)")
            tt_op("dve", xr_out, td[:, ds, 0, :], td[:, ds, 1, :], ALU.subtract, [td.buf((ds, 0)), td.buf((ds, 1))], xbf.bufs([(0, 4 * g4 + i) for i in range(4)]))
            tt_op("dve", xi_out, td[:, ds, 2, :], td[:, ds, 3, :], ALU.add, [td.buf((ds, 2)), td.buf((ds, 3))], xbf.bufs([(1, 4 * g4 + i) for i in range(4)]))
            l127 = lambda k, ds=ds: td[:, ds, k, :].rearrange("p (j t) -> p j t", t=128)[:, :, 127]
            tt_op("dve", car[:, 4 * g4:4 * g4 + 4, 0], l127(0), l127(1), ALU.subtract, [td.buf((ds, 0)), td.buf((ds, 1))], [car.buf(g4)])
            tt_op("dve", car[:, 4 * g4:4 * g4 + 4, 1], l127(2), l127(3), ALU.add, [td.buf((ds, 2)), td.buf((ds, 3))], [car.buf(g4)])
        if os.environ.get("S5_PART") == "2":
            continue
        ys = (c // 4) % 2
        for ctile in range(4):
            ybk = banks[4 + ctile % 2]
            ya = ybk[:, 0:128]
            n = 0
            for jm in range(4):
                j = ctile * 4 + jm
                mm(ya, Cmr[:, j, :], xbf[:, 0, j, :], n == 0, False, [Cmr.buf(), xbf.buf((0, j))], [ybk.buf()]); n += 1
                mm(ya, Cmi[:, j, :], xbf[:, 1, j, :], False, jm == 3, [Cmi.buf(), xbf.buf((1, j))], [ybk.buf()]); n += 1
            S.op("dve", lambda e, ya=ya, ctile=ctile, ys=ys, c=c: e.scalar_tensor_tensor(
                out=ypre[:, ys, ctile, (c % 4) * 128:(c % 4 + 1) * 128], in0=uT[:, ctile, c * 128:(c + 1) * 128], scalar=dcol[:, 0, ctile:ctile + 1], in1=ya,
                op0=ALU.mult, op1=ALU.add),
                reads=[uT.buf((ctile, c)), dcol.buf(), ybk.buf()], writes=[ypre.buf((ys, ctile))])
        if c % 4 == 3:
            tb = c // 4
            for ctile in range(4):
                xx = ypre[:, ys, ctile, :]
                xb_ = ypre.buf((ys, ctile))
                tt_op("dve", g1[:, 0, :], xx, xx, ALU.mult, [xb_], [g1.buf(0)])
                ts_op("dve", g1[:, 0, :], g1[:, 0, :], 0.044715, 1.0, ALU.mult, ALU.add, [g1.buf(0)], [g1.buf(0)])
                tt_op("dve", g1[:, 0, :], g1[:, 0, :], xx, ALU.mult, [g1.buf(0), xb_], [g1.buf(0)])
                S.op("act", lambda e: e.activation(out=g1[:, 1, :], in_=g1[:, 0, :], func=AF.Sigmoid, scale=1.5957691216057308), reads=[g1.buf(0)], writes=[g1.buf(1)])
                tt_op("dve", gl[:, ctile, :], xx, g1[:, 1, :], ALU.mult, [xb_, g1.buf(1)], [gl.buf(ctile)])
                copy_op("dve", glb[:, ctile, :], gl[:, ctile, :], [gl.buf(ctile)], [glb.buf(ctile)])
            for cp in range(0 if os.environ.get("S5_G") != "1" else 4, 4):
                gbk = banks[4 + cp % 2]
                for ctile in range(4):
                    mm(gbk[:, :], gw[:, ctile, cp * 128:(cp + 1) * 128], glb[:, ctile, :], ctile == 0, ctile == 3, [gw.buf(0), glb.buf(ctile)], [gbk.buf()])
                if os.environ.get("S5_G") == "2":
                    continue
                S.op("act", lambda e, cp=cp, gbk=gbk: e.activation(out=g1[:, 0, :], in_=gbk[:, :], func=AF.Sigmoid, bias=dcol[:, 1, cp:cp + 1], scale=1.0),
                     reads=[gbk.buf(), dcol.buf()], writes=[g1.buf(0)])
                tt_op("dve", catT[:, cp, tb * 512:(tb + 1) * 512], gl[:, cp, :], g1[:, 0, :], ALU.mult, [gl.buf(cp), g1.buf(0)],
                      catT.bufs([(cp, 4 * tb + i) for i in range(4)]))
    S.free("Wmr", "Wmi", "WpTr", "WpTi", "Bmr", "Bmi", "Cmr", "Cmi", "dcol", "gw", "zb", "tm", "td", "wc", "xbf", "car", "ypre", "gl", "glb", "g1", "uT")
    if debug:
        S.dma("sp", "dbg", dbg["d_cat"], catT[:, :, :], reads=catT.bufs([(k, t) for k in range(8) for t in range(16)]), final=True)
    if upto == "C":
        S.emit()
        return nc


    if str(4) in os.environ.get("BARRIERS", ""):
        S.barrier()
    hs = S.alloc("hs", [16, D], F32, 4)
    hT = S.alloc("hT", [8, T], BF16, 2)
    wob = wload("wob", w_out, 8, D, 512, "w_e")
    lnp0 = load_lnp("lnp0", ln_in_g, ln_in_b)
    lnp1 = load_lnp("lnp1", ln1_g, ln1_b)
    lbt = S.alloc("lbt", [2, D], BF16, 2)
    bi = 0

    def resid_stats(tt, acc_banks):
        for nh in range(2):
            bk = acc_banks[nh]
            S.op("dve", lambda e, nh=nh, bk=bk: e.scalar_tensor_tensor(out=hs[:, tt, nh * 512:(nh + 1) * 512], in0=hs[:, tt, nh * 512:(nh + 1) * 512],
                                                                     scalar=ALPHA, in1=bk[:, :], op0=ALU.mult, op1=ALU.add),
                 reads=[hs.buf(tt), bk.buf()], writes=[hs.buf(tt)])
        return ln_stats(hs[:, tt, :], hs.buf(tt))

    def ln_finish_a(tt, sl, lnp):
        ln_apply(hs[:, tt, :], hs.buf(tt), lnp, None, None, sl)

    def ln_finish(tt, sl, lnp, do_T, split=False):
        if not split:
            ln_apply(hs[:, tt, :], hs.buf(tt), lnp, hs[:, tt, :], hs.buf(tt), sl)
        else:
            ln_apply_b(hs[:, tt, :], hs.buf(tt), lnp, hs[:, tt, :], hs.buf(tt))
        if do_T:
            s2 = tt % 2
            copy_op("act", lbt[:, s2, :], hs[:, tt, :], [hs.buf(tt)], [lbt.buf(s2)])
            transpose_to_hT(lbt[:, s2, :], lbt.buf(s2), hT, tt)

    def resid_ln(tt, acc_banks, lnp, do_T):
        sl = resid_stats(tt, acc_banks)
        ln_finish(tt, sl, lnp, do_T)

    sl_in = {}
    sl_1 = {}
    accs_of = {}
    for step in range(NTT + 6):
        if step >= 6:
            ln_finish(step - 6, None, lnp1, True, split=True)
        if step < NTT:
            tt = step
            S.dma("sp", "xin", hs[:, tt, :], x[tt * 128:(tt + 1) * 128, :], writes=[hs.buf(tt)])
            sl_in[tt] = ln_stats(hs[:, tt, :], hs.buf(tt))
        if 1 <= step <= NTT:
            tt = step - 1
            ln_apply(hs[:, tt, :], hs.buf(tt), lnp0, None, None, sl_in[tt])
        if 2 <= step <= NTT + 1:
            tt = step - 2
            ln_apply_b(hs[:, tt, :], hs.buf(tt), lnp0, hs[:, tt, :], hs.buf(tt))
            accs = []
            for nh in range(2):
                bk = banks[bi % 4]; bi += 1
                for kt in range(8):
                    mm(bk[:, :], catT[:, kt, tt * 128:(tt + 1) * 128], wob[:, kt, nh * 512:(nh + 1) * 512], kt == 0, kt == 7,
                       [catT.buf((kt, tt)), wob.buf(nh)], [bk.buf()])
                accs.append(bk)
            accs_of[tt] = accs
        if 3 <= step <= NTT + 2:
            tt = step - 3
            sl_1[tt] = resid_stats(tt, accs_of[tt])
        if 4 <= step <= NTT + 3:
            tt = step - 4
            ln_finish_a(tt, sl_1[tt], lnp1)
    S.free("catT", "wob", "lnp0", "lnp1")
    if debug:
        S.dma("sp", "dbg", dbg["d_h1"], hs[:, :, :], reads=hs.bufs(range(16)), final=True)
    if upto == "E":
        S.emit()
        return nc


    if str(5) in os.environ.get("BARRIERS", ""):
        S.barrier()
    wkv = wload("wkv", ca_wkv, 8, 2 * D, 512, "w_f")
    memf = S.alloc("memf", [2, D], F32, 4)
    memb = S.alloc("memb", [2, D], BF16, 2)
    memT = S.alloc("memT", [8, 256], BF16, 2)
    for mt in range(2):
        S.dma("sp", "mem", memf[:, mt, :], mem[mt * 128:(mt + 1) * 128, :], writes=[memf.buf(mt)])
        copy_op("act", memb[:, mt, :], memf[:, mt, :], [memf.buf(mt)], [memb.buf(mt)])
        ctr["pt"] += 1
        pt = ptb[ctr["pt"] % 2]
        for k in range(8):
            S.op("pe", lambda e, k=k, pt=pt, mt=mt: e.transpose(out=pt[:, k, :], in_=memb[:, mt, k * 128:(k + 1) * 128], identity=idb[:, :]),
                 reads=[memb.buf(mt), idb.buf()], writes=[pt.buf()])
        copy_op(evac_eng(), memT[:, :, mt * 128:(mt + 1) * 128], pt[:, :, :], [pt.buf()], [memT.buf()])
    kTca = S.alloc("kTca", [8, 256], BF16, 2)
    vca = S.alloc("vca", [2, D], BF16, 2)
    for ct in range(8):
        bk = banks[bi % 4]; bi += 1
        for kt in range(8):
            mm(bk[:, 0:256], wkv[:, kt, ct * 128:(ct + 1) * 128], memT[:, kt, :], kt == 0, kt == 7, [wkv.buf(ct // 4), memT.buf()], [bk.buf()])
        copy_op(evac_eng(), kTca[:, ct, :], bk[:, 0:256], [bk.buf()], [kTca.buf()])
    for mt in range(2):
        for nh in range(2):
            bk = banks[bi % 4]; bi += 1
            for kt in range(8):
                mm(bk[:, :], memT[:, kt, mt * 128:(mt + 1) * 128], wkv[:, kt, D + nh * 512:D + (nh + 1) * 512], kt == 0, kt == 7,
                   [wkv.buf(2 + nh), memT.buf()], [bk.buf()])
            copy_op(evac_eng(), vca[:, mt, nh * 512:(nh + 1) * 512], bk[:, :], [bk.buf()], [vca.buf()])
    S.free("wkv", "memf", "memb", "memT")
    wqb = wload("wqb", ca_wq, 8, D, 512, "w_f2")
    wo2 = wload("wo2", ca_wo, 8, D, 512, "w_f2")
    lnp2 = load_lnp("lnp2", ln2_g, ln2_b)
    qTc = S.alloc("qTc", [2, 8, 512], BF16, 2)
    PTc = S.alloc("PTc", [2, 2, 512], BF16, 2)
    oTc = S.alloc("oTc", [8, 512], BF16, 2)
    rcs = S.alloc("rcs", [2, 512], F32, 4)
    pendF = None
    pendF2 = None
    fst = {"bi": 0, "sc": 0}

    def nbank():
        fst["bi"] += 1
        return banks[fst["bi"] % 4]

    def F_Q(tb):
        qs = tb % 2
        tcols = slice(tb * 512, (tb + 1) * 512)
        hbufs = hT.bufs(range(4 * tb, 4 * tb + 4))
        for ct in range(8):
            bk = nbank()
            for kt in range(8):
                mm(bk[:, :], wqb[:, kt, ct * 128:(ct + 1) * 128], hT[:, kt, tcols], kt == 0, kt == 7, [wqb.buf(ct // 4)] + hbufs, [bk.buf()])
            copy_op(evac_eng(), qTc[:, qs, ct, :], bk[:, :], [bk.buf()], [qTc.buf((qs, ct))])

    def F_HS(tb, hd):
        qs = tb % 2
        ps = hd % 2
        for mt in range(2):
            fst["sc"] += 1
            bk = banks[4 + fst["sc"] % 4]
            for i in range(2):
                ct = 2 * hd + i
                mm(bk[:, :], kTca[:, ct, mt * 128:(mt + 1) * 128], qTc[:, qs, ct, :], i == 0, i == 1, [kTca.buf(), qTc.buf((qs, ct))], [bk.buf()])
            S.op("act", lambda e, bk=bk, ps=ps, mt=mt: e.activation(out=PTc[:, ps, mt, :], in_=bk[:, :], func=AF.Exp, scale=1.0 / 16.0),
                 reads=[bk.buf()], writes=[PTc.buf((ps, mt))])

    def F_HP(tb, hd):
        ps = hd % 2
        sb = nbank()
        for mt in range(2):
            mm(sb[:, :], onesb[:, :], PTc[:, ps, mt, :], mt == 0, mt == 1, [onesb.buf(), PTc.buf((ps, mt))], [sb.buf()])
        S.op("act", lambda e, sb=sb, ps=ps: e.activation(out=rcs[:, ps, :], in_=sb[:, :], func=AF.Ln), reads=[sb.buf()], writes=[rcs.buf(ps)])
        S.op("act", lambda e, ps=ps: e.activation(out=rcs[:, ps, :], in_=rcs[:, ps, :], func=AF.Exp, scale=-1.0), reads=[rcs.buf(ps)], writes=[rcs.buf(ps)])
        for dti in range(2):
            ct = 2 * hd + dti
            bk = nbank()
            for mt in range(2):
                mm(bk[:, :], vca[:, mt, ct * 128:(ct + 1) * 128], PTc[:, ps, mt, :], mt == 0, mt == 1, [vca.buf(), PTc.buf((ps, mt))], [bk.buf()])
            tt_op("dve", oTc[:, ct, :], bk[:, :], rcs[:, ps, :], ALU.mult, [bk.buf(), rcs.buf(ps)], [oTc.buf(ct)])

    def F_W(tb):
        nonlocal_state = None
        prev_acc = None
        for tl in range(5):
            tt = 4 * tb + tl
            if tl < 4:
                if pstate["p3"] is not None:
                    ln_finish(pstate["p3"][0], None, lnp2, True, split=True)
                    pstate["p3"] = None
                accs = []
                for nh in range(2):
                    bk = nbank()
                    for kt in range(8):
                        mm(bk[:, :], oTc[:, kt, tl * 128:(tl + 1) * 128], wo2[:, kt, nh * 512:(nh + 1) * 512], kt == 0, kt == 7,
                           [oTc.buf(kt), wo2.buf(nh)], [bk.buf()])
                    accs.append(bk)
            if prev_acc is not None:
                pt_, pa_ = prev_acc
                slF = resid_stats(pt_, pa_)
                if pstate["p1"] is not None:
                    ln_finish_a(pstate["p1"][0], pstate["p1"][1], lnp2)
                if pstate["p3"] is not None:
                    ln_finish(pstate["p3"][0], None, lnp2, True, split=True)
                pstate["p3"] = pstate["p2"]
                pstate["p2"] = pstate["p1"]
                pstate["p1"] = (pt_, slF)
            prev_acc = (tt, accs) if tl < 4 else None

    pstate = {"p1": None, "p2": None, "p3": None}
    F_Q(0)
    for tb in range(4):
        F_HS(tb, 0)
        for hd in range(4):
            if hd < 3:
                F_HS(tb, hd + 1)
            F_HP(tb, hd)
        if tb < 3:
            F_Q(tb + 1)
        F_W(tb)
    if pstate["p3"] is not None:
        ln_finish(pstate["p3"][0], None, lnp2, True, split=True)
    ln_finish_a(pstate["p1"][0], pstate["p1"][1], lnp2)
    if pstate["p2"] is not None:
        ln_finish(pstate["p2"][0], None, lnp2, True, split=True)
    ln_finish(pstate["p1"][0], None, lnp2, True, split=True)
    S.free("wqb", "wo2", "lnp2", "qTc", "PTc", "oTc", "rcs", "kTca", "vca", "lbt")
    if debug:
        S.dma("sp", "dbg", dbg["d_h2"], hs[:, :, :], reads=hs.bufs(range(16)), final=True)
    if upto == "F":
        S.emit()
        return nc


    if str(6) in os.environ.get("BARRIERS", ""):
        S.barrier()
    lnp3 = load_lnp("lnp3", ln3_g, ln3_b)
    gus = S.alloc("gus", [3, 8, 2, 128], BF16, 2)
    actT = S.alloc("actT", [11, T], BF16, 2)
    sg = S.alloc("sg", [2, 512], F32, 4)
    guv = w_gu.rearrange("(kt p) n -> p kt n", p=128)
    gi = 0
    pendG = None
    pendG2 = None

    def g_finish(t_):
        ln_apply_b(hs[:, t_, :], hs.buf(t_), lnp3, hs[:, t_, :], hs.buf(t_))
        S.dma("sp", "out", out[t_ * 128:(t_ + 1) * 128, :], hs[:, t_, :], reads=[hs.buf(t_)], final=True)

    for ps_ in range(2):
        wd = S.alloc("wd", [11, D], BF16, 2)
        dv = w_dn[ps_ * 11 * 128:(ps_ + 1) * 11 * 128, :].rearrange("(j p) n -> p j n", p=128)
        for jl in range(11):
            j = ps_ * 11 + jl
            gs = gi % 3; gi += 1
            S.dma("pool", f"w_gu{gs}", gus[:, gs, :, 0, :], guv[:, :, j * 128:(j + 1) * 128], writes=[gus.buf(gs)])
            S.dma("pool", f"w_gu{gs}", gus[:, gs, :, 1, :], guv[:, :, FFN_H + j * 128:FFN_H + (j + 1) * 128], writes=[gus.buf(gs)])
            if jl < 11:
                S.dma("pool", "w_dn", wd[:, jl, :], dv[:, jl, :], writes=[wd.buf(jl)])
            for tb in range(4):
                tcols = slice(tb * 512, (tb + 1) * 512)
                hbufs = hT.bufs(range(4 * tb, 4 * tb + 4))
                bg = banks[(bi % 2) * 2]; bu_ = banks[(bi % 2) * 2 + 1]; bi += 1
                for kt in range(8):
                    mm(bg[:, :], gus[:, gs, kt, 0, :], hT[:, kt, tcols], kt == 0, kt == 7, [gus.buf(gs)] + hbufs, [bg.buf()])
                for kt in range(8):
                    mm(bu_[:, :], gus[:, gs, kt, 1, :], hT[:, kt, tcols], kt == 0, kt == 7, [gus.buf(gs)] + hbufs, [bu_.buf()])
                s2 = bi % 2
                S.op("act", lambda e, bg=bg, s2=s2: e.activation(out=sg[:, s2, :], in_=bg[:, :], func=AF.Silu), reads=[bg.buf()], writes=[sg.buf(s2)])
                tt_op("dve", actT[:, jl, tcols], sg[:, s2, :], bu_[:, :], ALU.mult, [sg.buf(s2), bu_.buf()], actT.bufs([(jl, 4 * tb + i) for i in range(4)]))
        for tt in range(NTT):
            accs = []
            for nh in range(2):
                bk = banks[4 + nh + 2 * (tt % 2)]
                for jl in range(11):
                    mm(bk[:, :], actT[:, jl, tt * 128:(tt + 1) * 128], wd[:, jl, nh * 512:(nh + 1) * 512], jl == 0, jl == 10,
                       [actT.buf((jl, tt)), wd.buf(jl)], [bk.buf()])
                accs.append(bk)
            if ps_ == 0:
                for nh in range(2):
                    bk = accs[nh]
                    S.op("dve", lambda e, nh=nh, bk=bk, tt=tt: e.scalar_tensor_tensor(out=hs[:, tt, nh * 512:(nh + 1) * 512], in0=hs[:, tt, nh * 512:(nh + 1) * 512],
                                                                                 scalar=ALPHA, in1=bk[:, :], op0=ALU.mult, op1=ALU.add),
                         reads=[hs.buf(tt), bk.buf()], writes=[hs.buf(tt)])
            else:
                for nh in range(2):
                    bk = accs[nh]
                    tt_op("dve", hs[:, tt, nh * 512:(nh + 1) * 512], hs[:, tt, nh * 512:(nh + 1) * 512], bk[:, :], ALU.add, [hs.buf(tt), bk.buf()], [hs.buf(tt)])
                slG = ln_stats(hs[:, tt, :], hs.buf(tt))
                if pendG2 is not None:
                    g_finish(pendG2[0])
                if pendG is not None:
                    ln_apply(hs[:, pendG[0], :], hs.buf(pendG[0]), lnp3, None, None, pendG[1])
                pendG2 = pendG
                pendG = (tt, slG)
        if ps_ == 1:
            if pendG2 is not None:
                g_finish(pendG2[0])
            ln_apply(hs[:, pendG[0], :], hs.buf(pendG[0]), lnp3, None, None, pendG[1])
            g_finish(pendG[0])
        S.free("wd")
    S.emit()
    return nc


_CACHE = {}


def kernel(**inputs):
    consts = host_consts()
    shared = {}
    for k, v in inputs.items():
        if k in ("x", "mem"):
            continue
        a = np.ascontiguousarray(np.asarray(v, dtype=np.float32))
        if k == "rel_bias":
            shared[k] = a
        elif a.ndim >= 2 and a.shape[0] == 1:
            a = a[0]
            if k in ("s5_lambda_re", "s5_lambda_im", "s5_d"):
                a = a.reshape(-1)
            elif k in ("s5_b_re", "s5_b_im"):
                a = a.reshape(2048, 16)
            elif k in ("s5_c_re", "s5_c_im"):
                a = a.reshape(512, 64)
            shared[k] = np.ascontiguousarray(a)
        else:
            shared[k] = a
    shared.update(consts)
    xs = np.asarray(inputs["x"], dtype=np.float32)
    ms = np.asarray(inputs["mem"], dtype=np.float32)
    if "nc" not in _CACHE:
        _CACHE["nc"] = build_program(False)
    nc = _CACHE["nc"]
    in_maps = []
    for b in range(8):
        m = dict(shared)
        m["x"] = np.ascontiguousarray(xs[b])
        m["mem"] = np.ascontiguousarray(ms[b])
        in_maps.append(m)
    res = run_bass_kernel_spmd(nc, in_maps, core_ids=list(range(8)))
    return np.stack([np.asarray(r["out"], dtype=np.float32) for r in res.results], axis=0)
```

```python
import numpy as np
import concourse.bass as bass
import concourse.mybir as mybir
from concourse.bass_utils import run_bass_kernel_spmd

F32 = mybir.dt.float32
BF16 = mybir.dt.bfloat16
ALU = mybir.AluOpType
AF = mybir.ActivationFunctionType
AX = mybir.AxisListType

ENGS = ("pe", "act", "dve", "pool", "sp")


class Instr:
    __slots__ = ("eng", "fn", "waits", "signal", "idx", "val", "dma_sem", "dma_val")

    def __init__(self, eng, fn):
        self.eng = eng
        self.fn = fn
        self.waits = []
        self.signal = False
        self.idx = -1
        self.val = 0
        self.dma_sem = None
        self.dma_val = 0


class Buf:
    __slots__ = ("name", "last_w", "readers")

    def __init__(self, name, inherit=()):
        self.name = name
        self.last_w = None
        self.readers = list(inherit)


class Ten:
    def __init__(self, h, name, inherit=()):
        self.h = h
        self.name = name
        self.inherit = list(inherit)
        self._bufs = {}

    def __getitem__(self, k):
        return self.h[k]

    def buf(self, key=0):
        b = self._bufs.get(key)
        if b is None:
            b = Buf(f"{self.name}{key}", self.inherit)
            self._bufs[key] = b
        return b

    def bufs(self, keys):
        return [self.buf(k) for k in keys]

    def all_instrs(self):
        out = list(self.inherit)
        for b in self._bufs.values():
            if b.last_w is not None:
                out.append(b.last_w)
            out.extend(b.readers)
        return out


class Sched:
    def __init__(self, nc, sbuf_base=16512, sbuf_bytes=229312):
        self.nc = nc
        self.q = {e: [] for e in ENGS}
        self.waited = {e: {} for e in ENGS}
        self.dma_count = {}
        self.sbuf_bytes = sbuf_bytes
        self.sbuf_base = sbuf_base
        self.live = {}
        self.hist = []
        self.n_alloc = 0
        self.final_dma = []
        self.ring_pos = {}
        self.ring_last = {}

    def alloc(self, name, free_shape, dtype, nbytes_el):
        size = int(np.prod(free_shape)) * nbytes_el
        size = (size + 63) // 64 * 64
        segs = sorted((o, s) for (o, s, _) in self.live.values())
        off = self.sbuf_base
        for (o, s) in segs:
            if off + size <= o:
                break
            off = max(off, o + s)
        if off + size > self.sbuf_bytes:
            raise RuntimeError(f"SBUF arena overflow allocating {name} ({size} B); live={[(k, v[0], v[1]) for k, v in self.live.items()]}")
        inherit = []
        for (o, s, t) in self.hist:
            if o < off + size and off < o + s:
                inherit.extend(t.all_instrs())
        comp = {}
        for ins in inherit:
            key = ins.dma_sem if ins.dma_sem is not None else ins.eng
            cur = comp.get(key)
            if cur is None or (ins.dma_val if ins.dma_sem is not None else ins.idx) > (cur.dma_val if cur.dma_sem is not None else cur.idx):
                comp[key] = ins
        self.n_alloc += 1
        h = self.nc.alloc_sbuf_tensor_at(f"{name}_{self.n_alloc}", [128] + list(free_shape), dtype, offset=off)
        t = Ten(h, name, list(comp.values()))
        self.live[name] = (off, size, t)
        return t

    def free(self, *names):
        for name in names:
            o, s, t = self.live.pop(name)
            self.hist.append((o, s, t))

    def _need(self, c, p, raw):
        if p is None or p is c:
            return
        E = c.eng
        if p.dma_sem is not None:
            key = "dma:" + p.dma_sem
            if self.waited[E].get(key, 0) >= p.dma_val:
                return
            self.waited[E][key] = p.dma_val
            c.waits.append(p)
            return
        if p.eng == E and c.dma_sem is None:
            if E in ("pe", "sp"):
                return
            if not raw:
                return
        key = p.eng
        if self.waited[E].get(key, -1) >= p.idx:
            return
        self.waited[E][key] = p.idx
        p.signal = True
        c.waits.append(p)

    def _deps(self, c, reads, writes):
        for b in reads:
            self._need(c, b.last_w, True)
        for b in writes:
            self._need(c, b.last_w, False)
            for r in b.readers:
                self._need(c, r, False)
        for b in reads:
            b.readers.append(c)
            if len(b.readers) > 12:
                comp = {}
                for ins in b.readers:
                    key = ins.dma_sem if ins.dma_sem is not None else ins.eng
                    cur = comp.get(key)
                    if cur is None or (ins.dma_val if ins.dma_sem is not None else ins.idx) >= (cur.dma_val if cur.dma_sem is not None else cur.idx):
                        comp[key] = ins
                b.readers = list(comp.values())
        for b in writes:
            b.last_w = c
            b.readers = []

    def op(self, eng, fn, reads=(), writes=()):
        c = Instr(eng, fn)
        c.idx = len(self.q[eng])
        self.q[eng].append(c)
        self._deps(c, reads, writes)
        return c

    NRING = 28

    def dma(self, eng, stream, out, in_, reads=(), writes=(), final=False):
        def fn(e, out=out, in_=in_):
            return e.dma_start(out=out, in_=in_)
        c = Instr(eng, fn)
        c.idx = len(self.q[eng])
        pos = self.ring_pos.get(eng, 0)
        self.ring_pos[eng] = pos + 1
        sem = f"{eng}{pos % self.NRING}"
        n = self.dma_count.get(sem, 0) + 1
        self.dma_count[sem] = n
        c.dma_sem = sem
        c.dma_val = 16 * n
        prev = self.ring_last.get(sem)
        if prev is not None:
            self._need(c, prev, False)
        self.ring_last[sem] = c
        self.q[eng].append(c)
        self._deps(c, reads, writes)
        if final:
            self.final_dma.append(c)
        return c

    def barrier(self):
        lasts = {}
        for e in ENGS:
            for ins in reversed(self.q[e]):
                if ins.dma_sem is None:
                    lasts[e] = ins
                    break
        dmas = []
        for sname, n in self.dma_count.items():
            f = Instr("sp", None)
            f.dma_sem = sname
            f.dma_val = 16 * n
            dmas.append(f)
        for e in ENGS:
            c = Instr(e, lambda eng: eng.nop())
            c.idx = len(self.q[e])
            for e2, p in lasts.items():
                if e2 != e:
                    self._need(c, p, True)
            for f in dmas:
                self._need(c, f, True)
            self.q[e].append(c)

    def emit(self):
        nc = self.nc
        import contextlib
        with contextlib.ExitStack() as st:
            esem = {e: st.enter_context(nc.semaphore(f"s_{e}")) for e in ENGS}
            dsem = {s: st.enter_context(nc.semaphore(f"d_{s}")) for s in self.dma_count}
            for e in ENGS:
                cnt = 0
                for ins in self.q[e]:
                    if ins.signal:
                        cnt += 1
                        ins.val = cnt
            block = st.enter_context(nc.Block())

            def run(eng_name, eobj):
                for ins in self.q[eng_name]:
                    for p in ins.waits:
                        if p.dma_sem is not None:
                            eobj.wait_ge(dsem[p.dma_sem], p.dma_val)
                        else:
                            eobj.wait_ge(esem[p.eng], p.val)
                    r = ins.fn(eobj)
                    if ins.dma_sem is not None:
                        r.then_inc(dsem[ins.dma_sem], 16)
                    elif ins.signal:
                        r.then_inc(esem[eng_name], 1)
                if eng_name == "sp":
                    done = {}
                    for ins in self.final_dma:
                        done[ins.dma_sem] = max(done.get(ins.dma_sem, 0), ins.dma_val)
                    for s, v in done.items():
                        eobj.wait_ge(dsem[s], v)

            @block.tensor
            def _(e):
                run("pe", e)

            @block.scalar
            def _(e):
                run("act", e)

            @block.vector
            def _(e):
                run("dve", e)

            @block.gpsimd
            def _(e):
                run("pool", e)

            @block.sync
            def _(e):
                run("sp", e)


T = 2048
D = 1024
NTT = 16
ALPHA = float(2.0 ** 0.25)
PI = float(np.pi)
LAMBDA_INIT = 0.8 - 0.6 * 1.0
FFN_H = 2816
NJ = 22


def t5_bucket_np(d):
    d = np.asarray(d, dtype=np.int64)
    df = np.maximum(d, 1).astype(np.float32)
    large = 16 + (np.log(df / np.float32(16)) / np.float32(np.log(128 / 16)) * np.float32(16)).astype(np.int32)
    large = np.minimum(large, 31)
    return np.where(d < 16, d, large)


def host_consts():
    c = {}
    c["c_ident"] = np.eye(128, dtype=np.float32)
    c["c_tri"] = np.triu(np.ones((128, 128), dtype=np.float32))
    cst = np.zeros((128, 8), dtype=np.float32)
    tp1 = np.arange(1, 129, dtype=np.float32)
    cst[:, 0] = tp1
    cst[:, 1] = -tp1
    cst[:, 2] = tp1 / np.float32(2 * np.pi)
    cst[:, 3] = 1.5
    cst[:, 4] = 1.75
    cst[:, 5] = -np.pi
    cst[:, 6] = 1e-5
    cst[:, 7] = 1.0
    c["c_cst"] = cst
    c["c_trow"] = np.broadcast_to(tp1[None, :], (128, 128)).astype(np.float32).copy()
    p = np.arange(128)
    mask = np.zeros((128, 4, 128), dtype=np.float32)
    for jm in range(4):
        mask[:, jm, :] = ((p[None, :] // 16) == (2 * jm + p[:, None] // 64)).astype(np.float32)
    c["c_maskC"] = mask
    c["c_maskB"] = np.ascontiguousarray(mask.transpose(2, 1, 0))
    oh = np.zeros((33, 384), dtype=np.float32)
    for m in range(384):
        d = m - 127
        if d < 0:
            oh[32, m] = -30000.0
        else:
            b = int(t5_bucket_np(d))
            oh[b, m] += 8.0
            oh[31, m] -= 8.0
    c["c_oh"] = oh
    return c


def build_program(debug=False, upto=None):
    import os
    nc = bass.Bass("TRN2", target_bir_lowering=False)
    S = Sched(nc)

    def din(name, shape):
        return nc.dram_tensor(name, list(shape), F32, kind="ExternalInput").ap()

    x = din("x", [T, D]); mem = din("mem", [256, D])
    ln_in_g = din("ln_in_g", [D]); ln_in_b = din("ln_in_b", [D])
    w_in = din("w_in", [D, 2048])
    lam_re = din("s5_lambda_re", [2048]); lam_im = din("s5_lambda_im", [2048]); log_dt = din("s5_log_dt", [32])
    b_re = din("s5_b_re", [2048, 16]); b_im = din("s5_b_im", [2048, 16])
    c_re = din("s5_c_re", [512, 64]); c_im = din("s5_c_im", [512, 64])
    s5_d = din("s5_d", [512]); glu_w = din("s5_glu_w", [512, 512]); glu_b = din("s5_glu_b", [512])
    lq1 = din("diff_lq1", [64]); lk1 = din("diff_lk1", [64]); lq2 = din("diff_lq2", [64]); lk2 = din("diff_lk2", [64])
    subln_g = din("diff_subln_g", [128]); rel_bias = din("rel_bias", [32, 4])
    w_out = din("w_out", [D, D]); ln1_g = din("ln1_g", [D]); ln1_b = din("ln1_b", [D])
    ca_wq = din("ca_wq", [D, D]); ca_wkv = din("ca_wkv", [D, 2 * D]); ca_wo = din("ca_wo", [D, D])
    ln2_g = din("ln2_g", [D]); ln2_b = din("ln2_b", [D])
    w_gu = din("ffn_w_gate_up", [D, 2 * FFN_H]); w_dn = din("ffn_w_down", [FFN_H, D])
    ln3_g = din("ln3_g", [D]); ln3_b = din("ln3_b", [D])
    c_ident = din("c_ident", [128, 128]); c_tri = din("c_tri", [128, 128]); c_cst = din("c_cst", [128, 8])
    c_trow = din("c_trow", [128, 128]); c_maskC = din("c_maskC", [128, 4, 128]); c_maskB = din("c_maskB", [128, 4, 128])
    c_oh = din("c_oh", [33, 384])
    out = nc.dram_tensor("out", [T, D], F32, kind="ExternalOutput").ap()
    scr = nc.dram_tensor("bias_scr", [128, 4, 384], F32, kind="Internal").ap()
    dbg = {}
    if debug:
        for nm, shp, dt_ in (("d_hT", [128, 8, T], BF16), ("d_uT", [128, 4, T], BF16), ("d_qT", [128, 4, T], BF16),
                             ("d_kT", [128, 4, T], BF16), ("d_v", [128, 16, 4, 129], BF16), ("d_cat", [128, 8, T], BF16),
                             ("d_h1", [128, 16, D], F32), ("d_h2", [128, 16, D], F32)):
            dbg[nm] = nc.dram_tensor(nm, shp, dt_, kind="ExternalOutput").ap()

    banks = [Ten(nc.alloc_psum_tensor(f"bank{i}", [128, 512], F32), f"bank{i}") for i in range(8)]

    class View:
        def __init__(self, ten):
            self.ten = ten
            self.ap = ten[:, :].bitcast(BF16).rearrange("p (k c) -> p k c", k=8)

        def __getitem__(self, k):
            return self.ap[k]

        def buf(self, key=0):
            return self.ten.buf(key)

    ptb = [View(banks[6]), View(banks[7])]

    idf = S.alloc("idf", [128], F32, 4)
    idb = S.alloc("idb", [128], BF16, 2)
    trib = S.alloc("trib", [128], BF16, 2)
    trif = S.alloc("trif", [128], F32, 4)
    onesb = S.alloc("onesb", [128], BF16, 2)
    onesf = S.alloc("onesf", [128], F32, 4)
    cst = S.alloc("cst", [8], F32, 4)
    stt = S.alloc("ln_st", [8, 12], F32, 4)
    mvt = S.alloc("ln_mv", [8, 2], F32, 4)
    rsd = S.alloc("ln_rs", [8, 1], F32, 4)
    S.dma("sp", "c0", idf[:, :], c_ident, writes=[idf.buf()])
    S.dma("sp", "c0", trif[:, :], c_tri, writes=[trif.buf()])
    S.dma("sp", "c0", cst[:, :], c_cst, writes=[cst.buf()])
    S.op("dve", lambda e: e.tensor_copy(out=idb[:, :], in_=idf[:, :]), reads=[idf.buf()], writes=[idb.buf()])
    S.op("dve", lambda e: e.tensor_copy(out=trib[:, :], in_=trif[:, :]), reads=[trif.buf()], writes=[trib.buf()])
    S.op("pool", lambda e: e.memset(onesb[:, :], 1.0), writes=[onesb.buf()])
    S.op("pool", lambda e: e.memset(onesf[:, :], 1.0), writes=[onesf.buf()])
    EPS = cst[:, 6:7]

    ctr = {"ev": 0, "ln": 0, "pt": 0}

    def evac_eng():
        ctr["ev"] += 1
        return "act" if ctr["ev"] % 2 else "dve"

    def copy_op(eng, out_ap, in_ap, reads, writes):
        if eng == "act":
            S.op("act", lambda e: e.activation(out=out_ap, in_=in_ap, func=AF.Copy), reads, writes)
        else:
            S.op(eng, lambda e: e.tensor_copy(out=out_ap, in_=in_ap), reads, writes)

    def mm(out_ap, lhsT, rhs, start, stop, reads, writes):
        S.op("pe", lambda e: e.matmul(out=out_ap, lhsT=lhsT, rhs=rhs, start=start, stop=stop), reads, writes)

    def load_lnp(name, g, b):
        t = S.alloc(name, [2, D], F32, 4)
        S.dma("sp", "lnp", t[:, 0, :], g.partition_broadcast(128), writes=[t.buf()])
        S.dma("sp", "lnp", t[:, 1, :], b.partition_broadcast(128), writes=[t.buf()])
        return t

    def ln_stats(x_ap, x_buf):
        ctr["ln"] += 1
        sl = ctr["ln"] % 8
        for c in range(2):
            S.op("dve", lambda e, c=c: e.bn_stats(out=stt[:, sl, c * 6:(c + 1) * 6], in_=x_ap[:, c * 512:(c + 1) * 512]),
                 reads=[x_buf], writes=[stt.buf(sl)])
        S.op("dve", lambda e: e.bn_aggr(out=mvt[:, sl, :], in_=stt[:, sl, :]), reads=[stt.buf(sl)], writes=[mvt.buf(sl)])
        S.op("act", lambda e: e.activation(out=rsd[:, sl, :], in_=mvt[:, sl, 1:2], func=AF.Sqrt, bias=EPS, scale=1.0),
             reads=[mvt.buf(sl), cst.buf()], writes=[rsd.buf(sl)])
        S.op("dve", lambda e: e.reciprocal(out=rsd[:, sl, :], in_=rsd[:, sl, :]), reads=[rsd.buf(sl)], writes=[rsd.buf(sl)])
        return sl

    def ln_apply(x_ap, x_buf, lnp, out_ap, out_buf, sl):
        S.op("dve", lambda e: e.scalar_tensor_tensor(out=mvt[:, sl, 1:2], in0=mvt[:, sl, 0:1], scalar=-1.0, in1=rsd[:, sl, :],
                                                      op0=ALU.mult, op1=ALU.mult),
             reads=[mvt.buf(sl), rsd.buf(sl)], writes=[mvt.buf(sl)])
        S.op("act", lambda e: e.activation(out=x_ap, in_=x_ap, func=AF.Identity, bias=mvt[:, sl, 1:2], scale=rsd[:, sl, 0:1]),
             reads=[x_buf, mvt.buf(sl), rsd.buf(sl)], writes=[x_buf])
        S.op("pool", lambda e: e.tensor_tensor(out=x_ap, in0=x_ap, in1=lnp[:, 0, :], op=ALU.mult), reads=[x_buf, lnp.buf()], writes=[x_buf])
        if out_ap is not None:
            ln_apply_b(x_ap, x_buf, lnp, out_ap, out_buf)

    def ln_apply_b(x_ap, x_buf, lnp, out_ap, out_buf):
        S.op("dve", lambda e: e.tensor_tensor(out=out_ap, in0=x_ap, in1=lnp[:, 1, :], op=ALU.add), reads=[x_buf, lnp.buf()],
             writes=[out_buf] if out_buf is not x_buf else [x_buf])

    def transpose_to_hT(lb, lb_buf, hT, tt):
        ctr["pt"] += 1
        pt = ptb[ctr["pt"] % 2]
        for k in range(8):
            S.op("pe", lambda e, k=k: e.transpose(out=pt[:, k, :], in_=lb[:, k * 128:(k + 1) * 128], identity=idb[:, :]),
                 reads=[lb_buf, idb.buf()], writes=[pt.buf()])
        copy_op(evac_eng(), hT[:, :, tt * 128:(tt + 1) * 128], pt[:, :, :], [pt.buf()], [hT.buf(tt)])

    def wload(name, src, kt, ncols, chunk, stream):
        t = S.alloc(name, [kt, ncols], BF16, 2)
        sv = src.rearrange("(kt p) n -> p kt n", p=128)
        for c in range(ncols // chunk):
            S.dma("pool", stream, t[:, :, c * chunk:(c + 1) * chunk], sv[:, :, c * chunk:(c + 1) * chunk], writes=[t.buf(c)])
        return t

    hT = S.alloc("hT", [8, T], BF16, 2)
    wi = wload("wi", w_in, 8, 2048, 512, "w_a")
    lnp0 = load_lnp("lnp0", ln_in_g, ln_in_b)
    xin = S.alloc("xin", [4, D], F32, 4)
    lbt = S.alloc("lbt", [2, D], BF16, 2)
    sls = {}
    for tt in range(NTT + 3):
        if tt < NTT:
            s3 = tt % 4
            S.dma("sp", "xin", xin[:, s3, :], x[tt * 128:(tt + 1) * 128, :], writes=[xin.buf(s3)])
            sls[tt] = ln_stats(xin[:, s3, :], xin.buf(s3))
        if 1 <= tt <= NTT:
            t1 = tt - 1
            ln_apply(xin[:, t1 % 4, :], xin.buf(t1 % 4), lnp0, None, None, sls[t1])
        if 2 <= tt <= NTT + 1:
            t2 = tt - 2
            ln_apply_b(xin[:, t2 % 4, :], xin.buf(t2 % 4), lnp0, lbt[:, t2 % 2, :], lbt.buf(t2 % 2))
        if tt >= 3:
            t3 = tt - 3
            transpose_to_hT(lbt[:, t3 % 2, :], lbt.buf(t3 % 2), hT, t3)
    S.free("xin", "lnp0")
    if debug:
        S.dma("sp", "dbg", dbg["d_hT"], hT[:, :, :], reads=hT.bufs(range(16)), final=True)
    if upto == "A":
        S.emit()
        return nc


    if str(0) in os.environ.get("BARRIERS", ""):
        S.barrier()
    uT = S.alloc("uT", [4, T], BF16, 2)
    qT = S.alloc("qT", [4, T], BF16, 2)
    kT = S.alloc("kT", [4, T], BF16, 2)
    vaug = S.alloc("vaug", [16, 4, 129], BF16, 2)
    import os
    for tt in range(NTT):
        S.op("dve", lambda e, tt=tt: e.tensor_copy(out=vaug[:, tt, :, 128:129], in_=onesb[:, 0:4].unsqueeze(2)), reads=[onesb.buf()], writes=[vaug.buf(tt)])
    bi = 0
    for grp, dst in enumerate((uT, qT, kT)):
        for ct in range(4):
            col = grp * 512 + ct * 128
            for tb in range(4):
                bk = banks[bi % 4]; bi += 1
                for kt in range(8):
                    mm(bk[:, :], wi[:, kt, col:col + 128], hT[:, kt, tb * 512:(tb + 1) * 512], kt == 0, kt == 7,
                       [wi.buf(grp)] + hT.bufs(range(4 * tb, 4 * tb + 4)), [bk.buf()])
                copy_op(evac_eng(), dst[:, ct, tb * 512:(tb + 1) * 512], bk[:, :], [bk.buf()], dst.bufs([(ct, 4 * tb + i) for i in range(4)]))
    for tt in range(0 if not os.environ.get("NO_V") else NTT, NTT):
        bk = banks[bi % 4]; bi += 1
        for kt in range(8):
            mm(bk[:, :], hT[:, kt, tt * 128:(tt + 1) * 128], wi[:, kt, 1536:2048], kt == 0, kt == 7,
               [wi.buf(3), hT.buf(tt)], [bk.buf()])
        copy_op(evac_eng(), vaug[:, tt, :, 0:128], bk[:, :].rearrange("p (h d) -> p h d", h=4), [bk.buf()], [vaug.buf(tt)])
    S.free("wi", "lbt", "hT")
    if debug:
        S.dma("sp", "dbg", dbg["d_uT"], uT[:, :, :], reads=uT.bufs([(c, t) for c in range(4) for t in range(16)]), final=True)
        S.dma("sp", "dbg", dbg["d_qT"], qT[:, :, :], reads=qT.bufs([(c, t) for c in range(4) for t in range(16)]), final=True)
        S.dma("sp", "dbg", dbg["d_kT"], kT[:, :, :], reads=kT.bufs([(c, t) for c in range(4) for t in range(16)]), final=True)
        S.dma("sp", "dbg", dbg["d_v"], vaug[:, :, :, :], reads=vaug.bufs(range(16)), final=True)
    if upto == "B":
        S.emit()
        return nc


    catT = S.alloc("catT", [8, T], BF16, 2)

    if str(1) in os.environ.get("BARRIERS", ""):
        S.barrier()
    rb = S.alloc("rb", [4], F32, 4)
    S.dma("sp", "c1", rb[0:32, :], rel_bias, writes=[rb.buf()])
    chc = S.alloc("chc", [4], F32, 4)
    S.dma("sp", "c1", chc[:, :], rel_bias[31, :].partition_broadcast(128), writes=[chc.buf()])
    ohs = S.alloc("ohs", [384], F32, 4)
    S.dma("sp", "c1", ohs[0:33, :], c_oh, writes=[ohs.buf()])
    Lh = S.alloc("Lh", [4, 128], F32, 4)
    S.op("pool", lambda e: e.memset(Lh[0:33, :, :], 1.0), writes=[Lh.buf()])
    for h in range(4):
        S.op("dve", lambda e, h=h: e.tensor_scalar(out=Lh[0:32, h, :], in0=onesf[0:32, :], scalar1=rb[0:32, h:h + 1], scalar2=None, op0=ALU.mult),
             reads=[onesf.buf(), rb.buf(), Lh.buf()], writes=[Lh.buf()])
    gsb = S.alloc("gsb", [4, 384], F32, 4)
    for h in range(4):
        bk = banks[h % 4]
        mm(bk[:, 0:384], Lh[0:33, h, :], ohs[0:33, :], True, True, [Lh.buf(), ohs.buf()], [bk.buf()])
        copy_op("dve", gsb[:, h, :], bk[:, 0:384], [bk.buf()], [gsb.buf()])
    scrb = Ten(None, "scr")
    S.dma("sp", "scr", scr, gsb[:, :, :], reads=[gsb.buf()], writes=[scrb.buf()])
    biasf = S.alloc("biasf", [4, 2, 128], F32, 4)
    biasb = S.alloc("biasb", [4, 2, 128], BF16, 2)
    for h in range(4):
        for dsub, off in enumerate((127, 255)):
            S.dma("sp", "scr2", biasf[:, h, dsub, :], bass.AP(scr.tensor, h * 384 + off, [[1535, 128], [1, 128]]),
                  reads=[scrb.buf()], writes=[biasf.buf()])
    S.op("dve", lambda e: e.tensor_copy(out=biasb[:, :, :, :], in_=biasf[:, :, :, :]), reads=[biasf.buf()], writes=[biasb.buf()])
    lqk = S.alloc("lqk", [4, 64], F32, 4)
    for i, v in enumerate((lq1, lk1, lq2, lk2)):
        S.dma("sp", "c1", lqk[:, i, :], v.partition_broadcast(128), writes=[lqk.buf()])
    lsm = S.alloc("lsm", [4], F32, 4)
    S.op("dve", lambda e: e.tensor_mul(out=lqk[:, 0, :], in0=lqk[:, 0, :], in1=lqk[:, 1, :]), reads=[lqk.buf()], writes=[lqk.buf()])
    S.op("dve", lambda e: e.tensor_mul(out=lqk[:, 2, :], in0=lqk[:, 2, :], in1=lqk[:, 3, :]), reads=[lqk.buf()], writes=[lqk.buf()])
    S.op("dve", lambda e: e.reduce_sum(out=lsm[:, 0:1], in_=lqk[:, 0, :], axis=AX.X), reads=[lqk.buf()], writes=[lsm.buf()])
    S.op("dve", lambda e: e.reduce_sum(out=lsm[:, 1:2], in_=lqk[:, 2, :], axis=AX.X), reads=[lqk.buf()], writes=[lsm.buf()])
    S.op("act", lambda e: e.activation(out=lsm[:, 0:2], in_=lsm[:, 0:2], func=AF.Exp), reads=[lsm.buf()], writes=[lsm.buf()])
    S.op("dve", lambda e: e.tensor_sub(out=lsm[:, 2:3], in0=lsm[:, 1:2], in1=lsm[:, 0:1]), reads=[lsm.buf()], writes=[lsm.buf()])
    S.op("dve", lambda e: e.tensor_scalar(out=lsm[:, 2:3], in0=lsm[:, 2:3], scalar1=-LAMBDA_INIT, scalar2=None, op0=ALU.add),
         reads=[lsm.buf()], writes=[lsm.buf()])
    NEGLAM = lsm[:, 2:3]
    gsub = S.alloc("gsub", [128], F32, 4)
    S.dma("sp", "c1", gsub[:, :], subln_g.partition_broadcast(128), writes=[gsub.buf()])
    S.op("dve", lambda e: e.tensor_scalar(out=gsub[:, :], in0=gsub[:, :], scalar1=1.0 - LAMBDA_INIT, scalar2=None, op0=ALU.mult),
         reads=[gsub.buf()], writes=[gsub.buf()])

    PT = S.alloc("PT", [6, 512], BF16, 2)
    gcol = S.alloc("gcol", [1], F32, 4)
    S.dma("sp", "c1", gcol[:, :], subln_g.rearrange("(p o) -> p o", o=1), writes=[gcol.buf()])
    S.op("dve", lambda e: e.tensor_scalar(out=gcol[:, :], in0=gcol[:, :], scalar1=1.0 - LAMBDA_INIT, scalar2=None, op0=ALU.mult),
         reads=[gcol.buf()], writes=[gcol.buf()])
    rr = S.alloc("rr", [2, 2, 512], F32, 4)
    o1 = S.alloc("o1", [2, 512], F32, 4)
    oo = S.alloc("oo", [2, 512], F32, 4)
    sqb = S.alloc("sqb", [2, 512], BF16, 2)
    rst = S.alloc("rst", [2, 512], F32, 4)
    stb = (banks[0], banks[1])
    Ab = (banks[2], banks[3])
    Sb = (banks[4], banks[5])
    MSb = banks[6]

    def s_stage(it):
        h, I, s, j, st, pt = it
        r0 = s * 64
        qstart = max(512 * I, 128 * j)
        N = 512 * (I + 1) - qstart
        col0 = qstart - 512 * I
        has_diag = j >= 4 * I
        has_sub = (4 * I - 1) <= j <= (4 * I + 2)
        mm(st[:, col0:col0 + N], kT[r0:r0 + 64, h, j * 128:(j + 1) * 128], qT[r0:r0 + 64, h, qstart:qstart + N],
           True, not (has_diag or has_sub),
           [kT.buf((h, j))] + qT.bufs([(h, t) for t in range(qstart // 128, 4 * I + 4)]), [st.buf()])
        if has_diag:
            c = j * 128 - 512 * I
            mm(st[:, c:c + 128], idb[:, :], biasb[:, h, 0, :], False, not has_sub, [idb.buf(), biasb.buf()], [st.buf()])
        if has_sub:
            c = (j + 1) * 128 - 512 * I
            mm(st[:, c:c + 128], idb[:, :], biasb[:, h, 1, :], False, True, [idb.buf(), biasb.buf()], [st.buf()])
        S.op("act", lambda e, st=st, pt=pt, col0=col0, N=N, h=h: e.activation(
            out=PT[:, pt, col0:col0 + N], in_=st[:, col0:col0 + N], func=AF.Exp, bias=chc[:, h:h + 1], scale=0.125),
            reads=[st.buf(), chc.buf()], writes=[PT.buf(pt)])

    def pv_stage(it):
        h, I, s, j, st, pt = it
        qstart = max(512 * I, 128 * j)
        N = 512 * (I + 1) - qstart
        col0 = qstart - 512 * I
        last = (j == 4 * I + 3)
        S.op("pe", lambda e, s=s, pt=pt, col0=col0, N=N, j=j, h=h, last=last: e.matmul(
            out=Ab[s][:, col0:col0 + N], lhsT=vaug[:, j, h, 0:128], rhs=PT[:, pt, col0:col0 + N], start=(j == 0), stop=last, skip_group_check=True),
            [PT.buf(pt), vaug.buf(j)], [Ab[s].buf()])
        S.op("pe", lambda e, s=s, pt=pt, col0=col0, N=N, j=j, last=last: e.matmul(
            out=Sb[s][:, col0:col0 + N], lhsT=onesb[:, :], rhs=PT[:, pt, col0:col0 + N], start=(j == 0), stop=last, skip_group_check=True),
            [PT.buf(pt), onesb.buf()], [Sb[s].buf()])

    def epilogue_a(h, I, rnd):
        r2 = rnd % 2
        for s in range(2):
            S.op("act", lambda e, s=s, r2=r2: e.activation(out=rr[:, r2, s, :], in_=Sb[s][:, :], func=AF.Ln), reads=[Sb[s].buf()], writes=[rr.buf((r2, s))])
            S.op("act", lambda e, s=s, r2=r2: e.activation(out=rr[:, r2, s, :], in_=rr[:, r2, s, :], func=AF.Exp, scale=-1.0), reads=[rr.buf((r2, s))], writes=[rr.buf((r2, s))])
        tt_op("dve", o1[:, r2, :], Ab[0][:, :], rr[:, r2, 0, :], ALU.mult, [Ab[0].buf(), rr.buf((r2, 0))], [o1.buf(r2)])
        tt_op("dve", oo[:, r2, :], Ab[1][:, :], rr[:, r2, 1, :], ALU.mult, [Ab[1].buf(), rr.buf((r2, 1))], [oo.buf(r2)])
        S.op("dve", lambda e, r2=r2: e.scalar_tensor_tensor(out=oo[:, r2, :], in0=oo[:, r2, :], scalar=NEGLAM, in1=o1[:, r2, :], op0=ALU.mult, op1=ALU.add),
             reads=[oo.buf(r2), o1.buf(r2), lsm.buf()], writes=[oo.buf(r2)])
        tt_op("dve", sqb[:, r2, :], oo[:, r2, :], oo[:, r2, :], ALU.mult, [oo.buf(r2)], [sqb.buf(r2)])

    def epilogue_b(h, I, rnd):
        r2 = rnd % 2
        mm(MSb[:, :], onesb[:, :], sqb[:, r2, :], True, True, [onesb.buf(), sqb.buf(r2)], [MSb.buf()])
        S.op("act", lambda e, r2=r2: e.activation(out=rst[:, r2, :], in_=MSb[:, :], func=AF.Ln, bias=EPS, scale=1.0 / 128.0),
             reads=[MSb.buf(), cst.buf()], writes=[rst.buf(r2)])
        S.op("act", lambda e, r2=r2: e.activation(out=rst[:, r2, :], in_=rst[:, r2, :], func=AF.Exp, scale=-0.5), reads=[rst.buf(r2)], writes=[rst.buf(r2)])
        S.op("dve", lambda e, r2=r2, h=h, I=I: e.scalar_tensor_tensor(out=catT[:, 4 + h, I * 512:(I + 1) * 512], in0=oo[:, r2, :], scalar=gcol[:, 0:1], in1=rst[:, r2, :],
                                                                    op0=ALU.mult, op1=ALU.mult),
             reads=[oo.buf(r2), gcol.buf(), rst.buf(r2)], writes=catT.bufs([(4 + h, 4 * I + i) for i in range(4)]))

    def tt_op(eng, out_ap, a_ap, b_ap, op, reads, writes):
        S.op(eng, lambda e: e.tensor_tensor(out=out_ap, in0=a_ap, in1=b_ap, op=op), reads, writes)

    stb4 = (banks[0], banks[1], banks[7], banks[6])
    iters = []
    k = 0
    for h in range(4):
        for I in range(4):
            for j in range(4 * I + 4):
                for s in range(2):
                    iters.append((h, I, s, j, stb4[k % 4], k % 6))
                    k += 1
    npair = len(iters) // 2
    pending = None
    since = 0
    rnd = 0
    for p in range(npair):
        s_stage(iters[2 * p]); s_stage(iters[2 * p + 1])
        since += 1
        if pending is not None and since >= 2:
            epilogue_b(*pending); pending = None
        if p >= 1:
            pv_stage(iters[2 * p - 2]); pv_stage(iters[2 * p - 1])
            prev, it = iters[2 * p - 1], iters[2 * p]
            if (prev[0], prev[1]) != (it[0], it[1]):
                if pending is not None:
                    epilogue_b(*pending); pending = None
                epilogue_a(prev[0], prev[1], rnd)
                pending = (prev[0], prev[1], rnd); since = 0
                rnd += 1
    pv_stage(iters[-2]); pv_stage(iters[-1])
    if pending is not None:
        epilogue_b(*pending)
    epilogue_a(iters[-1][0], iters[-1][1], rnd)
    epilogue_b(iters[-1][0], iters[-1][1], rnd)
    S.free("rb", "chc", "ohs", "Lh", "gsb", "biasf", "biasb", "lqk", "lsm", "gsub", "PT", "qT", "kT", "vaug", "gcol", "rr", "o1", "oo", "sqb", "rst")
    if upto == "D":
        S.emit()
        return nc


    if str(2) in os.environ.get("BARRIERS", ""):
        S.barrier()
    I32 = mybir.dt.int32
    W2 = 2048

    def A8(name, dt_=F32):
        return S.alloc(name, [W2], dt_, 4)

    def tt_op(eng, out_ap, a_ap, b_ap, op, reads, writes):
        S.op(eng, lambda e: e.tensor_tensor(out=out_ap, in0=a_ap, in1=b_ap, op=op), reads, writes)

    def ts_op(eng, out_ap, a_ap, s1, s2, op0, op1, reads, writes):
        if s2 is None:
            S.op(eng, lambda e: e.tensor_scalar(out=out_ap, in0=a_ap, scalar1=s1, scalar2=None, op0=op0), reads, writes)
        else:
            S.op(eng, lambda e: e.tensor_scalar(out=out_ap, in0=a_ap, scalar1=s1, scalar2=s2, op0=op0, op1=op1), reads, writes)

    ki = A8("ki", I32)
    kf = A8("kf")

    def sin_from_u(u, out):
        S.op("dve", lambda e: e.tensor_copy(out=ki[:, :], in_=u[:, :]), reads=[u.buf()], writes=[ki.buf()])
        S.op("dve", lambda e: e.tensor_copy(out=kf[:, :], in_=ki[:, :]), reads=[ki.buf()], writes=[kf.buf()])
        tt_op("dve", u[:, :], u[:, :], kf[:, :], ALU.subtract, [u.buf(), kf.buf()], [u.buf()])
        S.op("dve", lambda e: e.scalar_tensor_tensor(out=u[:, :], in0=u[:, :], scalar=0.0, in1=u[:, :], op0=ALU.is_lt, op1=ALU.add),
             reads=[u.buf()], writes=[u.buf()])
        S.op("act", lambda e: e.activation(out=out[:, :], in_=u[:, :], func=AF.Sin, bias=cst[:, 5:6], scale=2 * PI),
             reads=[u.buf(), cst.buf()], writes=[out.buf()])

    lr = A8("lr"); li = A8("li"); lrdt = A8("lrdt"); ang = A8("ang")
    dtr = S.alloc("dtr", [32], F32, 4)
    S.dma("sp", "c2", lr[:, :], lam_re.partition_broadcast(128), writes=[lr.buf()])
    S.dma("sp", "c2", li[:, :], lam_im.partition_broadcast(128), writes=[li.buf()])
    S.dma("sp", "c2", dtr[:, :], log_dt.partition_broadcast(128), writes=[dtr.buf()])
    S.op("act", lambda e: e.activation(out=dtr[:, :], in_=dtr[:, :], func=AF.Exp), reads=[dtr.buf()], writes=[dtr.buf()])
    dt_b = dtr[:, :].unsqueeze(2).to_broadcast([128, 32, 64])
    v3 = lambda t: t[:, :].rearrange("p (g s) -> p g s", s=64)
    tt_op("dve", v3(lrdt), v3(lr), dt_b, ALU.mult, [lr.buf(), dtr.buf()], [lrdt.buf()])
    tt_op("dve", v3(ang), v3(li), dt_b, ALU.mult, [li.buf(), dtr.buf()], [ang.buf()])
    mg = A8("mg"); sn = A8("sn"); cs = A8("cs"); ua = A8("ua"); fre = A8("fre"); fim = A8("fim")
    S.op("act", lambda e: e.activation(out=mg[:, :], in_=lrdt[:, :], func=AF.Exp), reads=[lrdt.buf()], writes=[mg.buf()])
    ts_op("dve", ua[:, :], ang[:, :], 1.0 / (2 * PI), 1.5, ALU.mult, ALU.add, [ang.buf()], [ua.buf()])
    sin_from_u(ua, sn)
    ts_op("dve", ua[:, :], ang[:, :], 1.0 / (2 * PI), 1.75, ALU.mult, ALU.add, [ang.buf()], [ua.buf()])
    sin_from_u(ua, cs)
    tt_op("dve", cs[:, :], mg[:, :], cs[:, :], ALU.mult, [mg.buf(), cs.buf()], [cs.buf()])
    ts_op("dve", cs[:, :], cs[:, :], -1.0, None, ALU.add, None, [cs.buf()], [cs.buf()])
    tt_op("dve", sn[:, :], mg[:, :], sn[:, :], ALU.mult, [mg.buf(), sn.buf()], [sn.buf()])
    tt_op("dve", mg[:, :], lr[:, :], lr[:, :], ALU.mult, [lr.buf()], [mg.buf()])
    tt_op("dve", kf[:, :], li[:, :], li[:, :], ALU.mult, [li.buf()], [kf.buf()])
    tt_op("dve", mg[:, :], mg[:, :], kf[:, :], ALU.add, [mg.buf(), kf.buf()], [mg.buf()])
    S.op("dve", lambda e: e.reciprocal(out=mg[:, :], in_=mg[:, :]), reads=[mg.buf()], writes=[mg.buf()])
    tt_op("dve", fre[:, :], cs[:, :], lr[:, :], ALU.mult, [cs.buf(), lr.buf()], [fre.buf()])
    tt_op("dve", kf[:, :], sn[:, :], li[:, :], ALU.mult, [sn.buf(), li.buf()], [kf.buf()])
    tt_op("dve", fre[:, :], fre[:, :], kf[:, :], ALU.add, [fre.buf(), kf.buf()], [fre.buf()])
    tt_op("dve", fre[:, :], fre[:, :], mg[:, :], ALU.mult, [fre.buf(), mg.buf()], [fre.buf()])
    tt_op("dve", fim[:, :], sn[:, :], lr[:, :], ALU.mult, [sn.buf(), lr.buf()], [fim.buf()])
    tt_op("dve", kf[:, :], cs[:, :], li[:, :], ALU.mult, [cs.buf(), li.buf()], [kf.buf()])
    tt_op("dve", fim[:, :], fim[:, :], kf[:, :], ALU.subtract, [fim.buf(), kf.buf()], [fim.buf()])
    tt_op("dve", fim[:, :], fim[:, :], mg[:, :], ALU.mult, [fim.buf(), mg.buf()], [fim.buf()])
    if os.environ.get("S5_STOP") == "1":
        S.emit()
        return nc
    S.free("lr", "li", "dtr")
    Wmr = A8("Wmr"); Wmi = A8("Wmi")
    S.op("act", lambda e: e.activation(out=mg[:, :], in_=lrdt[:, :], func=AF.Exp, scale=cst[:, 1:2]), reads=[lrdt.buf(), cst.buf()], writes=[mg.buf()])
    ts_op("dve", ua[:, :], ang[:, :], cst[:, 2:3], cst[:, 3:4], ALU.mult, ALU.add, [ang.buf(), cst.buf()], [ua.buf()])
    sin_from_u(ua, sn)
    ts_op("dve", ua[:, :], ang[:, :], cst[:, 2:3], cst[:, 4:5], ALU.mult, ALU.add, [ang.buf(), cst.buf()], [ua.buf()])
    sin_from_u(ua, cs)
    tt_op("dve", Wmr[:, :], mg[:, :], cs[:, :], ALU.mult, [mg.buf(), cs.buf()], [Wmr.buf()])
    S.op("dve", lambda e: e.scalar_tensor_tensor(out=Wmi[:, :], in0=mg[:, :], scalar=-1.0, in1=sn[:, :], op0=ALU.mult, op1=ALU.mult),
         reads=[mg.buf(), sn.buf()], writes=[Wmi.buf()])
    if os.environ.get("S5_STOP") == "2":
        S.emit()
        return nc
    trow = S.alloc("trow", [128], F32, 4)
    S.dma("sp", "c2", trow[:, :], c_trow, writes=[trow.buf()])
    lrdtT = A8("lrdtT"); angT = A8("angT")
    bi = 0
    for (src, dst) in ((lrdt, lrdtT), (ang, angT)):
        for q4 in range(4):
            bk = banks[bi % 4]; bi += 1
            for i in range(4):
                j = q4 * 4 + i
                S.op("pe", lambda e, bk=bk, i=i, j=j, src=src: e.transpose(out=bk[:, i * 128:(i + 1) * 128], in_=src[:, j * 128:(j + 1) * 128], identity=idf[:, :]),
                     reads=[src.buf(), idf.buf()], writes=[bk.buf()])
            copy_op("dve", dst[:, q4 * 512:(q4 + 1) * 512], bk[:, :], [bk.buf()], [dst.buf()])
    S.free("lrdt", "ang")
    WpTr = A8("WpTr"); WpTi = A8("WpTi")
    trow_b = trow[:, :].unsqueeze(1).to_broadcast([128, 16, 128])
    v16 = lambda t: t[:, :].rearrange("p (j t) -> p j t", t=128)
    tt_op("dve", v16(lrdtT), v16(lrdtT), trow_b, ALU.mult, [lrdtT.buf(), trow.buf()], [lrdtT.buf()])
    S.op("act", lambda e: e.activation(out=mg[:, :], in_=lrdtT[:, :], func=AF.Exp), reads=[lrdtT.buf()], writes=[mg.buf()])
    tt_op("dve", v16(angT), v16(angT), trow_b, ALU.mult, [angT.buf(), trow.buf()], [angT.buf()])
    ts_op("dve", ua[:, :], angT[:, :], 1.0 / (2 * PI), 1.5, ALU.mult, ALU.add, [angT.buf()], [ua.buf()])
    sin_from_u(ua, sn)
    ts_op("dve", ua[:, :], angT[:, :], 1.0 / (2 * PI), 1.75, ALU.mult, ALU.add, [angT.buf()], [ua.buf()])
    sin_from_u(ua, cs)
    tt_op("dve", WpTr[:, :], mg[:, :], cs[:, :], ALU.mult, [mg.buf(), cs.buf()], [WpTr.buf()])
    tt_op("dve", WpTi[:, :], mg[:, :], sn[:, :], ALU.mult, [mg.buf(), sn.buf()], [WpTi.buf()])
    S.free("lrdtT", "angT", "ua", "ki", "kf", "mg", "trow")
    if os.environ.get("S5_STOP") == "3":
        S.emit()
        return nc
    maskB = S.alloc("maskB", [4, 128], F32, 4)
    maskC = S.alloc("maskC", [4, 128], F32, 4)
    S.dma("sp", "c2", maskB[:, :, :], c_maskB, writes=[maskB.buf()])
    S.dma("sp", "c2", maskC[:, :, :], c_maskC, writes=[maskC.buf()])
    bnat = S.alloc("bnat", [2, 16, 16], F32, 4)
    S.dma("sp", "c2", bnat[:, 0, :, :], b_re.rearrange("(j p) h -> p j h", p=128), writes=[bnat.buf()])
    S.dma("sp", "c2", bnat[:, 1, :, :], b_im.rearrange("(j p) h -> p j h", p=128), writes=[bnat.buf()])
    bn8 = S.alloc("bn8", [2, 16, 8, 16], F32, 4)
    for ri in range(2):
        S.op("dve", lambda e, ri=ri: e.tensor_copy(out=bn8[:, ri, :, :, :], in_=bnat[:, ri, :, :].unsqueeze(2).to_broadcast([128, 16, 8, 16])),
             reads=[bnat.buf()], writes=[bn8.buf()])
    Bmr = S.alloc("Bmr", [4, 512], BF16, 2)
    Bmi = S.alloc("Bmi", [4, 512], BF16, 2)
    tq = S.alloc("tq", [4, 128], F32, 4)
    for j in range(16):
        ctile, jm = j // 4, j % 4
        bk = banks[j % 4]
        for ri in range(2):
            S.op("pe", lambda e, bk=bk, ri=ri, j=j: e.transpose(out=bk[:, ri * 128:(ri + 1) * 128],
                                                             in_=bn8[:, ri, j, :, :].rearrange("p c h -> p (c h)"), identity=idf[:, :]),
                 reads=[bn8.buf(), idf.buf()], writes=[bk.buf()])
        BTr, BTi = bk[:, 0:128], bk[:, 128:256]
        fr, fi = fre[:, j * 128:(j + 1) * 128], fim[:, j * 128:(j + 1) * 128]
        rd = [bk.buf(), fre.buf(), fim.buf()]
        tt_op("dve", tq[:, 0, :], BTr, fr, ALU.mult, rd, [tq.buf()])
        tt_op("dve", tq[:, 1, :], BTi, fi, ALU.mult, rd, [tq.buf()])
        tt_op("dve", tq[:, 2, :], BTr, fi, ALU.mult, rd, [tq.buf()])
        tt_op("dve", tq[:, 3, :], BTi, fr, ALU.mult, rd, [tq.buf()])
        tt_op("dve", tq[:, 0, :], tq[:, 0, :], tq[:, 1, :], ALU.subtract, [tq.buf()], [tq.buf()])
        tt_op("dve", tq[:, 2, :], tq[:, 2, :], tq[:, 3, :], ALU.add, [tq.buf()], [tq.buf()])
        tt_op("dve", Bmr[:, ctile, jm * 128:(jm + 1) * 128], tq[:, 0, :], maskB[:, jm, :], ALU.mult, [tq.buf(), maskB.buf()], [Bmr.buf()])
        tt_op("dve", Bmi[:, ctile, jm * 128:(jm + 1) * 128], tq[:, 2, :], maskB[:, jm, :], ALU.mult, [tq.buf(), maskB.buf()], [Bmi.buf()])
    S.free("bnat", "bn8", "fre", "fim")
    if os.environ.get("S5_STOP") == "4":
        S.emit()
        return nc
    cnat = S.alloc("cnat", [2, 4, 64], F32, 4)
    S.dma("sp", "c2", cnat[:, 0, :, :], c_re.rearrange("(ct p) s -> p ct s", p=128), writes=[cnat.buf()])
    S.dma("sp", "c2", cnat[:, 1, :, :], c_im.rearrange("(ct p) s -> p ct s", p=128), writes=[cnat.buf()])
    cn2 = S.alloc("cn2", [2, 4, 2, 64], F32, 4)
    for ri in range(2):
        S.op("dve", lambda e, ri=ri: e.tensor_copy(out=cn2[:, ri, :, :, :], in_=cnat[:, ri, :, :].unsqueeze(2).to_broadcast([128, 4, 2, 64])),
             reads=[cnat.buf()], writes=[cn2.buf()])
    Cmr = S.alloc("Cmr", [16, 128], BF16, 2)
    Cmi = S.alloc("Cmi", [16, 128], BF16, 2)
    for ctile in range(4):
        bk = banks[ctile % 4]
        for ri in range(2):
            S.op("pe", lambda e, bk=bk, ri=ri, ctile=ctile: e.transpose(out=bk[:, ri * 128:(ri + 1) * 128],
                                                                    in_=cn2[:, ri, ctile, :, :].rearrange("p c s -> p (c s)"), identity=idf[:, :]),
                 reads=[cn2.buf(), idf.buf()], writes=[bk.buf()])
        for jm in range(4):
            j = ctile * 4 + jm
            tt_op("dve", Cmr[:, j, :], bk[:, 0:128], maskC[:, jm, :], ALU.mult, [bk.buf(), maskC.buf()], [Cmr.buf()])
            S.op("dve", lambda e, bk=bk, j=j, jm=jm: e.scalar_tensor_tensor(out=Cmi[:, j, :], in0=bk[:, 128:256], scalar=-1.0, in1=maskC[:, jm, :],
                                                                         op0=ALU.mult, op1=ALU.mult),
                 reads=[bk.buf(), maskC.buf()], writes=[Cmi.buf()])
    S.free("cnat", "cn2", "maskB", "maskC", "tq", "sn", "cs")
    if os.environ.get("S5_STOP") == "5":
        S.emit()
        return nc
    dnat = S.alloc("dnat", [2, 128], F32, 4)
    S.dma("sp", "c2", dnat[0:4, 0, :], s5_d.rearrange("(ct p) -> ct p", p=128), writes=[dnat.buf()])
    S.dma("sp", "c2", dnat[0:4, 1, :], glu_b.rearrange("(ct p) -> ct p", p=128), writes=[dnat.buf()])
    dcol = S.alloc("dcol", [2, 4], F32, 4)
    bk = banks[0]
    for i in range(2):
        S.op("pe", lambda e, i=i, bk=bk: e.transpose(out=bk[:, i * 4:(i + 1) * 4], in_=dnat[0:4, i, :], identity=idf[0:4, 0:4]),
             reads=[dnat.buf(), idf.buf()], writes=[bk.buf()])
    copy_op("dve", dcol[:, 0, :], bk[:, 0:4], [bk.buf()], [dcol.buf()])
    copy_op("dve", dcol[:, 1, :], bk[:, 4:8], [bk.buf()], [dcol.buf()])
    S.free("dnat")
    if os.environ.get("S5_STOP") == "6":
        S.emit()
        return nc
    gw = wload("gw", glu_w, 4, 512, 512, "w_s5")

    if os.environ.get("S5_STOP") == "7":
        S.emit()
        return nc
    if str(3) in os.environ.get("BARRIERS", ""):
        S.barrier()
    zb = S.alloc("zb", [2, 2, W2], BF16, 2)
    tm = S.alloc("tm", [2, 4, 512], F32, 4)
    bis = S.alloc("bis", [2, 512], F32, 4)
    td = S.alloc("td", [2, 4, 512], F32, 4)
    wc = S.alloc("wc", [2, 2, 512], F32, 4)
    xbf = S.alloc("xbf", [2, 16, 128], BF16, 2)
    car = S.alloc("car", [16, 2], F32, 4)
    ypre = S.alloc("ypre", [2, 4, 512], F32, 4)
    S.op("pool", lambda e: e.memset(car[:, :, :], 0.0), writes=[car.buf(g) for g in range(4)])
    gl = S.alloc("gl", [4, 512], F32, 4)
    glb = S.alloc("glb", [4, 512], BF16, 2)
    g1 = S.alloc("g1", [2, 512], F32, 4)
    bR, bI = banks[0], banks[1]
    wbk = (banks[2], banks[3])
    ybk = banks[4]
    gbk = banks[5]
    mi = 0
    di = 0
    for c in range(int(os.environ.get("S5_CHUNKS", NTT))):
        zs = c % 2
        if os.environ.get("S5_PART") == "1" and c == 0:
            pass
        for ctile in range(4):
            mm(bR[:, :], uT[:, ctile, c * 128:(c + 1) * 128], Bmr[:, ctile, :], True, True, [uT.buf((ctile, c)), Bmr.buf()], [bR.buf()])
            mm(bI[:, :], uT[:, ctile, c * 128:(c + 1) * 128], Bmi[:, ctile, :], True, True, [uT.buf((ctile, c)), Bmi.buf()], [bI.buf()])
            ms = mi % 2; mi += 1
            blk = slice(ctile * 512, (ctile + 1) * 512)
            copy_op("act", bis[:, ms, :], bI[:, :], [bI.buf()], [bis.buf(ms)])
            tt_op("dve", tm[:, ms, 0, :], bR[:, :], Wmr[:, blk], ALU.mult, [bR.buf(), Wmr.buf()], [tm.buf((ms, 0))])
            tt_op("dve", tm[:, ms, 2, :], bR[:, :], Wmi[:, blk], ALU.mult, [bR.buf(), Wmi.buf()], [tm.buf((ms, 2))])
            tt_op("dve", tm[:, ms, 3, :], bI[:, :], Wmr[:, blk], ALU.mult, [bI.buf(), Wmr.buf(), bis.buf(ms)], [tm.buf((ms, 3))])
            tt_op("pool", tm[:, ms, 1, :], bis[:, ms, :], Wmi[:, blk], ALU.mult, [bis.buf(ms), Wmi.buf()], [tm.buf((ms, 1))])
            tt_op("pool", zb[:, zs, 0, blk], tm[:, ms, 0, :], tm[:, ms, 1, :], ALU.subtract, [tm.buf((ms, 0)), tm.buf((ms, 1))], [zb.buf((zs, 0, ctile))])
            tt_op("pool", zb[:, zs, 1, blk], tm[:, ms, 2, :], tm[:, ms, 3, :], ALU.add, [tm.buf((ms, 2)), tm.buf((ms, 3))], [zb.buf((zs, 1, ctile))])
        for g4 in range(4):
            WR, WI = (banks[2], banks[3]) if g4 % 2 == 0 else (banks[6], banks[7])
            for jj in range(4):
                j = 4 * g4 + jj
                mm(WR[:, jj * 128:(jj + 1) * 128], zb[:, zs, 0, j * 128:(j + 1) * 128], trib[:, :], True, True, [zb.buf((zs, 0, j // 4)), trib.buf()], [WR.buf()])
                mm(WI[:, jj * 128:(jj + 1) * 128], zb[:, zs, 1, j * 128:(j + 1) * 128], trib[:, :], True, True, [zb.buf((zs, 1, j // 4)), trib.buf()], [WI.buf()])
            ds = di % 2; di += 1
            for jj in range(4):
                j = 4 * g4 + jj
                S.op("act", lambda e, WR=WR, ds=ds, jj=jj, j=j: e.activation(out=wc[:, ds, 0, jj * 128:(jj + 1) * 128], in_=WR[:, jj * 128:(jj + 1) * 128],
                                                                          func=AF.Identity, bias=car[:, j, 0:1], scale=1.0),
                     reads=[WR.buf(), car.buf(g4)], writes=[wc.buf((ds, 0))])
                S.op("act", lambda e, WI=WI, ds=ds, jj=jj, j=j: e.activation(out=wc[:, ds, 1, jj * 128:(jj + 1) * 128], in_=WI[:, jj * 128:(jj + 1) * 128],
                                                                          func=AF.Identity, bias=car[:, j, 1:2], scale=1.0),
                     reads=[WI.buf(), car.buf(g4)], writes=[wc.buf((ds, 1))])
            gcols = slice(g4 * 512, (g4 + 1) * 512)
            pr, pi_ = WpTr[:, gcols], WpTi[:, gcols]
            wr_, wi_ = wc[:, ds, 0, :], wc[:, ds, 1, :]
            for k, (w_, p_, wk) in enumerate(((wr_, pr, 0), (wi_, pi_, 1), (wr_, pi_, 0), (wi_, pr, 1))):
                tt_op("dve", td[:, ds, k, :], w_, p_, ALU.mult, [wc.buf((ds, wk)), WpTr.buf(), WpTi.buf()], [td.buf((ds, k))])
            xr_out = xbf[:, 0, 4 * g4:4 * g4 + 4, :].rearrange("p j t -> p (j t)")
            xi_out = xbf[:, 1, 4 * g4:4 * g4 + 4, :].rearrange("p j t -> p (j t)")
            tt_op("dve", xr_out, td[:, ds, 0, :], td[:, ds, 1, :], ALU.subtract, [td.buf((ds, 0)), td.buf((ds, 1))], xbf.bufs([(0, 4 * g4 + i) for i in range(4)]))
            tt_op("dve", xi_out, td[:, ds, 2, :], td[:, ds, 3, :], ALU.add, [td.buf((ds, 2)), td.buf((ds, 3))], xbf.bufs([(1, 4 * g4 + i) for i in range(4)]))
            l127 = lambda k, ds=ds: td[:, ds, k, :].rearrange("p (j t) -> p j t", t=128)[:, :, 127]
            tt_op("dve", car[:, 4 * g4:4 * g4 + 4, 0], l127(0), l127(1), ALU.subtract, [td.buf((ds, 0)), td.buf((ds, 1))], [car.buf(g4)])
            tt_op("dve", car[:, 4 * g4:4 * g4 + 4, 1], l127(2), l127(3), ALU.add, [td.buf((ds, 2)), td.buf((ds, 3))], [car.buf(g4)])
        if os.environ.get("S5_PART") == "2":
            continue
        ys = (c // 4) % 2
        for ctile in range(4):
            ybk = banks[4 + ctile % 2]
            ya = ybk[:, 0:128]
            n = 0
            for jm in range(4):
                j = ctile * 4 + jm
                mm(ya, Cmr[:, j, :], xbf[:, 0, j, :], n == 0, False, [Cmr.buf(), xbf.buf((0, j))], [ybk.buf()]); n += 1
                mm(ya, Cmi[:, j, :], xbf[:, 1, j, :], False, jm == 3, [Cmi.buf(), xbf.buf((1, j))], [ybk.buf()]); n += 1
            S.op("dve", lambda e, ya=ya, ctile=ctile, ys=ys, c=c: e.scalar_tensor_tensor(
                out=ypre[:, ys, ctile, (c % 4) * 128:(c % 4 + 1) * 128], in0=uT[:, ctile, c * 128:(c + 1) * 128], scalar=dcol[:, 0, ctile:ctile + 1], in1=ya,
                op0=ALU.mult, op1=ALU.add),
                reads=[uT.buf((ctile, c)), dcol.buf(), ybk.buf()], writes=[ypre.buf((ys, ctile))])
        if c % 4 == 3:
            tb = c // 4
            for ctile in range(4):
                xx = ypre[:, ys, ctile, :]
                xb_ = ypre.buf((ys, ctile))
                tt_op("dve", g1[:, 0, :], xx, xx, ALU.mult, [xb_], [g1.buf(0)])
                ts_op("dve", g1[:, 0, :], g1[:, 0, :], 0.044715, 1.0, ALU.mult, ALU.add, [g1.buf(0)], [g1.buf(0)])
                tt_op("dve", g1[:, 0, :], g1[:, 0, :], xx, ALU.mult, [g1.buf(0), xb_], [g1.buf(0)])
                S.op("act", lambda e: e.activation(out=g1[:, 1, :], in_=g1[:, 0, :], func=AF.Sigmoid, scale=1.5957691216057308), reads=[g1.buf(0)], writes=[g1.buf(1)])
                tt_op("dve", gl[:, ctile, :], xx, g1[:, 1, :], ALU.mult, [xb_, g1.buf(1)], [gl.buf(ctile)])
                copy_op("dve", glb[:, ctile, :], gl[:, ctile, :], [gl.buf(ctile)], [glb.buf(ctile)])
            for cp in range(0 if os.environ.get("S5_G") != "1" else 4, 4):
                gbk = banks[4 + cp % 2]
                for ctile in range(4):
                    mm(gbk[:, :], gw[:, ctile, cp * 128:(cp + 1) * 128], glb[:, ctile, :], ctile == 0, ctile == 3, [gw.buf(0), glb.buf(ctile)], [gbk.buf()])
                if os.environ.get("S5_G") == "2":
                    continue
                S.op("act", lambda e, cp=cp, gbk=gbk: e.activation(out=g1[:, 0, :], in_=gbk[:, :], func=AF.Sigmoid, bias=dcol[:, 1, cp:cp + 1], scale=1.0),
                     reads=[gbk.buf(), dcol.buf()], writes=[g1.buf(0)])
                tt_op("dve", catT[:, cp, tb * 512:(tb + 1) * 512], gl[:, cp, :], g1[:, 0, :], ALU.mult, [gl.buf(cp), g1.buf(0)],
                      catT.bufs([(cp, 4 * tb + i) for i in range(4)]))
    S.free("Wmr", "Wmi", "WpTr", "WpTi", "Bmr", "Bmi", "Cmr", "Cmi", "dcol", "gw", "zb", "tm", "bis", "td", "wc", "xbf", "car", "ypre", "gl", "glb", "g1", "uT")
    if debug:
        S.dma("sp", "dbg", dbg["d_cat"], catT[:, :, :], reads=catT.bufs([(k, t) for k in range(8) for t in range(16)]), final=True)
    if upto == "C":
        S.emit()
        return nc


    if str(4) in os.environ.get("BARRIERS", ""):
        S.barrier()
    hs = S.alloc("hs", [16, D], F32, 4)
    hT = S.alloc("hT", [8, T], BF16, 2)
    wob = wload("wob", w_out, 8, D, 512, "w_e")
    lnp0 = load_lnp("lnp0", ln_in_g, ln_in_b)
    lnp1 = load_lnp("lnp1", ln1_g, ln1_b)
    lbt = S.alloc("lbt", [2, D], BF16, 2)
    bi = 0

    def resid_stats(tt, acc_banks):
        for nh in range(2):
            bk = acc_banks[nh]
            S.op("dve", lambda e, nh=nh, bk=bk: e.scalar_tensor_tensor(out=hs[:, tt, nh * 512:(nh + 1) * 512], in0=hs[:, tt, nh * 512:(nh + 1) * 512],
                                                                     scalar=ALPHA, in1=bk[:, :], op0=ALU.mult, op1=ALU.add),
                 reads=[hs.buf(tt), bk.buf()], writes=[hs.buf(tt)])
        return ln_stats(hs[:, tt, :], hs.buf(tt))

    def ln_finish_a(tt, sl, lnp):
        ln_apply(hs[:, tt, :], hs.buf(tt), lnp, None, None, sl)

    def ln_finish(tt, sl, lnp, do_T, split=False):
        if not split:
            ln_apply(hs[:, tt, :], hs.buf(tt), lnp, hs[:, tt, :], hs.buf(tt), sl)
        else:
            ln_apply_b(hs[:, tt, :], hs.buf(tt), lnp, hs[:, tt, :], hs.buf(tt))
        if do_T:
            s2 = tt % 2
            copy_op("act", lbt[:, s2, :], hs[:, tt, :], [hs.buf(tt)], [lbt.buf(s2)])
            transpose_to_hT(lbt[:, s2, :], lbt.buf(s2), hT, tt)

    def resid_ln(tt, acc_banks, lnp, do_T):
        sl = resid_stats(tt, acc_banks)
        ln_finish(tt, sl, lnp, do_T)

    sl_in = {}
    sl_1 = {}
    accs_of = {}
    for step in range(NTT + 6):
        if step >= 6:
            ln_finish(step - 6, None, lnp1, True, split=True)
        if step < NTT:
            tt = step
            S.dma("sp", "xin", hs[:, tt, :], x[tt * 128:(tt + 1) * 128, :], writes=[hs.buf(tt)])
            sl_in[tt] = ln_stats(hs[:, tt, :], hs.buf(tt))
        if 1 <= step <= NTT:
            tt = step - 1
            ln_apply(hs[:, tt, :], hs.buf(tt), lnp0, None, None, sl_in[tt])
        if 2 <= step <= NTT + 1:
            tt = step - 2
            ln_apply_b(hs[:, tt, :], hs.buf(tt), lnp0, hs[:, tt, :], hs.buf(tt))
            accs = []
            for nh in range(2):
                bk = banks[bi % 4]; bi += 1
                for kt in range(8):
                    mm(bk[:, :], catT[:, kt, tt * 128:(tt + 1) * 128], wob[:, kt, nh * 512:(nh + 1) * 512], kt == 0, kt == 7,
                       [catT.buf((kt, tt)), wob.buf(nh)], [bk.buf()])
                accs.append(bk)
            accs_of[tt] = accs
        if 3 <= step <= NTT + 2:
            tt = step - 3
            sl_1[tt] = resid_stats(tt, accs_of[tt])
        if 4 <= step <= NTT + 3:
            tt = step - 4
            ln_finish_a(tt, sl_1[tt], lnp1)
    S.free("catT", "wob", "lnp0", "lnp1")
    if debug:
        S.dma("sp", "dbg", dbg["d_h1"], hs[:, :, :], reads=hs.bufs(range(16)), final=True)
    if upto == "E":
        S.emit()
        return nc


    if str(5) in os.environ.get("BARRIERS", ""):
        S.barrier()
    wkv = wload("wkv", ca_wkv, 8, 2 * D, 512, "w_f")
    memf = S.alloc("memf", [2, D], F32, 4)
    memb = S.alloc("memb", [2, D], BF16, 2)
    memT = S.alloc("memT", [8, 256], BF16, 2)
    for mt in range(2):
        S.dma("sp", "mem", memf[:, mt, :], mem[mt * 128:(mt + 1) * 128, :], writes=[memf.buf(mt)])
        copy_op("act", memb[:, mt, :], memf[:, mt, :], [memf.buf(mt)], [memb.buf(mt)])
        ctr["pt"] += 1
        pt = ptb[ctr["pt"] % 2]
        for k in range(8):
            S.op("pe", lambda e, k=k, pt=pt, mt=mt: e.transpose(out=pt[:, k, :], in_=memb[:, mt, k * 128:(k + 1) * 128], identity=idb[:, :]),
                 reads=[memb.buf(mt), idb.buf()], writes=[pt.buf()])
        copy_op(evac_eng(), memT[:, :, mt * 128:(mt + 1) * 128], pt[:, :, :], [pt.buf()], [memT.buf()])
    kTca = S.alloc("kTca", [8, 256], BF16, 2)
    vca = S.alloc("vca", [2, D], BF16, 2)
    for ct in range(8):
        bk = banks[bi % 4]; bi += 1
        for kt in range(8):
            mm(bk[:, 0:256], wkv[:, kt, ct * 128:(ct + 1) * 128], memT[:, kt, :], kt == 0, kt == 7, [wkv.buf(ct // 4), memT.buf()], [bk.buf()])
        copy_op(evac_eng(), kTca[:, ct, :], bk[:, 0:256], [bk.buf()], [kTca.buf()])
    for mt in range(2):
        for nh in range(2):
            bk = banks[bi % 4]; bi += 1
            for kt in range(8):
                mm(bk[:, :], memT[:, kt, mt * 128:(mt + 1) * 128], wkv[:, kt, D + nh * 512:D + (nh + 1) * 512], kt == 0, kt == 7,
                   [wkv.buf(2 + nh), memT.buf()], [bk.buf()])
            copy_op(evac_eng(), vca[:, mt, nh * 512:(nh + 1) * 512], bk[:, :], [bk.buf()], [vca.buf()])
    S.free("wkv", "memf", "memb", "memT")
    wqb = wload("wqb", ca_wq, 8, D, 512, "w_f2")
    wo2 = wload("wo2", ca_wo, 8, D, 512, "w_f2")
    lnp2 = load_lnp("lnp2", ln2_g, ln2_b)
    qTc = S.alloc("qTc", [2, 8, 512], BF16, 2)
    PTc = S.alloc("PTc", [2, 2, 512], BF16, 2)
    oTc = S.alloc("oTc", [8, 512], BF16, 2)
    rcs = S.alloc("rcs", [2, 512], F32, 4)
    pendF = None
    pendF2 = None
    fst = {"bi": 0, "sc": 0}

    def nbank():
        fst["bi"] += 1
        return banks[fst["bi"] % 4]

    def F_Q(tb):
        qs = tb % 2
        tcols = slice(tb * 512, (tb + 1) * 512)
        hbufs = hT.bufs(range(4 * tb, 4 * tb + 4))
        for ct in range(8):
            bk = nbank()
            for kt in range(8):
                mm(bk[:, :], wqb[:, kt, ct * 128:(ct + 1) * 128], hT[:, kt, tcols], kt == 0, kt == 7, [wqb.buf(ct // 4)] + hbufs, [bk.buf()])
            copy_op(evac_eng(), qTc[:, qs, ct, :], bk[:, :], [bk.buf()], [qTc.buf((qs, ct))])

    def F_HS(tb, hd):
        qs = tb % 2
        ps = hd % 2
        for mt in range(2):
            fst["sc"] += 1
            bk = banks[4 + fst["sc"] % 4]
            for i in range(2):
                ct = 2 * hd + i
                mm(bk[:, :], kTca[:, ct, mt * 128:(mt + 1) * 128], qTc[:, qs, ct, :], i == 0, i == 1, [kTca.buf(), qTc.buf((qs, ct))], [bk.buf()])
            S.op("act", lambda e, bk=bk, ps=ps, mt=mt: e.activation(out=PTc[:, ps, mt, :], in_=bk[:, :], func=AF.Exp, scale=1.0 / 16.0),
                 reads=[bk.buf()], writes=[PTc.buf((ps, mt))])

    def F_HP(tb, hd):
        ps = hd % 2
        sb = nbank()
        for mt in range(2):
            mm(sb[:, :], onesb[:, :], PTc[:, ps, mt, :], mt == 0, mt == 1, [onesb.buf(), PTc.buf((ps, mt))], [sb.buf()])
        S.op("act", lambda e, sb=sb, ps=ps: e.activation(out=rcs[:, ps, :], in_=sb[:, :], func=AF.Ln), reads=[sb.buf()], writes=[rcs.buf(ps)])
        S.op("act", lambda e, ps=ps: e.activation(out=rcs[:, ps, :], in_=rcs[:, ps, :], func=AF.Exp, scale=-1.0), reads=[rcs.buf(ps)], writes=[rcs.buf(ps)])
        for dti in range(2):
            ct = 2 * hd + dti
            bk = nbank()
            for mt in range(2):
                mm(bk[:, :], vca[:, mt, ct * 128:(ct + 1) * 128], PTc[:, ps, mt, :], mt == 0, mt == 1, [vca.buf(), PTc.buf((ps, mt))], [bk.buf()])
            tt_op("dve", oTc[:, ct, :], bk[:, :], rcs[:, ps, :], ALU.mult, [bk.buf(), rcs.buf(ps)], [oTc.buf(ct)])

    def F_W(tb):
        nonlocal_state = None
        prev_acc = None
        for tl in range(5):
            tt = 4 * tb + tl
            if tl < 4:
                if pstate["p3"] is not None:
                    ln_finish(pstate["p3"][0], None, lnp2, True, split=True)
                    pstate["p3"] = None
                accs = []
                for nh in range(2):
                    bk = nbank()
                    for kt in range(8):
                        mm(bk[:, :], oTc[:, kt, tl * 128:(tl + 1) * 128], wo2[:, kt, nh * 512:(nh + 1) * 512], kt == 0, kt == 7,
                           [oTc.buf(kt), wo2.buf(nh)], [bk.buf()])
                    accs.append(bk)
            if prev_acc is not None:
                pt_, pa_ = prev_acc
                slF = resid_stats(pt_, pa_)
                if pstate["p1"] is not None:
                    ln_finish_a(pstate["p1"][0], pstate["p1"][1], lnp2)
                if pstate["p3"] is not None:
                    ln_finish(pstate["p3"][0], None, lnp2, True, split=True)
                pstate["p3"] = pstate["p2"]
                pstate["p2"] = pstate["p1"]
                pstate["p1"] = (pt_, slF)
            prev_acc = (tt, accs) if tl < 4 else None

    pstate = {"p1": None, "p2": None, "p3": None}
    F_Q(0)
    for tb in range(4):
        F_HS(tb, 0)
        for hd in range(4):
            if hd < 3:
                F_HS(tb, hd + 1)
            F_HP(tb, hd)
        if tb < 3:
            F_Q(tb + 1)
        F_W(tb)
    if pstate["p3"] is not None:
        ln_finish(pstate["p3"][0], None, lnp2, True, split=True)
    ln_finish_a(pstate["p1"][0], pstate["p1"][1], lnp2)
    if pstate["p2"] is not None:
        ln_finish(pstate["p2"][0], None, lnp2, True, split=True)
    ln_finish(pstate["p1"][0], None, lnp2, True, split=True)
    S.free("wqb", "wo2", "lnp2", "qTc", "PTc", "oTc", "rcs", "kTca", "vca", "lbt")
    if debug:
        S.dma("sp", "dbg", dbg["d_h2"], hs[:, :, :], reads=hs.bufs(range(16)), final=True)
    if upto == "F":
        S.emit()
        return nc


    if str(6) in os.environ.get("BARRIERS", ""):
        S.barrier()
    lnp3 = load_lnp("lnp3", ln3_g, ln3_b)
    gus = S.alloc("gus", [3, 8, 2, 128], BF16, 2)
    actT = S.alloc("actT", [11, T], BF16, 2)
    sg = S.alloc("sg", [2, 512], F32, 4)
    guv = w_gu.rearrange("(kt p) n -> p kt n", p=128)
    gi = 0
    pendG = None
    pendG2 = None

    def g_finish(t_):
        ln_apply_b(hs[:, t_, :], hs.buf(t_), lnp3, hs[:, t_, :], hs.buf(t_))
        S.dma("sp", "out", out[t_ * 128:(t_ + 1) * 128, :], hs[:, t_, :], reads=[hs.buf(t_)], final=True)

    for ps_ in range(2):
        wd = S.alloc("wd", [11, D], BF16, 2)
        dv = w_dn[ps_ * 11 * 128:(ps_ + 1) * 11 * 128, :].rearrange("(j p) n -> p j n", p=128)
        for jl in range(11):
            j = ps_ * 11 + jl
            gs = gi % 3; gi += 1
            S.dma("pool", f"w_gu{gs}", gus[:, gs, :, 0, :], guv[:, :, j * 128:(j + 1) * 128], writes=[gus.buf(gs)])
            S.dma("pool", f"w_gu{gs}", gus[:, gs, :, 1, :], guv[:, :, FFN_H + j * 128:FFN_H + (j + 1) * 128], writes=[gus.buf(gs)])
            if jl < 11:
                S.dma("pool", "w_dn", wd[:, jl, :], dv[:, jl, :], writes=[wd.buf(jl)])
            for tb in range(4):
                tcols = slice(tb * 512, (tb + 1) * 512)
                hbufs = hT.bufs(range(4 * tb, 4 * tb + 4))
                bg = banks[(bi % 2) * 2]; bu_ = banks[(bi % 2) * 2 + 1]; bi += 1
                for kt in range(8):
                    mm(bg[:, :], gus[:, gs, kt, 0, :], hT[:, kt, tcols], kt == 0, kt == 7, [gus.buf(gs)] + hbufs, [bg.buf()])
                for kt in range(8):
                    mm(bu_[:, :], gus[:, gs, kt, 1, :], hT[:, kt, tcols], kt == 0, kt == 7, [gus.buf(gs)] + hbufs, [bu_.buf()])
                s2 = bi % 2
                S.op("act", lambda e, bg=bg, s2=s2: e.activation(out=sg[:, s2, :], in_=bg[:, :], func=AF.Silu), reads=[bg.buf()], writes=[sg.buf(s2)])
                tt_op("dve", actT[:, jl, tcols], sg[:, s2, :], bu_[:, :], ALU.mult, [sg.buf(s2), bu_.buf()], actT.bufs([(jl, 4 * tb + i) for i in range(4)]))
        for tt in range(NTT):
            accs = []
            for nh in range(2):
                bk = banks[4 + nh + 2 * (tt % 2)]
                for jl in range(11):
                    mm(bk[:, :], actT[:, jl, tt * 128:(tt + 1) * 128], wd[:, jl, nh * 512:(nh + 1) * 512], jl == 0, jl == 10,
                       [actT.buf((jl, tt)), wd.buf(jl)], [bk.buf()])
                accs.append(bk)
            if ps_ == 0:
                for nh in range(2):
                    bk = accs[nh]
                    S.op("dve", lambda e, nh=nh, bk=bk, tt=tt: e.scalar_tensor_tensor(out=hs[:, tt, nh * 512:(nh + 1) * 512], in0=hs[:, tt, nh * 512:(nh + 1) * 512],
                                                                                 scalar=ALPHA, in1=bk[:, :], op0=ALU.mult, op1=ALU.add),
                         reads=[hs.buf(tt), bk.buf()], writes=[hs.buf(tt)])
            else:
                for nh in range(2):
                    bk = accs[nh]
                    tt_op("dve", hs[:, tt, nh * 512:(nh + 1) * 512], hs[:, tt, nh * 512:(nh + 1) * 512], bk[:, :], ALU.add, [hs.buf(tt), bk.buf()], [hs.buf(tt)])
                slG = ln_stats(hs[:, tt, :], hs.buf(tt))
                if pendG2 is not None:
                    g_finish(pendG2[0])
                if pendG is not None:
                    ln_apply(hs[:, pendG[0], :], hs.buf(pendG[0]), lnp3, None, None, pendG[1])
                pendG2 = pendG
                pendG = (tt, slG)
        if ps_ == 1:
            if pendG2 is not None:
                g_finish(pendG2[0])
            ln_apply(hs[:, pendG[0], :], hs.buf(pendG[0]), lnp3, None, None, pendG[1])
            g_finish(pendG[0])
        S.free("wd")
    S.emit()
    return nc


_CACHE = {}


def kernel(**inputs):
    consts = host_consts()
    shared = {}
    for k, v in inputs.items():
        if k in ("x", "mem"):
            continue
        a = np.ascontiguousarray(np.asarray(v, dtype=np.float32))
        if k == "rel_bias":
            shared[k] = a
        elif a.ndim >= 2 and a.shape[0] == 1:
            a = a[0]
            if k in ("s5_lambda_re", "s5_lambda_im", "s5_d"):
                a = a.reshape(-1)
            elif k in ("s5_b_re", "s5_b_im"):
                a = a.reshape(2048, 16)
            elif k in ("s5_c_re", "s5_c_im"):
                a = a.reshape(512, 64)
            shared[k] = np.ascontiguousarray(a)
        else:
            shared[k] = a
    shared.update(consts)
    xs = np.asarray(inputs["x"], dtype=np.float32)
    ms = np.asarray(inputs["mem"], dtype=np.float32)
    if "nc" not in _CACHE:
        _CACHE["nc"] = build_program(False)
    nc = _CACHE["nc"]
    in_maps = []
    for b in range(8):
        m = dict(shared)
        m["x"] = np.ascontiguousarray(xs[b])
        m["mem"] = np.ascontiguousarray(ms[b])
        in_maps.append(m)
    res = run_bass_kernel_spmd(nc, in_maps, core_ids=list(range(8)))
    return np.stack([np.asarray(r["out"], dtype=np.float32) for r in res.results], axis=0)
```

```python
import numpy as np
import concourse.bass as bass
import concourse.mybir as mybir
from concourse.bass_utils import run_bass_kernel_spmd

F32 = mybir.dt.float32
BF16 = mybir.dt.bfloat16
ALU = mybir.AluOpType
AF = mybir.ActivationFunctionType
AX = mybir.AxisListType

ENGS = ("pe", "act", "dve", "pool", "sp")


class Instr:
    __slots__ = ("eng", "fn", "waits", "signal", "idx", "val", "dma_sem", "dma_val")

    def __init__(self, eng, fn):
        self.eng = eng
        self.fn = fn
        self.waits = []
        self.signal = False
        self.idx = -1
        self.val = 0
        self.dma_sem = None
        self.dma_val = 0


class Buf:
    __slots__ = ("name", "last_w", "readers")

    def __init__(self, name, inherit=()):
        self.name = name
        self.last_w = None
        self.readers = list(inherit)


class Ten:
    def __init__(self, h, name, inherit=()):
        self.h = h
        self.name = name
        self.inherit = list(inherit)
        self._bufs = {}

    def __getitem__(self, k):
        return self.h[k]

    def buf(self, key=0):
        b = self._bufs.get(key)
        if b is None:
            b = Buf(f"{self.name}{key}", self.inherit)
            self._bufs[key] = b
        return b

    def bufs(self, keys):
        return [self.buf(k) for k in keys]

    def all_instrs(self):
        out = list(self.inherit)
        for b in self._bufs.values():
            if b.last_w is not None:
                out.append(b.last_w)
            out.extend(b.readers)
        return out


class Sched:
    def __init__(self, nc, sbuf_base=16512, sbuf_bytes=229312):
        self.nc = nc
        self.q = {e: [] for e in ENGS}
        self.waited = {e: {} for e in ENGS}
        self.dma_count = {}
        self.sbuf_bytes = sbuf_bytes
        self.sbuf_base = sbuf_base
        self.live = {}
        self.hist = []
        self.n_alloc = 0
        self.final_dma = []
        self.ring_pos = {}
        self.ring_last = {}

    def alloc(self, name, free_shape, dtype, nbytes_el):
        size = int(np.prod(free_shape)) * nbytes_el
        size = (size + 63) // 64 * 64
        segs = sorted((o, s) for (o, s, _) in self.live.values())
        off = self.sbuf_base
        for (o, s) in segs:
            if off + size <= o:
                break
            off = max(off, o + s)
        if off + size > self.sbuf_bytes:
            raise RuntimeError(f"SBUF arena overflow allocating {name} ({size} B); live={[(k, v[0], v[1]) for k, v in self.live.items()]}")
        inherit = []
        for (o, s, t) in self.hist:
            if o < off + size and off < o + s:
                inherit.extend(t.all_instrs())
        comp = {}
        for ins in inherit:
            key = ins.dma_sem if ins.dma_sem is not None else ins.eng
            cur = comp.get(key)
            if cur is None or (ins.dma_val if ins.dma_sem is not None else ins.idx) > (cur.dma_val if cur.dma_sem is not None else cur.idx):
                comp[key] = ins
        self.n_alloc += 1
        h = self.nc.alloc_sbuf_tensor_at(f"{name}_{self.n_alloc}", [128] + list(free_shape), dtype, offset=off)
        t = Ten(h, name, list(comp.values()))
        self.live[name] = (off, size, t)
        return t

    def free(self, *names):
        for name in names:
            o, s, t = self.live.pop(name)
            self.hist.append((o, s, t))

    def _need(self, c, p, raw):
        if p is None or p is c:
            return
        E = c.eng
        if p.dma_sem is not None:
            key = "dma:" + p.dma_sem
            if self.waited[E].get(key, 0) >= p.dma_val:
                return
            self.waited[E][key] = p.dma_val
            c.waits.append(p)
            return
        if p.eng == E and c.dma_sem is None:
            if E in ("pe", "sp"):
                return
            if not raw:
                return
        key = p.eng
        if self.waited[E].get(key, -1) >= p.idx:
            return
        self.waited[E][key] = p.idx
        p.signal = True
        c.waits.append(p)

    def _deps(self, c, reads, writes):
        for b in reads:
            self._need(c, b.last_w, True)
        for b in writes:
            self._need(c, b.last_w, False)
            for r in b.readers:
                self._need(c, r, False)
        for b in reads:
            b.readers.append(c)
            if len(b.readers) > 12:
                comp = {}
                for ins in b.readers:
                    key = ins.dma_sem if ins.dma_sem is not None else ins.eng
                    cur = comp.get(key)
                    if cur is None or (ins.dma_val if ins.dma_sem is not None else ins.idx) >= (cur.dma_val if cur.dma_sem is not None else cur.idx):
                        comp[key] = ins
                b.readers = list(comp.values())
        for b in writes:
            b.last_w = c
            b.readers = []

    def op(self, eng, fn, reads=(), writes=()):
        c = Instr(eng, fn)
        c.idx = len(self.q[eng])
        self.q[eng].append(c)
        self._deps(c, reads, writes)
        return c

    NRING = 28

    def dma(self, eng, stream, out, in_, reads=(), writes=(), final=False):
        def fn(e, out=out, in_=in_):
            return e.dma_start(out=out, in_=in_)
        c = Instr(eng, fn)
        c.idx = len(self.q[eng])
        pos = self.ring_pos.get(eng, 0)
        self.ring_pos[eng] = pos + 1
        sem = f"{eng}{pos % self.NRING}"
        n = self.dma_count.get(sem, 0) + 1
        self.dma_count[sem] = n
        c.dma_sem = sem
        c.dma_val = 16 * n
        prev = self.ring_last.get(sem)
        if prev is not None:
            self._need(c, prev, False)
        self.ring_last[sem] = c
        self.q[eng].append(c)
        self._deps(c, reads, writes)
        if final:
            self.final_dma.append(c)
        return c

    def barrier(self):
        lasts = {}
        for e in ENGS:
            for ins in reversed(self.q[e]):
                if ins.dma_sem is None:
                    lasts[e] = ins
                    break
        dmas = []
        for sname, n in self.dma_count.items():
            f = Instr("sp", None)
            f.dma_sem = sname
            f.dma_val = 16 * n
            dmas.append(f)
        for e in ENGS:
            c = Instr(e, lambda eng: eng.nop())
            c.idx = len(self.q[e])
            for e2, p in lasts.items():
                if e2 != e:
                    self._need(c, p, True)
            for f in dmas:
                self._need(c, f, True)
            self.q[e].append(c)

    def emit(self):
        nc = self.nc
        import contextlib
        with contextlib.ExitStack() as st:
            esem = {e: st.enter_context(nc.semaphore(f"s_{e}")) for e in ENGS}
            dsem = {s: st.enter_context(nc.semaphore(f"d_{s}")) for s in self.dma_count}
            for e in ENGS:
                cnt = 0
                for ins in self.q[e]:
                    if ins.signal:
                        cnt += 1
                        ins.val = cnt
            block = st.enter_context(nc.Block())

            def run(eng_name, eobj):
                for ins in self.q[eng_name]:
                    for p in ins.waits:
                        if p.dma_sem is not None:
                            eobj.wait_ge(dsem[p.dma_sem], p.dma_val)
                        else:
                            eobj.wait_ge(esem[p.eng], p.val)
                    r = ins.fn(eobj)
                    if ins.dma_sem is not None:
                        r.then_inc(dsem[ins.dma_sem], 16)
                    elif ins.signal:
                        r.then_inc(esem[eng_name], 1)
                if eng_name == "sp":
                    done = {}
                    for ins in self.final_dma:
                        done[ins.dma_sem] = max(done.get(ins.dma_sem, 0), ins.dma_val)
                    for s, v in done.items():
                        eobj.wait_ge(dsem[s], v)

            @block.tensor
            def _(e):
                run("pe", e)

            @block.scalar
            def _(e):
                run("act", e)

            @block.vector
            def _(e):
                run("dve", e)

            @block.gpsimd
            def _(e):
                run("pool", e)

            @block.sync
            def _(e):
                run("sp", e)


T = 2048
D = 1024
NTT = 16
ALPHA = float(2.0 ** 0.25)
PI = float(np.pi)
LAMBDA_INIT = 0.8 - 0.6 * 1.0
FFN_H = 2816
NJ = 22


def t5_bucket_np(d):
    d = np.asarray(d, dtype=np.int64)
    df = np.maximum(d, 1).astype(np.float32)
    large = 16 + (np.log(df / np.float32(16)) / np.float32(np.log(128 / 16)) * np.float32(16)).astype(np.int32)
    large = np.minimum(large, 31)
    return np.where(d < 16, d, large)


def host_consts():
    c = {}
    c["c_ident"] = np.eye(128, dtype=np.float32)
    c["c_tri"] = np.triu(np.ones((128, 128), dtype=np.float32))
    cst = np.zeros((128, 8), dtype=np.float32)
    tp1 = np.arange(1, 129, dtype=np.float32)
    cst[:, 0] = tp1
    cst[:, 1] = -tp1
    cst[:, 2] = tp1 / np.float32(2 * np.pi)
    cst[:, 3] = 1.5
    cst[:, 4] = 1.75
    cst[:, 5] = -np.pi
    cst[:, 6] = 1e-5
    cst[:, 7] = 1.0
    c["c_cst"] = cst
    c["c_trow"] = np.broadcast_to(tp1[None, :], (128, 128)).astype(np.float32).copy()
    p = np.arange(128)
    mask = np.zeros((128, 4, 128), dtype=np.float32)
    for jm in range(4):
        mask[:, jm, :] = ((p[None, :] // 16) == (2 * jm + p[:, None] // 64)).astype(np.float32)
    c["c_maskC"] = mask
    c["c_maskB"] = np.ascontiguousarray(mask.transpose(2, 1, 0))
    oh = np.zeros((33, 384), dtype=np.float32)
    for m in range(384):
        d = m - 127
        if d < 0:
            oh[32, m] = -30000.0
        else:
            b = int(t5_bucket_np(d))
            oh[b, m] += 8.0
            oh[31, m] -= 8.0
    c["c_oh"] = oh
    return c


def build_program(debug=False, upto=None):
    import os
    nc = bass.Bass("TRN2", target_bir_lowering=False)
    S = Sched(nc)

    def din(name, shape):
        return nc.dram_tensor(name, list(shape), F32, kind="ExternalInput").ap()

    x = din("x", [T, D]); mem = din("mem", [256, D])
    ln_in_g = din("ln_in_g", [D]); ln_in_b = din("ln_in_b", [D])
    w_in = din("w_in", [D, 2048])
    lam_re = din("s5_lambda_re", [2048]); lam_im = din("s5_lambda_im", [2048]); log_dt = din("s5_log_dt", [32])
    b_re = din("s5_b_re", [2048, 16]); b_im = din("s5_b_im", [2048, 16])
    c_re = din("s5_c_re", [512, 64]); c_im = din("s5_c_im", [512, 64])
    s5_d = din("s5_d", [512]); glu_w = din("s5_glu_w", [512, 512]); glu_b = din("s5_glu_b", [512])
    lq1 = din("diff_lq1", [64]); lk1 = din("diff_lk1", [64]); lq2 = din("diff_lq2", [64]); lk2 = din("diff_lk2", [64])
    subln_g = din("diff_subln_g", [128]); rel_bias = din("rel_bias", [32, 4])
    w_out = din("w_out", [D, D]); ln1_g = din("ln1_g", [D]); ln1_b = din("ln1_b", [D])
    ca_wq = din("ca_wq", [D, D]); ca_wkv = din("ca_wkv", [D, 2 * D]); ca_wo = din("ca_wo", [D, D])
    ln2_g = din("ln2_g", [D]); ln2_b = din("ln2_b", [D])
    w_gu = din("ffn_w_gate_up", [D, 2 * FFN_H]); w_dn = din("ffn_w_down", [FFN_H, D])
    ln3_g = din("ln3_g", [D]); ln3_b = din("ln3_b", [D])
    c_ident = din("c_ident", [128, 128]); c_tri = din("c_tri", [128, 128]); c_cst = din("c_cst", [128, 8])
    c_trow = din("c_trow", [128, 128]); c_maskC = din("c_maskC", [128, 4, 128]); c_maskB = din("c_maskB", [128, 4, 128])
    c_oh = din("c_oh", [33, 384])
    out = nc.dram_tensor("out", [T, D], F32, kind="ExternalOutput").ap()
    scr = nc.dram_tensor("bias_scr", [128, 4, 384], F32, kind="Internal").ap()
    dbg = {}
    if debug:
        for nm, shp, dt_ in (("d_hT", [128, 8, T], BF16), ("d_uT", [128, 4, T], BF16), ("d_qT", [128, 4, T], BF16),
                             ("d_kT", [128, 4, T], BF16), ("d_v", [128, 16, 4, 129], BF16), ("d_cat", [128, 8, T], BF16),
                             ("d_h1", [128, 16, D], F32), ("d_h2", [128, 16, D], F32)):
            dbg[nm] = nc.dram_tensor(nm, shp, dt_, kind="ExternalOutput").ap()

    banks = [Ten(nc.alloc_psum_tensor(f"bank{i}", [128, 512], F32), f"bank{i}") for i in range(8)]

    class View:
        def __init__(self, ten):
            self.ten = ten
            self.ap = ten[:, :].bitcast(BF16).rearrange("p (k c) -> p k c", k=8)

        def __getitem__(self, k):
            return self.ap[k]

        def buf(self, key=0):
            return self.ten.buf(key)

    ptb = [View(banks[6]), View(banks[7])]

    idf = S.alloc("idf", [128], F32, 4)
    idb = S.alloc("idb", [128], BF16, 2)
    trib = S.alloc("trib", [128], BF16, 2)
    trif = S.alloc("trif", [128], F32, 4)
    onesb = S.alloc("onesb", [128], BF16, 2)
    onesf = S.alloc("onesf", [128], F32, 4)
    cst = S.alloc("cst", [8], F32, 4)
    stt = S.alloc("ln_st", [8, 12], F32, 4)
    mvt = S.alloc("ln_mv", [8, 2], F32, 4)
    rsd = S.alloc("ln_rs", [8, 1], F32, 4)
    S.dma("sp", "c0", idf[:, :], c_ident, writes=[idf.buf()])
    S.dma("sp", "c0", trif[:, :], c_tri, writes=[trif.buf()])
    S.dma("sp", "c0", cst[:, :], c_cst, writes=[cst.buf()])
    S.op("dve", lambda e: e.tensor_copy(out=idb[:, :], in_=idf[:, :]), reads=[idf.buf()], writes=[idb.buf()])
    S.op("dve", lambda e: e.tensor_copy(out=trib[:, :], in_=trif[:, :]), reads=[trif.buf()], writes=[trib.buf()])
    S.op("pool", lambda e: e.memset(onesb[:, :], 1.0), writes=[onesb.buf()])
    S.op("pool", lambda e: e.memset(onesf[:, :], 1.0), writes=[onesf.buf()])
    EPS = cst[:, 6:7]

    ctr = {"ev": 0, "ln": 0, "pt": 0}

    def evac_eng():
        ctr["ev"] += 1
        return "act" if ctr["ev"] % 2 else "dve"

    def copy_op(eng, out_ap, in_ap, reads, writes):
        if eng == "act":
            S.op("act", lambda e: e.activation(out=out_ap, in_=in_ap, func=AF.Copy), reads, writes)
        else:
            S.op(eng, lambda e: e.tensor_copy(out=out_ap, in_=in_ap), reads, writes)

    def mm(out_ap, lhsT, rhs, start, stop, reads, writes):
        S.op("pe", lambda e: e.matmul(out=out_ap, lhsT=lhsT, rhs=rhs, start=start, stop=stop), reads, writes)

    def load_lnp(name, g, b):
        t = S.alloc(name, [2, D], F32, 4)
        S.dma("sp", "lnp", t[:, 0, :], g.partition_broadcast(128), writes=[t.buf()])
        S.dma("sp", "lnp", t[:, 1, :], b.partition_broadcast(128), writes=[t.buf()])
        return t

    def ln_stats(x_ap, x_buf):
        ctr["ln"] += 1
        sl = ctr["ln"] % 8
        for c in range(2):
            S.op("dve", lambda e, c=c: e.bn_stats(out=stt[:, sl, c * 6:(c + 1) * 6], in_=x_ap[:, c * 512:(c + 1) * 512]),
                 reads=[x_buf], writes=[stt.buf(sl)])
        S.op("dve", lambda e: e.bn_aggr(out=mvt[:, sl, :], in_=stt[:, sl, :]), reads=[stt.buf(sl)], writes=[mvt.buf(sl)])
        S.op("act", lambda e: e.activation(out=rsd[:, sl, :], in_=mvt[:, sl, 1:2], func=AF.Sqrt, bias=EPS, scale=1.0),
             reads=[mvt.buf(sl), cst.buf()], writes=[rsd.buf(sl)])
        S.op("dve", lambda e: e.reciprocal(out=rsd[:, sl, :], in_=rsd[:, sl, :]), reads=[rsd.buf(sl)], writes=[rsd.buf(sl)])
        return sl

    def ln_apply(x_ap, x_buf, lnp, out_ap, out_buf, sl):
        S.op("dve", lambda e: e.scalar_tensor_tensor(out=mvt[:, sl, 1:2], in0=mvt[:, sl, 0:1], scalar=-1.0, in1=rsd[:, sl, :],
                                                      op0=ALU.mult, op1=ALU.mult),
             reads=[mvt.buf(sl), rsd.buf(sl)], writes=[mvt.buf(sl)])
        S.op("act", lambda e: e.activation(out=x_ap, in_=x_ap, func=AF.Identity, bias=mvt[:, sl, 1:2], scale=rsd[:, sl, 0:1]),
             reads=[x_buf, mvt.buf(sl), rsd.buf(sl)], writes=[x_buf])
        S.op("pool", lambda e: e.tensor_tensor(out=x_ap, in0=x_ap, in1=lnp[:, 0, :], op=ALU.mult), reads=[x_buf, lnp.buf()], writes=[x_buf])
        if out_ap is not None:
            ln_apply_b(x_ap, x_buf, lnp, out_ap, out_buf)

    def ln_apply_b(x_ap, x_buf, lnp, out_ap, out_buf):
        S.op("dve", lambda e: e.tensor_tensor(out=out_ap, in0=x_ap, in1=lnp[:, 1, :], op=ALU.add), reads=[x_buf, lnp.buf()],
             writes=[out_buf] if out_buf is not x_buf else [x_buf])

    def transpose_to_hT(lb, lb_buf, hT, tt):
        ctr["pt"] += 1
        pt = ptb[ctr["pt"] % 2]
        for k in range(8):
            S.op("pe", lambda e, k=k: e.transpose(out=pt[:, k, :], in_=lb[:, k * 128:(k + 1) * 128], identity=idb[:, :]),
                 reads=[lb_buf, idb.buf()], writes=[pt.buf()])
        copy_op(evac_eng(), hT[:, :, tt * 128:(tt + 1) * 128], pt[:, :, :], [pt.buf()], [hT.buf(tt)])

    def wload(name, src, kt, ncols, chunk, stream):
        t = S.alloc(name, [kt, ncols], BF16, 2)
        sv = src.rearrange("(kt p) n -> p kt n", p=128)
        for c in range(ncols // chunk):
            S.dma("pool", stream, t[:, :, c * chunk:(c + 1) * chunk], sv[:, :, c * chunk:(c + 1) * chunk], writes=[t.buf(c)])
        return t

    hT = S.alloc("hT", [8, T], BF16, 2)
    wi = wload("wi", w_in, 8, 2048, 512, "w_a")
    lnp0 = load_lnp("lnp0", ln_in_g, ln_in_b)
    xin = S.alloc("xin", [4, D], F32, 4)
    lbt = S.alloc("lbt", [2, D], BF16, 2)
    sls = {}
    for tt in range(NTT + 3):
        if tt < NTT:
            s3 = tt % 4
            S.dma("sp", "xin", xin[:, s3, :], x[tt * 128:(tt + 1) * 128, :], writes=[xin.buf(s3)])
            sls[tt] = ln_stats(xin[:, s3, :], xin.buf(s3))
        if 1 <= tt <= NTT:
            t1 = tt - 1
            ln_apply(xin[:, t1 % 4, :], xin.buf(t1 % 4), lnp0, None, None, sls[t1])
        if 2 <= tt <= NTT + 1:
            t2 = tt - 2
            ln_apply_b(xin[:, t2 % 4, :], xin.buf(t2 % 4), lnp0, lbt[:, t2 % 2, :], lbt.buf(t2 % 2))
        if tt >= 3:
            t3 = tt - 3
            transpose_to_hT(lbt[:, t3 % 2, :], lbt.buf(t3 % 2), hT, t3)
    S.free("xin", "lnp0")
    if debug:
        S.dma("sp", "dbg", dbg["d_hT"], hT[:, :, :], reads=hT.bufs(range(16)), final=True)
    if upto == "A":
        S.emit()
        return nc


    if str(0) in os.environ.get("BARRIERS", ""):
        S.barrier()
    rb = S.alloc("rb", [4], F32, 4)
    S.dma("sp", "c1", rb[0:32, :], rel_bias, writes=[rb.buf()])
    chc = S.alloc("chc", [4], F32, 4)
    S.dma("sp", "c1", chc[:, :], rel_bias[31, :].partition_broadcast(128), writes=[chc.buf()])
    ohs = S.alloc("ohs", [384], F32, 4)
    S.dma("sp", "c1", ohs[0:33, :], c_oh, writes=[ohs.buf()])
    Lh = S.alloc("Lh", [4, 128], F32, 4)
    S.op("pool", lambda e: e.memset(Lh[0:33, :, :], 1.0), writes=[Lh.buf()])
    for h in range(4):
        S.op("dve", lambda e, h=h: e.tensor_scalar(out=Lh[0:32, h, :], in0=onesf[0:32, :], scalar1=rb[0:32, h:h + 1], scalar2=None, op0=ALU.mult),
             reads=[onesf.buf(), rb.buf(), Lh.buf()], writes=[Lh.buf()])
    gsb = S.alloc("gsb", [4, 384], F32, 4)
    for h in range(4):
        bk = banks[h % 4]
        mm(bk[:, 0:384], Lh[0:33, h, :], ohs[0:33, :], True, True, [Lh.buf(), ohs.buf()], [bk.buf()])
        copy_op("dve", gsb[:, h, :], bk[:, 0:384], [bk.buf()], [gsb.buf()])
    scrb = Ten(None, "scr")
    S.dma("sp", "scr", scr, gsb[:, :, :], reads=[gsb.buf()], writes=[scrb.buf()])
    biasf = S.alloc("biasf", [4, 2, 128], F32, 4)
    biasb = S.alloc("biasb", [4, 2, 128], BF16, 2)
    for h in range(4):
        for dsub, off in enumerate((127, 255)):
            S.dma("sp", "scr2", biasf[:, h, dsub, :], bass.AP(scr.tensor, h * 384 + off, [[1535, 128], [1, 128]]),
                  reads=[scrb.buf()], writes=[biasf.buf()])
    S.op("dve", lambda e: e.tensor_copy(out=biasb[:, :, :, :], in_=biasf[:, :, :, :]), reads=[biasf.buf()], writes=[biasb.buf()])
    lqk = S.alloc("lqk", [4, 64], F32, 4)
    for i, v in enumerate((lq1, lk1, lq2, lk2)):
        S.dma("sp", "c1", lqk[:, i, :], v.partition_broadcast(128), writes=[lqk.buf()])
    lsm = S.alloc("lsm", [4], F32, 4)
    S.op("dve", lambda e: e.tensor_mul(out=lqk[:, 0, :], in0=lqk[:, 0, :], in1=lqk[:, 1, :]), reads=[lqk.buf()], writes=[lqk.buf()])
    S.op("dve", lambda e: e.tensor_mul(out=lqk[:, 2, :], in0=lqk[:, 2, :], in1=lqk[:, 3, :]), reads=[lqk.buf()], writes=[lqk.buf()])
    S.op("dve", lambda e: e.reduce_sum(out=lsm[:, 0:1], in_=lqk[:, 0, :], axis=AX.X), reads=[lqk.buf()], writes=[lsm.buf()])
    S.op("dve", lambda e: e.reduce_sum(out=lsm[:, 1:2], in_=lqk[:, 2, :], axis=AX.X), reads=[lqk.buf()], writes=[lsm.buf()])
    S.op("act", lambda e: e.activation(out=lsm[:, 0:2], in_=lsm[:, 0:2], func=AF.Exp), reads=[lsm.buf()], writes=[lsm.buf()])
    S.op("dve", lambda e: e.tensor_sub(out=lsm[:, 2:3], in0=lsm[:, 1:2], in1=lsm[:, 0:1]), reads=[lsm.buf()], writes=[lsm.buf()])
    S.op("dve", lambda e: e.tensor_scalar(out=lsm[:, 2:3], in0=lsm[:, 2:3], scalar1=-LAMBDA_INIT, scalar2=None, op0=ALU.add),
         reads=[lsm.buf()], writes=[lsm.buf()])
    NEGLAM = lsm[:, 2:3]
    gsub = S.alloc("gsub", [128], F32, 4)
    S.dma("sp", "c1", gsub[:, :], subln_g.partition_broadcast(128), writes=[gsub.buf()])
    S.op("dve", lambda e: e.tensor_scalar(out=gsub[:, :], in0=gsub[:, :], scalar1=1.0 - LAMBDA_INIT, scalar2=None, op0=ALU.mult),
         reads=[gsub.buf()], writes=[gsub.buf()])

    uT = S.alloc("uT", [4, T], BF16, 2)
    qT = S.alloc("qT", [4, T], BF16, 2)
    kT = S.alloc("kT", [4, T], BF16, 2)
    vaug = S.alloc("vaug", [16, 4, 129], BF16, 2)
    import os
    for tt in range(NTT):
        S.op("dve", lambda e, tt=tt: e.tensor_copy(out=vaug[:, tt, :, 128:129], in_=onesb[:, 0:4].unsqueeze(2)), reads=[onesb.buf()], writes=[vaug.buf(tt)])
    bi = 0
    for grp, dst in enumerate((uT, qT, kT)):
        for ct in range(4):
            col = grp * 512 + ct * 128
            for tb in range(4):
                bk = banks[bi % 4]; bi += 1
                for kt in range(8):
                    mm(bk[:, :], wi[:, kt, col:col + 128], hT[:, kt, tb * 512:(tb + 1) * 512], kt == 0, kt == 7,
                       [wi.buf(grp)] + hT.bufs(range(4 * tb, 4 * tb + 4)), [bk.buf()])
                copy_op(evac_eng(), dst[:, ct, tb * 512:(tb + 1) * 512], bk[:, :], [bk.buf()], dst.bufs([(ct, 4 * tb + i) for i in range(4)]))
    for tt in range(0 if not os.environ.get("NO_V") else NTT, NTT):
        bk = banks[bi % 4]; bi += 1
        for kt in range(8):
            mm(bk[:, :], hT[:, kt, tt * 128:(tt + 1) * 128], wi[:, kt, 1536:2048], kt == 0, kt == 7,
               [wi.buf(3), hT.buf(tt)], [bk.buf()])
        copy_op(evac_eng(), vaug[:, tt, :, 0:128], bk[:, :].rearrange("p (h d) -> p h d", h=4), [bk.buf()], [vaug.buf(tt)])
    S.free("wi", "lbt", "hT")
    if debug:
        S.dma("sp", "dbg", dbg["d_uT"], uT[:, :, :], reads=uT.bufs([(c, t) for c in range(4) for t in range(16)]), final=True)
        S.dma("sp", "dbg", dbg["d_qT"], qT[:, :, :], reads=qT.bufs([(c, t) for c in range(4) for t in range(16)]), final=True)
        S.dma("sp", "dbg", dbg["d_kT"], kT[:, :, :], reads=kT.bufs([(c, t) for c in range(4) for t in range(16)]), final=True)
        S.dma("sp", "dbg", dbg["d_v"], vaug[:, :, :, :], reads=vaug.bufs(range(16)), final=True)
    if upto == "B":
        S.emit()
        return nc


    catT = S.alloc("catT", [8, T], BF16, 2)

    if str(1) in os.environ.get("BARRIERS", ""):
        S.barrier()
    PT = S.alloc("PT", [6, 512], BF16, 2)
    gcol = S.alloc("gcol", [1], F32, 4)
    S.dma("sp", "c1", gcol[:, :], subln_g.rearrange("(p o) -> p o", o=1), writes=[gcol.buf()])
    S.op("dve", lambda e: e.tensor_scalar(out=gcol[:, :], in0=gcol[:, :], scalar1=1.0 - LAMBDA_INIT, scalar2=None, op0=ALU.mult),
         reads=[gcol.buf()], writes=[gcol.buf()])
    rr = S.alloc("rr", [2, 2, 512], F32, 4)
    o1 = S.alloc("o1", [2, 512], F32, 4)
    oo = S.alloc("oo", [2, 512], F32, 4)
    sqb = S.alloc("sqb", [2, 512], BF16, 2)
    rst = S.alloc("rst", [2, 512], F32, 4)
    stb = (banks[0], banks[1])
    Ab = (banks[2], banks[3])
    Sb = (banks[4], banks[5])
    MSb = banks[6]

    def s_stage(it):
        h, I, s, j, st, pt = it
        r0 = s * 64
        qstart = max(512 * I, 128 * j)
        N = 512 * (I + 1) - qstart
        col0 = qstart - 512 * I
        has_diag = j >= 4 * I
        has_sub = (4 * I - 1) <= j <= (4 * I + 2)
        mm(st[:, col0:col0 + N], kT[r0:r0 + 64, h, j * 128:(j + 1) * 128], qT[r0:r0 + 64, h, qstart:qstart + N],
           True, not (has_diag or has_sub),
           [kT.buf((h, j))] + qT.bufs([(h, t) for t in range(qstart // 128, 4 * I + 4)]), [st.buf()])
        if has_diag:
            c = j * 128 - 512 * I
            mm(st[:, c:c + 128], idb[:, :], biasb[:, h, 0, :], False, not has_sub, [idb.buf(), biasb.buf()], [st.buf()])
        if has_sub:
            c = (j + 1) * 128 - 512 * I
            mm(st[:, c:c + 128], idb[:, :], biasb[:, h, 1, :], False, True, [idb.buf(), biasb.buf()], [st.buf()])
        S.op("act", lambda e, st=st, pt=pt, col0=col0, N=N, h=h: e.activation(
            out=PT[:, pt, col0:col0 + N], in_=st[:, col0:col0 + N], func=AF.Exp, bias=chc[:, h:h + 1], scale=0.125),
            reads=[st.buf(), chc.buf()], writes=[PT.buf(pt)])

    def pv_stage(it):
        h, I, s, j, st, pt = it
        qstart = max(512 * I, 128 * j)
        N = 512 * (I + 1) - qstart
        col0 = qstart - 512 * I
        last = (j == 4 * I + 3)
        S.op("pe", lambda e, s=s, pt=pt, col0=col0, N=N, j=j, h=h, last=last: e.matmul(
            out=Ab[s][:, col0:col0 + N], lhsT=vaug[:, j, h, 0:128], rhs=PT[:, pt, col0:col0 + N], start=(j == 0), stop=last, skip_group_check=True),
            [PT.buf(pt), vaug.buf(j)], [Ab[s].buf()])
        S.op("pe", lambda e, s=s, pt=pt, col0=col0, N=N, j=j, last=last: e.matmul(
            out=Sb[s][:, col0:col0 + N], lhsT=onesb[:, :], rhs=PT[:, pt, col0:col0 + N], start=(j == 0), stop=last, skip_group_check=True),
            [PT.buf(pt), onesb.buf()], [Sb[s].buf()])

    def epilogue_a(h, I, rnd):
        r2 = rnd % 2
        for s in range(2):
            S.op("act", lambda e, s=s, r2=r2: e.activation(out=rr[:, r2, s, :], in_=Sb[s][:, :], func=AF.Ln), reads=[Sb[s].buf()], writes=[rr.buf((r2, s))])
            S.op("act", lambda e, s=s, r2=r2: e.activation(out=rr[:, r2, s, :], in_=rr[:, r2, s, :], func=AF.Exp, scale=-1.0), reads=[rr.buf((r2, s))], writes=[rr.buf((r2, s))])
        tt_op("dve", o1[:, r2, :], Ab[0][:, :], rr[:, r2, 0, :], ALU.mult, [Ab[0].buf(), rr.buf((r2, 0))], [o1.buf(r2)])
        tt_op("dve", oo[:, r2, :], Ab[1][:, :], rr[:, r2, 1, :], ALU.mult, [Ab[1].buf(), rr.buf((r2, 1))], [oo.buf(r2)])
        S.op("dve", lambda e, r2=r2: e.scalar_tensor_tensor(out=oo[:, r2, :], in0=oo[:, r2, :], scalar=NEGLAM, in1=o1[:, r2, :], op0=ALU.mult, op1=ALU.add),
             reads=[oo.buf(r2), o1.buf(r2), lsm.buf()], writes=[oo.buf(r2)])
        tt_op("dve", sqb[:, r2, :], oo[:, r2, :], oo[:, r2, :], ALU.mult, [oo.buf(r2)], [sqb.buf(r2)])

    def epilogue_b(h, I, rnd):
        r2 = rnd % 2
        mm(MSb[:, :], onesb[:, :], sqb[:, r2, :], True, True, [onesb.buf(), sqb.buf(r2)], [MSb.buf()])
        S.op("act", lambda e, r2=r2: e.activation(out=rst[:, r2, :], in_=MSb[:, :], func=AF.Ln, bias=EPS, scale=1.0 / 128.0),
             reads=[MSb.buf(), cst.buf()], writes=[rst.buf(r2)])
        S.op("act", lambda e, r2=r2: e.activation(out=rst[:, r2, :], in_=rst[:, r2, :], func=AF.Exp, scale=-0.5), reads=[rst.buf(r2)], writes=[rst.buf(r2)])
        S.op("dve", lambda e, r2=r2, h=h, I=I: e.scalar_tensor_tensor(out=catT[:, 4 + h, I * 512:(I + 1) * 512], in0=oo[:, r2, :], scalar=gcol[:, 0:1], in1=rst[:, r2, :],
                                                                    op0=ALU.mult, op1=ALU.mult),
             reads=[oo.buf(r2), gcol.buf(), rst.buf(r2)], writes=catT.bufs([(4 + h, 4 * I + i) for i in range(4)]))

    def tt_op(eng, out_ap, a_ap, b_ap, op, reads, writes):
        S.op(eng, lambda e: e.tensor_tensor(out=out_ap, in0=a_ap, in1=b_ap, op=op), reads, writes)

    stb4 = (banks[0], banks[1], banks[7], banks[6])
    iters = []
    k = 0
    for h in range(4):
        for I in range(4):
            for j in range(4 * I + 4):
                for s in range(2):
                    iters.append((h, I, s, j, stb4[k % 4], k % 6))
                    k += 1
    npair = len(iters) // 2
    pending = None
    since = 0
    rnd = 0
    for p in range(npair):
        s_stage(iters[2 * p]); s_stage(iters[2 * p + 1])
        since += 1
        if pending is not None and since >= 2:
            epilogue_b(*pending); pending = None
        if p >= 1:
            pv_stage(iters[2 * p - 2]); pv_stage(iters[2 * p - 1])
            prev, it = iters[2 * p - 1], iters[2 * p]
            if (prev[0], prev[1]) != (it[0], it[1]):
                if pending is not None:
                    epilogue_b(*pending); pending = None
                epilogue_a(prev[0], prev[1], rnd)
                pending = (prev[0], prev[1], rnd); since = 0
                rnd += 1
    pv_stage(iters[-2]); pv_stage(iters[-1])
    if pending is not None:
        epilogue_b(*pending)
    epilogue_a(iters[-1][0], iters[-1][1], rnd)
    epilogue_b(iters[-1][0], iters[-1][1], rnd)
    S.free("rb", "chc", "ohs", "Lh", "gsb", "biasf", "biasb", "lqk", "lsm", "gsub", "PT", "qT", "kT", "vaug", "gcol", "rr", "o1", "oo", "sqb", "rst")
    if upto == "D":
        S.emit()
        return nc


    if str(2) in os.environ.get("BARRIERS", ""):
        S.barrier()
    I32 = mybir.dt.int32
    W2 = 2048

    def A8(name, dt_=F32):
        return S.alloc(name, [W2], dt_, 4)

    def tt_op(eng, out_ap, a_ap, b_ap, op, reads, writes):
        S.op(eng, lambda e: e.tensor_tensor(out=out_ap, in0=a_ap, in1=b_ap, op=op), reads, writes)

    def ts_op(eng, out_ap, a_ap, s1, s2, op0, op1, reads, writes):
        if s2 is None:
            S.op(eng, lambda e: e.tensor_scalar(out=out_ap, in0=a_ap, scalar1=s1, scalar2=None, op0=op0), reads, writes)
        else:
            S.op(eng, lambda e: e.tensor_scalar(out=out_ap, in0=a_ap, scalar1=s1, scalar2=s2, op0=op0, op1=op1), reads, writes)

    ki = A8("ki", I32)
    kf = A8("kf")

    def sin_from_u(u, out):
        S.op("dve", lambda e: e.tensor_copy(out=ki[:, :], in_=u[:, :]), reads=[u.buf()], writes=[ki.buf()])
        S.op("dve", lambda e: e.tensor_copy(out=kf[:, :], in_=ki[:, :]), reads=[ki.buf()], writes=[kf.buf()])
        tt_op("dve", u[:, :], u[:, :], kf[:, :], ALU.subtract, [u.buf(), kf.buf()], [u.buf()])
        S.op("dve", lambda e: e.scalar_tensor_tensor(out=u[:, :], in0=u[:, :], scalar=0.0, in1=u[:, :], op0=ALU.is_lt, op1=ALU.add),
             reads=[u.buf()], writes=[u.buf()])
        S.op("act", lambda e: e.activation(out=out[:, :], in_=u[:, :], func=AF.Sin, bias=cst[:, 5:6], scale=2 * PI),
             reads=[u.buf(), cst.buf()], writes=[out.buf()])

    lr = A8("lr"); li = A8("li"); lrdt = A8("lrdt"); ang = A8("ang")
    dtr = S.alloc("dtr", [32], F32, 4)
    S.dma("sp", "c2", lr[:, :], lam_re.partition_broadcast(128), writes=[lr.buf()])
    S.dma("sp", "c2", li[:, :], lam_im.partition_broadcast(128), writes=[li.buf()])
    S.dma("sp", "c2", dtr[:, :], log_dt.partition_broadcast(128), writes=[dtr.buf()])
    S.op("act", lambda e: e.activation(out=dtr[:, :], in_=dtr[:, :], func=AF.Exp), reads=[dtr.buf()], writes=[dtr.buf()])
    dt_b = dtr[:, :].unsqueeze(2).to_broadcast([128, 32, 64])
    v3 = lambda t: t[:, :].rearrange("p (g s) -> p g s", s=64)
    tt_op("dve", v3(lrdt), v3(lr), dt_b, ALU.mult, [lr.buf(), dtr.buf()], [lrdt.buf()])
    tt_op("dve", v3(ang), v3(li), dt_b, ALU.mult, [li.buf(), dtr.buf()], [ang.buf()])
    mg = A8("mg"); sn = A8("sn"); cs = A8("cs"); ua = A8("ua"); fre = A8("fre"); fim = A8("fim")
    S.op("act", lambda e: e.activation(out=mg[:, :], in_=lrdt[:, :], func=AF.Exp), reads=[lrdt.buf()], writes=[mg.buf()])
    ts_op("dve", ua[:, :], ang[:, :], 1.0 / (2 * PI), 1.5, ALU.mult, ALU.add, [ang.buf()], [ua.buf()])
    sin_from_u(ua, sn)
    ts_op("dve", ua[:, :], ang[:, :], 1.0 / (2 * PI), 1.75, ALU.mult, ALU.add, [ang.buf()], [ua.buf()])
    sin_from_u(ua, cs)
    tt_op("dve", cs[:, :], mg[:, :], cs[:, :], ALU.mult, [mg.buf(), cs.buf()], [cs.buf()])
    ts_op("dve", cs[:, :], cs[:, :], -1.0, None, ALU.add, None, [cs.buf()], [cs.buf()])
    tt_op("dve", sn[:, :], mg[:, :], sn[:, :], ALU.mult, [mg.buf(), sn.buf()], [sn.buf()])
    tt_op("dve", mg[:, :], lr[:, :], lr[:, :], ALU.mult, [lr.buf()], [mg.buf()])
    tt_op("dve", kf[:, :], li[:, :], li[:, :], ALU.mult, [li.buf()], [kf.buf()])
    tt_op("dve", mg[:, :], mg[:, :], kf[:, :], ALU.add, [mg.buf(), kf.buf()], [mg.buf()])
    S.op("dve", lambda e: e.reciprocal(out=mg[:, :], in_=mg[:, :]), reads=[mg.buf()], writes=[mg.buf()])
    tt_op("dve", fre[:, :], cs[:, :], lr[:, :], ALU.mult, [cs.buf(), lr.buf()], [fre.buf()])
    tt_op("dve", kf[:, :], sn[:, :], li[:, :], ALU.mult, [sn.buf(), li.buf()], [kf.buf()])
    tt_op("dve", fre[:, :], fre[:, :], kf[:, :], ALU.add, [fre.buf(), kf.buf()], [fre.buf()])
    tt_op("dve", fre[:, :], fre[:, :], mg[:, :], ALU.mult, [fre.buf(), mg.buf()], [fre.buf()])
    tt_op("dve", fim[:, :], sn[:, :], lr[:, :], ALU.mult, [sn.buf(), lr.buf()], [fim.buf()])
    tt_op("dve", kf[:, :], cs[:, :], li[:, :], ALU.mult, [cs.buf(), li.buf()], [kf.buf()])
    tt_op("dve", fim[:, :], fim[:, :], kf[:, :], ALU.subtract, [fim.buf(), kf.buf()], [fim.buf()])
    tt_op("dve", fim[:, :], fim[:, :], mg[:, :], ALU.mult, [fim.buf(), mg.buf()], [fim.buf()])
    if os.environ.get("S5_STOP") == "1":
        S.emit()
        return nc
    S.free("lr", "li", "dtr")
    Wmr = A8("Wmr"); Wmi = A8("Wmi")
    S.op("act", lambda e: e.activation(out=mg[:, :], in_=lrdt[:, :], func=AF.Exp, scale=cst[:, 1:2]), reads=[lrdt.buf(), cst.buf()], writes=[mg.buf()])
    ts_op("dve", ua[:, :], ang[:, :], cst[:, 2:3], cst[:, 3:4], ALU.mult, ALU.add, [ang.buf(), cst.buf()], [ua.buf()])
    sin_from_u(ua, sn)
    ts_op("dve", ua[:, :], ang[:, :], cst[:, 2:3], cst[:, 4:5], ALU.mult, ALU.add, [ang.buf(), cst.buf()], [ua.buf()])
    sin_from_u(ua, cs)
    tt_op("dve", Wmr[:, :], mg[:, :], cs[:, :], ALU.mult, [mg.buf(), cs.buf()], [Wmr.buf()])
    S.op("dve", lambda e: e.scalar_tensor_tensor(out=Wmi[:, :], in0=mg[:, :], scalar=-1.0, in1=sn[:, :], op0=ALU.mult, op1=ALU.mult),
         reads=[mg.buf(), sn.buf()], writes=[Wmi.buf()])
    if os.environ.get("S5_STOP") == "2":
        S.emit()
        return nc
    trow = S.alloc("trow", [128], F32, 4)
    S.dma("sp", "c2", trow[:, :], c_trow, writes=[trow.buf()])
    lrdtT = A8("lrdtT"); angT = A8("angT")
    bi = 0
    for (src, dst) in ((lrdt, lrdtT), (ang, angT)):
        for q4 in range(4):
            bk = banks[bi % 4]; bi += 1
            for i in range(4):
                j = q4 * 4 + i
                S.op("pe", lambda e, bk=bk, i=i, j=j, src=src: e.transpose(out=bk[:, i * 128:(i + 1) * 128], in_=src[:, j * 128:(j + 1) * 128], identity=idf[:, :]),
                     reads=[src.buf(), idf.buf()], writes=[bk.buf()])
            copy_op("dve", dst[:, q4 * 512:(q4 + 1) * 512], bk[:, :], [bk.buf()], [dst.buf()])
    S.free("lrdt", "ang")
    WpTr = A8("WpTr"); WpTi = A8("WpTi")
    trow_b = trow[:, :].unsqueeze(1).to_broadcast([128, 16, 128])
    v16 = lambda t: t[:, :].rearrange("p (j t) -> p j t", t=128)
    tt_op("dve", v16(lrdtT), v16(lrdtT), trow_b, ALU.mult, [lrdtT.buf(), trow.buf()], [lrdtT.buf()])
    S.op("act", lambda e: e.activation(out=mg[:, :], in_=lrdtT[:, :], func=AF.Exp), reads=[lrdtT.buf()], writes=[mg.buf()])
    tt_op("dve", v16(angT), v16(angT), trow_b, ALU.mult, [angT.buf(), trow.buf()], [angT.buf()])
    ts_op("dve", ua[:, :], angT[:, :], 1.0 / (2 * PI), 1.5, ALU.mult, ALU.add, [angT.buf()], [ua.buf()])
    sin_from_u(ua, sn)
    ts_op("dve", ua[:, :], angT[:, :], 1.0 / (2 * PI), 1.75, ALU.mult, ALU.add, [angT.buf()], [ua.buf()])
    sin_from_u(ua, cs)
    tt_op("dve", WpTr[:, :], mg[:, :], cs[:, :], ALU.mult, [mg.buf(), cs.buf()], [WpTr.buf()])
    tt_op("dve", WpTi[:, :], mg[:, :], sn[:, :], ALU.mult, [mg.buf(), sn.buf()], [WpTi.buf()])
    S.free("lrdtT", "angT", "ua", "ki", "kf", "mg", "trow")
    if os.environ.get("S5_STOP") == "3":
        S.emit()
        return nc
    maskB = S.alloc("maskB", [4, 128], F32, 4)
    maskC = S.alloc("maskC", [4, 128], F32, 4)
    S.dma("sp", "c2", maskB[:, :, :], c_maskB, writes=[maskB.buf()])
    S.dma("sp", "c2", maskC[:, :, :], c_maskC, writes=[maskC.buf()])
    bnat = S.alloc("bnat", [2, 16, 16], F32, 4)
    S.dma("sp", "c2", bnat[:, 0, :, :], b_re.rearrange("(j p) h -> p j h", p=128), writes=[bnat.buf()])
    S.dma("sp", "c2", bnat[:, 1, :, :], b_im.rearrange("(j p) h -> p j h", p=128), writes=[bnat.buf()])
    bn8 = S.alloc("bn8", [2, 16, 8, 16], F32, 4)
    for ri in range(2):
        S.op("dve", lambda e, ri=ri: e.tensor_copy(out=bn8[:, ri, :, :, :], in_=bnat[:, ri, :, :].unsqueeze(2).to_broadcast([128, 16, 8, 16])),
             reads=[bnat.buf()], writes=[bn8.buf()])
    Bmr = S.alloc("Bmr", [4, 512], BF16, 2)
    Bmi = S.alloc("Bmi", [4, 512], BF16, 2)
    tq = S.alloc("tq", [4, 128], F32, 4)
    for j in range(16):
        ctile, jm = j // 4, j % 4
        bk = banks[j % 4]
        for ri in range(2):
            S.op("pe", lambda e, bk=bk, ri=ri, j=j: e.transpose(out=bk[:, ri * 128:(ri + 1) * 128],
                                                             in_=bn8[:, ri, j, :, :].rearrange("p c h -> p (c h)"), identity=idf[:, :]),
                 reads=[bn8.buf(), idf.buf()], writes=[bk.buf()])
        BTr, BTi = bk[:, 0:128], bk[:, 128:256]
        fr, fi = fre[:, j * 128:(j + 1) * 128], fim[:, j * 128:(j + 1) * 128]
        rd = [bk.buf(), fre.buf(), fim.buf()]
        tt_op("dve", tq[:, 0, :], BTr, fr, ALU.mult, rd, [tq.buf()])
        tt_op("dve", tq[:, 1, :], BTi, fi, ALU.mult, rd, [tq.buf()])
        tt_op("dve", tq[:, 2, :], BTr, fi, ALU.mult, rd, [tq.buf()])
        tt_op("dve", tq[:, 3, :], BTi, fr, ALU.mult, rd, [tq.buf()])
        tt_op("dve", tq[:, 0, :], tq[:, 0, :], tq[:, 1, :], ALU.subtract, [tq.buf()], [tq.buf()])
        tt_op("dve", tq[:, 2, :], tq[:, 2, :], tq[:, 3, :], ALU.add, [tq.buf()], [tq.buf()])
        tt_op("dve", Bmr[:, ctile, jm * 128:(jm + 1) * 128], tq[:, 0, :], maskB[:, jm, :], ALU.mult, [tq.buf(), maskB.buf()], [Bmr.buf()])
        tt_op("dve", Bmi[:, ctile, jm * 128:(jm + 1) * 128], tq[:, 2, :], maskB[:, jm, :], ALU.mult, [tq.buf(), maskB.buf()], [Bmi.buf()])
    S.free("bnat", "bn8", "fre", "fim")
    if os.environ.get("S5_STOP") == "4":
        S.emit()
        return nc
    cnat = S.alloc("cnat", [2, 4, 64], F32, 4)
    S.dma("sp", "c2", cnat[:, 0, :, :], c_re.rearrange("(ct p) s -> p ct s", p=128), writes=[cnat.buf()])
    S.dma("sp", "c2", cnat[:, 1, :, :], c_im.rearrange("(ct p) s -> p ct s", p=128), writes=[cnat.buf()])
    cn2 = S.alloc("cn2", [2, 4, 2, 64], F32, 4)
    for ri in range(2):
        S.op("dve", lambda e, ri=ri: e.tensor_copy(out=cn2[:, ri, :, :, :], in_=cnat[:, ri, :, :].unsqueeze(2).to_broadcast([128, 4, 2, 64])),
             reads=[cnat.buf()], writes=[cn2.buf()])
    Cmr = S.alloc("Cmr", [16, 128], BF16, 2)
    Cmi = S.alloc("Cmi", [16, 128], BF16, 2)
    for ctile in range(4):
        bk = banks[ctile % 4]
        for ri in range(2):
            S.op("pe", lambda e, bk=bk, ri=ri, ctile=ctile: e.transpose(out=bk[:, ri * 128:(ri + 1) * 128],
                                                                    in_=cn2[:, ri, ctile, :, :].rearrange("p c s -> p (c s)"), identity=idf[:, :]),
                 reads=[cn2.buf(), idf.buf()], writes=[bk.buf()])
        for jm in range(4):
            j = ctile * 4 + jm
            tt_op("dve", Cmr[:, j, :], bk[:, 0:128], maskC[:, jm, :], ALU.mult, [bk.buf(), maskC.buf()], [Cmr.buf()])
            S.op("dve", lambda e, bk=bk, j=j, jm=jm: e.scalar_tensor_tensor(out=Cmi[:, j, :], in0=bk[:, 128:256], scalar=-1.0, in1=maskC[:, jm, :],
                                                                         op0=ALU.mult, op1=ALU.mult),
                 reads=[bk.buf(), maskC.buf()], writes=[Cmi.buf()])
    S.free("cnat", "cn2", "maskB", "maskC", "tq", "sn", "cs")
    if os.environ.get("S5_STOP") == "5":
        S.emit()
        return nc
    dnat = S.alloc("dnat", [2, 128], F32, 4)
    S.dma("sp", "c2", dnat[0:4, 0, :], s5_d.rearrange("(ct p) -> ct p", p=128), writes=[dnat.buf()])
    S.dma("sp", "c2", dnat[0:4, 1, :], glu_b.rearrange("(ct p) -> ct p", p=128), writes=[dnat.buf()])
    dcol = S.alloc("dcol", [2, 4], F32, 4)
    bk = banks[0]
    for i in range(2):
        S.op("pe", lambda e, i=i, bk=bk: e.transpose(out=bk[:, i * 4:(i + 1) * 4], in_=dnat[0:4, i, :], identity=idf[0:4, 0:4]),
             reads=[dnat.buf(), idf.buf()], writes=[bk.buf()])
    copy_op("dve", dcol[:, 0, :], bk[:, 0:4], [bk.buf()], [dcol.buf()])
    copy_op("dve", dcol[:, 1, :], bk[:, 4:8], [bk.buf()], [dcol.buf()])
    S.free("dnat")
    if os.environ.get("S5_STOP") == "6":
        S.emit()
        return nc
    gw = wload("gw", glu_w, 4, 512, 512, "w_s5")

    if os.environ.get("S5_STOP") == "7":
        S.emit()
        return nc
    if str(3) in os.environ.get("BARRIERS", ""):
        S.barrier()
    zb = S.alloc("zb", [2, 2, W2], BF16, 2)
    tm = S.alloc("tm", [2, 4, 512], F32, 4)
    td = S.alloc("td", [2, 4, 512], F32, 4)
    wc = S.alloc("wc", [2, 2, 512], F32, 4)
    xbf = S.alloc("xbf", [2, 16, 128], BF16, 2)
    car = S.alloc("car", [16, 2], F32, 4)
    ypre = S.alloc("ypre", [2, 4, 512], F32, 4)
    S.op("pool", lambda e: e.memset(car[:, :, :], 0.0), writes=[car.buf(g) for g in range(4)])
    gl = S.alloc("gl", [4, 512], F32, 4)
    glb = S.alloc("glb", [4, 512], BF16, 2)
    g1 = S.alloc("g1", [2, 512], F32, 4)
    bR, bI = banks[0], banks[1]
    wbk = (banks[2], banks[3])
    ybk = banks[4]
    gbk = banks[5]
    mi = 0
    di = 0
    for c in range(int(os.environ.get("S5_CHUNKS", NTT))):
        zs = c % 2
        if os.environ.get("S5_PART") == "1" and c == 0:
            pass
        for ctile in range(4):
            mm(bR[:, :], uT[:, ctile, c * 128:(c + 1) * 128], Bmr[:, ctile, :], True, True, [uT.buf((ctile, c)), Bmr.buf()], [bR.buf()])
            mm(bI[:, :], uT[:, ctile, c * 128:(c + 1) * 128], Bmi[:, ctile, :], True, True, [uT.buf((ctile, c)), Bmi.buf()], [bI.buf()])
            ms = mi % 2; mi += 1
            blk = slice(ctile * 512, (ctile + 1) * 512)
            tt_op("dve", tm[:, ms, 0, :], bR[:, :], Wmr[:, blk], ALU.mult, [bR.buf(), Wmr.buf()], [tm.buf((ms, 0))])
            tt_op("dve", tm[:, ms, 1, :], bI[:, :], Wmi[:, blk], ALU.mult, [bI.buf(), Wmi.buf()], [tm.buf((ms, 1))])
            tt_op("dve", tm[:, ms, 2, :], bR[:, :], Wmi[:, blk], ALU.mult, [bR.buf(), Wmi.buf()], [tm.buf((ms, 2))])
            tt_op("dve", tm[:, ms, 3, :], bI[:, :], Wmr[:, blk], ALU.mult, [bI.buf(), Wmr.buf()], [tm.buf((ms, 3))])
            tt_op("pool", zb[:, zs, 0, blk], tm[:, ms, 0, :], tm[:, ms, 1, :], ALU.subtract, [tm.buf((ms, 0)), tm.buf((ms, 1))], [zb.buf((zs, 0, ctile))])
            tt_op("pool", zb[:, zs, 1, blk], tm[:, ms, 2, :], tm[:, ms, 3, :], ALU.add, [tm.buf((ms, 2)), tm.buf((ms, 3))], [zb.buf((zs, 1, ctile))])
        if os.environ.get("S5_PART") == "1":
            continue
        for g4 in range(4):
            WR, WI = (banks[2], banks[3]) if g4 % 2 == 0 else (banks[6], banks[7])
            for jj in range(4):
                j = 4 * g4 + jj
                mm(WR[:, jj * 128:(jj + 1) * 128], zb[:, zs, 0, j * 128:(j + 1) * 128], trib[:, :], True, True, [zb.buf((zs, 0, j // 4)), trib.buf()], [WR.buf()])
                mm(WI[:, jj * 128:(jj + 1) * 128], zb[:, zs, 1, j * 128:(j + 1) * 128], trib[:, :], True, True, [zb.buf((zs, 1, j // 4)), trib.buf()], [WI.buf()])
            ds = di % 2; di += 1
            for jj in range(4):
                j = 4 * g4 + jj
                S.op("act", lambda e, WR=WR, ds=ds, jj=jj, j=j: e.activation(out=wc[:, ds, 0, jj * 128:(jj + 1) * 128], in_=WR[:, jj * 128:(jj + 1) * 128],
                                                                          func=AF.Identity, bias=car[:, j, 0:1], scale=1.0),
                     reads=[WR.buf(), car.buf(g4)], writes=[wc.buf((ds, 0))])
                S.op("act", lambda e, WI=WI, ds=ds, jj=jj, j=j: e.activation(out=wc[:, ds, 1, jj * 128:(jj + 1) * 128], in_=WI[:, jj * 128:(jj + 1) * 128],
                                                                          func=AF.Identity, bias=car[:, j, 1:2], scale=1.0),
                     reads=[WI.buf(), car.buf(g4)], writes=[wc.buf((ds, 1))])
            gcols = slice(g4 * 512, (g4 + 1) * 512)
            pr, pi_ = WpTr[:, gcols], WpTi[:, gcols]
            wr_, wi_ = wc[:, ds, 0, :], wc[:, ds, 1, :]
            for k, (w_, p_, wk) in enumerate(((wr_, pr, 0), (wi_, pi_, 1), (wr_, pi_, 0), (wi_, pr, 1))):
                tt_op("dve", td[:, ds, k, :], w_, p_, ALU.mult, [wc.buf((ds, wk)), WpTr.buf(), WpTi.buf()], [td.buf((ds, k))])
            xr_out = xbf[:, 0, 4 * g4:4 * g4 + 4, :].rearrange("p j t -> p (j t)")
            xi_out = xbf[:, 1, 4 * g4:4 * g4 + 4, :].rearrange("p j t -> p (j t)")
            tt_op("dve", xr_out, td[:, ds, 0, :], td[:, ds, 1, :], ALU.subtract, [td.buf((ds, 0)), td.buf((ds, 1))], xbf.bufs([(0, 4 * g4 + i) for i in range(4)]))
            tt_op("dve", xi_out, td[:, ds, 2, :], td[:, ds, 3, :], ALU.add, [td.buf((ds, 2)), td.buf((ds, 3))], xbf.bufs([(1, 4 * g4 + i) for i in range(4)]))
            l127 = lambda k, ds=ds: td[:, ds, k, :].rearrange("p (j t) -> p j t", t=128)[:, :, 127]
            tt_op("dve", car[:, 4 * g4:4 * g4 + 4, 0], l127(0), l127(1), ALU.subtract, [td.buf((ds, 0)), td.buf((ds, 1))], [car.buf(g4)])
            tt_op("dve", car[:, 4 * g4:4 * g4 + 4, 1], l127(2), l127(3), ALU.add, [td.buf((ds, 2)), td.buf((ds, 3))], [car.buf(g4)])
        if os.environ.get("S5_PART") == "2":
            continue
        ys = (c // 4) % 2
        for ctile in range(4):
            ybk = banks[4 + ctile % 2]
            ya = ybk[:, 0:128]
            n = 0
            for jm in range(4):
                j = ctile * 4 + jm
                mm(ya, Cmr[:, j, :], xbf[:, 0, j, :], n == 0, False, [Cmr.buf(), xbf.buf((0, j))], [ybk.buf()]); n += 1
                mm(ya, Cmi[:, j, :], xbf[:, 1, j, :], False, jm == 3, [Cmi.buf(), xbf.buf((1, j))], [ybk.buf()]); n += 1
            S.op("dve", lambda e, ya=ya, ctile=ctile, ys=ys, c=c: e.scalar_tensor_tensor(
                out=ypre[:, ys, ctile, (c % 4) * 128:(c % 4 + 1) * 128], in0=uT[:, ctile, c * 128:(c + 1) * 128], scalar=dcol[:, 0, ctile:ctile + 1], in1=ya,
                op0=ALU.mult, op1=ALU.add),
                reads=[uT.buf((ctile, c)), dcol.buf(), ybk.buf()], writes=[ypre.buf((ys, ctile))])
        if c % 4 == 3:
            tb = c // 4
            for ctile in range(4):
                xx = ypre[:, ys, ctile, :]
                xb_ = ypre.buf((ys, ctile))
                tt_op("dve", g1[:, 0, :], xx, xx, ALU.mult, [xb_], [g1.buf(0)])
                ts_op("dve", g1[:, 0, :], g1[:, 0, :], 0.044715, 1.0, ALU.mult, ALU.add, [g1.buf(0)], [g1.buf(0)])
                tt_op("dve", g1[:, 0, :], g1[:, 0, :], xx, ALU.mult, [g1.buf(0), xb_], [g1.buf(0)])
                S.op("act", lambda e: e.activation(out=g1[:, 1, :], in_=g1[:, 0, :], func=AF.Sigmoid, scale=1.5957691216057308), reads=[g1.buf(0)], writes=[g1.buf(1)])
                tt_op("dve", gl[:, ctile, :], xx, g1[:, 1, :], ALU.mult, [xb_, g1.buf(1)], [gl.buf(ctile)])
                copy_op("dve", glb[:, ctile, :], gl[:, ctile, :], [gl.buf(ctile)], [glb.buf(ctile)])
            for cp in range(0 if os.environ.get("S5_G") != "1" else 4, 4):
                gbk = banks[4 + cp % 2]
                for ctile in range(4):
                    mm(gbk[:, :], gw[:, ctile, cp * 128:(cp + 1) * 128], glb[:, ctile, :], ctile == 0, ctile == 3, [gw.buf(0), glb.buf(ctile)], [gbk.buf()])
                if os.environ.get("S5_G") == "2":
                    continue
                S.op("act", lambda e, cp=cp, gbk=gbk: e.activation(out=g1[:, 0, :], in_=gbk[:, :], func=AF.Sigmoid, bias=dcol[:, 1, cp:cp + 1], scale=1.0),
                     reads=[gbk.buf(), dcol.buf()], writes=[g1.buf(0)])
                tt_op("dve", catT[:, cp, tb * 512:(tb + 1) * 512], gl[:, cp, :], g1[:, 0, :], ALU.mult, [gl.buf(cp), g1.buf(0)],
                      catT.bufs([(cp, 4 * tb + i) for i in range(4)]))
    S.free("Wmr", "Wmi", "WpTr", "WpTi", "Bmr", "Bmi", "Cmr", "Cmi", "dcol", "gw", "zb", "tm", "td", "wc", "xbf", "car", "ypre", "gl", "glb", "g1", "uT")
    if debug:
        S.dma("sp", "dbg", dbg["d_cat"], catT[:, :, :], reads=catT.bufs([(k, t) for k in range(8) for t in range(16)]), final=True)
    if upto == "C":
        S.emit()
        return nc


    if str(4) in os.environ.get("BARRIERS", ""):
        S.barrier()
    hs = S.alloc("hs", [16, D], F32, 4)
    hT = S.alloc("hT", [8, T], BF16, 2)
    wob = wload("wob", w_out, 8, D, 512, "w_e")
    lnp0 = load_lnp("lnp0", ln_in_g, ln_in_b)
    lnp1 = load_lnp("lnp1", ln1_g, ln1_b)
    lbt = S.alloc("lbt", [2, D], BF16, 2)
    bi = 0

    def resid_stats(tt, acc_banks):
        for nh in range(2):
            bk = acc_banks[nh]
            S.op("dve", lambda e, nh=nh, bk=bk: e.scalar_tensor_tensor(out=hs[:, tt, nh * 512:(nh + 1) * 512], in0=hs[:, tt, nh * 512:(nh + 1) * 512],
                                                                     scalar=ALPHA, in1=bk[:, :], op0=ALU.mult, op1=ALU.add),
                 reads=[hs.buf(tt), bk.buf()], writes=[hs.buf(tt)])
        return ln_stats(hs[:, tt, :], hs.buf(tt))

    def ln_finish_a(tt, sl, lnp):
        ln_apply(hs[:, tt, :], hs.buf(tt), lnp, None, None, sl)

    def ln_finish(tt, sl, lnp, do_T, split=False):
        if not split:
            ln_apply(hs[:, tt, :], hs.buf(tt), lnp, hs[:, tt, :], hs.buf(tt), sl)
        else:
            ln_apply_b(hs[:, tt, :], hs.buf(tt), lnp, hs[:, tt, :], hs.buf(tt))
        if do_T:
            s2 = tt % 2
            copy_op("act", lbt[:, s2, :], hs[:, tt, :], [hs.buf(tt)], [lbt.buf(s2)])
            transpose_to_hT(lbt[:, s2, :], lbt.buf(s2), hT, tt)

    def resid_ln(tt, acc_banks, lnp, do_T):
        sl = resid_stats(tt, acc_banks)
        ln_finish(tt, sl, lnp, do_T)

    sl_in = {}
    sl_1 = {}
    accs_of = {}
    for step in range(NTT + 6):
        if step >= 6:
            ln_finish(step - 6, None, lnp1, True, split=True)
        if step < NTT:
            tt = step
            S.dma("sp", "xin", hs[:, tt, :], x[tt * 128:(tt + 1) * 128, :], writes=[hs.buf(tt)])
            sl_in[tt] = ln_stats(hs[:, tt, :], hs.buf(tt))
        if 1 <= step <= NTT:
            tt = step - 1
            ln_apply(hs[:, tt, :], hs.buf(tt), lnp0, None, None, sl_in[tt])
        if 2 <= step <= NTT + 1:
            tt = step - 2
            ln_apply_b(hs[:, tt, :], hs.buf(tt), lnp0, hs[:, tt, :], hs.buf(tt))
            accs = []
            for nh in range(2):
                bk = banks[bi % 4]; bi += 1
                for kt in range(8):
                    mm(bk[:, :], catT[:, kt, tt * 128:(tt + 1) * 128], wob[:, kt, nh * 512:(nh + 1) * 512], kt == 0, kt == 7,
                       [catT.buf((kt, tt)), wob.buf(nh)], [bk.buf()])
                accs.append(bk)
            accs_of[tt] = accs
        if 3 <= step <= NTT + 2:
            tt = step - 3
            sl_1[tt] = resid_stats(tt, accs_of[tt])
        if 4 <= step <= NTT + 3:
            tt = step - 4
            ln_finish_a(tt, sl_1[tt], lnp1)
    S.free("catT", "wob", "lnp0", "lnp1")
    if debug:
        S.dma("sp", "dbg", dbg["d_h1"], hs[:, :, :], reads=hs.bufs(range(16)), final=True)
    if upto == "E":
        S.emit()
        return nc


    if str(5) in os.environ.get("BARRIERS", ""):
        S.barrier()
    wkv = wload("wkv", ca_wkv, 8, 2 * D, 512, "w_f")
    memf = S.alloc("memf", [2, D], F32, 4)
    memb = S.alloc("memb", [2, D], BF16, 2)
    memT = S.alloc("memT", [8, 256], BF16, 2)
    for mt in range(2):
        S.dma("sp", "mem", memf[:, mt, :], mem[mt * 128:(mt + 1) * 128, :], writes=[memf.buf(mt)])
        copy_op("act", memb[:, mt, :], memf[:, mt, :], [memf.buf(mt)], [memb.buf(mt)])
        ctr["pt"] += 1
        pt = ptb[ctr["pt"] % 2]
        for k in range(8):
            S.op("pe", lambda e, k=k, pt=pt, mt=mt: e.transpose(out=pt[:, k, :], in_=memb[:, mt, k * 128:(k + 1) * 128], identity=idb[:, :]),
                 reads=[memb.buf(mt), idb.buf()], writes=[pt.buf()])
        copy_op(evac_eng(), memT[:, :, mt * 128:(mt + 1) * 128], pt[:, :, :], [pt.buf()], [memT.buf()])
    kTca = S.alloc("kTca", [8, 256], BF16, 2)
    vca = S.alloc("vca", [2, D], BF16, 2)
    for ct in range(8):
        bk = banks[bi % 4]; bi += 1
        for kt in range(8):
            mm(bk[:, 0:256], wkv[:, kt, ct * 128:(ct + 1) * 128], memT[:, kt, :], kt == 0, kt == 7, [wkv.buf(ct // 4), memT.buf()], [bk.buf()])
        copy_op(evac_eng(), kTca[:, ct, :], bk[:, 0:256], [bk.buf()], [kTca.buf()])
    for mt in range(2):
        for nh in range(2):
            bk = banks[bi % 4]; bi += 1
            for kt in range(8):
                mm(bk[:, :], memT[:, kt, mt * 128:(mt + 1) * 128], wkv[:, kt, D + nh * 512:D + (nh + 1) * 512], kt == 0, kt == 7,
                   [wkv.buf(2 + nh), memT.buf()], [bk.buf()])
            copy_op(evac_eng(), vca[:, mt, nh * 512:(nh + 1) * 512], bk[:, :], [bk.buf()], [vca.buf()])
    S.free("wkv", "memf", "memb", "memT")
    wqb = wload("wqb", ca_wq, 8, D, 512, "w_f2")
    wo2 = wload("wo2", ca_wo, 8, D, 512, "w_f2")
    lnp2 = load_lnp("lnp2", ln2_g, ln2_b)
    qTc = S.alloc("qTc", [2, 8, 512], BF16, 2)
    PTc = S.alloc("PTc", [2, 2, 512], BF16, 2)
    oTc = S.alloc("oTc", [8, 512], BF16, 2)
    rcs = S.alloc("rcs", [2, 512], F32, 4)
    pendF = None
    pendF2 = None
    fst = {"bi": 0, "sc": 0}

    def nbank():
        fst["bi"] += 1
        return banks[fst["bi"] % 4]

    def F_Q(tb):
        qs = tb % 2
        tcols = slice(tb * 512, (tb + 1) * 512)
        hbufs = hT.bufs(range(4 * tb, 4 * tb + 4))
        for ct in range(8):
            bk = nbank()
            for kt in range(8):
                mm(bk[:, :], wqb[:, kt, ct * 128:(ct + 1) * 128], hT[:, kt, tcols], kt == 0, kt == 7, [wqb.buf(ct // 4)] + hbufs, [bk.buf()])
            copy_op(evac_eng(), qTc[:, qs, ct, :], bk[:, :], [bk.buf()], [qTc.buf((qs, ct))])

    def F_HS(tb, hd):
        qs = tb % 2
        ps = hd % 2
        for mt in range(2):
            fst["sc"] += 1
            bk = banks[4 + fst["sc"] % 4]
            for i in range(2):
                ct = 2 * hd + i
                mm(bk[:, :], kTca[:, ct, mt * 128:(mt + 1) * 128], qTc[:, qs, ct, :], i == 0, i == 1, [kTca.buf(), qTc.buf((qs, ct))], [bk.buf()])
            S.op("act", lambda e, bk=bk, ps=ps, mt=mt: e.activation(out=PTc[:, ps, mt, :], in_=bk[:, :], func=AF.Exp, scale=1.0 / 16.0),
                 reads=[bk.buf()], writes=[PTc.buf((ps, mt))])

    def F_HP(tb, hd):
        ps = hd % 2
        sb = nbank()
        for mt in range(2):
            mm(sb[:, :], onesb[:, :], PTc[:, ps, mt, :], mt == 0, mt == 1, [onesb.buf(), PTc.buf((ps, mt))], [sb.buf()])
        S.op("act", lambda e, sb=sb, ps=ps: e.activation(out=rcs[:, ps, :], in_=sb[:, :], func=AF.Ln), reads=[sb.buf()], writes=[rcs.buf(ps)])
        S.op("act", lambda e, ps=ps: e.activation(out=rcs[:, ps, :], in_=rcs[:, ps, :], func=AF.Exp, scale=-1.0), reads=[rcs.buf(ps)], writes=[rcs.buf(ps)])
        for dti in range(2):
            ct = 2 * hd + dti
            bk = nbank()
            for mt in range(2):
                mm(bk[:, :], vca[:, mt, ct * 128:(ct + 1) * 128], PTc[:, ps, mt, :], mt == 0, mt == 1, [vca.buf(), PTc.buf((ps, mt))], [bk.buf()])
            tt_op("dve", oTc[:, ct, :], bk[:, :], rcs[:, ps, :], ALU.mult, [bk.buf(), rcs.buf(ps)], [oTc.buf(ct)])

    def F_W(tb):
        nonlocal_state = None
        prev_acc = None
        for tl in range(5):
            tt = 4 * tb + tl
            if tl < 4:
                if pstate["p3"] is not None:
                    ln_finish(pstate["p3"][0], None, lnp2, True, split=True)
                    pstate["p3"] = None
                accs = []
                for nh in range(2):
                    bk = nbank()
                    for kt in range(8):
                        mm(bk[:, :], oTc[:, kt, tl * 128:(tl + 1) * 128], wo2[:, kt, nh * 512:(nh + 1) * 512], kt == 0, kt == 7,
                           [oTc.buf(kt), wo2.buf(nh)], [bk.buf()])
                    accs.append(bk)
            if prev_acc is not None:
                pt_, pa_ = prev_acc
                slF = resid_stats(pt_, pa_)
                if pstate["p1"] is not None:
                    ln_finish_a(pstate["p1"][0], pstate["p1"][1], lnp2)
                if pstate["p3"] is not None:
                    ln_finish(pstate["p3"][0], None, lnp2, True, split=True)
                pstate["p3"] = pstate["p2"]
                pstate["p2"] = pstate["p1"]
                pstate["p1"] = (pt_, slF)
            prev_acc = (tt, accs) if tl < 4 else None

    pstate = {"p1": None, "p2": None, "p3": None}
    F_Q(0)
    for tb in range(4):
        F_HS(tb, 0)
        for hd in range(4):
            if hd < 3:
                F_HS(tb, hd + 1)
            F_HP(tb, hd)
        if tb < 3:
            F_Q(tb + 1)
        F_W(tb)
    if pstate["p3"] is not None:
        ln_finish(pstate["p3"][0], None, lnp2, True, split=True)
    ln_finish_a(pstate["p1"][0], pstate["p1"][1], lnp2)
    if pstate["p2"] is not None:
        ln_finish(pstate["p2"][0], None, lnp2, True, split=True)
    ln_finish(pstate["p1"][0], None, lnp2, True, split=True)
    S.free("wqb", "wo2", "lnp2", "qTc", "PTc", "oTc", "rcs", "kTca", "vca", "lbt")
    if debug:
        S.dma("sp", "dbg", dbg["d_h2"], hs[:, :, :], reads=hs.bufs(range(16)), final=True)
    if upto == "F":
        S.emit()
        return nc


    if str(6) in os.environ.get("BARRIERS", ""):
        S.barrier()
    lnp3 = load_lnp("lnp3", ln3_g, ln3_b)
    gus = S.alloc("gus", [3, 8, 2, 128], BF16, 2)
    actT = S.alloc("actT", [11, T], BF16, 2)
    sg = S.alloc("sg", [2, 512], F32, 4)
    guv = w_gu.rearrange("(kt p) n -> p kt n", p=128)
    gi = 0
    pendG = None
    pendG2 = None

    def g_finish(t_):
        ln_apply_b(hs[:, t_, :], hs.buf(t_), lnp3, hs[:, t_, :], hs.buf(t_))
        S.dma("sp", "out", out[t_ * 128:(t_ + 1) * 128, :], hs[:, t_, :], reads=[hs.buf(t_)], final=True)

    for ps_ in range(2):
        wd = S.alloc("wd", [11, D], BF16, 2)
        dv = w_dn[ps_ * 11 * 128:(ps_ + 1) * 11 * 128, :].rearrange("(j p) n -> p j n", p=128)
        for jl in range(11):
            j = ps_ * 11 + jl
            gs = gi % 3; gi += 1
            S.dma("pool", f"w_gu{gs}", gus[:, gs, :, 0, :], guv[:, :, j * 128:(j + 1) * 128], writes=[gus.buf(gs)])
            S.dma("pool", f"w_gu{gs}", gus[:, gs, :, 1, :], guv[:, :, FFN_H + j * 128:FFN_H + (j + 1) * 128], writes=[gus.buf(gs)])
            if jl < 11:
                S.dma("pool", "w_dn", wd[:, jl, :], dv[:, jl, :], writes=[wd.buf(jl)])
            for tb in range(4):
                tcols = slice(tb * 512, (tb + 1) * 512)
                hbufs = hT.bufs(range(4 * tb, 4 * tb + 4))
                bg = banks[(bi % 2) * 2]; bu_ = banks[(bi % 2) * 2 + 1]; bi += 1
                for kt in range(8):
                    mm(bg[:, :], gus[:, gs, kt, 0, :], hT[:, kt, tcols], kt == 0, kt == 7, [gus.buf(gs)] + hbufs, [bg.buf()])
                for kt in range(8):
                    mm(bu_[:, :], gus[:, gs, kt, 1, :], hT[:, kt, tcols], kt == 0, kt == 7, [gus.buf(gs)] + hbufs, [bu_.buf()])
                s2 = bi % 2
                S.op("act", lambda e, bg=bg, s2=s2: e.activation(out=sg[:, s2, :], in_=bg[:, :], func=AF.Silu), reads=[bg.buf()], writes=[sg.buf(s2)])
                tt_op("dve", actT[:, jl, tcols], sg[:, s2, :], bu_[:, :], ALU.mult, [sg.buf(s2), bu_.buf()], actT.bufs([(jl, 4 * tb + i) for i in range(4)]))
        for tt in range(NTT):
            accs = []
            for nh in range(2):
                bk = banks[4 + nh + 2 * (tt % 2)]
                for jl in range(11):
                    mm(bk[:, :], actT[:, jl, tt * 128:(tt + 1) * 128], wd[:, jl, nh * 512:(nh + 1) * 512], jl == 0, jl == 10,
                       [actT.buf((jl, tt)), wd.buf(jl)], [bk.buf()])
                accs.append(bk)
            if ps_ == 0:
                for nh in range(2):
                    bk = accs[nh]
                    S.op("dve", lambda e, nh=nh, bk=bk, tt=tt: e.scalar_tensor_tensor(out=hs[:, tt, nh * 512:(nh + 1) * 512], in0=hs[:, tt, nh * 512:(nh + 1) * 512],
                                                                                 scalar=ALPHA, in1=bk[:, :], op0=ALU.mult, op1=ALU.add),
                         reads=[hs.buf(tt), bk.buf()], writes=[hs.buf(tt)])
            else:
                for nh in range(2):
                    bk = accs[nh]
                    tt_op("dve", hs[:, tt, nh * 512:(nh + 1) * 512], hs[:, tt, nh * 512:(nh + 1) * 512], bk[:, :], ALU.add, [hs.buf(tt), bk.buf()], [hs.buf(tt)])
                slG = ln_stats(hs[:, tt, :], hs.buf(tt))
                if pendG2 is not None:
                    g_finish(pendG2[0])
                if pendG is not None:
                    ln_apply(hs[:, pendG[0], :], hs.buf(pendG[0]), lnp3, None, None, pendG[1])
                pendG2 = pendG
                pendG = (tt, slG)
        if ps_ == 1:
            if pendG2 is not None:
                g_finish(pendG2[0])
            ln_apply(hs[:, pendG[0], :], hs.buf(pendG[0]), lnp3, None, None, pendG[1])
            g_finish(pendG[0])
        S.free("wd")
    S.emit()
    return nc


_CACHE = {}


def kernel(**inputs):
    consts = host_consts()
    shared = {}
    for k, v in inputs.items():
        if k in ("x", "mem"):
            continue
        a = np.ascontiguousarray(np.asarray(v, dtype=np.float32))
        if k == "rel_bias":
            shared[k] = a
        elif a.ndim >= 2 and a.shape[0] == 1:
            a = a[0]
            if k in ("s5_lambda_re", "s5_lambda_im", "s5_d"):
                a = a.reshape(-1)
            elif k in ("s5_b_re", "s5_b_im"):
                a = a.reshape(2048, 16)
            elif k in ("s5_c_re", "s5_c_im"):
                a = a.reshape(512, 64)
            shared[k] = np.ascontiguousarray(a)
        else:
            shared[k] = a
    shared.update(consts)
    xs = np.asarray(inputs["x"], dtype=np.float32)
    ms = np.asarray(inputs["mem"], dtype=np.float32)
    if "nc" not in _CACHE:
        _CACHE["nc"] = build_program(False)
    nc = _CACHE["nc"]
    in_maps = []
    for b in range(8):
        m = dict(shared)
        m["x"] = np.ascontiguousarray(xs[b])
        m["mem"] = np.ascontiguousarray(ms[b])
        in_maps.append(m)
    res = run_bass_kernel_spmd(nc, in_maps, core_ids=list(range(8)))
    return np.stack([np.asarray(r["out"], dtype=np.float32) for r in res.results], axis=0)
```

```python
import numpy as np
import concourse.bass as bass
import concourse.mybir as mybir
from concourse.bass_utils import run_bass_kernel_spmd

F32 = mybir.dt.float32
BF16 = mybir.dt.bfloat16
ALU = mybir.AluOpType
AF = mybir.ActivationFunctionType
AX = mybir.AxisListType

ENGS = ("pe", "act", "dve", "pool", "sp")


class Instr:
    __slots__ = ("eng", "fn", "waits", "signal", "idx", "val", "dma_sem", "dma_val")

    def __init__(self, eng, fn):
        self.eng = eng
        self.fn = fn
        self.waits = []
        self.signal = False
        self.idx = -1
        self.val = 0
        self.dma_sem = None
        self.dma_val = 0


class Buf:
    __slots__ = ("name", "last_w", "readers")

    def __init__(self, name, inherit=()):
        self.name = name
        self.last_w = None
        self.readers = list(inherit)


class Ten:
    def __init__(self, h, name, inherit=()):
        self.h = h
        self.name = name
        self.inherit = list(inherit)
        self._bufs = {}

    def __getitem__(self, k):
        return self.h[k]

    def buf(self, key=0):
        b = self._bufs.get(key)
        if b is None:
            b = Buf(f"{self.name}{key}", self.inherit)
            self._bufs[key] = b
        return b

    def bufs(self, keys):
        return [self.buf(k) for k in keys]

    def all_instrs(self):
        out = list(self.inherit)
        for b in self._bufs.values():
            if b.last_w is not None:
                out.append(b.last_w)
            out.extend(b.readers)
        return out


class Sched:
    def __init__(self, nc, sbuf_base=16512, sbuf_bytes=229312):
        self.nc = nc
        self.q = {e: [] for e in ENGS}
        self.waited = {e: {} for e in ENGS}
        self.dma_count = {}
        self.sbuf_bytes = sbuf_bytes
        self.sbuf_base = sbuf_base
        self.live = {}
        self.hist = []
        self.n_alloc = 0
        self.final_dma = []
        self.ring_pos = {}
        self.ring_last = {}

    def alloc(self, name, free_shape, dtype, nbytes_el):
        size = int(np.prod(free_shape)) * nbytes_el
        size = (size + 63) // 64 * 64
        segs = sorted((o, s) for (o, s, _) in self.live.values())
        off = self.sbuf_base
        for (o, s) in segs:
            if off + size <= o:
                break
            off = max(off, o + s)
        if off + size > self.sbuf_bytes:
            raise RuntimeError(f"SBUF arena overflow allocating {name} ({size} B); live={[(k, v[0], v[1]) for k, v in self.live.items()]}")
        inherit = []
        for (o, s, t) in self.hist:
            if o < off + size and off < o + s:
                inherit.extend(t.all_instrs())
        comp = {}
        for ins in inherit:
            key = ins.dma_sem if ins.dma_sem is not None else ins.eng
            cur = comp.get(key)
            if cur is None or (ins.dma_val if ins.dma_sem is not None else ins.idx) > (cur.dma_val if cur.dma_sem is not None else cur.idx):
                comp[key] = ins
        self.n_alloc += 1
        h = self.nc.alloc_sbuf_tensor_at(f"{name}_{self.n_alloc}", [128] + list(free_shape), dtype, offset=off)
        t = Ten(h, name, list(comp.values()))
        self.live[name] = (off, size, t)
        return t

    def free(self, *names):
        for name in names:
            o, s, t = self.live.pop(name)
            self.hist.append((o, s, t))

    def _need(self, c, p, raw):
        if p is None or p is c:
            return
        E = c.eng
        if p.dma_sem is not None:
            key = "dma:" + p.dma_sem
            if self.waited[E].get(key, 0) >= p.dma_val:
                return
            self.waited[E][key] = p.dma_val
            c.waits.append(p)
            return
        if p.eng == E and c.dma_sem is None:
            if E in ("pe", "sp"):
                return
            if not raw:
                return
        key = p.eng
        if self.waited[E].get(key, -1) >= p.idx:
            return
        self.waited[E][key] = p.idx
        p.signal = True
        c.waits.append(p)

    def _deps(self, c, reads, writes):
        for b in reads:
            self._need(c, b.last_w, True)
        for b in writes:
            self._need(c, b.last_w, False)
            for r in b.readers:
                self._need(c, r, False)
        for b in reads:
            b.readers.append(c)
            if len(b.readers) > 12:
                comp = {}
                for ins in b.readers:
                    key = ins.dma_sem if ins.dma_sem is not None else ins.eng
                    cur = comp.get(key)
                    if cur is None or (ins.dma_val if ins.dma_sem is not None else ins.idx) >= (cur.dma_val if cur.dma_sem is not None else cur.idx):
                        comp[key] = ins
                b.readers = list(comp.values())
        for b in writes:
            b.last_w = c
            b.readers = []

    def op(self, eng, fn, reads=(), writes=()):
        c = Instr(eng, fn)
        c.idx = len(self.q[eng])
        self.q[eng].append(c)
        self._deps(c, reads, writes)
        return c

    NRING = 28

    def dma(self, eng, stream, out, in_, reads=(), writes=(), final=False):
        def fn(e, out=out, in_=in_):
            return e.dma_start(out=out, in_=in_)
        c = Instr(eng, fn)
        c.idx = len(self.q[eng])
        pos = self.ring_pos.get(eng, 0)
        self.ring_pos[eng] = pos + 1
        sem = f"{eng}{pos % self.NRING}"
        n = self.dma_count.get(sem, 0) + 1
        self.dma_count[sem] = n
        c.dma_sem = sem
        c.dma_val = 16 * n
        prev = self.ring_last.get(sem)
        if prev is not None:
            self._need(c, prev, False)
        self.ring_last[sem] = c
        self.q[eng].append(c)
        self._deps(c, reads, writes)
        if final:
            self.final_dma.append(c)
        return c

    def barrier(self):
        lasts = {}
        for e in ENGS:
            for ins in reversed(self.q[e]):
                if ins.dma_sem is None:
                    lasts[e] = ins
                    break
        dmas = []
        for sname, n in self.dma_count.items():
            f = Instr("sp", None)
            f.dma_sem = sname
            f.dma_val = 16 * n
            dmas.append(f)
        for e in ENGS:
            c = Instr(e, lambda eng: eng.nop())
            c.idx = len(self.q[e])
            for e2, p in lasts.items():
                if e2 != e:
                    self._need(c, p, True)
            for f in dmas:
                self._need(c, f, True)
            self.q[e].append(c)

    def emit(self):
        nc = self.nc
        import contextlib
        with contextlib.ExitStack() as st:
            esem = {e: st.enter_context(nc.semaphore(f"s_{e}")) for e in ENGS}
            dsem = {s: st.enter_context(nc.semaphore(f"d_{s}")) for s in self.dma_count}
            for e in ENGS:
                cnt = 0
                for ins in self.q[e]:
                    if ins.signal:
                        cnt += 1
                        ins.val = cnt
            block = st.enter_context(nc.Block())

            def run(eng_name, eobj):
                for ins in self.q[eng_name]:
                    for p in ins.waits:
                        if p.dma_sem is not None:
                            eobj.wait_ge(dsem[p.dma_sem], p.dma_val)
                        else:
                            eobj.wait_ge(esem[p.eng], p.val)
                    r = ins.fn(eobj)
                    if ins.dma_sem is not None:
                        r.then_inc(dsem[ins.dma_sem], 16)
                    elif ins.signal:
                        r.then_inc(esem[eng_name], 1)
                if eng_name == "sp":
                    done = {}
                    for ins in self.final_dma:
                        done[ins.dma_sem] = max(done.get(ins.dma_sem, 0), ins.dma_val)
                    for s, v in done.items():
                        eobj.wait_ge(dsem[s], v)

            @block.tensor
            def _(e):
                run("pe", e)

            @block.scalar
            def _(e):
                run("act", e)

            @block.vector
            def _(e):
                run("dve", e)

            @block.gpsimd
            def _(e):
                run("pool", e)

            @block.sync
            def _(e):
                run("sp", e)


T = 2048
D = 1024
NTT = 16
ALPHA = float(2.0 ** 0.25)
PI = float(np.pi)
LAMBDA_INIT = 0.8 - 0.6 * 1.0
FFN_H = 2816
NJ = 22


def t5_bucket_np(d):
    d = np.asarray(d, dtype=np.int64)
    df = np.maximum(d, 1).astype(np.float32)
    large = 16 + (np.log(df / np.float32(16)) / np.float32(np.log(128 / 16)) * np.float32(16)).astype(np.int32)
    large = np.minimum(large, 31)
    return np.where(d < 16, d, large)


def host_consts():
    c = {}
    c["c_ident"] = np.eye(128, dtype=np.float32)
    c["c_tri"] = np.triu(np.ones((128, 128), dtype=np.float32))
    cst = np.zeros((128, 8), dtype=np.float32)
    tp1 = np.arange(1, 129, dtype=np.float32)
    cst[:, 0] = tp1
    cst[:, 1] = -tp1
    cst[:, 2] = tp1 / np.float32(2 * np.pi)
    cst[:, 3] = 1.5
    cst[:, 4] = 1.75
    cst[:, 5] = -np.pi
    cst[:, 6] = 1e-5
    cst[:, 7] = 1.0
    c["c_cst"] = cst
    c["c_trow"] = np.broadcast_to(tp1[None, :], (128, 128)).astype(np.float32).copy()
    p = np.arange(128)
    mask = np.zeros((128, 4, 128), dtype=np.float32)
    for jm in range(4):
        mask[:, jm, :] = ((p[None, :] // 16) == (2 * jm + p[:, None] // 64)).astype(np.float32)
    c["c_maskC"] = mask
    c["c_maskB"] = np.ascontiguousarray(mask.transpose(2, 1, 0))
    oh = np.zeros((33, 384), dtype=np.float32)
    for m in range(384):
        d = m - 127
        if d < 0:
            oh[32, m] = -30000.0
        else:
            b = int(t5_bucket_np(d))
            oh[b, m] += 8.0
            oh[31, m] -= 8.0
    c["c_oh"] = oh
    return c


def build_program(debug=False, upto=None):
    import os
    nc = bass.Bass("TRN2", target_bir_lowering=False)
    S = Sched(nc)

    def din(name, shape):
        return nc.dram_tensor(name, list(shape), F32, kind="ExternalInput").ap()

    x = din("x", [T, D]); mem = din("mem", [256, D])
    ln_in_g = din("ln_in_g", [D]); ln_in_b = din("ln_in_b", [D])
    w_in = din("w_in", [D, 2048])
    lam_re = din("s5_lambda_re", [2048]); lam_im = din("s5_lambda_im", [2048]); log_dt = din("s5_log_dt", [32])
    b_re = din("s5_b_re", [2048, 16]); b_im = din("s5_b_im", [2048, 16])
    c_re = din("s5_c_re", [512, 64]); c_im = din("s5_c_im", [512, 64])
    s5_d = din("s5_d", [512]); glu_w = din("s5_glu_w", [512, 512]); glu_b = din("s5_glu_b", [512])
    lq1 = din("diff_lq1", [64]); lk1 = din("diff_lk1", [64]); lq2 = din("diff_lq2", [64]); lk2 = din("diff_lk2", [64])
    subln_g = din("diff_subln_g", [128]); rel_bias = din("rel_bias", [32, 4])
    w_out = din("w_out", [D, D]); ln1_g = din("ln1_g", [D]); ln1_b = din("ln1_b", [D])
    ca_wq = din("ca_wq", [D, D]); ca_wkv = din("ca_wkv", [D, 2 * D]); ca_wo = din("ca_wo", [D, D])
    ln2_g = din("ln2_g", [D]); ln2_b = din("ln2_b", [D])
    w_gu = din("ffn_w_gate_up", [D, 2 * FFN_H]); w_dn = din("ffn_w_down", [FFN_H, D])
    ln3_g = din("ln3_g", [D]); ln3_b = din("ln3_b", [D])
    c_ident = din("c_ident", [128, 128]); c_tri = din("c_tri", [128, 128]); c_cst = din("c_cst", [128, 8])
    c_trow = din("c_trow", [128, 128]); c_maskC = din("c_maskC", [128, 4, 128]); c_maskB = din("c_maskB", [128, 4, 128])
    c_oh = din("c_oh", [33, 384])
    out = nc.dram_tensor("out", [T, D], F32, kind="ExternalOutput").ap()
    scr = nc.dram_tensor("bias_scr", [128, 4, 384], F32, kind="Internal").ap()
    dbg = {}
    if debug:
        for nm, shp, dt_ in (("d_hT", [128, 8, T], BF16), ("d_uT", [128, 4, T], BF16), ("d_qT", [128, 4, T], BF16),
                             ("d_kT", [128, 4, T], BF16), ("d_v", [128, 16, 4, 129], BF16), ("d_cat", [128, 8, T], BF16),
                             ("d_h1", [128, 16, D], F32), ("d_h2", [128, 16, D], F32)):
            dbg[nm] = nc.dram_tensor(nm, shp, dt_, kind="ExternalOutput").ap()

    banks = [Ten(nc.alloc_psum_tensor(f"bank{i}", [128, 512], F32), f"bank{i}") for i in range(8)]

    class View:
        def __init__(self, ten):
            self.ten = ten
            self.ap = ten[:, :].bitcast(BF16).rearrange("p (k c) -> p k c", k=8)

        def __getitem__(self, k):
            return self.ap[k]

        def buf(self, key=0):
            return self.ten.buf(key)

    ptb = [View(banks[6]), View(banks[7])]

    idf = S.alloc("idf", [128], F32, 4)
    idb = S.alloc("idb", [128], BF16, 2)
    trib = S.alloc("trib", [128], BF16, 2)
    trif = S.alloc("trif", [128], F32, 4)
    onesb = S.alloc("onesb", [128], BF16, 2)
    onesf = S.alloc("onesf", [128], F32, 4)
    cst = S.alloc("cst", [8], F32, 4)
    stt = S.alloc("ln_st", [8, 12], F32, 4)
    mvt = S.alloc("ln_mv", [8, 2], F32, 4)
    rsd = S.alloc("ln_rs", [8, 1], F32, 4)
    S.dma("sp", "c0", idf[:, :], c_ident, writes=[idf.buf()])
    S.dma("sp", "c0", trif[:, :], c_tri, writes=[trif.buf()])
    S.dma("sp", "c0", cst[:, :], c_cst, writes=[cst.buf()])
    S.op("dve", lambda e: e.tensor_copy(out=idb[:, :], in_=idf[:, :]), reads=[idf.buf()], writes=[idb.buf()])
    S.op("dve", lambda e: e.tensor_copy(out=trib[:, :], in_=trif[:, :]), reads=[trif.buf()], writes=[trib.buf()])
    S.op("pool", lambda e: e.memset(onesb[:, :], 1.0), writes=[onesb.buf()])
    S.op("pool", lambda e: e.memset(onesf[:, :], 1.0), writes=[onesf.buf()])
    EPS = cst[:, 6:7]

    ctr = {"ev": 0, "ln": 0, "pt": 0}

    def evac_eng():
        ctr["ev"] += 1
        return "act" if ctr["ev"] % 2 else "dve"

    def copy_op(eng, out_ap, in_ap, reads, writes):
        if eng == "act":
            S.op("act", lambda e: e.activation(out=out_ap, in_=in_ap, func=AF.Copy), reads, writes)
        else:
            S.op(eng, lambda e: e.tensor_copy(out=out_ap, in_=in_ap), reads, writes)

    def mm(out_ap, lhsT, rhs, start, stop, reads, writes):
        S.op("pe", lambda e: e.matmul(out=out_ap, lhsT=lhsT, rhs=rhs, start=start, stop=stop), reads, writes)

    def load_lnp(name, g, b):
        t = S.alloc(name, [2, D], F32, 4)
        S.dma("sp", "lnp", t[:, 0, :], g.partition_broadcast(128), writes=[t.buf()])
        S.dma("sp", "lnp", t[:, 1, :], b.partition_broadcast(128), writes=[t.buf()])
        return t

    def ln_stats(x_ap, x_buf):
        ctr["ln"] += 1
        sl = ctr["ln"] % 8
        for c in range(2):
            S.op("dve", lambda e, c=c: e.bn_stats(out=stt[:, sl, c * 6:(c + 1) * 6], in_=x_ap[:, c * 512:(c + 1) * 512]),
                 reads=[x_buf], writes=[stt.buf(sl)])
        S.op("dve", lambda e: e.bn_aggr(out=mvt[:, sl, :], in_=stt[:, sl, :]), reads=[stt.buf(sl)], writes=[mvt.buf(sl)])
        S.op("act", lambda e: e.activation(out=rsd[:, sl, :], in_=mvt[:, sl, 1:2], func=AF.Sqrt, bias=EPS, scale=1.0),
             reads=[mvt.buf(sl), cst.buf()], writes=[rsd.buf(sl)])
        S.op("dve", lambda e: e.reciprocal(out=rsd[:, sl, :], in_=rsd[:, sl, :]), reads=[rsd.buf(sl)], writes=[rsd.buf(sl)])
        return sl

    def ln_apply(x_ap, x_buf, lnp, out_ap, out_buf, sl):
        S.op("dve", lambda e: e.scalar_tensor_tensor(out=mvt[:, sl, 1:2], in0=mvt[:, sl, 0:1], scalar=-1.0, in1=rsd[:, sl, :],
                                                      op0=ALU.mult, op1=ALU.mult),
             reads=[mvt.buf(sl), rsd.buf(sl)], writes=[mvt.buf(sl)])
        S.op("act", lambda e: e.activation(out=x_ap, in_=x_ap, func=AF.Identity, bias=mvt[:, sl, 1:2], scale=rsd[:, sl, 0:1]),
             reads=[x_buf, mvt.buf(sl), rsd.buf(sl)], writes=[x_buf])
        S.op("pool", lambda e: e.tensor_tensor(out=x_ap, in0=x_ap, in1=lnp[:, 0, :], op=ALU.mult), reads=[x_buf, lnp.buf()], writes=[x_buf])
        if out_ap is not None:
            ln_apply_b(x_ap, x_buf, lnp, out_ap, out_buf)

    def ln_apply_b(x_ap, x_buf, lnp, out_ap, out_buf):
        S.op("dve", lambda e: e.tensor_tensor(out=out_ap, in0=x_ap, in1=lnp[:, 1, :], op=ALU.add), reads=[x_buf, lnp.buf()],
             writes=[out_buf] if out_buf is not x_buf else [x_buf])

    def transpose_to_hT(lb, lb_buf, hT, tt):
        ctr["pt"] += 1
        pt = ptb[ctr["pt"] % 2]
        for k in range(8):
            S.op("pe", lambda e, k=k: e.transpose(out=pt[:, k, :], in_=lb[:, k * 128:(k + 1) * 128], identity=idb[:, :]),
                 reads=[lb_buf, idb.buf()], writes=[pt.buf()])
        copy_op(evac_eng(), hT[:, :, tt * 128:(tt + 1) * 128], pt[:, :, :], [pt.buf()], [hT.buf(tt)])

    def wload(name, src, kt, ncols, chunk, stream):
        t = S.alloc(name, [kt, ncols], BF16, 2)
        sv = src.rearrange("(kt p) n -> p kt n", p=128)
        for c in range(ncols // chunk):
            S.dma("pool", stream, t[:, :, c * chunk:(c + 1) * chunk], sv[:, :, c * chunk:(c + 1) * chunk], writes=[t.buf(c)])
        return t

    hT = S.alloc("hT", [8, T], BF16, 2)
    wi = wload("wi", w_in, 8, 2048, 512, "w_a")
    lnp0 = load_lnp("lnp0", ln_in_g, ln_in_b)
    xin = S.alloc("xin", [4, D], F32, 4)
    lbt = S.alloc("lbt", [2, D], BF16, 2)
    sls = {}
    for tt in range(NTT + 3):
        if tt < NTT:
            s3 = tt % 4
            S.dma("sp", "xin", xin[:, s3, :], x[tt * 128:(tt + 1) * 128, :], writes=[xin.buf(s3)])
            sls[tt] = ln_stats(xin[:, s3, :], xin.buf(s3))
        if 1 <= tt <= NTT:
            t1 = tt - 1
            ln_apply(xin[:, t1 % 4, :], xin.buf(t1 % 4), lnp0, None, None, sls[t1])
        if 2 <= tt <= NTT + 1:
            t2 = tt - 2
            ln_apply_b(xin[:, t2 % 4, :], xin.buf(t2 % 4), lnp0, lbt[:, t2 % 2, :], lbt.buf(t2 % 2))
        if tt >= 3:
            t3 = tt - 3
            transpose_to_hT(lbt[:, t3 % 2, :], lbt.buf(t3 % 2), hT, t3)
    S.free("xin", "lnp0")
    if debug:
        S.dma("sp", "dbg", dbg["d_hT"], hT[:, :, :], reads=hT.bufs(range(16)), final=True)
    if upto == "A":
        S.emit()
        return nc


    if str(0) in os.environ.get("BARRIERS", ""):
        S.barrier()
    rb = S.alloc("rb", [4], F32, 4)
    chc = S.alloc("chc", [4], F32, 4)
    ohs = S.alloc("ohs", [384], F32, 4)
    Lh = S.alloc("Lh", [4, 128], F32, 4)
    gsb = S.alloc("gsb", [4, 384], F32, 4)
    biasf = S.alloc("biasf", [4, 2, 128], F32, 4)
    biasb = S.alloc("biasb", [4, 2, 128], BF16, 2)
    lqk = S.alloc("lqk", [4, 64], F32, 4)
    lsm = S.alloc("lsm", [4], F32, 4)
    gsub = S.alloc("gsub", [128], F32, 4)
    scrb = Ten(None, "scr")
    NEGLAM = lsm[:, 2:3]

    def dsetup_1():
        S.dma("sp", "c1", rb[0:32, :], rel_bias, writes=[rb.buf()])
        S.dma("sp", "c1", chc[:, :], rel_bias[31, :].partition_broadcast(128), writes=[chc.buf()])
        S.dma("sp", "c1", ohs[0:33, :], c_oh, writes=[ohs.buf()])
        for i, v in enumerate((lq1, lk1, lq2, lk2)):
            S.dma("sp", "c1", lqk[:, i, :], v.partition_broadcast(128), writes=[lqk.buf()])
        S.dma("sp", "c1", gsub[:, :], subln_g.partition_broadcast(128), writes=[gsub.buf()])
        S.op("pool", lambda e: e.memset(Lh[0:33, :, :], 1.0), writes=[Lh.buf()])

    def dsetup_2a():
        for h in range(4):
            S.op("dve", lambda e, h=h: e.tensor_scalar(out=Lh[0:32, h, :], in0=onesf[0:32, :], scalar1=rb[0:32, h:h + 1], scalar2=None, op0=ALU.mult),
                 reads=[onesf.buf(), rb.buf(), Lh.buf()], writes=[Lh.buf()])
        S.op("dve", lambda e: e.tensor_mul(out=lqk[:, 0, :], in0=lqk[:, 0, :], in1=lqk[:, 1, :]), reads=[lqk.buf()], writes=[lqk.buf()])
        S.op("dve", lambda e: e.tensor_mul(out=lqk[:, 2, :], in0=lqk[:, 2, :], in1=lqk[:, 3, :]), reads=[lqk.buf()], writes=[lqk.buf()])
        S.op("dve", lambda e: e.reduce_sum(out=lsm[:, 0:1], in_=lqk[:, 0, :], axis=AX.X), reads=[lqk.buf()], writes=[lsm.buf()])
        S.op("dve", lambda e: e.reduce_sum(out=lsm[:, 1:2], in_=lqk[:, 2, :], axis=AX.X), reads=[lqk.buf()], writes=[lsm.buf()])
        S.op("dve", lambda e: e.tensor_scalar(out=gsub[:, :], in0=gsub[:, :], scalar1=1.0 - LAMBDA_INIT, scalar2=None, op0=ALU.mult),
             reads=[gsub.buf()], writes=[gsub.buf()])

    def dsetup_2b():
        S.op("act", lambda e: e.activation(out=lsm[:, 0:2], in_=lsm[:, 0:2], func=AF.Exp), reads=[lsm.buf()], writes=[lsm.buf()])
        for h in range(4):
            bk = banks[4 + h % 2]
            mm(bk[:, 0:384], Lh[0:33, h, :], ohs[0:33, :], True, True, [Lh.buf(), ohs.buf()], [bk.buf()])
            copy_op("act", gsb[:, h, :], bk[:, 0:384], [bk.buf()], [gsb.buf()])
        S.dma("sp", "scr", scr, gsb[:, :, :], reads=[gsb.buf()], writes=[scrb.buf()])

    def dsetup_3():
        S.op("dve", lambda e: e.tensor_sub(out=lsm[:, 2:3], in0=lsm[:, 1:2], in1=lsm[:, 0:1]), reads=[lsm.buf()], writes=[lsm.buf()])
        S.op("dve", lambda e: e.tensor_scalar(out=lsm[:, 2:3], in0=lsm[:, 2:3], scalar1=-LAMBDA_INIT, scalar2=None, op0=ALU.add),
             reads=[lsm.buf()], writes=[lsm.buf()])
        for h in range(4):
            for dsub, off in enumerate((127, 255)):
                S.dma("sp", "scr2", biasf[:, h, dsub, :], bass.AP(scr.tensor, h * 384 + off, [[1535, 128], [1, 128]]),
                      reads=[scrb.buf()], writes=[biasf.buf()])

    def dsetup_4():
        S.op("dve", lambda e: e.tensor_copy(out=biasb[:, :, :, :], in_=biasf[:, :, :, :]), reads=[biasf.buf()], writes=[biasb.buf()])

    dsetup_hooks = {0: dsetup_1, 6: dsetup_2a, 12: dsetup_2b, 28: dsetup_3, 44: dsetup_4}

    uT = S.alloc("uT", [4, T], BF16, 2)
    qT = S.alloc("qT", [4, T], BF16, 2)
    kT = S.alloc("kT", [4, T], BF16, 2)
    vaug = S.alloc("vaug", [16, 4, 129], BF16, 2)
    import os
    for tt in range(NTT):
        S.op("dve", lambda e, tt=tt: e.tensor_copy(out=vaug[:, tt, :, 128:129], in_=onesb[:, 0:4].unsqueeze(2)), reads=[onesb.buf()], writes=[vaug.buf(tt)])
    bi = 0
    for grp, dst in enumerate((uT, qT, kT)):
        for ct in range(4):
            col = grp * 512 + ct * 128
            for tb in range(4):
                if bi in dsetup_hooks:
                    dsetup_hooks[bi]()
                bk = banks[bi % 4]; bi += 1
                for kt in range(8):
                    mm(bk[:, :], wi[:, kt, col:col + 128], hT[:, kt, tb * 512:(tb + 1) * 512], kt == 0, kt == 7,
                       [wi.buf(grp)] + hT.bufs(range(4 * tb, 4 * tb + 4)), [bk.buf()])
                copy_op(evac_eng(), dst[:, ct, tb * 512:(tb + 1) * 512], bk[:, :], [bk.buf()], dst.bufs([(ct, 4 * tb + i) for i in range(4)]))
    for tt in range(0 if not os.environ.get("NO_V") else NTT, NTT):
        bk = banks[bi % 4]; bi += 1
        for kt in range(8):
            mm(bk[:, :], hT[:, kt, tt * 128:(tt + 1) * 128], wi[:, kt, 1536:2048], kt == 0, kt == 7,
               [wi.buf(3), hT.buf(tt)], [bk.buf()])
        copy_op(evac_eng(), vaug[:, tt, :, 0:128], bk[:, :].rearrange("p (h d) -> p h d", h=4), [bk.buf()], [vaug.buf(tt)])
    S.free("wi", "lbt", "hT")
    if debug:
        S.dma("sp", "dbg", dbg["d_uT"], uT[:, :, :], reads=uT.bufs([(c, t) for c in range(4) for t in range(16)]), final=True)
        S.dma("sp", "dbg", dbg["d_qT"], qT[:, :, :], reads=qT.bufs([(c, t) for c in range(4) for t in range(16)]), final=True)
        S.dma("sp", "dbg", dbg["d_kT"], kT[:, :, :], reads=kT.bufs([(c, t) for c in range(4) for t in range(16)]), final=True)
        S.dma("sp", "dbg", dbg["d_v"], vaug[:, :, :, :], reads=vaug.bufs(range(16)), final=True)
    if upto == "B":
        S.emit()
        return nc


    catT = S.alloc("catT", [8, T], BF16, 2)

    if str(1) in os.environ.get("BARRIERS", ""):
        S.barrier()
    PT = S.alloc("PT", [6, 512], BF16, 2)
    gcol = S.alloc("gcol", [1], F32, 4)
    S.dma("sp", "c1", gcol[:, :], subln_g.rearrange("(p o) -> p o", o=1), writes=[gcol.buf()])
    S.op("dve", lambda e: e.tensor_scalar(out=gcol[:, :], in0=gcol[:, :], scalar1=1.0 - LAMBDA_INIT, scalar2=None, op0=ALU.mult),
         reads=[gcol.buf()], writes=[gcol.buf()])
    rr = S.alloc("rr", [2, 2, 512], F32, 4)
    o1 = S.alloc("o1", [2, 512], F32, 4)
    oo = S.alloc("oo", [2, 512], F32, 4)
    sqb = S.alloc("sqb", [2, 512], BF16, 2)
    rst = S.alloc("rst", [2, 512], F32, 4)
    stb = (banks[0], banks[1])
    Ab = (banks[2], banks[3])
    Sb = (banks[4], banks[5])
    MSb = banks[6]

    def s_stage(it):
        h, I, s, j, st, pt = it
        r0 = s * 64
        qstart = max(512 * I, 128 * j)
        N = 512 * (I + 1) - qstart
        col0 = qstart - 512 * I
        has_diag = j >= 4 * I
        has_sub = (4 * I - 1) <= j <= (4 * I + 2)
        mm(st[:, col0:col0 + N], kT[r0:r0 + 64, h, j * 128:(j + 1) * 128], qT[r0:r0 + 64, h, qstart:qstart + N],
           True, not (has_diag or has_sub),
           [kT.buf((h, j))] + qT.bufs([(h, t) for t in range(qstart // 128, 4 * I + 4)]), [st.buf()])
        if has_diag:
            c = j * 128 - 512 * I
            mm(st[:, c:c + 128], idb[:, :], biasb[:, h, 0, :], False, not has_sub, [idb.buf(), biasb.buf()], [st.buf()])
        if has_sub:
            c = (j + 1) * 128 - 512 * I
            mm(st[:, c:c + 128], idb[:, :], biasb[:, h, 1, :], False, True, [idb.buf(), biasb.buf()], [st.buf()])
        S.op("act", lambda e, st=st, pt=pt, col0=col0, N=N, h=h: e.activation(
            out=PT[:, pt, col0:col0 + N], in_=st[:, col0:col0 + N], func=AF.Exp, bias=chc[:, h:h + 1], scale=0.125),
            reads=[st.buf(), chc.buf()], writes=[PT.buf(pt)])

    def pv_stage(it):
        h, I, s, j, st, pt = it
        qstart = max(512 * I, 128 * j)
        N = 512 * (I + 1) - qstart
        col0 = qstart - 512 * I
        last = (j == 4 * I + 3)
        S.op("pe", lambda e, s=s, pt=pt, col0=col0, N=N, j=j, h=h, last=last: e.matmul(
            out=Ab[s][:, col0:col0 + N], lhsT=vaug[:, j, h, 0:128], rhs=PT[:, pt, col0:col0 + N], start=(j == 0), stop=last, skip_group_check=True),
            [PT.buf(pt), vaug.buf(j)], [Ab[s].buf()])
        S.op("pe", lambda e, s=s, pt=pt, col0=col0, N=N, j=j, last=last: e.matmul(
            out=Sb[s][:, col0:col0 + N], lhsT=onesb[:, :], rhs=PT[:, pt, col0:col0 + N], start=(j == 0), stop=last, skip_group_check=True),
            [PT.buf(pt), onesb.buf()], [Sb[s].buf()])

    def epilogue_a(h, I, rnd):
        r2 = rnd % 2
        for s in range(2):
            S.op("act", lambda e, s=s, r2=r2: e.activation(out=rr[:, r2, s, :], in_=Sb[s][:, :], func=AF.Ln), reads=[Sb[s].buf()], writes=[rr.buf((r2, s))])
            S.op("act", lambda e, s=s, r2=r2: e.activation(out=rr[:, r2, s, :], in_=rr[:, r2, s, :], func=AF.Exp, scale=-1.0), reads=[rr.buf((r2, s))], writes=[rr.buf((r2, s))])
        tt_op("dve", o1[:, r2, :], Ab[0][:, :], rr[:, r2, 0, :], ALU.mult, [Ab[0].buf(), rr.buf((r2, 0))], [o1.buf(r2)])
        tt_op("dve", oo[:, r2, :], Ab[1][:, :], rr[:, r2, 1, :], ALU.mult, [Ab[1].buf(), rr.buf((r2, 1))], [oo.buf(r2)])
        S.op("dve", lambda e, r2=r2: e.scalar_tensor_tensor(out=oo[:, r2, :], in0=oo[:, r2, :], scalar=NEGLAM, in1=o1[:, r2, :], op0=ALU.mult, op1=ALU.add),
             reads=[oo.buf(r2), o1.buf(r2), lsm.buf()], writes=[oo.buf(r2)])
        tt_op("dve", sqb[:, r2, :], oo[:, r2, :], oo[:, r2, :], ALU.mult, [oo.buf(r2)], [sqb.buf(r2)])

    def epilogue_b(h, I, rnd):
        r2 = rnd % 2
        mm(MSb[:, :], onesb[:, :], sqb[:, r2, :], True, True, [onesb.buf(), sqb.buf(r2)], [MSb.buf()])
        S.op("act", lambda e, r2=r2: e.activation(out=rst[:, r2, :], in_=MSb[:, :], func=AF.Ln, bias=EPS, scale=1.0 / 128.0),
             reads=[MSb.buf(), cst.buf()], writes=[rst.buf(r2)])
        S.op("act", lambda e, r2=r2: e.activation(out=rst[:, r2, :], in_=rst[:, r2, :], func=AF.Exp, scale=-0.5), reads=[rst.buf(r2)], writes=[rst.buf(r2)])
        S.op("dve", lambda e, r2=r2, h=h, I=I: e.scalar_tensor_tensor(out=catT[:, 4 + h, I * 512:(I + 1) * 512], in0=oo[:, r2, :], scalar=gcol[:, 0:1], in1=rst[:, r2, :],
                                                                    op0=ALU.mult, op1=ALU.mult),
             reads=[oo.buf(r2), gcol.buf(), rst.buf(r2)], writes=catT.bufs([(4 + h, 4 * I + i) for i in range(4)]))

    def tt_op(eng, out_ap, a_ap, b_ap, op, reads, writes):
        S.op(eng, lambda e: e.tensor_tensor(out=out_ap, in0=a_ap, in1=b_ap, op=op), reads, writes)

    stb4 = (banks[0], banks[1], banks[7], banks[6])
    iters = []
    k = 0
    for h in range(4):
        for I in range(4):
            for j in range(4 * I + 4):
                for s in range(2):
                    iters.append((h, I, s, j, stb4[k % 4], k % 6))
                    k += 1
    npair = len(iters) // 2
    pending = None
    since = 0
    rnd = 0
    for p in range(npair):
        s_stage(iters[2 * p]); s_stage(iters[2 * p + 1])
        since += 1
        if pending is not None and since >= 2:
            epilogue_b(*pending); pending = None
        if p >= 1:
            pv_stage(iters[2 * p - 2]); pv_stage(iters[2 * p - 1])
            prev, it = iters[2 * p - 1], iters[2 * p]
            if (prev[0], prev[1]) != (it[0], it[1]):
                if pending is not None:
                    epilogue_b(*pending); pending = None
                epilogue_a(prev[0], prev[1], rnd)
                pending = (prev[0], prev[1], rnd); since = 0
                rnd += 1
    pv_stage(iters[-2]); pv_stage(iters[-1])
    if pending is not None:
        epilogue_b(*pending)
    epilogue_a(iters[-1][0], iters[-1][1], rnd)
    epilogue_b(iters[-1][0], iters[-1][1], rnd)
    S.free("rb", "chc", "ohs", "Lh", "gsb", "biasf", "biasb", "lqk", "lsm", "gsub", "PT", "qT", "kT", "vaug", "gcol", "rr", "o1", "oo", "sqb", "rst")
    if upto == "D":
        S.emit()
        return nc


    if str(2) in os.environ.get("BARRIERS", ""):
        S.barrier()
    I32 = mybir.dt.int32
    W2 = 2048

    def A8(name, dt_=F32):
        return S.alloc(name, [W2], dt_, 4)

    def tt_op(eng, out_ap, a_ap, b_ap, op, reads, writes):
        S.op(eng, lambda e: e.tensor_tensor(out=out_ap, in0=a_ap, in1=b_ap, op=op), reads, writes)

    def ts_op(eng, out_ap, a_ap, s1, s2, op0, op1, reads, writes):
        if s2 is None:
            S.op(eng, lambda e: e.tensor_scalar(out=out_ap, in0=a_ap, scalar1=s1, scalar2=None, op0=op0), reads, writes)
        else:
            S.op(eng, lambda e: e.tensor_scalar(out=out_ap, in0=a_ap, scalar1=s1, scalar2=s2, op0=op0, op1=op1), reads, writes)

    ki = A8("ki", I32)
    kf = A8("kf")

    def sin_from_u(u, out):
        S.op("dve", lambda e: e.tensor_copy(out=ki[:, :], in_=u[:, :]), reads=[u.buf()], writes=[ki.buf()])
        S.op("dve", lambda e: e.tensor_copy(out=kf[:, :], in_=ki[:, :]), reads=[ki.buf()], writes=[kf.buf()])
        tt_op("dve", u[:, :], u[:, :], kf[:, :], ALU.subtract, [u.buf(), kf.buf()], [u.buf()])
        S.op("dve", lambda e: e.scalar_tensor_tensor(out=u[:, :], in0=u[:, :], scalar=0.0, in1=u[:, :], op0=ALU.is_lt, op1=ALU.add),
             reads=[u.buf()], writes=[u.buf()])
        S.op("act", lambda e: e.activation(out=out[:, :], in_=u[:, :], func=AF.Sin, bias=cst[:, 5:6], scale=2 * PI),
             reads=[u.buf(), cst.buf()], writes=[out.buf()])

    lr = A8("lr"); li = A8("li"); lrdt = A8("lrdt"); ang = A8("ang")
    dtr = S.alloc("dtr", [32], F32, 4)
    S.dma("sp", "c2", lr[:, :], lam_re.partition_broadcast(128), writes=[lr.buf()])
    S.dma("sp", "c2", li[:, :], lam_im.partition_broadcast(128), writes=[li.buf()])
    S.dma("sp", "c2", dtr[:, :], log_dt.partition_broadcast(128), writes=[dtr.buf()])
    S.op("act", lambda e: e.activation(out=dtr[:, :], in_=dtr[:, :], func=AF.Exp), reads=[dtr.buf()], writes=[dtr.buf()])
    dt_b = dtr[:, :].unsqueeze(2).to_broadcast([128, 32, 64])
    v3 = lambda t: t[:, :].rearrange("p (g s) -> p g s", s=64)
    tt_op("dve", v3(lrdt), v3(lr), dt_b, ALU.mult, [lr.buf(), dtr.buf()], [lrdt.buf()])
    tt_op("dve", v3(ang), v3(li), dt_b, ALU.mult, [li.buf(), dtr.buf()], [ang.buf()])
    mg = A8("mg"); sn = A8("sn"); cs = A8("cs"); ua = A8("ua"); fre = A8("fre"); fim = A8("fim")
    S.op("act", lambda e: e.activation(out=mg[:, :], in_=lrdt[:, :], func=AF.Exp), reads=[lrdt.buf()], writes=[mg.buf()])
    ts_op("dve", ua[:, :], ang[:, :], 1.0 / (2 * PI), 1.5, ALU.mult, ALU.add, [ang.buf()], [ua.buf()])
    sin_from_u(ua, sn)
    ts_op("dve", ua[:, :], ang[:, :], 1.0 / (2 * PI), 1.75, ALU.mult, ALU.add, [ang.buf()], [ua.buf()])
    sin_from_u(ua, cs)
    tt_op("dve", cs[:, :], mg[:, :], cs[:, :], ALU.mult, [mg.buf(), cs.buf()], [cs.buf()])
    ts_op("dve", cs[:, :], cs[:, :], -1.0, None, ALU.add, None, [cs.buf()], [cs.buf()])
    tt_op("dve", sn[:, :], mg[:, :], sn[:, :], ALU.mult, [mg.buf(), sn.buf()], [sn.buf()])
    tt_op("dve", mg[:, :], lr[:, :], lr[:, :], ALU.mult, [lr.buf()], [mg.buf()])
    tt_op("dve", kf[:, :], li[:, :], li[:, :], ALU.mult, [li.buf()], [kf.buf()])
    tt_op("dve", mg[:, :], mg[:, :], kf[:, :], ALU.add, [mg.buf(), kf.buf()], [mg.buf()])
    S.op("dve", lambda e: e.reciprocal(out=mg[:, :], in_=mg[:, :]), reads=[mg.buf()], writes=[mg.buf()])
    tt_op("dve", fre[:, :], cs[:, :], lr[:, :], ALU.mult, [cs.buf(), lr.buf()], [fre.buf()])
    tt_op("dve", kf[:, :], sn[:, :], li[:, :], ALU.mult, [sn.buf(), li.buf()], [kf.buf()])
    tt_op("dve", fre[:, :], fre[:, :], kf[:, :], ALU.add, [fre.buf(), kf.buf()], [fre.buf()])
    tt_op("dve", fre[:, :], fre[:, :], mg[:, :], ALU.mult, [fre.buf(), mg.buf()], [fre.buf()])
    tt_op("dve", fim[:, :], sn[:, :], lr[:, :], ALU.mult, [sn.buf(), lr.buf()], [fim.buf()])
    tt_op("dve", kf[:, :], cs[:, :], li[:, :], ALU.mult, [cs.buf(), li.buf()], [kf.buf()])
    tt_op("dve", fim[:, :], fim[:, :], kf[:, :], ALU.subtract, [fim.buf(), kf.buf()], [fim.buf()])
    tt_op("dve", fim[:, :], fim[:, :], mg[:, :], ALU.mult, [fim.buf(), mg.buf()], [fim.buf()])
    if os.environ.get("S5_STOP") == "1":
        S.emit()
        return nc
    S.free("lr", "li", "dtr")
    Wmr = A8("Wmr"); Wmi = A8("Wmi")
    S.op("act", lambda e: e.activation(out=mg[:, :], in_=lrdt[:, :], func=AF.Exp, scale=cst[:, 1:2]), reads=[lrdt.buf(), cst.buf()], writes=[mg.buf()])
    ts_op("dve", ua[:, :], ang[:, :], cst[:, 2:3], cst[:, 3:4], ALU.mult, ALU.add, [ang.buf(), cst.buf()], [ua.buf()])
    sin_from_u(ua, sn)
    ts_op("dve", ua[:, :], ang[:, :], cst[:, 2:3], cst[:, 4:5], ALU.mult, ALU.add, [ang.buf(), cst.buf()], [ua.buf()])
    sin_from_u(ua, cs)
    tt_op("dve", Wmr[:, :], mg[:, :], cs[:, :], ALU.mult, [mg.buf(), cs.buf()], [Wmr.buf()])
    S.op("dve", lambda e: e.scalar_tensor_tensor(out=Wmi[:, :], in0=mg[:, :], scalar=-1.0, in1=sn[:, :], op0=ALU.mult, op1=ALU.mult),
         reads=[mg.buf(), sn.buf()], writes=[Wmi.buf()])
    if os.environ.get("S5_STOP") == "2":
        S.emit()
        return nc
    trow = S.alloc("trow", [128], F32, 4)
    S.dma("sp", "c2", trow[:, :], c_trow, writes=[trow.buf()])
    lrdtT = A8("lrdtT"); angT = A8("angT")
    bi = 0
    for (src, dst) in ((lrdt, lrdtT), (ang, angT)):
        for q4 in range(4):
            bk = banks[bi % 4]; bi += 1
            for i in range(4):
                j = q4 * 4 + i
                S.op("pe", lambda e, bk=bk, i=i, j=j, src=src: e.transpose(out=bk[:, i * 128:(i + 1) * 128], in_=src[:, j * 128:(j + 1) * 128], identity=idf[:, :]),
                     reads=[src.buf(), idf.buf()], writes=[bk.buf()])
            copy_op("dve", dst[:, q4 * 512:(q4 + 1) * 512], bk[:, :], [bk.buf()], [dst.buf()])
    S.free("lrdt", "ang")
    WpTr = A8("WpTr"); WpTi = A8("WpTi")
    trow_b = trow[:, :].unsqueeze(1).to_broadcast([128, 16, 128])
    v16 = lambda t: t[:, :].rearrange("p (j t) -> p j t", t=128)
    tt_op("dve", v16(lrdtT), v16(lrdtT), trow_b, ALU.mult, [lrdtT.buf(), trow.buf()], [lrdtT.buf()])
    S.op("act", lambda e: e.activation(out=mg[:, :], in_=lrdtT[:, :], func=AF.Exp), reads=[lrdtT.buf()], writes=[mg.buf()])
    tt_op("dve", v16(angT), v16(angT), trow_b, ALU.mult, [angT.buf(), trow.buf()], [angT.buf()])
    ts_op("dve", ua[:, :], angT[:, :], 1.0 / (2 * PI), 1.5, ALU.mult, ALU.add, [angT.buf()], [ua.buf()])
    sin_from_u(ua, sn)
    ts_op("dve", ua[:, :], angT[:, :], 1.0 / (2 * PI), 1.75, ALU.mult, ALU.add, [angT.buf()], [ua.buf()])
    sin_from_u(ua, cs)
    tt_op("dve", WpTr[:, :], mg[:, :], cs[:, :], ALU.mult, [mg.buf(), cs.buf()], [WpTr.buf()])
    tt_op("dve", WpTi[:, :], mg[:, :], sn[:, :], ALU.mult, [mg.buf(), sn.buf()], [WpTi.buf()])
    S.free("lrdtT", "angT", "ua", "ki", "kf", "mg", "trow")
    if os.environ.get("S5_STOP") == "3":
        S.emit()
        return nc
    maskB = S.alloc("maskB", [4, 128], F32, 4)
    maskC = S.alloc("maskC", [4, 128], F32, 4)
    S.dma("sp", "c2", maskB[:, :, :], c_maskB, writes=[maskB.buf()])
    S.dma("sp", "c2", maskC[:, :, :], c_maskC, writes=[maskC.buf()])
    bnat = S.alloc("bnat", [2, 16, 16], F32, 4)
    S.dma("sp", "c2", bnat[:, 0, :, :], b_re.rearrange("(j p) h -> p j h", p=128), writes=[bnat.buf()])
    S.dma("sp", "c2", bnat[:, 1, :, :], b_im.rearrange("(j p) h -> p j h", p=128), writes=[bnat.buf()])
    bn8 = S.alloc("bn8", [2, 16, 8, 16], F32, 4)
    for ri in range(2):
        S.op("dve", lambda e, ri=ri: e.tensor_copy(out=bn8[:, ri, :, :, :], in_=bnat[:, ri, :, :].unsqueeze(2).to_broadcast([128, 16, 8, 16])),
             reads=[bnat.buf()], writes=[bn8.buf()])
    Bmr = S.alloc("Bmr", [4, 512], BF16, 2)
    Bmi = S.alloc("Bmi", [4, 512], BF16, 2)
    tq = S.alloc("tq", [4, 128], F32, 4)
    for j in range(16):
        ctile, jm = j // 4, j % 4
        bk = banks[j % 4]
        for ri in range(2):
            S.op("pe", lambda e, bk=bk, ri=ri, j=j: e.transpose(out=bk[:, ri * 128:(ri + 1) * 128],
                                                             in_=bn8[:, ri, j, :, :].rearrange("p c h -> p (c h)"), identity=idf[:, :]),
                 reads=[bn8.buf(), idf.buf()], writes=[bk.buf()])
        BTr, BTi = bk[:, 0:128], bk[:, 128:256]
        fr, fi = fre[:, j * 128:(j + 1) * 128], fim[:, j * 128:(j + 1) * 128]
        rd = [bk.buf(), fre.buf(), fim.buf()]
        tt_op("dve", tq[:, 0, :], BTr, fr, ALU.mult, rd, [tq.buf()])
        tt_op("dve", tq[:, 1, :], BTi, fi, ALU.mult, rd, [tq.buf()])
        tt_op("dve", tq[:, 2, :], BTr, fi, ALU.mult, rd, [tq.buf()])
        tt_op("dve", tq[:, 3, :], BTi, fr, ALU.mult, rd, [tq.buf()])
        tt_op("dve", tq[:, 0, :], tq[:, 0, :], tq[:, 1, :], ALU.subtract, [tq.buf()], [tq.buf()])
        tt_op("dve", tq[:, 2, :], tq[:, 2, :], tq[:, 3, :], ALU.add, [tq.buf()], [tq.buf()])
        tt_op("dve", Bmr[:, ctile, jm * 128:(jm + 1) * 128], tq[:, 0, :], maskB[:, jm, :], ALU.mult, [tq.buf(), maskB.buf()], [Bmr.buf()])
        tt_op("dve", Bmi[:, ctile, jm * 128:(jm + 1) * 128], tq[:, 2, :], maskB[:, jm, :], ALU.mult, [tq.buf(), maskB.buf()], [Bmi.buf()])
    S.free("bnat", "bn8", "fre", "fim")
    if os.environ.get("S5_STOP") == "4":
        S.emit()
        return nc
    cnat = S.alloc("cnat", [2, 4, 64], F32, 4)
    S.dma("sp", "c2", cnat[:, 0, :, :], c_re.rearrange("(ct p) s -> p ct s", p=128), writes=[cnat.buf()])
    S.dma("sp", "c2", cnat[:, 1, :, :], c_im.rearrange("(ct p) s -> p ct s", p=128), writes=[cnat.buf()])
    cn2 = S.alloc("cn2", [2, 4, 2, 64], F32, 4)
    for ri in range(2):
        S.op("dve", lambda e, ri=ri: e.tensor_copy(out=cn2[:, ri, :, :, :], in_=cnat[:, ri, :, :].unsqueeze(2).to_broadcast([128, 4, 2, 64])),
             reads=[cnat.buf()], writes=[cn2.buf()])
    Cmr = S.alloc("Cmr", [16, 128], BF16, 2)
    Cmi = S.alloc("Cmi", [16, 128], BF16, 2)
    for ctile in range(4):
        bk = banks[ctile % 4]
        for ri in range(2):
            S.op("pe", lambda e, bk=bk, ri=ri, ctile=ctile: e.transpose(out=bk[:, ri * 128:(ri + 1) * 128],
                                                                    in_=cn2[:, ri, ctile, :, :].rearrange("p c s -> p (c s)"), identity=idf[:, :]),
                 reads=[cn2.buf(), idf.buf()], writes=[bk.buf()])
        for jm in range(4):
            j = ctile * 4 + jm
            tt_op("dve", Cmr[:, j, :], bk[:, 0:128], maskC[:, jm, :], ALU.mult, [bk.buf(), maskC.buf()], [Cmr.buf()])
            S.op("dve", lambda e, bk=bk, j=j, jm=jm: e.scalar_tensor_tensor(out=Cmi[:, j, :], in0=bk[:, 128:256], scalar=-1.0, in1=maskC[:, jm, :],
                                                                         op0=ALU.mult, op1=ALU.mult),
                 reads=[bk.buf(), maskC.buf()], writes=[Cmi.buf()])
    S.free("cnat", "cn2", "maskB", "maskC", "tq", "sn", "cs")
    if os.environ.get("S5_STOP") == "5":
        S.emit()
        return nc
    dnat = S.alloc("dnat", [2, 128], F32, 4)
    S.dma("sp", "c2", dnat[0:4, 0, :], s5_d.rearrange("(ct p) -> ct p", p=128), writes=[dnat.buf()])
    S.dma("sp", "c2", dnat[0:4, 1, :], glu_b.rearrange("(ct p) -> ct p", p=128), writes=[dnat.buf()])
    dcol = S.alloc("dcol", [2, 4], F32, 4)
    bk = banks[0]
    for i in range(2):
        S.op("pe", lambda e, i=i, bk=bk: e.transpose(out=bk[:, i * 4:(i + 1) * 4], in_=dnat[0:4, i, :], identity=idf[0:4, 0:4]),
             reads=[dnat.buf(), idf.buf()], writes=[bk.buf()])
    copy_op("dve", dcol[:, 0, :], bk[:, 0:4], [bk.buf()], [dcol.buf()])
    copy_op("dve", dcol[:, 1, :], bk[:, 4:8], [bk.buf()], [dcol.buf()])
    S.free("dnat")
    if os.environ.get("S5_STOP") == "6":
        S.emit()
        return nc
    gw = wload("gw", glu_w, 4, 512, 512, "w_s5")

    if os.environ.get("S5_STOP") == "7":
        S.emit()
        return nc
    if str(3) in os.environ.get("BARRIERS", ""):
        S.barrier()
    zb = S.alloc("zb", [2, 2, W2], BF16, 2)
    tm = S.alloc("tm", [2, 4, 512], F32, 4)
    td = S.alloc("td", [2, 4, 512], F32, 4)
    wc = S.alloc("wc", [2, 2, 512], F32, 4)
    xbf = S.alloc("xbf", [2, 16, 128], BF16, 2)
    car = S.alloc("car", [16, 2], F32, 4)
    ypre = S.alloc("ypre", [2, 4, 512], F32, 4)
    S.op("pool", lambda e: e.memset(car[:, :, :], 0.0), writes=[car.buf(g) for g in range(4)])
    gl = S.alloc("gl", [4, 512], F32, 4)
    glb = S.alloc("glb", [4, 512], BF16, 2)
    g1 = S.alloc("g1", [2, 512], F32, 4)
    bR, bI = banks[0], banks[1]
    wbk = (banks[2], banks[3])
    ybk = banks[4]
    gbk = banks[5]
    mi = 0
    di = 0
    for c in range(int(os.environ.get("S5_CHUNKS", NTT))):
        zs = c % 2
        if os.environ.get("S5_PART") == "1" and c == 0:
            pass
        for ctile in range(4):
            mm(bR[:, :], uT[:, ctile, c * 128:(c + 1) * 128], Bmr[:, ctile, :], True, True, [uT.buf((ctile, c)), Bmr.buf()], [bR.buf()])
            mm(bI[:, :], uT[:, ctile, c * 128:(c + 1) * 128], Bmi[:, ctile, :], True, True, [uT.buf((ctile, c)), Bmi.buf()], [bI.buf()])
            ms = mi % 2; mi += 1
            blk = slice(ctile * 512, (ctile + 1) * 512)
            tt_op("dve", tm[:, ms, 0, :], bR[:, :], Wmr[:, blk], ALU.mult, [bR.buf(), Wmr.buf()], [tm.buf((ms, 0))])
            tt_op("dve", tm[:, ms, 1, :], bI[:, :], Wmi[:, blk], ALU.mult, [bI.buf(), Wmi.buf()], [tm.buf((ms, 1))])
            tt_op("dve", tm[:, ms, 2, :], bR[:, :], Wmi[:, blk], ALU.mult, [bR.buf(), Wmi.buf()], [tm.buf((ms, 2))])
            tt_op("dve", tm[:, ms, 3, :], bI[:, :], Wmr[:, blk], ALU.mult, [bI.buf(), Wmr.buf()], [tm.buf((ms, 3))])
            tt_op("pool", zb[:, zs, 0, blk], tm[:, ms, 0, :], tm[:, ms, 1, :], ALU.subtract, [tm.buf((ms, 0)), tm.buf((ms, 1))], [zb.buf((zs, 0, ctile))])
            tt_op("pool", zb[:, zs, 1, blk], tm[:, ms, 2, :], tm[:, ms, 3, :], ALU.add, [tm.buf((ms, 2)), tm.buf((ms, 3))], [zb.buf((zs, 1, ctile))])
        if os.environ.get("S5_PART") == "1":
            continue
        for g4 in range(4):
            WR, WI = (banks[2], banks[3]) if g4 % 2 == 0 else (banks[6], banks[7])
            for jj in range(4):
                j = 4 * g4 + jj
                mm(WR[:, jj * 128:(jj + 1) * 128], zb[:, zs, 0, j * 128:(j + 1) * 128], trib[:, :], True, True, [zb.buf((zs, 0, j // 4)), trib.buf()], [WR.buf()])
                mm(WI[:, jj * 128:(jj + 1) * 128], zb[:, zs, 1, j * 128:(j + 1) * 128], trib[:, :], True, True, [zb.buf((zs, 1, j // 4)), trib.buf()], [WI.buf()])
            ds = di % 2; di += 1
            for jj in range(4):
                j = 4 * g4 + jj
                S.op("act", lambda e, WR=WR, ds=ds, jj=jj, j=j: e.activation(out=wc[:, ds, 0, jj * 128:(jj + 1) * 128], in_=WR[:, jj * 128:(jj + 1) * 128],
                                                                          func=AF.Identity, bias=car[:, j, 0:1], scale=1.0),
                     reads=[WR.buf(), car.buf(g4)], writes=[wc.buf((ds, 0))])
                S.op("act", lambda e, WI=WI, ds=ds, jj=jj, j=j: e.activation(out=wc[:, ds, 1, jj * 128:(jj + 1) * 128], in_=WI[:, jj * 128:(jj + 1) * 128],
                                                                          func=AF.Identity, bias=car[:, j, 1:2], scale=1.0),
                     reads=[WI.buf(), car.buf(g4)], writes=[wc.buf((ds, 1))])
            gcols = slice(g4 * 512, (g4 + 1) * 512)
            pr, pi_ = WpTr[:, gcols], WpTi[:, gcols]
            wr_, wi_ = wc[:, ds, 0, :], wc[:, ds, 1, :]
            for k, (w_, p_, wk) in enumerate(((wr_, pr, 0), (wi_, pi_, 1), (wr_, pi_, 0), (wi_, pr, 1))):
                tt_op("dve", td[:, ds, k, :], w_, p_, ALU.mult, [wc.buf((ds, wk)), WpTr.buf(), WpTi.buf()], [td.buf((ds, k))])
            xr_out = xbf[:, 0, 4 * g4:4 * g4 + 4, :].rearrange("p j t -> p (j t)")
            xi_out = xbf[:, 1, 4 * g4:4 * g4 + 4, :].rearrange("p j t -> p (j t)")
            tt_op("dve", xr_out, td[:, ds, 0, :], td[:, ds, 1, :], ALU.subtract, [td.buf((ds, 0)), td.buf((ds, 1))], xbf.bufs([(0, 4 * g4 + i) for i in range(4)]))
            tt_op("dve", xi_out, td[:, ds, 2, :], td[:, ds, 3, :], ALU.add, [td.buf((ds, 2)), td.buf((ds, 3))], xbf.bufs([(1, 4 * g4 + i) for i in range(4)]))
            l127 = lambda k, ds=ds: td[:, ds, k, :].rearrange("p (j t) -> p j t", t=128)[:, :, 127]
            tt_op("dve", car[:, 4 * g4:4 * g4 + 4, 0], l127(0), l127(1), ALU.subtract, [td.buf((ds, 0)), td.buf((ds, 1))], [car.buf(g4)])
            tt_op("dve", car[:, 4 * g4:4 * g4 + 4, 1], l127(2), l127(3), ALU.add, [td.buf((ds, 2)), td.buf((ds, 3))], [car.buf(g4)])
        if os.environ.get("S5_PART") == "2":
            continue
        ys = (c // 4) % 2
        for ctile in range(4):
            ybk = banks[4 + ctile % 2]
            ya = ybk[:, 0:128]
            n = 0
            for jm in range(4):
                j = ctile * 4 + jm
                mm(ya, Cmr[:, j, :], xbf[:, 0, j, :], n == 0, False, [Cmr.buf(), xbf.buf((0, j))], [ybk.buf()]); n += 1
                mm(ya, Cmi[:, j, :], xbf[:, 1, j, :], False, jm == 3, [Cmi.buf(), xbf.buf((1, j))], [ybk.buf()]); n += 1
            S.op("dve", lambda e, ya=ya, ctile=ctile, ys=ys, c=c: e.scalar_tensor_tensor(
                out=ypre[:, ys, ctile, (c % 4) * 128:(c % 4 + 1) * 128], in0=uT[:, ctile, c * 128:(c + 1) * 128], scalar=dcol[:, 0, ctile:ctile + 1], in1=ya,
                op0=ALU.mult, op1=ALU.add),
                reads=[uT.buf((ctile, c)), dcol.buf(), ybk.buf()], writes=[ypre.buf((ys, ctile))])
        if c % 4 == 3:
            tb = c // 4
            for ctile in range(4):
                xx = ypre[:, ys, ctile, :]
                xb_ = ypre.buf((ys, ctile))
                tt_op("dve", g1[:, 0, :], xx, xx, ALU.mult, [xb_], [g1.buf(0)])
                ts_op("dve", g1[:, 0, :], g1[:, 0, :], 0.044715, 1.0, ALU.mult, ALU.add, [g1.buf(0)], [g1.buf(0)])
                tt_op("dve", g1[:, 0, :], g1[:, 0, :], xx, ALU.mult, [g1.buf(0), xb_], [g1.buf(0)])
                S.op("act", lambda e: e.activation(out=g1[:, 1, :], in_=g1[:, 0, :], func=AF.Sigmoid, scale=1.5957691216057308), reads=[g1.buf(0)], writes=[g1.buf(1)])
                tt_op("dve", gl[:, ctile, :], xx, g1[:, 1, :], ALU.mult, [xb_, g1.buf(1)], [gl.buf(ctile)])
                copy_op("dve", glb[:, ctile, :], gl[:, ctile, :], [gl.buf(ctile)], [glb.buf(ctile)])
            for cp in range(0 if os.environ.get("S5_G") != "1" else 4, 4):
                gbk = banks[4 + cp % 2]
                for ctile in range(4):
                    mm(gbk[:, :], gw[:, ctile, cp * 128:(cp + 1) * 128], glb[:, ctile, :], ctile == 0, ctile == 3, [gw.buf(0), glb.buf(ctile)], [gbk.buf()])
                if os.environ.get("S5_G") == "2":
                    continue
                S.op("act", lambda e, cp=cp, gbk=gbk: e.activation(out=g1[:, 0, :], in_=gbk[:, :], func=AF.Sigmoid, bias=dcol[:, 1, cp:cp + 1], scale=1.0),
                     reads=[gbk.buf(), dcol.buf()], writes=[g1.buf(0)])
                tt_op("dve", catT[:, cp, tb * 512:(tb + 1) * 512], gl[:, cp, :], g1[:, 0, :], ALU.mult, [gl.buf(cp), g1.buf(0)],
                      catT.bufs([(cp, 4 * tb + i) for i in range(4)]))
    S.free("Wmr", "Wmi", "WpTr", "WpTi", "Bmr", "Bmi", "Cmr", "Cmi", "dcol", "gw", "zb", "tm", "td", "wc", "xbf", "car", "ypre", "gl", "glb", "g1", "uT")
    if debug:
        S.dma("sp", "dbg", dbg["d_cat"], catT[:, :, :], reads=catT.bufs([(k, t) for k in range(8) for t in range(16)]), final=True)
    if upto == "C":
        S.emit()
        return nc


    if str(4) in os.environ.get("BARRIERS", ""):
        S.barrier()
    hs = S.alloc("hs", [16, D], F32, 4)
    hT = S.alloc("hT", [8, T], BF16, 2)
    wob = wload("wob", w_out, 8, D, 512, "w_e")
    lnp0 = load_lnp("lnp0", ln_in_g, ln_in_b)
    lnp1 = load_lnp("lnp1", ln1_g, ln1_b)
    lbt = S.alloc("lbt", [2, D], BF16, 2)
    bi = 0

    def resid_stats(tt, acc_banks):
        for nh in range(2):
            bk = acc_banks[nh]
            S.op("dve", lambda e, nh=nh, bk=bk: e.scalar_tensor_tensor(out=hs[:, tt, nh * 512:(nh + 1) * 512], in0=hs[:, tt, nh * 512:(nh + 1) * 512],
                                                                     scalar=ALPHA, in1=bk[:, :], op0=ALU.mult, op1=ALU.add),
                 reads=[hs.buf(tt), bk.buf()], writes=[hs.buf(tt)])
        return ln_stats(hs[:, tt, :], hs.buf(tt))

    def ln_finish_a(tt, sl, lnp):
        ln_apply(hs[:, tt, :], hs.buf(tt), lnp, None, None, sl)

    def ln_finish(tt, sl, lnp, do_T, split=False):
        if not split:
            ln_apply(hs[:, tt, :], hs.buf(tt), lnp, hs[:, tt, :], hs.buf(tt), sl)
        else:
            ln_apply_b(hs[:, tt, :], hs.buf(tt), lnp, hs[:, tt, :], hs.buf(tt))
        if do_T:
            s2 = tt % 2
            copy_op("act", lbt[:, s2, :], hs[:, tt, :], [hs.buf(tt)], [lbt.buf(s2)])
            transpose_to_hT(lbt[:, s2, :], lbt.buf(s2), hT, tt)

    def resid_ln(tt, acc_banks, lnp, do_T):
        sl = resid_stats(tt, acc_banks)
        ln_finish(tt, sl, lnp, do_T)

    sl_in = {}
    sl_1 = {}
    accs_of = {}
    for step in range(NTT + 6):
        if step >= 6:
            ln_finish(step - 6, None, lnp1, True, split=True)
        if step < NTT:
            tt = step
            S.dma("sp", "xin", hs[:, tt, :], x[tt * 128:(tt + 1) * 128, :], writes=[hs.buf(tt)])
            sl_in[tt] = ln_stats(hs[:, tt, :], hs.buf(tt))
        if 1 <= step <= NTT:
            tt = step - 1
            ln_apply(hs[:, tt, :], hs.buf(tt), lnp0, None, None, sl_in[tt])
        if 2 <= step <= NTT + 1:
            tt = step - 2
            ln_apply_b(hs[:, tt, :], hs.buf(tt), lnp0, hs[:, tt, :], hs.buf(tt))
            accs = []
            for nh in range(2):
                bk = banks[bi % 4]; bi += 1
                for kt in range(8):
                    mm(bk[:, :], catT[:, kt, tt * 128:(tt + 1) * 128], wob[:, kt, nh * 512:(nh + 1) * 512], kt == 0, kt == 7,
                       [catT.buf((kt, tt)), wob.buf(nh)], [bk.buf()])
                accs.append(bk)
            accs_of[tt] = accs
        if 3 <= step <= NTT + 2:
            tt = step - 3
            sl_1[tt] = resid_stats(tt, accs_of[tt])
        if 4 <= step <= NTT + 3:
            tt = step - 4
            ln_finish_a(tt, sl_1[tt], lnp1)
    S.free("catT", "wob", "lnp0", "lnp1")
    if debug:
        S.dma("sp", "dbg", dbg["d_h1"], hs[:, :, :], reads=hs.bufs(range(16)), final=True)
    if upto == "E":
        S.emit()
        return nc


    if str(5) in os.environ.get("BARRIERS", ""):
        S.barrier()
    wkv = wload("wkv", ca_wkv, 8, 2 * D, 512, "w_f")
    memf = S.alloc("memf", [2, D], F32, 4)
    memb = S.alloc("memb", [2, D], BF16, 2)
    memT = S.alloc("memT", [8, 256], BF16, 2)
    for mt in range(2):
        S.dma("sp", "mem", memf[:, mt, :], mem[mt * 128:(mt + 1) * 128, :], writes=[memf.buf(mt)])
        copy_op("act", memb[:, mt, :], memf[:, mt, :], [memf.buf(mt)], [memb.buf(mt)])
        ctr["pt"] += 1
        pt = ptb[ctr["pt"] % 2]
        for k in range(8):
            S.op("pe", lambda e, k=k, pt=pt, mt=mt: e.transpose(out=pt[:, k, :], in_=memb[:, mt, k * 128:(k + 1) * 128], identity=idb[:, :]),
                 reads=[memb.buf(mt), idb.buf()], writes=[pt.buf()])
        copy_op(evac_eng(), memT[:, :, mt * 128:(mt + 1) * 128], pt[:, :, :], [pt.buf()], [memT.buf()])
    kTca = S.alloc("kTca", [8, 256], BF16, 2)
    vca = S.alloc("vca", [2, D], BF16, 2)
    for ct in range(8):
        bk = banks[bi % 4]; bi += 1
        for kt in range(8):
            mm(bk[:, 0:256], wkv[:, kt, ct * 128:(ct + 1) * 128], memT[:, kt, :], kt == 0, kt == 7, [wkv.buf(ct // 4), memT.buf()], [bk.buf()])
        copy_op(evac_eng(), kTca[:, ct, :], bk[:, 0:256], [bk.buf()], [kTca.buf()])
    for mt in range(2):
        for nh in range(2):
            bk = banks[bi % 4]; bi += 1
            for kt in range(8):
                mm(bk[:, :], memT[:, kt, mt * 128:(mt + 1) * 128], wkv[:, kt, D + nh * 512:D + (nh + 1) * 512], kt == 0, kt == 7,
                   [wkv.buf(2 + nh), memT.buf()], [bk.buf()])
            copy_op(evac_eng(), vca[:, mt, nh * 512:(nh + 1) * 512], bk[:, :], [bk.buf()], [vca.buf()])
    S.free("wkv", "memf", "memb", "memT")
    wqb = wload("wqb", ca_wq, 8, D, 512, "w_f2")
    wo2 = wload("wo2", ca_wo, 8, D, 512, "w_f2")
    lnp2 = load_lnp("lnp2", ln2_g, ln2_b)
    qTc = S.alloc("qTc", [2, 8, 512], BF16, 2)
    PTc = S.alloc("PTc", [2, 2, 512], BF16, 2)
    oTc = S.alloc("oTc", [8, 512], BF16, 2)
    rcs = S.alloc("rcs", [2, 512], F32, 4)
    pendF = None
    pendF2 = None
    fst = {"bi": 0, "sc": 0}

    def nbank():
        fst["bi"] += 1
        return banks[fst["bi"] % 4]

    def F_Q(tb):
        qs = tb % 2
        tcols = slice(tb * 512, (tb + 1) * 512)
        hbufs = hT.bufs(range(4 * tb, 4 * tb + 4))
        for ct in range(8):
            bk = nbank()
            for kt in range(8):
                mm(bk[:, :], wqb[:, kt, ct * 128:(ct + 1) * 128], hT[:, kt, tcols], kt == 0, kt == 7, [wqb.buf(ct // 4)] + hbufs, [bk.buf()])
            copy_op(evac_eng(), qTc[:, qs, ct, :], bk[:, :], [bk.buf()], [qTc.buf((qs, ct))])

    def F_HS(tb, hd):
        qs = tb % 2
        ps = hd % 2
        for mt in range(2):
            fst["sc"] += 1
            bk = banks[4 + fst["sc"] % 4]
            for i in range(2):
                ct = 2 * hd + i
                mm(bk[:, :], kTca[:, ct, mt * 128:(mt + 1) * 128], qTc[:, qs, ct, :], i == 0, i == 1, [kTca.buf(), qTc.buf((qs, ct))], [bk.buf()])
            S.op("act", lambda e, bk=bk, ps=ps, mt=mt: e.activation(out=PTc[:, ps, mt, :], in_=bk[:, :], func=AF.Exp, scale=1.0 / 16.0),
                 reads=[bk.buf()], writes=[PTc.buf((ps, mt))])

    def F_HP(tb, hd):
        ps = hd % 2
        sb = nbank()
        for mt in range(2):
            mm(sb[:, :], onesb[:, :], PTc[:, ps, mt, :], mt == 0, mt == 1, [onesb.buf(), PTc.buf((ps, mt))], [sb.buf()])
        S.op("act", lambda e, sb=sb, ps=ps: e.activation(out=rcs[:, ps, :], in_=sb[:, :], func=AF.Ln), reads=[sb.buf()], writes=[rcs.buf(ps)])
        S.op("act", lambda e, ps=ps: e.activation(out=rcs[:, ps, :], in_=rcs[:, ps, :], func=AF.Exp, scale=-1.0), reads=[rcs.buf(ps)], writes=[rcs.buf(ps)])
        for dti in range(2):
            ct = 2 * hd + dti
            bk = nbank()
            for mt in range(2):
                mm(bk[:, :], vca[:, mt, ct * 128:(ct + 1) * 128], PTc[:, ps, mt, :], mt == 0, mt == 1, [vca.buf(), PTc.buf((ps, mt))], [bk.buf()])
            tt_op("dve", oTc[:, ct, :], bk[:, :], rcs[:, ps, :], ALU.mult, [bk.buf(), rcs.buf(ps)], [oTc.buf(ct)])

    def F_W(tb):
        nonlocal_state = None
        prev_acc = None
        for tl in range(5):
            tt = 4 * tb + tl
            if tl < 4:
                if pstate["p3"] is not None:
                    ln_finish(pstate["p3"][0], None, lnp2, True, split=True)
                    pstate["p3"] = None
                accs = []
                for nh in range(2):
                    bk = nbank()
                    for kt in range(8):
                        mm(bk[:, :], oTc[:, kt, tl * 128:(tl + 1) * 128], wo2[:, kt, nh * 512:(nh + 1) * 512], kt == 0, kt == 7,
                           [oTc.buf(kt), wo2.buf(nh)], [bk.buf()])
                    accs.append(bk)
            if prev_acc is not None:
                pt_, pa_ = prev_acc
                slF = resid_stats(pt_, pa_)
                if pstate["p1"] is not None:
                    ln_finish_a(pstate["p1"][0], pstate["p1"][1], lnp2)
                if pstate["p3"] is not None:
                    ln_finish(pstate["p3"][0], None, lnp2, True, split=True)
                pstate["p3"] = pstate["p2"]
                pstate["p2"] = pstate["p1"]
                pstate["p1"] = (pt_, slF)
            prev_acc = (tt, accs) if tl < 4 else None

    pstate = {"p1": None, "p2": None, "p3": None}
    F_Q(0)
    for tb in range(4):
        F_HS(tb, 0)
        for hd in range(4):
            if hd < 3:
                F_HS(tb, hd + 1)
            F_HP(tb, hd)
        if tb < 3:
            F_Q(tb + 1)
        F_W(tb)
    if pstate["p3"] is not None:
        ln_finish(pstate["p3"][0], None, lnp2, True, split=True)
    ln_finish_a(pstate["p1"][0], pstate["p1"][1], lnp2)
    if pstate["p2"] is not None:
        ln_finish(pstate["p2"][0], None, lnp2, True, split=True)
    ln_finish(pstate["p1"][0], None, lnp2, True, split=True)
    S.free("wqb", "wo2", "lnp2", "qTc", "PTc", "oTc", "rcs", "kTca", "vca", "lbt")
    if debug:
        S.dma("sp", "dbg", dbg["d_h2"], hs[:, :, :], reads=hs.bufs(range(16)), final=True)
    if upto == "F":
        S.emit()
        return nc


    if str(6) in os.environ.get("BARRIERS", ""):
        S.barrier()
    lnp3 = load_lnp("lnp3", ln3_g, ln3_b)
    gus = S.alloc("gus", [3, 8, 2, 128], BF16, 2)
    actT = S.alloc("actT", [11, T], BF16, 2)
    sg = S.alloc("sg", [2, 512], F32, 4)
    guv = w_gu.rearrange("(kt p) n -> p kt n", p=128)
    gi = 0
    pendG = None
    pendG2 = None

    def g_finish(t_):
        ln_apply_b(hs[:, t_, :], hs.buf(t_), lnp3, hs[:, t_, :], hs.buf(t_))
        S.dma("sp", "out", out[t_ * 128:(t_ + 1) * 128, :], hs[:, t_, :], reads=[hs.buf(t_)], final=True)

    for ps_ in range(2):
        wd = S.alloc("wd", [11, D], BF16, 2)
        dv = w_dn[ps_ * 11 * 128:(ps_ + 1) * 11 * 128, :].rearrange("(j p) n -> p j n", p=128)
        for jl in range(11):
            j = ps_ * 11 + jl
            gs = gi % 3; gi += 1
            S.dma("pool", f"w_gu{gs}", gus[:, gs, :, 0, :], guv[:, :, j * 128:(j + 1) * 128], writes=[gus.buf(gs)])
            S.dma("pool", f"w_gu{gs}", gus[:, gs, :, 1, :], guv[:, :, FFN_H + j * 128:FFN_H + (j + 1) * 128], writes=[gus.buf(gs)])
            if jl < 11:
                S.dma("pool", "w_dn", wd[:, jl, :], dv[:, jl, :], writes=[wd.buf(jl)])
            for tb in range(4):
                tcols = slice(tb * 512, (tb + 1) * 512)
                hbufs = hT.bufs(range(4 * tb, 4 * tb + 4))
                bg = banks[(bi % 2) * 2]; bu_ = banks[(bi % 2) * 2 + 1]; bi += 1
                for kt in range(8):
                    mm(bg[:, :], gus[:, gs, kt, 0, :], hT[:, kt, tcols], kt == 0, kt == 7, [gus.buf(gs)] + hbufs, [bg.buf()])
                for kt in range(8):
                    mm(bu_[:, :], gus[:, gs, kt, 1, :], hT[:, kt, tcols], kt == 0, kt == 7, [gus.buf(gs)] + hbufs, [bu_.buf()])
                s2 = bi % 2
                S.op("act", lambda e, bg=bg, s2=s2: e.activation(out=sg[:, s2, :], in_=bg[:, :], func=AF.Silu), reads=[bg.buf()], writes=[sg.buf(s2)])
                tt_op("dve", actT[:, jl, tcols], sg[:, s2, :], bu_[:, :], ALU.mult, [sg.buf(s2), bu_.buf()], actT.bufs([(jl, 4 * tb + i) for i in range(4)]))
        for tt in range(NTT):
            accs = []
            for nh in range(2):
                bk = banks[4 + nh + 2 * (tt % 2)]
                for jl in range(11):
                    mm(bk[:, :], actT[:, jl, tt * 128:(tt + 1) * 128], wd[:, jl, nh * 512:(nh + 1) * 512], jl == 0, jl == 10,
                       [actT.buf((jl, tt)), wd.buf(jl)], [bk.buf()])
                accs.append(bk)
            if ps_ == 0:
                for nh in range(2):
                    bk = accs[nh]
                    S.op("dve", lambda e, nh=nh, bk=bk, tt=tt: e.scalar_tensor_tensor(out=hs[:, tt, nh * 512:(nh + 1) * 512], in0=hs[:, tt, nh * 512:(nh + 1) * 512],
                                                                                 scalar=ALPHA, in1=bk[:, :], op0=ALU.mult, op1=ALU.add),
                         reads=[hs.buf(tt), bk.buf()], writes=[hs.buf(tt)])
            else:
                for nh in range(2):
                    bk = accs[nh]
                    tt_op("dve", hs[:, tt, nh * 512:(nh + 1) * 512], hs[:, tt, nh * 512:(nh + 1) * 512], bk[:, :], ALU.add, [hs.buf(tt), bk.buf()], [hs.buf(tt)])
                slG = ln_stats(hs[:, tt, :], hs.buf(tt))
                if pendG2 is not None:
                    g_finish(pendG2[0])
                if pendG is not None:
                    ln_apply(hs[:, pendG[0], :], hs.buf(pendG[0]), lnp3, None, None, pendG[1])
                pendG2 = pendG
                pendG = (tt, slG)
        if ps_ == 1:
            if pendG2 is not None:
                g_finish(pendG2[0])
            ln_apply(hs[:, pendG[0], :], hs.buf(pendG[0]), lnp3, None, None, pendG[1])
            g_finish(pendG[0])
        S.free("wd")
    S.emit()
    return nc


_CACHE = {}


def kernel(**inputs):
    consts = host_consts()
    shared = {}
    for k, v in inputs.items():
        if k in ("x", "mem"):
            continue
        a = np.ascontiguousarray(np.asarray(v, dtype=np.float32))
        if k == "rel_bias":
            shared[k] = a
        elif a.ndim >= 2 and a.shape[0] == 1:
            a = a[0]
            if k in ("s5_lambda_re", "s5_lambda_im", "s5_d"):
                a = a.reshape(-1)
            elif k in ("s5_b_re", "s5_b_im"):
                a = a.reshape(2048, 16)
            elif k in ("s5_c_re", "s5_c_im"):
                a = a.reshape(512, 64)
            shared[k] = np.ascontiguousarray(a)
        else:
            shared[k] = a
    shared.update(consts)
    xs = np.asarray(inputs["x"], dtype=np.float32)
    ms = np.asarray(inputs["mem"], dtype=np.float32)
    if "nc" not in _CACHE:
        _CACHE["nc"] = build_program(False)
    nc = _CACHE["nc"]
    in_maps = []
    for b in range(8):
        m = dict(shared)
        m["x"] = np.ascontiguousarray(xs[b])
        m["mem"] = np.ascontiguousarray(ms[b])
        in_maps.append(m)
    res = run_bass_kernel_spmd(nc, in_maps, core_ids=list(range(8)))
    return np.stack([np.asarray(r["out"], dtype=np.float32) for r in res.results], axis=0)
```

```python
import numpy as np
import concourse.bass as bass
import concourse.mybir as mybir
from concourse.bass_utils import run_bass_kernel_spmd

F32 = mybir.dt.float32
BF16 = mybir.dt.bfloat16
ALU = mybir.AluOpType
AF = mybir.ActivationFunctionType
AX = mybir.AxisListType

ENGS = ("pe", "act", "dve", "pool", "sp")


class Instr:
    __slots__ = ("eng", "fn", "waits", "signal", "idx", "val", "dma_sem", "dma_val")

    def __init__(self, eng, fn):
        self.eng = eng
        self.fn = fn
        self.waits = []
        self.signal = False
        self.idx = -1
        self.val = 0
        self.dma_sem = None
        self.dma_val = 0


class Buf:
    __slots__ = ("name", "last_w", "readers")

    def __init__(self, name, inherit=()):
        self.name = name
        self.last_w = None
        self.readers = list(inherit)


class Ten:
    def __init__(self, h, name, inherit=()):
        self.h = h
        self.name = name
        self.inherit = list(inherit)
        self._bufs = {}

    def __getitem__(self, k):
        return self.h[k]

    def buf(self, key=0):
        b = self._bufs.get(key)
        if b is None:
            b = Buf(f"{self.name}{key}", self.inherit)
            self._bufs[key] = b
        return b

    def bufs(self, keys):
        return [self.buf(k) for k in keys]

    def all_instrs(self):
        out = list(self.inherit)
        for b in self._bufs.values():
            if b.last_w is not None:
                out.append(b.last_w)
            out.extend(b.readers)
        return out


class Sched:
    def __init__(self, nc, sbuf_base=16512, sbuf_bytes=229312):
        self.nc = nc
        self.q = {e: [] for e in ENGS}
        self.waited = {e: {} for e in ENGS}
        self.dma_count = {}
        self.sbuf_bytes = sbuf_bytes
        self.sbuf_base = sbuf_base
        self.live = {}
        self.hist = []
        self.n_alloc = 0
        self.final_dma = []
        self.ring_pos = {}
        self.ring_last = {}

    def alloc(self, name, free_shape, dtype, nbytes_el):
        size = int(np.prod(free_shape)) * nbytes_el
        size = (size + 63) // 64 * 64
        segs = sorted((o, s) for (o, s, _) in self.live.values())
        off = self.sbuf_base
        for (o, s) in segs:
            if off + size <= o:
                break
            off = max(off, o + s)
        if off + size > self.sbuf_bytes:
            raise RuntimeError(f"SBUF arena overflow allocating {name} ({size} B); live={[(k, v[0], v[1]) for k, v in self.live.items()]}")
        inherit = []
        for (o, s, t) in self.hist:
            if o < off + size and off < o + s:
                inherit.extend(t.all_instrs())
        comp = {}
        for ins in inherit:
            key = ins.dma_sem if ins.dma_sem is not None else ins.eng
            cur = comp.get(key)
            if cur is None or (ins.dma_val if ins.dma_sem is not None else ins.idx) > (cur.dma_val if cur.dma_sem is not None else cur.idx):
                comp[key] = ins
        self.n_alloc += 1
        h = self.nc.alloc_sbuf_tensor_at(f"{name}_{self.n_alloc}", [128] + list(free_shape), dtype, offset=off)
        t = Ten(h, name, list(comp.values()))
        self.live[name] = (off, size, t)
        return t

    def free(self, *names):
        for name in names:
            o, s, t = self.live.pop(name)
            self.hist.append((o, s, t))

    def _need(self, c, p, raw):
        if p is None or p is c:
            return
        E = c.eng
        if p.dma_sem is not None:
            key = "dma:" + p.dma_sem
            if self.waited[E].get(key, 0) >= p.dma_val:
                return
            self.waited[E][key] = p.dma_val
            c.waits.append(p)
            return
        if p.eng == E and c.dma_sem is None:
            if E in ("pe", "sp"):
                return
            if not raw:
                return
        key = p.eng
        if self.waited[E].get(key, -1) >= p.idx:
            return
        self.waited[E][key] = p.idx
        p.signal = True
        c.waits.append(p)

    def _deps(self, c, reads, writes):
        for b in reads:
            self._need(c, b.last_w, True)
        for b in writes:
            self._need(c, b.last_w, False)
            for r in b.readers:
                self._need(c, r, False)
        for b in reads:
            b.readers.append(c)
            if len(b.readers) > 12:
                comp = {}
                for ins in b.readers:
                    key = ins.dma_sem if ins.dma_sem is not None else ins.eng
                    cur = comp.get(key)
                    if cur is None or (ins.dma_val if ins.dma_sem is not None else ins.idx) >= (cur.dma_val if cur.dma_sem is not None else cur.idx):
                        comp[key] = ins
                b.readers = list(comp.values())
        for b in writes:
            b.last_w = c
            b.readers = []

    def op(self, eng, fn, reads=(), writes=()):
        c = Instr(eng, fn)
        c.idx = len(self.q[eng])
        self.q[eng].append(c)
        self._deps(c, reads, writes)
        return c

    NRING = 28

    def dma(self, eng, stream, out, in_, reads=(), writes=(), final=False):
        def fn(e, out=out, in_=in_):
            return e.dma_start(out=out, in_=in_)
        c = Instr(eng, fn)
        c.idx = len(self.q[eng])
        pos = self.ring_pos.get(eng, 0)
        self.ring_pos[eng] = pos + 1
        sem = f"{eng}{pos % self.NRING}"
        n = self.dma_count.get(sem, 0) + 1
        self.dma_count[sem] = n
        c.dma_sem = sem
        c.dma_val = 16 * n
        prev = self.ring_last.get(sem)
        if prev is not None:
            self._need(c, prev, False)
        self.ring_last[sem] = c
        self.q[eng].append(c)
        self._deps(c, reads, writes)
        if final:
            self.final_dma.append(c)
        return c

    def barrier(self):
        lasts = {}
        for e in ENGS:
            for ins in reversed(self.q[e]):
                if ins.dma_sem is None:
                    lasts[e] = ins
                    break
        dmas = []
        for sname, n in self.dma_count.items():
            f = Instr("sp", None)
            f.dma_sem = sname
            f.dma_val = 16 * n
            dmas.append(f)
        for e in ENGS:
            c = Instr(e, lambda eng: eng.nop())
            c.idx = len(self.q[e])
            for e2, p in lasts.items():
                if e2 != e:
                    self._need(c, p, True)
            for f in dmas:
                self._need(c, f, True)
            self.q[e].append(c)

    def emit(self):
        nc = self.nc
        import contextlib
        with contextlib.ExitStack() as st:
            esem = {e: st.enter_context(nc.semaphore(f"s_{e}")) for e in ENGS}
            dsem = {s: st.enter_context(nc.semaphore(f"d_{s}")) for s in self.dma_count}
            for e in ENGS:
                cnt = 0
                for ins in self.q[e]:
                    if ins.signal:
                        cnt += 1
                        ins.val = cnt
            block = st.enter_context(nc.Block())

            def run(eng_name, eobj):
                for ins in self.q[eng_name]:
                    for p in ins.waits:
                        if p.dma_sem is not None:
                            eobj.wait_ge(dsem[p.dma_sem], p.dma_val)
                        else:
                            eobj.wait_ge(esem[p.eng], p.val)
                    r = ins.fn(eobj)
                    if ins.dma_sem is not None:
                        r.then_inc(dsem[ins.dma_sem], 16)
                    elif ins.signal:
                        r.then_inc(esem[eng_name], 1)
                if eng_name == "sp":
                    done = {}
                    for ins in self.final_dma:
                        done[ins.dma_sem] = max(done.get(ins.dma_sem, 0), ins.dma_val)
                    for s, v in done.items():
                        eobj.wait_ge(dsem[s], v)

            @block.tensor
            def _(e):
                run("pe", e)

            @block.scalar
            def _(e):
                run("act", e)

            @block.vector
            def _(e):
                run("dve", e)

            @block.gpsimd
            def _(e):
                run("pool", e)

            @block.sync
            def _(e):
                run("sp", e)


T = 2048
D = 1024
NTT = 16
ALPHA = float(2.0 ** 0.25)
PI = float(np.pi)
LAMBDA_INIT = 0.8 - 0.6 * 1.0
FFN_H = 2816
NJ = 22


def t5_bucket_np(d):
    d = np.asarray(d, dtype=np.int64)
    df = np.maximum(d, 1).astype(np.float32)
    large = 16 + (np.log(df / np.float32(16)) / np.float32(np.log(128 / 16)) * np.float32(16)).astype(np.int32)
    large = np.minimum(large, 31)
    return np.where(d < 16, d, large)


def host_consts():
    c = {}
    c["c_ident"] = np.eye(128, dtype=np.float32)
    c["c_tri"] = np.triu(np.ones((128, 128), dtype=np.float32))
    cst = np.zeros((128, 8), dtype=np.float32)
    tp1 = np.arange(1, 129, dtype=np.float32)
    cst[:, 0] = tp1
    cst[:, 1] = -tp1
    cst[:, 2] = tp1 / np.float32(2 * np.pi)
    cst[:, 3] = 1.5
    cst[:, 4] = 1.75
    cst[:, 5] = -np.pi
    cst[:, 6] = 1e-5
    cst[:, 7] = 1.0
    c["c_cst"] = cst
    c["c_trow"] = np.broadcast_to(tp1[None, :], (128, 128)).astype(np.float32).copy()
    p = np.arange(128)
    mask = np.zeros((128, 4, 128), dtype=np.float32)
    for jm in range(4):
        mask[:, jm, :] = ((p[None, :] // 16) == (2 * jm + p[:, None] // 64)).astype(np.float32)
    c["c_maskC"] = mask
    c["c_maskB"] = np.ascontiguousarray(mask.transpose(2, 1, 0))
    oh = np.zeros((33, 384), dtype=np.float32)
    for m in range(384):
        d = m - 127
        if d < 0:
            oh[32, m] = -30000.0
        else:
            b = int(t5_bucket_np(d))
            oh[b, m] += 8.0
            oh[31, m] -= 8.0
    c["c_oh"] = oh
    return c


def build_program(debug=False, upto=None):
    import os
    nc = bass.Bass("TRN2", target_bir_lowering=False)
    S = Sched(nc)

    def din(name, shape):
        return nc.dram_tensor(name, list(shape), F32, kind="ExternalInput").ap()

    x = din("x", [T, D]); mem = din("mem", [256, D])
    ln_in_g = din("ln_in_g", [D]); ln_in_b = din("ln_in_b", [D])
    w_in = din("w_in", [D, 2048])
    lam_re = din("s5_lambda_re", [2048]); lam_im = din("s5_lambda_im", [2048]); log_dt = din("s5_log_dt", [32])
    b_re = din("s5_b_re", [2048, 16]); b_im = din("s5_b_im", [2048, 16])
    c_re = din("s5_c_re", [512, 64]); c_im = din("s5_c_im", [512, 64])
    s5_d = din("s5_d", [512]); glu_w = din("s5_glu_w", [512, 512]); glu_b = din("s5_glu_b", [512])
    lq1 = din("diff_lq1", [64]); lk1 = din("diff_lk1", [64]); lq2 = din("diff_lq2", [64]); lk2 = din("diff_lk2", [64])
    subln_g = din("diff_subln_g", [128]); rel_bias = din("rel_bias", [32, 4])
    w_out = din("w_out", [D, D]); ln1_g = din("ln1_g", [D]); ln1_b = din("ln1_b", [D])
    ca_wq = din("ca_wq", [D, D]); ca_wkv = din("ca_wkv", [D, 2 * D]); ca_wo = din("ca_wo", [D, D])
    ln2_g = din("ln2_g", [D]); ln2_b = din("ln2_b", [D])
    w_gu = din("ffn_w_gate_up", [D, 2 * FFN_H]); w_dn = din("ffn_w_down", [FFN_H, D])
    ln3_g = din("ln3_g", [D]); ln3_b = din("ln3_b", [D])
    c_ident = din("c_ident", [128, 128]); c_tri = din("c_tri", [128, 128]); c_cst = din("c_cst", [128, 8])
    c_trow = din("c_trow", [128, 128]); c_maskC = din("c_maskC", [128, 4, 128]); c_maskB = din("c_maskB", [128, 4, 128])
    c_oh = din("c_oh", [33, 384])
    out = nc.dram_tensor("out", [T, D], F32, kind="ExternalOutput").ap()
    scr = nc.dram_tensor("bias_scr", [128, 4, 384], F32, kind="Internal").ap()
    dbg = {}
    if debug:
        for nm, shp, dt_ in (("d_hT", [128, 8, T], BF16), ("d_uT", [128, 4, T], BF16), ("d_qT", [128, 4, T], BF16),
                             ("d_kT", [128, 4, T], BF16), ("d_v", [128, 16, 4, 129], BF16), ("d_cat", [128, 8, T], BF16),
                             ("d_h1", [128, 16, D], F32), ("d_h2", [128, 16, D], F32)):
            dbg[nm] = nc.dram_tensor(nm, shp, dt_, kind="ExternalOutput").ap()

    banks = [Ten(nc.alloc_psum_tensor(f"bank{i}", [128, 512], F32), f"bank{i}") for i in range(8)]

    class View:
        def __init__(self, ten):
            self.ten = ten
            self.ap = ten[:, :].bitcast(BF16).rearrange("p (k c) -> p k c", k=8)

        def __getitem__(self, k):
            return self.ap[k]

        def buf(self, key=0):
            return self.ten.buf(key)

    ptb = [View(banks[6]), View(banks[7])]

    idf = S.alloc("idf", [128], F32, 4)
    idb = S.alloc("idb", [128], BF16, 2)
    trib = S.alloc("trib", [128], BF16, 2)
    trif = S.alloc("trif", [128], F32, 4)
    onesb = S.alloc("onesb", [128], BF16, 2)
    onesf = S.alloc("onesf", [128], F32, 4)
    cst = S.alloc("cst", [8], F32, 4)
    stt = S.alloc("ln_st", [8, 12], F32, 4)
    mvt = S.alloc("ln_mv", [8, 2], F32, 4)
    rsd = S.alloc("ln_rs", [8, 1], F32, 4)
    S.dma("sp", "c0", idf[:, :], c_ident, writes=[idf.buf()])
    S.dma("sp", "c0", trif[:, :], c_tri, writes=[trif.buf()])
    S.dma("sp", "c0", cst[:, :], c_cst, writes=[cst.buf()])
    S.op("dve", lambda e: e.tensor_copy(out=idb[:, :], in_=idf[:, :]), reads=[idf.buf()], writes=[idb.buf()])
    S.op("dve", lambda e: e.tensor_copy(out=trib[:, :], in_=trif[:, :]), reads=[trif.buf()], writes=[trib.buf()])
    S.op("pool", lambda e: e.memset(onesb[:, :], 1.0), writes=[onesb.buf()])
    S.op("pool", lambda e: e.memset(onesf[:, :], 1.0), writes=[onesf.buf()])
    EPS = cst[:, 6:7]

    ctr = {"ev": 0, "ln": 0, "pt": 0}

    def evac_eng():
        ctr["ev"] += 1
        return "act" if ctr["ev"] % 2 else "dve"

    def copy_op(eng, out_ap, in_ap, reads, writes):
        if eng == "act":
            S.op("act", lambda e: e.activation(out=out_ap, in_=in_ap, func=AF.Copy), reads, writes)
        else:
            S.op(eng, lambda e: e.tensor_copy(out=out_ap, in_=in_ap), reads, writes)

    def mm(out_ap, lhsT, rhs, start, stop, reads, writes):
        S.op("pe", lambda e: e.matmul(out=out_ap, lhsT=lhsT, rhs=rhs, start=start, stop=stop), reads, writes)

    def load_lnp(name, g, b):
        t = S.alloc(name, [2, D], F32, 4)
        S.dma("sp", "lnp", t[:, 0, :], g.partition_broadcast(128), writes=[t.buf()])
        S.dma("sp", "lnp", t[:, 1, :], b.partition_broadcast(128), writes=[t.buf()])
        return t

    def ln_stats(x_ap, x_buf):
        ctr["ln"] += 1
        sl = ctr["ln"] % 8
        for c in range(2):
            S.op("dve", lambda e, c=c: e.bn_stats(out=stt[:, sl, c * 6:(c + 1) * 6], in_=x_ap[:, c * 512:(c + 1) * 512]),
                 reads=[x_buf], writes=[stt.buf(sl)])
        S.op("dve", lambda e: e.bn_aggr(out=mvt[:, sl, :], in_=stt[:, sl, :]), reads=[stt.buf(sl)], writes=[mvt.buf(sl)])
        S.op("act", lambda e: e.activation(out=rsd[:, sl, :], in_=mvt[:, sl, 1:2], func=AF.Sqrt, bias=EPS, scale=1.0),
             reads=[mvt.buf(sl), cst.buf()], writes=[rsd.buf(sl)])
        S.op("dve", lambda e: e.reciprocal(out=rsd[:, sl, :], in_=rsd[:, sl, :]), reads=[rsd.buf(sl)], writes=[rsd.buf(sl)])
        return sl

    def ln_apply(x_ap, x_buf, lnp, out_ap, out_buf, sl):
        S.op("dve", lambda e: e.scalar_tensor_tensor(out=mvt[:, sl, 1:2], in0=mvt[:, sl, 0:1], scalar=-1.0, in1=rsd[:, sl, :],
                                                      op0=ALU.mult, op1=ALU.mult),
             reads=[mvt.buf(sl), rsd.buf(sl)], writes=[mvt.buf(sl)])
        S.op("act", lambda e: e.activation(out=x_ap, in_=x_ap, func=AF.Identity, bias=mvt[:, sl, 1:2], scale=rsd[:, sl, 0:1]),
             reads=[x_buf, mvt.buf(sl), rsd.buf(sl)], writes=[x_buf])
        S.op("pool", lambda e: e.tensor_tensor(out=x_ap, in0=x_ap, in1=lnp[:, 0, :], op=ALU.mult), reads=[x_buf, lnp.buf()], writes=[x_buf])
        if out_ap is not None:
            ln_apply_b(x_ap, x_buf, lnp, out_ap, out_buf)

    def ln_apply_b(x_ap, x_buf, lnp, out_ap, out_buf):
        S.op("dve", lambda e: e.tensor_tensor(out=out_ap, in0=x_ap, in1=lnp[:, 1, :], op=ALU.add), reads=[x_buf, lnp.buf()],
             writes=[out_buf] if out_buf is not x_buf else [x_buf])

    def transpose_to_hT(lb, lb_buf, hT, tt):
        ctr["pt"] += 1
        pt = ptb[ctr["pt"] % 2]
        for k in range(8):
            S.op("pe", lambda e, k=k: e.transpose(out=pt[:, k, :], in_=lb[:, k * 128:(k + 1) * 128], identity=idb[:, :]),
                 reads=[lb_buf, idb.buf()], writes=[pt.buf()])
        copy_op(evac_eng(), hT[:, :, tt * 128:(tt + 1) * 128], pt[:, :, :], [pt.buf()], [hT.buf(tt)])

    def wload(name, src, kt, ncols, chunk, stream):
        t = S.alloc(name, [kt, ncols], BF16, 2)
        sv = src.rearrange("(kt p) n -> p kt n", p=128)
        for c in range(ncols // chunk):
            S.dma("pool", stream, t[:, :, c * chunk:(c + 1) * chunk], sv[:, :, c * chunk:(c + 1) * chunk], writes=[t.buf(c)])
        return t

    hT = S.alloc("hT", [8, T], BF16, 2)
    wi = wload("wi", w_in, 8, 2048, 512, "w_a")
    lnp0 = load_lnp("lnp0", ln_in_g, ln_in_b)
    xin = S.alloc("xin", [4, D], F32, 4)
    lbt = S.alloc("lbt", [2, D], BF16, 2)
    sls = {}
    for tt in range(NTT + 3):
        if tt < NTT:
            s3 = tt % 4
            S.dma("sp", "xin", xin[:, s3, :], x[tt * 128:(tt + 1) * 128, :], writes=[xin.buf(s3)])
            sls[tt] = ln_stats(xin[:, s3, :], xin.buf(s3))
        if 1 <= tt <= NTT:
            t1 = tt - 1
            ln_apply(xin[:, t1 % 4, :], xin.buf(t1 % 4), lnp0, None, None, sls[t1])
        if 2 <= tt <= NTT + 1:
            t2 = tt - 2
            ln_apply_b(xin[:, t2 % 4, :], xin.buf(t2 % 4), lnp0, lbt[:, t2 % 2, :], lbt.buf(t2 % 2))
        if tt >= 3:
            t3 = tt - 3
            transpose_to_hT(lbt[:, t3 % 2, :], lbt.buf(t3 % 2), hT, t3)
    S.free("xin", "lnp0")
    if debug:
        S.dma("sp", "dbg", dbg["d_hT"], hT[:, :, :], reads=hT.bufs(range(16)), final=True)
    if upto == "A":
        S.emit()
        return nc


    if str(0) in os.environ.get("BARRIERS", ""):
        S.barrier()
    rb = S.alloc("rb", [4], F32, 4)
    chc = S.alloc("chc", [4], F32, 4)
    ohs = S.alloc("ohs", [384], F32, 4)
    Lh = S.alloc("Lh", [4, 128], F32, 4)
    gsb = S.alloc("gsb", [4, 384], F32, 4)
    biasf = S.alloc("biasf", [4, 2, 128], F32, 4)
    biasb = S.alloc("biasb", [4, 2, 128], BF16, 2)
    lqk = S.alloc("lqk", [4, 64], F32, 4)
    lsm = S.alloc("lsm", [4], F32, 4)
    gsub = S.alloc("gsub", [128], F32, 4)
    scrb = Ten(None, "scr")
    NEGLAM = lsm[:, 2:3]

    def dsetup_1():
        S.dma("sp", "c1", rb[0:32, :], rel_bias, writes=[rb.buf()])
        S.dma("sp", "c1", chc[:, :], rel_bias[31, :].partition_broadcast(128), writes=[chc.buf()])
        S.dma("sp", "c1", ohs[0:33, :], c_oh, writes=[ohs.buf()])
        for i, v in enumerate((lq1, lk1, lq2, lk2)):
            S.dma("sp", "c1", lqk[:, i, :], v.partition_broadcast(128), writes=[lqk.buf()])
        S.dma("sp", "c1", gsub[:, :], subln_g.partition_broadcast(128), writes=[gsub.buf()])
        S.op("pool", lambda e: e.memset(Lh[0:33, :, :], 1.0), writes=[Lh.buf()])

    def dsetup_2a():
        for h in range(4):
            S.op("dve", lambda e, h=h: e.tensor_scalar(out=Lh[0:32, h, :], in0=onesf[0:32, :], scalar1=rb[0:32, h:h + 1], scalar2=None, op0=ALU.mult),
                 reads=[onesf.buf(), rb.buf(), Lh.buf()], writes=[Lh.buf()])
        S.op("dve", lambda e: e.tensor_mul(out=lqk[:, 0, :], in0=lqk[:, 0, :], in1=lqk[:, 1, :]), reads=[lqk.buf()], writes=[lqk.buf()])
        S.op("dve", lambda e: e.tensor_mul(out=lqk[:, 2, :], in0=lqk[:, 2, :], in1=lqk[:, 3, :]), reads=[lqk.buf()], writes=[lqk.buf()])
        S.op("dve", lambda e: e.reduce_sum(out=lsm[:, 0:1], in_=lqk[:, 0, :], axis=AX.X), reads=[lqk.buf()], writes=[lsm.buf()])
        S.op("dve", lambda e: e.reduce_sum(out=lsm[:, 1:2], in_=lqk[:, 2, :], axis=AX.X), reads=[lqk.buf()], writes=[lsm.buf()])
        S.op("dve", lambda e: e.tensor_scalar(out=gsub[:, :], in0=gsub[:, :], scalar1=1.0 - LAMBDA_INIT, scalar2=None, op0=ALU.mult),
             reads=[gsub.buf()], writes=[gsub.buf()])

    def dsetup_2b():
        S.op("act", lambda e: e.activation(out=lsm[:, 0:2], in_=lsm[:, 0:2], func=AF.Exp), reads=[lsm.buf()], writes=[lsm.buf()])
        for h in range(4):
            bk = banks[4 + h % 2]
            mm(bk[:, 0:384], Lh[0:33, h, :], ohs[0:33, :], True, True, [Lh.buf(), ohs.buf()], [bk.buf()])
            copy_op("act", gsb[:, h, :], bk[:, 0:384], [bk.buf()], [gsb.buf()])
        S.dma("sp", "scr", scr, gsb[:, :, :], reads=[gsb.buf()], writes=[scrb.buf()])

    def dsetup_3():
        S.op("dve", lambda e: e.tensor_sub(out=lsm[:, 2:3], in0=lsm[:, 1:2], in1=lsm[:, 0:1]), reads=[lsm.buf()], writes=[lsm.buf()])
        S.op("dve", lambda e: e.tensor_scalar(out=lsm[:, 2:3], in0=lsm[:, 2:3], scalar1=-LAMBDA_INIT, scalar2=None, op0=ALU.add),
             reads=[lsm.buf()], writes=[lsm.buf()])
        for h in range(4):
            for dsub, off in enumerate((127, 255)):
                S.dma("sp", "scr2", biasf[:, h, dsub, :], bass.AP(scr.tensor, h * 384 + off, [[1535, 128], [1, 128]]),
                      reads=[scrb.buf()], writes=[biasf.buf()])

    def dsetup_4():
        S.op("dve", lambda e: e.tensor_copy(out=biasb[:, :, :, :], in_=biasf[:, :, :, :]), reads=[biasf.buf()], writes=[biasb.buf()])

    dsetup_hooks = {0: dsetup_1, 6: dsetup_2a, 12: dsetup_2b, 28: dsetup_3, 44: dsetup_4}

    uT = S.alloc("uT", [4, T], BF16, 2)
    qT = S.alloc("qT", [4, T], BF16, 2)
    kT = S.alloc("kT", [4, T], BF16, 2)
    vaug = S.alloc("vaug", [16, 4, 129], BF16, 2)
    import os
    for tt in range(NTT):
        S.op("dve", lambda e, tt=tt: e.tensor_copy(out=vaug[:, tt, :, 128:129], in_=onesb[:, 0:4].unsqueeze(2)), reads=[onesb.buf()], writes=[vaug.buf(tt)])
    bi = 0
    for grp, dst in enumerate((uT, qT, kT)):
        for ct in range(4):
            col = grp * 512 + ct * 128
            for tb in range(4):
                if bi in dsetup_hooks:
                    dsetup_hooks[bi]()
                bk = banks[bi % 4]; bi += 1
                for kt in range(8):
                    mm(bk[:, :], wi[:, kt, col:col + 128], hT[:, kt, tb * 512:(tb + 1) * 512], kt == 0, kt == 7,
                       [wi.buf(grp)] + hT.bufs(range(4 * tb, 4 * tb + 4)), [bk.buf()])
                copy_op(evac_eng(), dst[:, ct, tb * 512:(tb + 1) * 512], bk[:, :], [bk.buf()], dst.bufs([(ct, 4 * tb + i) for i in range(4)]))
    for tt in range(0 if not os.environ.get("NO_V") else NTT, NTT):
        bk = banks[bi % 4]; bi += 1
        for kt in range(8):
            mm(bk[:, :], hT[:, kt, tt * 128:(tt + 1) * 128], wi[:, kt, 1536:2048], kt == 0, kt == 7,
               [wi.buf(3), hT.buf(tt)], [bk.buf()])
        copy_op(evac_eng(), vaug[:, tt, :, 0:128], bk[:, :].rearrange("p (h d) -> p h d", h=4), [bk.buf()], [vaug.buf(tt)])
    S.free("wi", "lbt", "hT")
    if debug:
        S.dma("sp", "dbg", dbg["d_uT"], uT[:, :, :], reads=uT.bufs([(c, t) for c in range(4) for t in range(16)]), final=True)
        S.dma("sp", "dbg", dbg["d_qT"], qT[:, :, :], reads=qT.bufs([(c, t) for c in range(4) for t in range(16)]), final=True)
        S.dma("sp", "dbg", dbg["d_kT"], kT[:, :, :], reads=kT.bufs([(c, t) for c in range(4) for t in range(16)]), final=True)
        S.dma("sp", "dbg", dbg["d_v"], vaug[:, :, :, :], reads=vaug.bufs(range(16)), final=True)
    if upto == "B":
        S.emit()
        return nc


    catT = S.alloc("catT", [8, T], BF16, 2)

    if str(1) in os.environ.get("BARRIERS", ""):
        S.barrier()
    PT = S.alloc("PT", [6, 512], BF16, 2)
    gcol = S.alloc("gcol", [1], F32, 4)
    S.dma("sp", "c1", gcol[:, :], subln_g.rearrange("(p o) -> p o", o=1), writes=[gcol.buf()])
    S.op("dve", lambda e: e.tensor_scalar(out=gcol[:, :], in0=gcol[:, :], scalar1=1.0 - LAMBDA_INIT, scalar2=None, op0=ALU.mult),
         reads=[gcol.buf()], writes=[gcol.buf()])
    rr = S.alloc("rr", [2, 2, 512], F32, 4)
    o1 = S.alloc("o1", [2, 512], F32, 4)
    oo = S.alloc("oo", [2, 512], F32, 4)
    sqb = S.alloc("sqb", [2, 512], BF16, 2)
    rst = S.alloc("rst", [2, 512], F32, 4)
    stb = (banks[0], banks[1])
    Ab = (banks[2], banks[3])
    Sb = (banks[4], banks[5])
    MSb = banks[6]

    def s_stage(it):
        h, I, s, j, st, pt = it
        r0 = s * 64
        qstart = max(512 * I, 128 * j)
        N = 512 * (I + 1) - qstart
        col0 = qstart - 512 * I
        has_diag = j >= 4 * I
        has_sub = (4 * I - 1) <= j <= (4 * I + 2)
        mm(st[:, col0:col0 + N], kT[r0:r0 + 64, h, j * 128:(j + 1) * 128], qT[r0:r0 + 64, h, qstart:qstart + N],
           True, not (has_diag or has_sub),
           [kT.buf((h, j))] + qT.bufs([(h, t) for t in range(qstart // 128, 4 * I + 4)]), [st.buf()])
        if has_diag:
            c = j * 128 - 512 * I
            mm(st[:, c:c + 128], idb[:, :], biasb[:, h, 0, :], False, not has_sub, [idb.buf(), biasb.buf()], [st.buf()])
        if has_sub:
            c = (j + 1) * 128 - 512 * I
            mm(st[:, c:c + 128], idb[:, :], biasb[:, h, 1, :], False, True, [idb.buf(), biasb.buf()], [st.buf()])
        S.op("act", lambda e, st=st, pt=pt, col0=col0, N=N, h=h: e.activation(
            out=PT[:, pt, col0:col0 + N], in_=st[:, col0:col0 + N], func=AF.Exp, bias=chc[:, h:h + 1], scale=0.125),
            reads=[st.buf(), chc.buf()], writes=[PT.buf(pt)])

    def pv_stage(it):
        h, I, s, j, st, pt = it
        qstart = max(512 * I, 128 * j)
        N = 512 * (I + 1) - qstart
        col0 = qstart - 512 * I
        last = (j == 4 * I + 3)
        S.op("pe", lambda e, s=s, pt=pt, col0=col0, N=N, j=j, h=h, last=last: e.matmul(
            out=Ab[s][:, col0:col0 + N], lhsT=vaug[:, j, h, 0:128], rhs=PT[:, pt, col0:col0 + N], start=(j == 0), stop=last, skip_group_check=True),
            [PT.buf(pt), vaug.buf(j)], [Ab[s].buf()])
        S.op("pe", lambda e, s=s, pt=pt, col0=col0, N=N, j=j, last=last: e.matmul(
            out=Sb[s][:, col0:col0 + N], lhsT=onesb[:, :], rhs=PT[:, pt, col0:col0 + N], start=(j == 0), stop=last, skip_group_check=True),
            [PT.buf(pt), onesb.buf()], [Sb[s].buf()])

    def epilogue_a(h, I, rnd):
        r2 = rnd % 2
        for s in range(2):
            S.op("act", lambda e, s=s, r2=r2: e.activation(out=rr[:, r2, s, :], in_=Sb[s][:, :], func=AF.Ln), reads=[Sb[s].buf()], writes=[rr.buf((r2, s))])
            S.op("act", lambda e, s=s, r2=r2: e.activation(out=rr[:, r2, s, :], in_=rr[:, r2, s, :], func=AF.Exp, scale=-1.0), reads=[rr.buf((r2, s))], writes=[rr.buf((r2, s))])
        tt_op("dve", o1[:, r2, :], Ab[0][:, :], rr[:, r2, 0, :], ALU.mult, [Ab[0].buf(), rr.buf((r2, 0))], [o1.buf(r2)])
        tt_op("dve", oo[:, r2, :], Ab[1][:, :], rr[:, r2, 1, :], ALU.mult, [Ab[1].buf(), rr.buf((r2, 1))], [oo.buf(r2)])
        S.op("dve", lambda e, r2=r2: e.scalar_tensor_tensor(out=oo[:, r2, :], in0=oo[:, r2, :], scalar=NEGLAM, in1=o1[:, r2, :], op0=ALU.mult, op1=ALU.add),
             reads=[oo.buf(r2), o1.buf(r2), lsm.buf()], writes=[oo.buf(r2)])
        tt_op("dve", sqb[:, r2, :], oo[:, r2, :], oo[:, r2, :], ALU.mult, [oo.buf(r2)], [sqb.buf(r2)])

    def epilogue_b(h, I, rnd):
        r2 = rnd % 2
        mm(MSb[:, :], onesb[:, :], sqb[:, r2, :], True, True, [onesb.buf(), sqb.buf(r2)], [MSb.buf()])
        S.op("act", lambda e, r2=r2: e.activation(out=rst[:, r2, :], in_=MSb[:, :], func=AF.Ln, bias=EPS, scale=1.0 / 128.0),
             reads=[MSb.buf(), cst.buf()], writes=[rst.buf(r2)])
        S.op("act", lambda e, r2=r2: e.activation(out=rst[:, r2, :], in_=rst[:, r2, :], func=AF.Exp, scale=-0.5), reads=[rst.buf(r2)], writes=[rst.buf(r2)])
        S.op("dve", lambda e, r2=r2, h=h, I=I: e.scalar_tensor_tensor(out=catT[:, 4 + h, I * 512:(I + 1) * 512], in0=oo[:, r2, :], scalar=gcol[:, 0:1], in1=rst[:, r2, :],
                                                                    op0=ALU.mult, op1=ALU.mult),
             reads=[oo.buf(r2), gcol.buf(), rst.buf(r2)], writes=catT.bufs([(4 + h, 4 * I + i) for i in range(4)]))

    def tt_op(eng, out_ap, a_ap, b_ap, op, reads, writes):
        S.op(eng, lambda e: e.tensor_tensor(out=out_ap, in0=a_ap, in1=b_ap, op=op), reads, writes)

    stb4 = (banks[0], banks[1], banks[7], banks[6])
    iters = []
    k = 0
    for h in range(4):
        for I in range(4):
            for j in range(4 * I + 4):
                for s in range(2):
                    iters.append((h, I, s, j, stb4[k % 4], k % 6))
                    k += 1
    npair = len(iters) // 2
    pending = None
    since = 0
    rnd = 0
    for p in range(npair):
        s_stage(iters[2 * p]); s_stage(iters[2 * p + 1])
        since += 1
        if pending is not None and since >= 2:
            epilogue_b(*pending); pending = None
        if p >= 1:
            pv_stage(iters[2 * p - 2]); pv_stage(iters[2 * p - 1])
            prev, it = iters[2 * p - 1], iters[2 * p]
            if (prev[0], prev[1]) != (it[0], it[1]):
                if pending is not None:
                    epilogue_b(*pending); pending = None
                epilogue_a(prev[0], prev[1], rnd)
                pending = (prev[0], prev[1], rnd); since = 0
                rnd += 1
    pv_stage(iters[-2]); pv_stage(iters[-1])
    if pending is not None:
        epilogue_b(*pending)
    epilogue_a(iters[-1][0], iters[-1][1], rnd)
    epilogue_b(iters[-1][0], iters[-1][1], rnd)
    S.free("rb", "chc", "ohs", "Lh", "gsb", "biasf", "biasb", "lqk", "lsm", "gsub", "PT", "qT", "kT", "vaug", "gcol", "rr", "o1", "oo", "sqb", "rst")
    if upto == "D":
        S.emit()
        return nc


    if str(2) in os.environ.get("BARRIERS", ""):
        S.barrier()
    I32 = mybir.dt.int32
    W2 = 2048

    def A8(name, dt_=F32):
        return S.alloc(name, [W2], dt_, 4)

    def tt_op(eng, out_ap, a_ap, b_ap, op, reads, writes):
        S.op(eng, lambda e: e.tensor_tensor(out=out_ap, in0=a_ap, in1=b_ap, op=op), reads, writes)

    def ts_op(eng, out_ap, a_ap, s1, s2, op0, op1, reads, writes):
        if s2 is None:
            S.op(eng, lambda e: e.tensor_scalar(out=out_ap, in0=a_ap, scalar1=s1, scalar2=None, op0=op0), reads, writes)
        else:
            S.op(eng, lambda e: e.tensor_scalar(out=out_ap, in0=a_ap, scalar1=s1, scalar2=s2, op0=op0, op1=op1), reads, writes)

    ki = A8("ki", I32)
    kf = A8("kf")

    def sin_from_u(u, out):
        S.op("dve", lambda e: e.tensor_copy(out=ki[:, :], in_=u[:, :]), reads=[u.buf()], writes=[ki.buf()])
        S.op("dve", lambda e: e.tensor_copy(out=kf[:, :], in_=ki[:, :]), reads=[ki.buf()], writes=[kf.buf()])
        tt_op("dve", u[:, :], u[:, :], kf[:, :], ALU.subtract, [u.buf(), kf.buf()], [u.buf()])
        S.op("dve", lambda e: e.scalar_tensor_tensor(out=u[:, :], in0=u[:, :], scalar=0.0, in1=u[:, :], op0=ALU.is_lt, op1=ALU.add),
             reads=[u.buf()], writes=[u.buf()])
        S.op("act", lambda e: e.activation(out=out[:, :], in_=u[:, :], func=AF.Sin, bias=cst[:, 5:6], scale=2 * PI),
             reads=[u.buf(), cst.buf()], writes=[out.buf()])

    lr = A8("lr"); li = A8("li"); lrdt = A8("lrdt"); ang = A8("ang")
    dtr = S.alloc("dtr", [32], F32, 4)
    S.dma("sp", "c2", lr[:, :], lam_re.partition_broadcast(128), writes=[lr.buf()])
    S.dma("sp", "c2", li[:, :], lam_im.partition_broadcast(128), writes=[li.buf()])
    S.dma("sp", "c2", dtr[:, :], log_dt.partition_broadcast(128), writes=[dtr.buf()])
    S.op("act", lambda e: e.activation(out=dtr[:, :], in_=dtr[:, :], func=AF.Exp), reads=[dtr.buf()], writes=[dtr.buf()])
    dt_b = dtr[:, :].unsqueeze(2).to_broadcast([128, 32, 64])
    v3 = lambda t: t[:, :].rearrange("p (g s) -> p g s", s=64)
    tt_op("dve", v3(lrdt), v3(lr), dt_b, ALU.mult, [lr.buf(), dtr.buf()], [lrdt.buf()])
    tt_op("dve", v3(ang), v3(li), dt_b, ALU.mult, [li.buf(), dtr.buf()], [ang.buf()])
    mg = A8("mg"); sn = A8("sn"); cs = A8("cs"); ua = A8("ua")
    lnat = S.alloc("lnat", [2, 128], F32, 4)
    S.dma("sp", "c2", lnat[0:16, 0, :], lam_re.rearrange("(j p) -> j p", p=128), writes=[lnat.buf()])
    S.dma("sp", "c2", lnat[0:16, 1, :], lam_im.rearrange("(j p) -> j p", p=128), writes=[lnat.buf()])
    f16 = S.alloc("f16", [12, 16], F32, 4)
    bkf = banks[4]
    for i in range(2):
        S.op("pe", lambda e, i=i, bkf=bkf: e.transpose(out=bkf[:, i * 16:(i + 1) * 16], in_=lnat[0:16, i, :], identity=idf[0:16, 0:16]),
             reads=[lnat.buf(), idf.buf()], writes=[bkf.buf()])
    fb = [f16.buf()]
    copy_op("dve", f16[:, 0:2, :], bkf[:, 0:32].rearrange("p (a j) -> p a j", a=2), [bkf.buf()], fb)
    for hf in range(2):
        S.op("dve", lambda e, hf=hf: e.tensor_copy(out=f16[hf * 64:(hf + 1) * 64, 2, :],
                                                  in_=dtr[hf * 64:(hf + 1) * 64, :].rearrange("p (j two) -> p j two", two=2)[:, :, hf]),
             reads=[dtr.buf()] + fb, writes=fb)
    Fk = lambda k: f16[:, k, :]

    def f_tt(o, a, b, op):
        tt_op("dve", Fk(o), Fk(a), Fk(b), op, fb, fb)

    def sin_small(k):
        S.op("dve", lambda e: e.tensor_copy(out=ki[:, 0:16], in_=Fk(k)), reads=fb, writes=[ki.buf()])
        S.op("dve", lambda e: e.tensor_copy(out=kf[:, 0:16], in_=ki[:, 0:16]), reads=[ki.buf()], writes=[kf.buf()])
        tt_op("dve", Fk(k), Fk(k), kf[:, 0:16], ALU.subtract, fb + [kf.buf()], fb)
        S.op("dve", lambda e: e.scalar_tensor_tensor(out=Fk(k), in0=Fk(k), scalar=0.0, in1=Fk(k), op0=ALU.is_lt, op1=ALU.add), reads=fb, writes=fb)
        S.op("act", lambda e: e.activation(out=Fk(k), in_=Fk(k), func=AF.Sin, bias=cst[:, 5:6], scale=2 * PI), reads=fb + [cst.buf()], writes=fb)

    f_tt(3, 0, 2, ALU.mult)
    f_tt(4, 1, 2, ALU.mult)
    S.op("act", lambda e: e.activation(out=Fk(3), in_=Fk(3), func=AF.Exp), reads=fb, writes=fb)
    ts_op("dve", Fk(5), Fk(4), 1.0 / (2 * PI), 1.5, ALU.mult, ALU.add, fb, fb)
    sin_small(5)
    ts_op("dve", Fk(6), Fk(4), 1.0 / (2 * PI), 1.75, ALU.mult, ALU.add, fb, fb)
    sin_small(6)
    f_tt(6, 3, 6, ALU.mult)
    ts_op("dve", Fk(6), Fk(6), -1.0, None, ALU.add, None, fb, fb)
    f_tt(5, 3, 5, ALU.mult)
    f_tt(7, 0, 0, ALU.mult)
    f_tt(8, 1, 1, ALU.mult)
    f_tt(8, 7, 8, ALU.add)
    S.op("dve", lambda e: e.reciprocal(out=Fk(8), in_=Fk(8)), reads=fb, writes=fb)
    f_tt(9, 6, 0, ALU.mult)
    f_tt(7, 5, 1, ALU.mult)
    f_tt(9, 9, 7, ALU.add)
    f_tt(9, 9, 8, ALU.mult)
    f_tt(10, 5, 0, ALU.mult)
    f_tt(7, 6, 1, ALU.mult)
    f_tt(10, 10, 7, ALU.subtract)
    f_tt(10, 10, 8, ALU.mult)
    if os.environ.get("S5_STOP") == "1":
        S.emit()
        return nc
    S.free("lr", "li", "dtr", "lnat")
    Wmr = A8("Wmr"); Wmi = A8("Wmi")
    S.op("act", lambda e: e.activation(out=mg[:, :], in_=lrdt[:, :], func=AF.Exp, scale=cst[:, 1:2]), reads=[lrdt.buf(), cst.buf()], writes=[mg.buf()])
    ts_op("dve", ua[:, :], ang[:, :], cst[:, 2:3], cst[:, 3:4], ALU.mult, ALU.add, [ang.buf(), cst.buf()], [ua.buf()])
    sin_from_u(ua, sn)
    ts_op("dve", ua[:, :], ang[:, :], cst[:, 2:3], cst[:, 4:5], ALU.mult, ALU.add, [ang.buf(), cst.buf()], [ua.buf()])
    sin_from_u(ua, cs)
    tt_op("dve", Wmr[:, :], mg[:, :], cs[:, :], ALU.mult, [mg.buf(), cs.buf()], [Wmr.buf()])
    S.op("dve", lambda e: e.scalar_tensor_tensor(out=Wmi[:, :], in0=mg[:, :], scalar=-1.0, in1=sn[:, :], op0=ALU.mult, op1=ALU.mult),
         reads=[mg.buf(), sn.buf()], writes=[Wmi.buf()])
    if os.environ.get("S5_STOP") == "2":
        S.emit()
        return nc
    trow = S.alloc("trow", [128], F32, 4)
    S.dma("sp", "c2", trow[:, :], c_trow, writes=[trow.buf()])
    lrdtT = A8("lrdtT"); angT = A8("angT")
    bi = 0
    for (src, dst) in ((lrdt, lrdtT), (ang, angT)):
        for q4 in range(4):
            bk = banks[bi % 4]; bi += 1
            for i in range(4):
                j = q4 * 4 + i
                S.op("pe", lambda e, bk=bk, i=i, j=j, src=src: e.transpose(out=bk[:, i * 128:(i + 1) * 128], in_=src[:, j * 128:(j + 1) * 128], identity=idf[:, :]),
                     reads=[src.buf(), idf.buf()], writes=[bk.buf()])
            copy_op("dve", dst[:, q4 * 512:(q4 + 1) * 512], bk[:, :], [bk.buf()], [dst.buf()])
    S.free("lrdt", "ang")
    WpTr = A8("WpTr"); WpTi = A8("WpTi")
    trow_b = trow[:, :].unsqueeze(1).to_broadcast([128, 16, 128])
    v16 = lambda t: t[:, :].rearrange("p (j t) -> p j t", t=128)
    tt_op("dve", v16(lrdtT), v16(lrdtT), trow_b, ALU.mult, [lrdtT.buf(), trow.buf()], [lrdtT.buf()])
    S.op("act", lambda e: e.activation(out=mg[:, :], in_=lrdtT[:, :], func=AF.Exp), reads=[lrdtT.buf()], writes=[mg.buf()])
    tt_op("dve", v16(angT), v16(angT), trow_b, ALU.mult, [angT.buf(), trow.buf()], [angT.buf()])
    ts_op("dve", ua[:, :], angT[:, :], 1.0 / (2 * PI), 1.5, ALU.mult, ALU.add, [angT.buf()], [ua.buf()])
    sin_from_u(ua, sn)
    ts_op("dve", ua[:, :], angT[:, :], 1.0 / (2 * PI), 1.75, ALU.mult, ALU.add, [angT.buf()], [ua.buf()])
    sin_from_u(ua, cs)
    tt_op("dve", WpTr[:, :], mg[:, :], cs[:, :], ALU.mult, [mg.buf(), cs.buf()], [WpTr.buf()])
    tt_op("dve", WpTi[:, :], mg[:, :], sn[:, :], ALU.mult, [mg.buf(), sn.buf()], [WpTi.buf()])
    S.free("lrdtT", "angT", "ua", "ki", "kf", "mg", "trow")
    if os.environ.get("S5_STOP") == "3":
        S.emit()
        return nc
    maskB = S.alloc("maskB", [4, 128], F32, 4)
    maskC = S.alloc("maskC", [4, 128], F32, 4)
    S.dma("sp", "c2", maskB[:, :, :], c_maskB, writes=[maskB.buf()])
    S.dma("sp", "c2", maskC[:, :, :], c_maskC, writes=[maskC.buf()])
    bnat = S.alloc("bnat", [2, 16, 16], F32, 4)
    S.dma("sp", "c2", bnat[:, 0, :, :], b_re.rearrange("(j p) h -> p j h", p=128), writes=[bnat.buf()])
    S.dma("sp", "c2", bnat[:, 1, :, :], b_im.rearrange("(j p) h -> p j h", p=128), writes=[bnat.buf()])
    bbar = S.alloc("bbar", [2, 16, 16], F32, 4)
    tb4 = S.alloc("tb4", [4, 16, 16], F32, 4)
    fr_b = f16[:, 9, :].unsqueeze(2).to_broadcast([128, 16, 16])
    fi_b = f16[:, 10, :].unsqueeze(2).to_broadcast([128, 16, 16])
    rdb = [bnat.buf(), f16.buf()]
    tt_op("dve", tb4[:, 0, :, :], bnat[:, 0, :, :], fr_b, ALU.mult, rdb, [tb4.buf()])
    tt_op("dve", tb4[:, 1, :, :], bnat[:, 1, :, :], fi_b, ALU.mult, rdb, [tb4.buf()])
    tt_op("dve", tb4[:, 2, :, :], bnat[:, 0, :, :], fi_b, ALU.mult, rdb, [tb4.buf()])
    tt_op("dve", tb4[:, 3, :, :], bnat[:, 1, :, :], fr_b, ALU.mult, rdb, [tb4.buf()])
    tt_op("dve", bbar[:, 0, :, :], tb4[:, 0, :, :], tb4[:, 1, :, :], ALU.subtract, [tb4.buf()], [bbar.buf()])
    tt_op("dve", bbar[:, 1, :, :], tb4[:, 2, :, :], tb4[:, 3, :, :], ALU.add, [tb4.buf()], [bbar.buf()])
    bn8 = S.alloc("bn8", [2, 16, 8, 16], F32, 4)
    for ri in range(2):
        S.op("dve", lambda e, ri=ri: e.tensor_copy(out=bn8[:, ri, :, :, :], in_=bbar[:, ri, :, :].unsqueeze(2).to_broadcast([128, 16, 8, 16])),
             reads=[bbar.buf()], writes=[bn8.buf()])
    Bmr = S.alloc("Bmr", [4, 512], BF16, 2)
    Bmi = S.alloc("Bmi", [4, 512], BF16, 2)
    tq = S.alloc("tq", [4, 128], F32, 4)
    for j in range(16):
        ctile, jm = j // 4, j % 4
        bk = banks[j % 4]
        for ri in range(2):
            S.op("pe", lambda e, bk=bk, ri=ri, j=j: e.transpose(out=bk[:, ri * 128:(ri + 1) * 128],
                                                             in_=bn8[:, ri, j, :, :].rearrange("p c h -> p (c h)"), identity=idf[:, :]),
                 reads=[bn8.buf(), idf.buf()], writes=[bk.buf()])
        tt_op("dve", Bmr[:, ctile, jm * 128:(jm + 1) * 128], bk[:, 0:128], maskB[:, jm, :], ALU.mult, [bk.buf(), maskB.buf()], [Bmr.buf()])
        tt_op("dve", Bmi[:, ctile, jm * 128:(jm + 1) * 128], bk[:, 128:256], maskB[:, jm, :], ALU.mult, [bk.buf(), maskB.buf()], [Bmi.buf()])
    S.free("bnat", "bn8", "bbar", "tb4", "f16")
    if os.environ.get("S5_STOP") == "4":
        S.emit()
        return nc
    cnat = S.alloc("cnat", [2, 4, 64], F32, 4)
    S.dma("sp", "c2", cnat[:, 0, :, :], c_re.rearrange("(ct p) s -> p ct s", p=128), writes=[cnat.buf()])
    S.dma("sp", "c2", cnat[:, 1, :, :], c_im.rearrange("(ct p) s -> p ct s", p=128), writes=[cnat.buf()])
    cn2 = S.alloc("cn2", [2, 4, 2, 64], F32, 4)
    for ri in range(2):
        S.op("dve", lambda e, ri=ri: e.tensor_copy(out=cn2[:, ri, :, :, :], in_=cnat[:, ri, :, :].unsqueeze(2).to_broadcast([128, 4, 2, 64])),
             reads=[cnat.buf()], writes=[cn2.buf()])
    Cmr = S.alloc("Cmr", [16, 128], BF16, 2)
    Cmi = S.alloc("Cmi", [16, 128], BF16, 2)
    for ctile in range(4):
        bk = banks[ctile % 4]
        for ri in range(2):
            S.op("pe", lambda e, bk=bk, ri=ri, ctile=ctile: e.transpose(out=bk[:, ri * 128:(ri + 1) * 128],
                                                                    in_=cn2[:, ri, ctile, :, :].rearrange("p c s -> p (c s)"), identity=idf[:, :]),
                 reads=[cn2.buf(), idf.buf()], writes=[bk.buf()])
        for jm in range(4):
            j = ctile * 4 + jm
            tt_op("dve", Cmr[:, j, :], bk[:, 0:128], maskC[:, jm, :], ALU.mult, [bk.buf(), maskC.buf()], [Cmr.buf()])
            S.op("dve", lambda e, bk=bk, j=j, jm=jm: e.scalar_tensor_tensor(out=Cmi[:, j, :], in0=bk[:, 128:256], scalar=-1.0, in1=maskC[:, jm, :],
                                                                         op0=ALU.mult, op1=ALU.mult),
                 reads=[bk.buf(), maskC.buf()], writes=[Cmi.buf()])
    S.free("cnat", "cn2", "maskB", "maskC", "tq", "sn", "cs")
    if os.environ.get("S5_STOP") == "5":
        S.emit()
        return nc
    dnat = S.alloc("dnat", [2, 128], F32, 4)
    S.dma("sp", "c2", dnat[0:4, 0, :], s5_d.rearrange("(ct p) -> ct p", p=128), writes=[dnat.buf()])
    S.dma("sp", "c2", dnat[0:4, 1, :], glu_b.rearrange("(ct p) -> ct p", p=128), writes=[dnat.buf()])
    dcol = S.alloc("dcol", [2, 4], F32, 4)
    bk = banks[0]
    for i in range(2):
        S.op("pe", lambda e, i=i, bk=bk: e.transpose(out=bk[:, i * 4:(i + 1) * 4], in_=dnat[0:4, i, :], identity=idf[0:4, 0:4]),
             reads=[dnat.buf(), idf.buf()], writes=[bk.buf()])
    copy_op("dve", dcol[:, 0, :], bk[:, 0:4], [bk.buf()], [dcol.buf()])
    copy_op("dve", dcol[:, 1, :], bk[:, 4:8], [bk.buf()], [dcol.buf()])
    S.free("dnat")
    if os.environ.get("S5_STOP") == "6":
        S.emit()
        return nc
    gw = wload("gw", glu_w, 4, 512, 512, "w_s5")

    if os.environ.get("S5_STOP") == "7":
        S.emit()
        return nc
    if str(3) in os.environ.get("BARRIERS", ""):
        S.barrier()
    zb = S.alloc("zb", [2, 2, W2], BF16, 2)
    tm = S.alloc("tm", [2, 4, 512], F32, 4)
    td = S.alloc("td", [2, 4, 512], F32, 4)
    wc = S.alloc("wc", [2, 2, 512], F32, 4)
    xbf = S.alloc("xbf", [2, 16, 128], BF16, 2)
    car = S.alloc("car", [16, 2], F32, 4)
    ypre = S.alloc("ypre", [2, 4, 512], F32, 4)
    S.op("pool", lambda e: e.memset(car[:, :, :], 0.0), writes=[car.buf(g) for g in range(4)])
    gl = S.alloc("gl", [4, 512], F32, 4)
    glb = S.alloc("glb", [4, 512], BF16, 2)
    g1 = S.alloc("g1", [2, 512], F32, 4)
    bR, bI = banks[0], banks[1]
    wbk = (banks[2], banks[3])
    ybk = banks[4]
    gbk = banks[5]
    mi = 0
    di = 0
    for c in range(int(os.environ.get("S5_CHUNKS", NTT))):
        zs = c % 2
        if os.environ.get("S5_PART") == "1" and c == 0:
            pass
        for ctile in range(4):
            mm(bR[:, :], uT[:, ctile, c * 128:(c + 1) * 128], Bmr[:, ctile, :], True, True, [uT.buf((ctile, c)), Bmr.buf()], [bR.buf()])
            mm(bI[:, :], uT[:, ctile, c * 128:(c + 1) * 128], Bmi[:, ctile, :], True, True, [uT.buf((ctile, c)), Bmi.buf()], [bI.buf()])
            ms = mi % 2; mi += 1
            blk = slice(ctile * 512, (ctile + 1) * 512)
            tt_op("dve", tm[:, ms, 0, :], bR[:, :], Wmr[:, blk], ALU.mult, [bR.buf(), Wmr.buf()], [tm.buf((ms, 0))])
            tt_op("dve", tm[:, ms, 1, :], bI[:, :], Wmi[:, blk], ALU.mult, [bI.buf(), Wmi.buf()], [tm.buf((ms, 1))])
            tt_op("dve", tm[:, ms, 2, :], bR[:, :], Wmi[:, blk], ALU.mult, [bR.buf(), Wmi.buf()], [tm.buf((ms, 2))])
            tt_op("dve", tm[:, ms, 3, :], bI[:, :], Wmr[:, blk], ALU.mult, [bI.buf(), Wmr.buf()], [tm.buf((ms, 3))])
            tt_op("pool", zb[:, zs, 0, blk], tm[:, ms, 0, :], tm[:, ms, 1, :], ALU.subtract, [tm.buf((ms, 0)), tm.buf((ms, 1))], [zb.buf((zs, 0, ctile))])
            tt_op("pool", zb[:, zs, 1, blk], tm[:, ms, 2, :], tm[:, ms, 3, :], ALU.add, [tm.buf((ms, 2)), tm.buf((ms, 3))], [zb.buf((zs, 1, ctile))])
        if os.environ.get("S5_PART") == "1":
            continue
        for g4 in range(4):
            WR, WI = (banks[2], banks[3]) if g4 % 2 == 0 else (banks[6], banks[7])
            for jj in range(4):
                j = 4 * g4 + jj
                mm(WR[:, jj * 128:(jj + 1) * 128], zb[:, zs, 0, j * 128:(j + 1) * 128], trib[:, :], True, True, [zb.buf((zs, 0, j // 4)), trib.buf()], [WR.buf()])
                mm(WI[:, jj * 128:(jj + 1) * 128], zb[:, zs, 1, j * 128:(j + 1) * 128], trib[:, :], True, True, [zb.buf((zs, 1, j // 4)), trib.buf()], [WI.buf()])
            ds = di % 2; di += 1
            for jj in range(4):
                j = 4 * g4 + jj
                S.op("act", lambda e, WR=WR, ds=ds, jj=jj, j=j: e.activation(out=wc[:, ds, 0, jj * 128:(jj + 1) * 128], in_=WR[:, jj * 128:(jj + 1) * 128],
                                                                          func=AF.Identity, bias=car[:, j, 0:1], scale=1.0),
                     reads=[WR.buf(), car.buf(g4)], writes=[wc.buf((ds, 0))])
                S.op("act", lambda e, WI=WI, ds=ds, jj=jj, j=j: e.activation(out=wc[:, ds, 1, jj * 128:(jj + 1) * 128], in_=WI[:, jj * 128:(jj + 1) * 128],
                                                                          func=AF.Identity, bias=car[:, j, 1:2], scale=1.0),
                     reads=[WI.buf(), car.buf(g4)], writes=[wc.buf((ds, 1))])
            gcols = slice(g4 * 512, (g4 + 1) * 512)
            pr, pi_ = WpTr[:, gcols], WpTi[:, gcols]
            wr_, wi_ = wc[:, ds, 0, :], wc[:, ds, 1, :]
            for k, (w_, p_, wk) in enumerate(((wr_, pr, 0), (wi_, pi_, 1), (wr_, pi_, 0), (wi_, pr, 1))):
                tt_op("dve", td[:, ds, k, :], w_, p_, ALU.mult, [wc.buf((ds, wk)), WpTr.buf(), WpTi.buf()], [td.buf((ds, k))])
            xr_out = xbf[:, 0, 4 * g4:4 * g4 + 4, :].rearrange("p j t -> p (j t)")
            xi_out = xbf[:, 1, 4 * g4:4 * g4 + 4, :].rearrange("p j t -> p (j t)")
            tt_op("dve", xr_out, td[:, ds, 0, :], td[:, ds, 1, :], ALU.subtract, [td.buf((ds, 0)), td.buf((ds, 1))], xbf.bufs([(0, 4 * g4 + i) for i in range(4)]))
            tt_op("dve", xi_out, td[:, ds, 2, :], td[:, ds, 3, :], ALU.add, [td.buf((ds, 2)), td.buf((ds, 3))], xbf.bufs([(1, 4 * g4 + i) for i in range(4)]))
            l127 = lambda k, ds=ds: td[:, ds, k, :].rearrange("p (j t) -> p j t", t=128)[:, :, 127]
            tt_op("dve", car[:, 4 * g4:4 * g4 + 4, 0], l127(0), l127(1), ALU.subtract, [td.buf((ds, 0)), td.buf((ds, 1))], [car.buf(g4)])
            tt_op("dve", car[:, 4 * g4:4 * g4 + 4, 1], l127(2), l127(3), ALU.add, [td.buf((ds, 2)), td.buf((ds, 3))], [car.buf(g4)])
        if os.environ.get("S5_PART") == "2":
            continue
        ys = (c // 4) % 2
        for ctile in range(4):
            ybk = banks[4 + ctile % 2]
            ya = ybk[:, 0:128]
            n = 0
            for jm in range(4):
                j = ctile * 4 + jm
                mm(ya, Cmr[:, j, :], xbf[:, 0, j, :], n == 0, False, [Cmr.buf(), xbf.buf((0, j))], [ybk.buf()]); n += 1
                mm(ya, Cmi[:, j, :], xbf[:, 1, j, :], False, jm == 3, [Cmi.buf(), xbf.buf((1, j))], [ybk.buf()]); n += 1
            S.op("dve", lambda e, ya=ya, ctile=ctile, ys=ys, c=c: e.scalar_tensor_tensor(
                out=ypre[:, ys, ctile, (c % 4) * 128:(c % 4 + 1) * 128], in0=uT[:, ctile, c * 128:(c + 1) * 128], scalar=dcol[:, 0, ctile:ctile + 1], in1=ya,
                op0=ALU.mult, op1=ALU.add),
                reads=[uT.buf((ctile, c)), dcol.buf(), ybk.buf()], writes=[ypre.buf((ys, ctile))])
        if c % 4 == 3:
            tb = c // 4
            for ctile in range(4):
                xx = ypre[:, ys, ctile, :]
                xb_ = ypre.buf((ys, ctile))
                tt_op("dve", g1[:, 0, :], xx, xx, ALU.mult, [xb_], [g1.buf(0)])
                ts_op("dve", g1[:, 0, :], g1[:, 0, :], 0.044715, 1.0, ALU.mult, ALU.add, [g1.buf(0)], [g1.buf(0)])
                tt_op("dve", g1[:, 0, :], g1[:, 0, :], xx, ALU.mult, [g1.buf(0), xb_], [g1.buf(0)])
                S.op("act", lambda e: e.activation(out=g1[:, 1, :], in_=g1[:, 0, :], func=AF.Sigmoid, scale=1.5957691216057308), reads=[g1.buf(0)], writes=[g1.buf(1)])
                tt_op("dve", gl[:, ctile, :], xx, g1[:, 1, :], ALU.mult, [xb_, g1.buf(1)], [gl.buf(ctile)])
                copy_op("dve", glb[:, ctile, :], gl[:, ctile, :], [gl.buf(ctile)], [glb.buf(ctile)])
            for cp in range(0 if os.environ.get("S5_G") != "1" else 4, 4):
                gbk = banks[4 + cp % 2]
                for ctile in range(4):
                    mm(gbk[:, :], gw[:, ctile, cp * 128:(cp + 1) * 128], glb[:, ctile, :], ctile == 0, ctile == 3, [gw.buf(0), glb.buf(ctile)], [gbk.buf()])
                if os.environ.get("S5_G") == "2":
                    continue
                S.op("act", lambda e, cp=cp, gbk=gbk: e.activation(out=g1[:, 0, :], in_=gbk[:, :], func=AF.Sigmoid, bias=dcol[:, 1, cp:cp + 1], scale=1.0),
                     reads=[gbk.buf(), dcol.buf()], writes=[g1.buf(0)])
                tt_op("dve", catT[:, cp, tb * 512:(tb + 1) * 512], gl[:, cp, :], g1[:, 0, :], ALU.mult, [gl.buf(cp), g1.buf(0)],
                      catT.bufs([(cp, 4 * tb + i) for i in range(4)]))
    S.free("Wmr", "Wmi", "WpTr", "WpTi", "Bmr", "Bmi", "Cmr", "Cmi", "dcol", "gw", "zb", "tm", "td", "wc", "xbf", "car", "ypre", "gl", "glb", "g1", "uT")
    if debug:
        S.dma("sp", "dbg", dbg["d_cat"], catT[:, :, :], reads=catT.bufs([(k, t) for k in range(8) for t in range(16)]), final=True)
    if upto == "C":
        S.emit()
        return nc


    if str(4) in os.environ.get("BARRIERS", ""):
        S.barrier()
    hs = S.alloc("hs", [16, D], F32, 4)
    hT = S.alloc("hT", [8, T], BF16, 2)
    wob = wload("wob", w_out, 8, D, 512, "w_e")
    lnp0 = load_lnp("lnp0", ln_in_g, ln_in_b)
    lnp1 = load_lnp("lnp1", ln1_g, ln1_b)
    lbt = S.alloc("lbt", [2, D], BF16, 2)
    bi = 0

    def resid_stats(tt, acc_banks):
        for nh in range(2):
            bk = acc_banks[nh]
            S.op("dve", lambda e, nh=nh, bk=bk: e.scalar_tensor_tensor(out=hs[:, tt, nh * 512:(nh + 1) * 512], in0=hs[:, tt, nh * 512:(nh + 1) * 512],
                                                                     scalar=ALPHA, in1=bk[:, :], op0=ALU.mult, op1=ALU.add),
                 reads=[hs.buf(tt), bk.buf()], writes=[hs.buf(tt)])
        return ln_stats(hs[:, tt, :], hs.buf(tt))

    def ln_finish_a(tt, sl, lnp):
        ln_apply(hs[:, tt, :], hs.buf(tt), lnp, None, None, sl)

    def ln_finish(tt, sl, lnp, do_T, split=False):
        if not split:
            ln_apply(hs[:, tt, :], hs.buf(tt), lnp, hs[:, tt, :], hs.buf(tt), sl)
        else:
            ln_apply_b(hs[:, tt, :], hs.buf(tt), lnp, hs[:, tt, :], hs.buf(tt))
        if do_T:
            s2 = tt % 2
            copy_op("act", lbt[:, s2, :], hs[:, tt, :], [hs.buf(tt)], [lbt.buf(s2)])
            transpose_to_hT(lbt[:, s2, :], lbt.buf(s2), hT, tt)

    def resid_ln(tt, acc_banks, lnp, do_T):
        sl = resid_stats(tt, acc_banks)
        ln_finish(tt, sl, lnp, do_T)

    sl_in = {}
    sl_1 = {}
    accs_of = {}
    for step in range(NTT + 6):
        if step >= 6:
            ln_finish(step - 6, None, lnp1, True, split=True)
        if step < NTT:
            tt = step
            S.dma("sp", "xin", hs[:, tt, :], x[tt * 128:(tt + 1) * 128, :], writes=[hs.buf(tt)])
            sl_in[tt] = ln_stats(hs[:, tt, :], hs.buf(tt))
        if 1 <= step <= NTT:
            tt = step - 1
            ln_apply(hs[:, tt, :], hs.buf(tt), lnp0, None, None, sl_in[tt])
        if 2 <= step <= NTT + 1:
            tt = step - 2
            ln_apply_b(hs[:, tt, :], hs.buf(tt), lnp0, hs[:, tt, :], hs.buf(tt))
            accs = []
            for nh in range(2):
                bk = banks[bi % 4]; bi += 1
                for kt in range(8):
                    mm(bk[:, :], catT[:, kt, tt * 128:(tt + 1) * 128], wob[:, kt, nh * 512:(nh + 1) * 512], kt == 0, kt == 7,
                       [catT.buf((kt, tt)), wob.buf(nh)], [bk.buf()])
                accs.append(bk)
            accs_of[tt] = accs
        if 3 <= step <= NTT + 2:
            tt = step - 3
            sl_1[tt] = resid_stats(tt, accs_of[tt])
        if 4 <= step <= NTT + 3:
            tt = step - 4
            ln_finish_a(tt, sl_1[tt], lnp1)
    S.free("catT", "wob", "lnp0", "lnp1")
    if debug:
        S.dma("sp", "dbg", dbg["d_h1"], hs[:, :, :], reads=hs.bufs(range(16)), final=True)
    if upto == "E":
        S.emit()
        return nc


    if str(5) in os.environ.get("BARRIERS", ""):
        S.barrier()
    wkv = wload("wkv", ca_wkv, 8, 2 * D, 512, "w_f")
    memf = S.alloc("memf", [2, D], F32, 4)
    memb = S.alloc("memb", [2, D], BF16, 2)
    memT = S.alloc("memT", [8, 256], BF16, 2)
    for mt in range(2):
        S.dma("sp", "mem", memf[:, mt, :], mem[mt * 128:(mt + 1) * 128, :], writes=[memf.buf(mt)])
        copy_op("act", memb[:, mt, :], memf[:, mt, :], [memf.buf(mt)], [memb.buf(mt)])
        ctr["pt"] += 1
        pt = ptb[ctr["pt"] % 2]
        for k in range(8):
            S.op("pe", lambda e, k=k, pt=pt, mt=mt: e.transpose(out=pt[:, k, :], in_=memb[:, mt, k * 128:(k + 1) * 128], identity=idb[:, :]),
                 reads=[memb.buf(mt), idb.buf()], writes=[pt.buf()])
        copy_op(evac_eng(), memT[:, :, mt * 128:(mt + 1) * 128], pt[:, :, :], [pt.buf()], [memT.buf()])
    kTca = S.alloc("kTca", [8, 256], BF16, 2)
    vca = S.alloc("vca", [2, D], BF16, 2)
    for ct in range(8):
        bk = banks[bi % 4]; bi += 1
        for kt in range(8):
            mm(bk[:, 0:256], wkv[:, kt, ct * 128:(ct + 1) * 128], memT[:, kt, :], kt == 0, kt == 7, [wkv.buf(ct // 4), memT.buf()], [bk.buf()])
        copy_op(evac_eng(), kTca[:, ct, :], bk[:, 0:256], [bk.buf()], [kTca.buf()])
    for mt in range(2):
        for nh in range(2):
            bk = banks[bi % 4]; bi += 1
            for kt in range(8):
                mm(bk[:, :], memT[:, kt, mt * 128:(mt + 1) * 128], wkv[:, kt, D + nh * 512:D + (nh + 1) * 512], kt == 0, kt == 7,
                   [wkv.buf(2 + nh), memT.buf()], [bk.buf()])
            copy_op(evac_eng(), vca[:, mt, nh * 512:(nh + 1) * 512], bk[:, :], [bk.buf()], [vca.buf()])
    S.free("wkv", "memf", "memb", "memT")
    wqb = wload("wqb", ca_wq, 8, D, 512, "w_f2")
    wo2 = wload("wo2", ca_wo, 8, D, 512, "w_f2")
    lnp2 = load_lnp("lnp2", ln2_g, ln2_b)
    qTc = S.alloc("qTc", [2, 8, 512], BF16, 2)
    PTc = S.alloc("PTc", [2, 2, 512], BF16, 2)
    oTc = S.alloc("oTc", [8, 512], BF16, 2)
    rcs = S.alloc("rcs", [2, 512], F32, 4)
    pendF = None
    pendF2 = None
    fst = {"bi": 0, "sc": 0}

    def nbank():
        fst["bi"] += 1
        return banks[fst["bi"] % 4]

    def F_Q(tb):
        qs = tb % 2
        tcols = slice(tb * 512, (tb + 1) * 512)
        hbufs = hT.bufs(range(4 * tb, 4 * tb + 4))
        for ct in range(8):
            bk = nbank()
            for kt in range(8):
                mm(bk[:, :], wqb[:, kt, ct * 128:(ct + 1) * 128], hT[:, kt, tcols], kt == 0, kt == 7, [wqb.buf(ct // 4)] + hbufs, [bk.buf()])
            copy_op(evac_eng(), qTc[:, qs, ct, :], bk[:, :], [bk.buf()], [qTc.buf((qs, ct))])

    def F_HS(tb, hd):
        qs = tb % 2
        ps = hd % 2
        for mt in range(2):
            fst["sc"] += 1
            bk = banks[4 + fst["sc"] % 4]
            for i in range(2):
                ct = 2 * hd + i
                mm(bk[:, :], kTca[:, ct, mt * 128:(mt + 1) * 128], qTc[:, qs, ct, :], i == 0, i == 1, [kTca.buf(), qTc.buf((qs, ct))], [bk.buf()])
            S.op("act", lambda e, bk=bk, ps=ps, mt=mt: e.activation(out=PTc[:, ps, mt, :], in_=bk[:, :], func=AF.Exp, scale=1.0 / 16.0),
                 reads=[bk.buf()], writes=[PTc.buf((ps, mt))])

    def F_HP(tb, hd):
        ps = hd % 2
        sb = nbank()
        for mt in range(2):
            mm(sb[:, :], onesb[:, :], PTc[:, ps, mt, :], mt == 0, mt == 1, [onesb.buf(), PTc.buf((ps, mt))], [sb.buf()])
        S.op("act", lambda e, sb=sb, ps=ps: e.activation(out=rcs[:, ps, :], in_=sb[:, :], func=AF.Ln), reads=[sb.buf()], writes=[rcs.buf(ps)])
        S.op("act", lambda e, ps=ps: e.activation(out=rcs[:, ps, :], in_=rcs[:, ps, :], func=AF.Exp, scale=-1.0), reads=[rcs.buf(ps)], writes=[rcs.buf(ps)])
        for dti in range(2):
            ct = 2 * hd + dti
            bk = nbank()
            for mt in range(2):
                mm(bk[:, :], vca[:, mt, ct * 128:(ct + 1) * 128], PTc[:, ps, mt, :], mt == 0, mt == 1, [vca.buf(), PTc.buf((ps, mt))], [bk.buf()])
            tt_op("dve", oTc[:, ct, :], bk[:, :], rcs[:, ps, :], ALU.mult, [bk.buf(), rcs.buf(ps)], [oTc.buf(ct)])

    def F_W(tb):
        nonlocal_state = None
        prev_acc = None
        for tl in range(5):
            tt = 4 * tb + tl
            if tl < 4:
                if pstate["p3"] is not None:
                    ln_finish(pstate["p3"][0], None, lnp2, True, split=True)
                    pstate["p3"] = None
                accs = []
                for nh in range(2):
                    bk = nbank()
                    for kt in range(8):
                        mm(bk[:, :], oTc[:, kt, tl * 128:(tl + 1) * 128], wo2[:, kt, nh * 512:(nh + 1) * 512], kt == 0, kt == 7,
                           [oTc.buf(kt), wo2.buf(nh)], [bk.buf()])
                    accs.append(bk)
            if prev_acc is not None:
                pt_, pa_ = prev_acc
                slF = resid_stats(pt_, pa_)
                if pstate["p1"] is not None:
                    ln_finish_a(pstate["p1"][0], pstate["p1"][1], lnp2)
                if pstate["p3"] is not None:
                    ln_finish(pstate["p3"][0], None, lnp2, True, split=True)
                pstate["p3"] = pstate["p2"]
                pstate["p2"] = pstate["p1"]
                pstate["p1"] = (pt_, slF)
            prev_acc = (tt, accs) if tl < 4 else None

    pstate = {"p1": None, "p2": None, "p3": None}
    F_Q(0)
    for tb in range(4):
        F_HS(tb, 0)
        for hd in range(4):
            if hd < 3:
                F_HS(tb, hd + 1)
            F_HP(tb, hd)
        if tb < 3:
            F_Q(tb + 1)
        F_W(tb)
    if pstate["p3"] is not None:
        ln_finish(pstate["p3"][0], None, lnp2, True, split=True)
    ln_finish_a(pstate["p1"][0], pstate["p1"][1], lnp2)
    if pstate["p2"] is not None:
        ln_finish(pstate["p2"][0], None, lnp2, True, split=True)
    ln_finish(pstate["p1"][0], None, lnp2, True, split=True)
    S.free("wqb", "wo2", "lnp2", "qTc", "PTc", "oTc", "rcs", "kTca", "vca", "lbt")
    if debug:
        S.dma("sp", "dbg", dbg["d_h2"], hs[:, :, :], reads=hs.bufs(range(16)), final=True)
    if upto == "F":
        S.emit()
        return nc


    if str(6) in os.environ.get("BARRIERS", ""):
        S.barrier()
    lnp3 = load_lnp("lnp3", ln3_g, ln3_b)
    gus = S.alloc("gus", [3, 8, 2, 128], BF16, 2)
    actT = S.alloc("actT", [11, T], BF16, 2)
    sg = S.alloc("sg", [2, 512], F32, 4)
    guv = w_gu.rearrange("(kt p) n -> p kt n", p=128)
    gi = 0
    pendG = None
    pendG2 = None

    def g_finish(t_):
        ln_apply_b(hs[:, t_, :], hs.buf(t_), lnp3, hs[:, t_, :], hs.buf(t_))
        S.dma("sp", "out", out[t_ * 128:(t_ + 1) * 128, :], hs[:, t_, :], reads=[hs.buf(t_)], final=True)

    for ps_ in range(2):
        wd = S.alloc("wd", [11, D], BF16, 2)
        dv = w_dn[ps_ * 11 * 128:(ps_ + 1) * 11 * 128, :].rearrange("(j p) n -> p j n", p=128)
        for jl in range(11):
            j = ps_ * 11 + jl
            gs = gi % 3; gi += 1
            S.dma("pool", f"w_gu{gs}", gus[:, gs, :, 0, :], guv[:, :, j * 128:(j + 1) * 128], writes=[gus.buf(gs)])
            S.dma("pool", f"w_gu{gs}", gus[:, gs, :, 1, :], guv[:, :, FFN_H + j * 128:FFN_H + (j + 1) * 128], writes=[gus.buf(gs)])
            if jl < 11:
                S.dma("pool", "w_dn", wd[:, jl, :], dv[:, jl, :], writes=[wd.buf(jl)])
            for tb in range(4):
                tcols = slice(tb * 512, (tb + 1) * 512)
                hbufs = hT.bufs(range(4 * tb, 4 * tb + 4))
                bg = banks[(bi % 2) * 2]; bu_ = banks[(bi % 2) * 2 + 1]; bi += 1
                for kt in range(8):
                    mm(bg[:, :], gus[:, gs, kt, 0, :], hT[:, kt, tcols], kt == 0, kt == 7, [gus.buf(gs)] + hbufs, [bg.buf()])
                for kt in range(8):
                    mm(bu_[:, :], gus[:, gs, kt, 1, :], hT[:, kt, tcols], kt == 0, kt == 7, [gus.buf(gs)] + hbufs, [bu_.buf()])
                s2 = bi % 2
                S.op("act", lambda e, bg=bg, s2=s2: e.activation(out=sg[:, s2, :], in_=bg[:, :], func=AF.Silu), reads=[bg.buf()], writes=[sg.buf(s2)])
                tt_op("dve", actT[:, jl, tcols], sg[:, s2, :], bu_[:, :], ALU.mult, [sg.buf(s2), bu_.buf()], actT.bufs([(jl, 4 * tb + i) for i in range(4)]))
        for tt in range(NTT):
            accs = []
            for nh in range(2):
                bk = banks[4 + nh + 2 * (tt % 2)]
                for jl in range(11):
                    mm(bk[:, :], actT[:, jl, tt * 128:(tt + 1) * 128], wd[:, jl, nh * 512:(nh + 1) * 512], jl == 0, jl == 10,
                       [actT.buf((jl, tt)), wd.buf(jl)], [bk.buf()])
                accs.append(bk)
            if ps_ == 0:
                for nh in range(2):
                    bk = accs[nh]
                    S.op("dve", lambda e, nh=nh, bk=bk, tt=tt: e.scalar_tensor_tensor(out=hs[:, tt, nh * 512:(nh + 1) * 512], in0=hs[:, tt, nh * 512:(nh + 1) * 512],
                                                                                 scalar=ALPHA, in1=bk[:, :], op0=ALU.mult, op1=ALU.add),
                         reads=[hs.buf(tt), bk.buf()], writes=[hs.buf(tt)])
            else:
                for nh in range(2):
                    bk = accs[nh]
                    tt_op("dve", hs[:, tt, nh * 512:(nh + 1) * 512], hs[:, tt, nh * 512:(nh + 1) * 512], bk[:, :], ALU.add, [hs.buf(tt), bk.buf()], [hs.buf(tt)])
                slG = ln_stats(hs[:, tt, :], hs.buf(tt))
                if pendG2 is not None:
                    g_finish(pendG2[0])
                if pendG is not None:
                    ln_apply(hs[:, pendG[0], :], hs.buf(pendG[0]), lnp3, None, None, pendG[1])
                pendG2 = pendG
                pendG = (tt, slG)
        if ps_ == 1:
            if pendG2 is not None:
                g_finish(pendG2[0])
            ln_apply(hs[:, pendG[0], :], hs.buf(pendG[0]), lnp3, None, None, pendG[1])
            g_finish(pendG[0])
        S.free("wd")
    S.emit()
    return nc


_CACHE = {}


def kernel(**inputs):
    consts = host_consts()
    shared = {}
    for k, v in inputs.items():
        if k in ("x", "mem"):
            continue
        a = np.ascontiguousarray(np.asarray(v, dtype=np.float32))
        if k == "rel_bias":
            shared[k] = a
        elif a.ndim >= 2 and a.shape[0] == 1:
            a = a[0]
            if k in ("s5_lambda_re", "s5_lambda_im", "s5_d"):
                a = a.reshape(-1)
            elif k in ("s5_b_re", "s5_b_im"):
                a = a.reshape(2048, 16)
            elif k in ("s5_c_re", "s5_c_im"):
                a = a.reshape(512, 64)
            shared[k] = np.ascontiguousarray(a)
        else:
            shared[k] = a
    shared.update(consts)
    xs = np.asarray(inputs["x"], dtype=np.float32)
    ms = np.asarray(inputs["mem"], dtype=np.float32)
    if "nc" not in _CACHE:
        _CACHE["nc"] = build_program(False)
    nc = _CACHE["nc"]
    in_maps = []
    for b in range(8):
        m = dict(shared)
        m["x"] = np.ascontiguousarray(xs[b])
        m["mem"] = np.ascontiguousarray(ms[b])
        in_maps.append(m)
    res = run_bass_kernel_spmd(nc, in_maps, core_ids=list(range(8)))
    return np.stack([np.asarray(r["out"], dtype=np.float32) for r in res.results], axis=0)
```

```python
import numpy as np
import concourse.bass as bass
import concourse.mybir as mybir
from concourse.bass_utils import run_bass_kernel_spmd

F32 = mybir.dt.float32
BF16 = mybir.dt.bfloat16
ALU = mybir.AluOpType
AF = mybir.ActivationFunctionType
AX = mybir.AxisListType

ENGS = ("pe", "act", "dve", "pool", "sp")


class Instr:
    __slots__ = ("eng", "fn", "waits", "signal", "idx", "val", "dma_sem", "dma_val")

    def __init__(self, eng, fn):
        self.eng = eng
        self.fn = fn
        self.waits = []
        self.signal = False
        self.idx = -1
        self.val = 0
        self.dma_sem = None
        self.dma_val = 0


class Buf:
    __slots__ = ("name", "last_w", "readers")

    def __init__(self, name, inherit=()):
        self.name = name
        self.last_w = None
        self.readers = list(inherit)


class Ten:
    def __init__(self, h, name, inherit=()):
        self.h = h
        self.name = name
        self.inherit = list(inherit)
        self._bufs = {}

    def __getitem__(self, k):
        return self.h[k]

    def buf(self, key=0):
        b = self._bufs.get(key)
        if b is None:
            b = Buf(f"{self.name}{key}", self.inherit)
            self._bufs[key] = b
        return b

    def bufs(self, keys):
        return [self.buf(k) for k in keys]

    def all_instrs(self):
        out = list(self.inherit)
        for b in self._bufs.values():
            if b.last_w is not None:
                out.append(b.last_w)
            out.extend(b.readers)
        return out


class Sched:
    def __init__(self, nc, sbuf_base=16512, sbuf_bytes=229312):
        self.nc = nc
        self.q = {e: [] for e in ENGS}
        self.waited = {e: {} for e in ENGS}
        self.dma_count = {}
        self.sbuf_bytes = sbuf_bytes
        self.sbuf_base = sbuf_base
        self.live = {}
        self.hist = []
        self.n_alloc = 0
        self.final_dma = []
        self.ring_pos = {}
        self.ring_last = {}

    def alloc(self, name, free_shape, dtype, nbytes_el):
        size = int(np.prod(free_shape)) * nbytes_el
        size = (size + 63) // 64 * 64
        segs = sorted((o, s) for (o, s, _) in self.live.values())
        off = self.sbuf_base
        for (o, s) in segs:
            if off + size <= o:
                break
            off = max(off, o + s)
        if off + size > self.sbuf_bytes:
            raise RuntimeError(f"SBUF arena overflow allocating {name} ({size} B); live={[(k, v[0], v[1]) for k, v in self.live.items()]}")
        inherit = []
        for (o, s, t) in self.hist:
            if o < off + size and off < o + s:
                inherit.extend(t.all_instrs())
        comp = {}
        for ins in inherit:
            key = ins.dma_sem if ins.dma_sem is not None else ins.eng
            cur = comp.get(key)
            if cur is None or (ins.dma_val if ins.dma_sem is not None else ins.idx) > (cur.dma_val if cur.dma_sem is not None else cur.idx):
                comp[key] = ins
        self.n_alloc += 1
        h = self.nc.alloc_sbuf_tensor_at(f"{name}_{self.n_alloc}", [128] + list(free_shape), dtype, offset=off)
        t = Ten(h, name, list(comp.values()))
        self.live[name] = (off, size, t)
        return t

    def free(self, *names):
        for name in names:
            o, s, t = self.live.pop(name)
            self.hist.append((o, s, t))

    def _need(self, c, p, raw):
        if p is None or p is c:
            return
        E = c.eng
        if p.dma_sem is not None:
            key = "dma:" + p.dma_sem
            if self.waited[E].get(key, 0) >= p.dma_val:
                return
            self.waited[E][key] = p.dma_val
            c.waits.append(p)
            return
        if p.eng == E and c.dma_sem is None:
            if E in ("pe", "sp"):
                return
            if not raw:
                return
        key = p.eng
        if self.waited[E].get(key, -1) >= p.idx:
            return
        self.waited[E][key] = p.idx
        p.signal = True
        c.waits.append(p)

    def _deps(self, c, reads, writes):
        for b in reads:
            self._need(c, b.last_w, True)
        for b in writes:
            self._need(c, b.last_w, False)
            for r in b.readers:
                self._need(c, r, False)
        for b in reads:
            b.readers.append(c)
            if len(b.readers) > 12:
                comp = {}
                for ins in b.readers:
                    key = ins.dma_sem if ins.dma_sem is not None else ins.eng
                    cur = comp.get(key)
                    if cur is None or (ins.dma_val if ins.dma_sem is not None else ins.idx) >= (cur.dma_val if cur.dma_sem is not None else cur.idx):
                        comp[key] = ins
                b.readers = list(comp.values())
        for b in writes:
            b.last_w = c
            b.readers = []

    def op(self, eng, fn, reads=(), writes=()):
        c = Instr(eng, fn)
        c.idx = len(self.q[eng])
        self.q[eng].append(c)
        self._deps(c, reads, writes)
        return c

    NRING = 28

    def dma(self, eng, stream, out, in_, reads=(), writes=(), final=False):
        def fn(e, out=out, in_=in_):
            return e.dma_start(out=out, in_=in_)
        c = Instr(eng, fn)
        c.idx = len(self.q[eng])
        pos = self.ring_pos.get(eng, 0)
        self.ring_pos[eng] = pos + 1
        sem = f"{eng}{pos % self.NRING}"
        n = self.dma_count.get(sem, 0) + 1
        self.dma_count[sem] = n
        c.dma_sem = sem
        c.dma_val = 16 * n
        prev = self.ring_last.get(sem)
        if prev is not None:
            self._need(c, prev, False)
        self.ring_last[sem] = c
        self.q[eng].append(c)
        self._deps(c, reads, writes)
        if final:
            self.final_dma.append(c)
        return c

    def barrier(self):
        lasts = {}
        for e in ENGS:
            for ins in reversed(self.q[e]):
                if ins.dma_sem is None:
                    lasts[e] = ins
                    break
        dmas = []
        for sname, n in self.dma_count.items():
            f = Instr("sp", None)
            f.dma_sem = sname
            f.dma_val = 16 * n
            dmas.append(f)
        for e in ENGS:
            c = Instr(e, lambda eng: eng.nop())
            c.idx = len(self.q[e])
            for e2, p in lasts.items():
                if e2 != e:
                    self._need(c, p, True)
            for f in dmas:
                self._need(c, f, True)
            self.q[e].append(c)

    def emit(self):
        nc = self.nc
        import contextlib
        with contextlib.ExitStack() as st:
            esem = {e: st.enter_context(nc.semaphore(f"s_{e}")) for e in ENGS}
            dsem = {s: st.enter_context(nc.semaphore(f"d_{s}")) for s in self.dma_count}
            for e in ENGS:
                cnt = 0
                for ins in self.q[e]:
                    if ins.signal:
                        cnt += 1
                        ins.val = cnt
            block = st.enter_context(nc.Block())

            def run(eng_name, eobj):
                for ins in self.q[eng_name]:
                    for p in ins.waits:
                        if p.dma_sem is not None:
                            eobj.wait_ge(dsem[p.dma_sem], p.dma_val)
                        else:
                            eobj.wait_ge(esem[p.eng], p.val)
                    r = ins.fn(eobj)
                    if ins.dma_sem is not None:
                        r.then_inc(dsem[ins.dma_sem], 16)
                    elif ins.signal:
                        r.then_inc(esem[eng_name], 1)
                if eng_name == "sp":
                    done = {}
                    for ins in self.final_dma:
                        done[ins.dma_sem] = max(done.get(ins.dma_sem, 0), ins.dma_val)
                    for s, v in done.items():
                        eobj.wait_ge(dsem[s], v)

            @block.tensor
            def _(e):
                run("pe", e)

            @block.scalar
            def _(e):
                run("act", e)

            @block.vector
            def _(e):
                run("dve", e)

            @block.gpsimd
            def _(e):
                run("pool", e)

            @block.sync
            def _(e):
                run("sp", e)


T = 2048
D = 1024
NTT = 16
ALPHA = float(2.0 ** 0.25)
PI = float(np.pi)
LAMBDA_INIT = 0.8 - 0.6 * 1.0
FFN_H = 2816
NJ = 22


def t5_bucket_np(d):
    d = np.asarray(d, dtype=np.int64)
    df = np.maximum(d, 1).astype(np.float32)
    large = 16 + (np.log(df / np.float32(16)) / np.float32(np.log(128 / 16)) * np.float32(16)).astype(np.int32)
    large = np.minimum(large, 31)
    return np.where(d < 16, d, large)


def host_consts():
    c = {}
    c["c_ident"] = np.eye(128, dtype=np.float32)
    c["c_tri"] = np.triu(np.ones((128, 128), dtype=np.float32))
    cst = np.zeros((128, 8), dtype=np.float32)
    tp1 = np.arange(1, 129, dtype=np.float32)
    cst[:, 0] = tp1
    cst[:, 1] = -tp1
    cst[:, 2] = tp1 / np.float32(2 * np.pi)
    cst[:, 3] = 1.5
    cst[:, 4] = 1.75
    cst[:, 5] = -np.pi
    cst[:, 6] = 1e-5
    cst[:, 7] = 1.0
    c["c_cst"] = cst
    c["c_trow"] = np.broadcast_to(tp1[None, :], (128, 128)).astype(np.float32).copy()
    p = np.arange(128)
    mask = np.zeros((128, 4, 128), dtype=np.float32)
    for jm in range(4):
        mask[:, jm, :] = ((p[None, :] // 16) == (2 * jm + p[:, None] // 64)).astype(np.float32)
    c["c_maskC"] = mask
    c["c_maskB"] = np.ascontiguousarray(mask.transpose(2, 1, 0))
    oh = np.zeros((33, 384), dtype=np.float32)
    for m in range(384):
        d = m - 127
        if d < 0:
            oh[32, m] = -30000.0
        else:
            b = int(t5_bucket_np(d))
            oh[b, m] += 8.0
            oh[31, m] -= 8.0
    c["c_oh"] = oh
    return c


def build_program(debug=False, upto=None):
    import os
    nc = bass.Bass("TRN2", target_bir_lowering=False)
    S = Sched(nc)

    def din(name, shape):
        return nc.dram_tensor(name, list(shape), F32, kind="ExternalInput").ap()

    x = din("x", [T, D]); mem = din("mem", [256, D])
    ln_in_g = din("ln_in_g", [D]); ln_in_b = din("ln_in_b", [D])
    w_in = din("w_in", [D, 2048])
    lam_re = din("s5_lambda_re", [2048]); lam_im = din("s5_lambda_im", [2048]); log_dt = din("s5_log_dt", [32])
    b_re = din("s5_b_re", [2048, 16]); b_im = din("s5_b_im", [2048, 16])
    c_re = din("s5_c_re", [512, 64]); c_im = din("s5_c_im", [512, 64])
    s5_d = din("s5_d", [512]); glu_w = din("s5_glu_w", [512, 512]); glu_b = din("s5_glu_b", [512])
    lq1 = din("diff_lq1", [64]); lk1 = din("diff_lk1", [64]); lq2 = din("diff_lq2", [64]); lk2 = din("diff_lk2", [64])
    subln_g = din("diff_subln_g", [128]); rel_bias = din("rel_bias", [32, 4])
    w_out = din("w_out", [D, D]); ln1_g = din("ln1_g", [D]); ln1_b = din("ln1_b", [D])
    ca_wq = din("ca_wq", [D, D]); ca_wkv = din("ca_wkv", [D, 2 * D]); ca_wo = din("ca_wo", [D, D])
    ln2_g = din("ln2_g", [D]); ln2_b = din("ln2_b", [D])
    w_gu = din("ffn_w_gate_up", [D, 2 * FFN_H]); w_dn = din("ffn_w_down", [FFN_H, D])
    ln3_g = din("ln3_g", [D]); ln3_b = din("ln3_b", [D])
    c_ident = din("c_ident", [128, 128]); c_tri = din("c_tri", [128, 128]); c_cst = din("c_cst", [128, 8])
    c_trow = din("c_trow", [128, 128]); c_maskC = din("c_maskC", [128, 4, 128]); c_maskB = din("c_maskB", [128, 4, 128])
    c_oh = din("c_oh", [33, 384])
    out = nc.dram_tensor("out", [T, D], F32, kind="ExternalOutput").ap()
    scr = nc.dram_tensor("bias_scr", [128, 4, 384], F32, kind="Internal").ap()
    dbg = {}
    if debug:
        for nm, shp, dt_ in (("d_hT", [128, 8, T], BF16), ("d_uT", [128, 4, T], BF16), ("d_qT", [128, 4, T], BF16),
                             ("d_kT", [128, 4, T], BF16), ("d_v", [128, 16, 4, 129], BF16), ("d_cat", [128, 8, T], BF16),
                             ("d_h1", [128, 16, D], F32), ("d_h2", [128, 16, D], F32)):
            dbg[nm] = nc.dram_tensor(nm, shp, dt_, kind="ExternalOutput").ap()

    banks = [Ten(nc.alloc_psum_tensor(f"bank{i}", [128, 512], F32), f"bank{i}") for i in range(8)]

    class View:
        def __init__(self, ten):
            self.ten = ten
            self.ap = ten[:, :].bitcast(BF16).rearrange("p (k c) -> p k c", k=8)

        def __getitem__(self, k):
            return self.ap[k]

        def buf(self, key=0):
            return self.ten.buf(key)

    ptb = [View(banks[6]), View(banks[7])]

    idf = S.alloc("idf", [128], F32, 4)
    idb = S.alloc("idb", [128], BF16, 2)
    trib = S.alloc("trib", [128], BF16, 2)
    trif = S.alloc("trif", [128], F32, 4)
    onesb = S.alloc("onesb", [128], BF16, 2)
    onesf = S.alloc("onesf", [128], F32, 4)
    cst = S.alloc("cst", [8], F32, 4)
    stt = S.alloc("ln_st", [8, 12], F32, 4)
    mvt = S.alloc("ln_mv", [8, 2], F32, 4)
    rsd = S.alloc("ln_rs", [8, 1], F32, 4)
    S.dma("sp", "c0", idf[:, :], c_ident, writes=[idf.buf()])
    S.dma("sp", "c0", trif[:, :], c_tri, writes=[trif.buf()])
    S.dma("sp", "c0", cst[:, :], c_cst, writes=[cst.buf()])
    S.op("dve", lambda e: e.tensor_copy(out=idb[:, :], in_=idf[:, :]), reads=[idf.buf()], writes=[idb.buf()])
    S.op("dve", lambda e: e.tensor_copy(out=trib[:, :], in_=trif[:, :]), reads=[trif.buf()], writes=[trib.buf()])
    S.op("pool", lambda e: e.memset(onesb[:, :], 1.0), writes=[onesb.buf()])
    S.op("pool", lambda e: e.memset(onesf[:, :], 1.0), writes=[onesf.buf()])
    EPS = cst[:, 6:7]

    ctr = {"ev": 0, "ln": 0, "pt": 0}

    def evac_eng():
        ctr["ev"] += 1
        return "act" if ctr["ev"] % 2 else "dve"

    def copy_op(eng, out_ap, in_ap, reads, writes):
        if eng == "act":
            S.op("act", lambda e: e.activation(out=out_ap, in_=in_ap, func=AF.Copy), reads, writes)
        else:
            S.op(eng, lambda e: e.tensor_copy(out=out_ap, in_=in_ap), reads, writes)

    def mm(out_ap, lhsT, rhs, start, stop, reads, writes):
        S.op("pe", lambda e: e.matmul(out=out_ap, lhsT=lhsT, rhs=rhs, start=start, stop=stop), reads, writes)

    def load_lnp(name, g, b):
        t = S.alloc(name, [2, D], F32, 4)
        S.dma("sp", "lnp", t[:, 0, :], g.partition_broadcast(128), writes=[t.buf()])
        S.dma("sp", "lnp", t[:, 1, :], b.partition_broadcast(128), writes=[t.buf()])
        return t

    def ln_stats(x_ap, x_buf):
        ctr["ln"] += 1
        sl = ctr["ln"] % 8
        for c in range(2):
            S.op("dve", lambda e, c=c: e.bn_stats(out=stt[:, sl, c * 6:(c + 1) * 6], in_=x_ap[:, c * 512:(c + 1) * 512]),
                 reads=[x_buf], writes=[stt.buf(sl)])
        S.op("dve", lambda e: e.bn_aggr(out=mvt[:, sl, :], in_=stt[:, sl, :]), reads=[stt.buf(sl)], writes=[mvt.buf(sl)])
        S.op("act", lambda e: e.activation(out=rsd[:, sl, :], in_=mvt[:, sl, 1:2], func=AF.Sqrt, bias=EPS, scale=1.0),
             reads=[mvt.buf(sl), cst.buf()], writes=[rsd.buf(sl)])
        S.op("dve", lambda e: e.reciprocal(out=rsd[:, sl, :], in_=rsd[:, sl, :]), reads=[rsd.buf(sl)], writes=[rsd.buf(sl)])
        return sl

    def ln_apply(x_ap, x_buf, lnp, out_ap, out_buf, sl):
        S.op("dve", lambda e: e.scalar_tensor_tensor(out=mvt[:, sl, 1:2], in0=mvt[:, sl, 0:1], scalar=-1.0, in1=rsd[:, sl, :],
                                                      op0=ALU.mult, op1=ALU.mult),
             reads=[mvt.buf(sl), rsd.buf(sl)], writes=[mvt.buf(sl)])
        S.op("act", lambda e: e.activation(out=x_ap, in_=x_ap, func=AF.Identity, bias=mvt[:, sl, 1:2], scale=rsd[:, sl, 0:1]),
             reads=[x_buf, mvt.buf(sl), rsd.buf(sl)], writes=[x_buf])
        S.op("pool", lambda e: e.tensor_tensor(out=x_ap, in0=x_ap, in1=lnp[:, 0, :], op=ALU.mult), reads=[x_buf, lnp.buf()], writes=[x_buf])
        if out_ap is not None:
            ln_apply_b(x_ap, x_buf, lnp, out_ap, out_buf)

    def ln_apply_b(x_ap, x_buf, lnp, out_ap, out_buf):
        S.op("dve", lambda e: e.tensor_tensor(out=out_ap, in0=x_ap, in1=lnp[:, 1, :], op=ALU.add), reads=[x_buf, lnp.buf()],
             writes=[out_buf] if out_buf is not x_buf else [x_buf])

    def transpose_to_hT(lb, lb_buf, hT, tt):
        ctr["pt"] += 1
        pt = ptb[ctr["pt"] % 2]
        for k in range(8):
            S.op("pe", lambda e, k=k: e.transpose(out=pt[:, k, :], in_=lb[:, k * 128:(k + 1) * 128], identity=idb[:, :]),
                 reads=[lb_buf, idb.buf()], writes=[pt.buf()])
        copy_op(evac_eng(), hT[:, :, tt * 128:(tt + 1) * 128], pt[:, :, :], [pt.buf()], [hT.buf(tt)])

    def wload(name, src, kt, ncols, chunk, stream):
        t = S.alloc(name, [kt, ncols], BF16, 2)
        sv = src.rearrange("(kt p) n -> p kt n", p=128)
        for c in range(ncols // chunk):
            S.dma("pool", stream, t[:, :, c * chunk:(c + 1) * chunk], sv[:, :, c * chunk:(c + 1) * chunk], writes=[t.buf(c)])
        return t

    hT = S.alloc("hT", [8, T], BF16, 2)
    wi = wload("wi", w_in, 8, 2048, 512, "w_a")
    lnp0 = load_lnp("lnp0", ln_in_g, ln_in_b)
    xin = S.alloc("xin", [4, D], F32, 4)
    lbt = S.alloc("lbt", [2, D], BF16, 2)
    sls = {}
    for tt in range(NTT + 3):
        if tt < NTT:
            s3 = tt % 4
            S.dma("sp", "xin", xin[:, s3, :], x[tt * 128:(tt + 1) * 128, :], writes=[xin.buf(s3)])
            sls[tt] = ln_stats(xin[:, s3, :], xin.buf(s3))
        if 1 <= tt <= NTT:
            t1 = tt - 1
            ln_apply(xin[:, t1 % 4, :], xin.buf(t1 % 4), lnp0, None, None, sls[t1])
        if 2 <= tt <= NTT + 1:
            t2 = tt - 2
            ln_apply_b(xin[:, t2 % 4, :], xin.buf(t2 % 4), lnp0, lbt[:, t2 % 2, :], lbt.buf(t2 % 2))
        if tt >= 3:
            t3 = tt - 3
            transpose_to_hT(lbt[:, t3 % 2, :], lbt.buf(t3 % 2), hT, t3)
    S.free("xin", "lnp0")
    if debug:
        S.dma("sp", "dbg", dbg["d_hT"], hT[:, :, :], reads=hT.bufs(range(16)), final=True)
    if upto == "A":
        S.emit()
        return nc


    if str(0) in os.environ.get("BARRIERS", ""):
        S.barrier()
    rb = S.alloc("rb", [4], F32, 4)
    chc = S.alloc("chc", [4], F32, 4)
    ohs = S.alloc("ohs", [384], F32, 4)
    Lh = S.alloc("Lh", [4, 128], F32, 4)
    gsb = S.alloc("gsb", [4, 384], F32, 4)
    biasf = S.alloc("biasf", [4, 2, 128], F32, 4)
    biasb = S.alloc("biasb", [4, 2, 128], BF16, 2)
    lqk = S.alloc("lqk", [4, 64], F32, 4)
    lsm = S.alloc("lsm", [4], F32, 4)
    gsub = S.alloc("gsub", [128], F32, 4)
    scrb = Ten(None, "scr")
    NEGLAM = lsm[:, 2:3]

    def dsetup_1():
        S.dma("sp", "c1", rb[0:32, :], rel_bias, writes=[rb.buf()])
        S.dma("sp", "c1", chc[:, :], rel_bias[31, :].partition_broadcast(128), writes=[chc.buf()])
        S.dma("sp", "c1", ohs[0:33, :], c_oh, writes=[ohs.buf()])
        for i, v in enumerate((lq1, lk1, lq2, lk2)):
            S.dma("sp", "c1", lqk[:, i, :], v.partition_broadcast(128), writes=[lqk.buf()])
        S.dma("sp", "c1", gsub[:, :], subln_g.partition_broadcast(128), writes=[gsub.buf()])
        S.op("pool", lambda e: e.memset(Lh[0:33, :, :], 1.0), writes=[Lh.buf()])

    def dsetup_2a():
        for h in range(4):
            S.op("dve", lambda e, h=h: e.tensor_scalar(out=Lh[0:32, h, :], in0=onesf[0:32, :], scalar1=rb[0:32, h:h + 1], scalar2=None, op0=ALU.mult),
                 reads=[onesf.buf(), rb.buf(), Lh.buf()], writes=[Lh.buf()])
        S.op("dve", lambda e: e.tensor_mul(out=lqk[:, 0, :], in0=lqk[:, 0, :], in1=lqk[:, 1, :]), reads=[lqk.buf()], writes=[lqk.buf()])
        S.op("dve", lambda e: e.tensor_mul(out=lqk[:, 2, :], in0=lqk[:, 2, :], in1=lqk[:, 3, :]), reads=[lqk.buf()], writes=[lqk.buf()])
        S.op("dve", lambda e: e.reduce_sum(out=lsm[:, 0:1], in_=lqk[:, 0, :], axis=AX.X), reads=[lqk.buf()], writes=[lsm.buf()])
        S.op("dve", lambda e: e.reduce_sum(out=lsm[:, 1:2], in_=lqk[:, 2, :], axis=AX.X), reads=[lqk.buf()], writes=[lsm.buf()])
        S.op("dve", lambda e: e.tensor_scalar(out=gsub[:, :], in0=gsub[:, :], scalar1=1.0 - LAMBDA_INIT, scalar2=None, op0=ALU.mult),
             reads=[gsub.buf()], writes=[gsub.buf()])

    def dsetup_2b():
        S.op("act", lambda e: e.activation(out=lsm[:, 0:2], in_=lsm[:, 0:2], func=AF.Exp), reads=[lsm.buf()], writes=[lsm.buf()])
        for h in range(4):
            bk = banks[4 + h % 2]
            mm(bk[:, 0:384], Lh[0:33, h, :], ohs[0:33, :], True, True, [Lh.buf(), ohs.buf()], [bk.buf()])
            copy_op("act", gsb[:, h, :], bk[:, 0:384], [bk.buf()], [gsb.buf()])
        S.dma("sp", "scr", scr, gsb[:, :, :], reads=[gsb.buf()], writes=[scrb.buf()])

    def dsetup_3():
        S.op("dve", lambda e: e.tensor_sub(out=lsm[:, 2:3], in0=lsm[:, 1:2], in1=lsm[:, 0:1]), reads=[lsm.buf()], writes=[lsm.buf()])
        S.op("dve", lambda e: e.tensor_scalar(out=lsm[:, 2:3], in0=lsm[:, 2:3], scalar1=-LAMBDA_INIT, scalar2=None, op0=ALU.add),
             reads=[lsm.buf()], writes=[lsm.buf()])
        for h in range(4):
            for dsub, off in enumerate((127, 255)):
                S.dma("sp", "scr2", biasf[:, h, dsub, :], bass.AP(scr.tensor, h * 384 + off, [[1535, 128], [1, 128]]),
                      reads=[scrb.buf()], writes=[biasf.buf()])

    def dsetup_4():
        S.op("dve", lambda e: e.tensor_copy(out=biasb[:, :, :, :], in_=biasf[:, :, :, :]), reads=[biasf.buf()], writes=[biasb.buf()])

    dsetup_hooks = {0: dsetup_1, 6: dsetup_2a, 12: dsetup_2b, 28: dsetup_3, 44: dsetup_4}

    uT = S.alloc("uT", [4, T], BF16, 2)
    qT = S.alloc("qT", [4, T], BF16, 2)
    kT = S.alloc("kT", [4, T], BF16, 2)
    vaug = S.alloc("vaug", [16, 4, 129], BF16, 2)
    import os
    for tt in range(NTT):
        S.op("dve", lambda e, tt=tt: e.tensor_copy(out=vaug[:, tt, :, 128:129], in_=onesb[:, 0:4].unsqueeze(2)), reads=[onesb.buf()], writes=[vaug.buf(tt)])
    bi = 0
    for grp, dst in enumerate((uT, qT, kT)):
        for ct in range(4):
            col = grp * 512 + ct * 128
            for tb in range(4):
                if bi in dsetup_hooks:
                    dsetup_hooks[bi]()
                bk = banks[bi % 4]; bi += 1
                for kt in range(8):
                    mm(bk[:, :], wi[:, kt, col:col + 128], hT[:, kt, tb * 512:(tb + 1) * 512], kt == 0, kt == 7,
                       [wi.buf(grp)] + hT.bufs(range(4 * tb, 4 * tb + 4)), [bk.buf()])
                copy_op(evac_eng(), dst[:, ct, tb * 512:(tb + 1) * 512], bk[:, :], [bk.buf()], dst.bufs([(ct, 4 * tb + i) for i in range(4)]))
    for tt in range(0 if not os.environ.get("NO_V") else NTT, NTT):
        bk = banks[bi % 4]; bi += 1
        for kt in range(8):
            mm(bk[:, :], hT[:, kt, tt * 128:(tt + 1) * 128], wi[:, kt, 1536:2048], kt == 0, kt == 7,
               [wi.buf(3), hT.buf(tt)], [bk.buf()])
        copy_op(evac_eng(), vaug[:, tt, :, 0:128], bk[:, :].rearrange("p (h d) -> p h d", h=4), [bk.buf()], [vaug.buf(tt)])
    S.free("wi", "lbt", "hT")
    if debug:
        S.dma("sp", "dbg", dbg["d_uT"], uT[:, :, :], reads=uT.bufs([(c, t) for c in range(4) for t in range(16)]), final=True)
        S.dma("sp", "dbg", dbg["d_qT"], qT[:, :, :], reads=qT.bufs([(c, t) for c in range(4) for t in range(16)]), final=True)
        S.dma("sp", "dbg", dbg["d_kT"], kT[:, :, :], reads=kT.bufs([(c, t) for c in range(4) for t in range(16)]), final=True)
        S.dma("sp", "dbg", dbg["d_v"], vaug[:, :, :, :], reads=vaug.bufs(range(16)), final=True)
    if upto == "B":
        S.emit()
        return nc


    catT = S.alloc("catT", [8, T], BF16, 2)

    if str(1) in os.environ.get("BARRIERS", ""):
        S.barrier()
    PT = S.alloc("PT", [6, 512], BF16, 2)
    gcol = S.alloc("gcol", [1], F32, 4)
    S.dma("sp", "c1", gcol[:, :], subln_g.rearrange("(p o) -> p o", o=1), writes=[gcol.buf()])
    S.op("dve", lambda e: e.tensor_scalar(out=gcol[:, :], in0=gcol[:, :], scalar1=1.0 - LAMBDA_INIT, scalar2=None, op0=ALU.mult),
         reads=[gcol.buf()], writes=[gcol.buf()])
    rr = S.alloc("rr", [2, 2, 512], F32, 4)
    o1 = S.alloc("o1", [2, 512], F32, 4)
    oo = S.alloc("oo", [2, 512], F32, 4)
    sqb = S.alloc("sqb", [2, 512], BF16, 2)
    rst = S.alloc("rst", [2, 512], F32, 4)
    stb = (banks[0], banks[1])
    Ab = (banks[2], banks[3])
    Sb = (banks[4], banks[5])
    MSb = banks[6]

    def s_stage(it):
        h, I, s, j, st, pt = it
        r0 = s * 64
        qstart = max(512 * I, 128 * j)
        N = 512 * (I + 1) - qstart
        col0 = qstart - 512 * I
        has_diag = j >= 4 * I
        has_sub = (4 * I - 1) <= j <= (4 * I + 2)
        mm(st[:, col0:col0 + N], kT[r0:r0 + 64, h, j * 128:(j + 1) * 128], qT[r0:r0 + 64, h, qstart:qstart + N],
           True, not (has_diag or has_sub),
           [kT.buf((h, j))] + qT.bufs([(h, t) for t in range(qstart // 128, 4 * I + 4)]), [st.buf()])
        if has_diag:
            c = j * 128 - 512 * I
            mm(st[:, c:c + 128], idb[:, :], biasb[:, h, 0, :], False, not has_sub, [idb.buf(), biasb.buf()], [st.buf()])
        if has_sub:
            c = (j + 1) * 128 - 512 * I
            mm(st[:, c:c + 128], idb[:, :], biasb[:, h, 1, :], False, True, [idb.buf(), biasb.buf()], [st.buf()])
        S.op("act", lambda e, st=st, pt=pt, col0=col0, N=N, h=h: e.activation(
            out=PT[:, pt, col0:col0 + N], in_=st[:, col0:col0 + N], func=AF.Exp, bias=chc[:, h:h + 1], scale=0.125),
            reads=[st.buf(), chc.buf()], writes=[PT.buf(pt)])

    def pv_stage(it):
        h, I, s, j, st, pt = it
        qstart = max(512 * I, 128 * j)
        N = 512 * (I + 1) - qstart
        col0 = qstart - 512 * I
        last = (j == 4 * I + 3)
        S.op("pe", lambda e, s=s, pt=pt, col0=col0, N=N, j=j, h=h, last=last: e.matmul(
            out=Ab[s][:, col0:col0 + N], lhsT=vaug[:, j, h, 0:128], rhs=PT[:, pt, col0:col0 + N], start=(j == 0), stop=last, skip_group_check=True),
            [PT.buf(pt), vaug.buf(j)], [Ab[s].buf()])
        S.op("pe", lambda e, s=s, pt=pt, col0=col0, N=N, j=j, last=last: e.matmul(
            out=Sb[s][:, col0:col0 + N], lhsT=onesb[:, :], rhs=PT[:, pt, col0:col0 + N], start=(j == 0), stop=last, skip_group_check=True),
            [PT.buf(pt), onesb.buf()], [Sb[s].buf()])

    def epilogue_a(h, I, rnd):
        r2 = rnd % 2
        for s in range(2):
            S.op("act", lambda e, s=s, r2=r2: e.activation(out=rr[:, r2, s, :], in_=Sb[s][:, :], func=AF.Ln), reads=[Sb[s].buf()], writes=[rr.buf((r2, s))])
            S.op("act", lambda e, s=s, r2=r2: e.activation(out=rr[:, r2, s, :], in_=rr[:, r2, s, :], func=AF.Exp, scale=-1.0), reads=[rr.buf((r2, s))], writes=[rr.buf((r2, s))])
        tt_op("dve", o1[:, r2, :], Ab[0][:, :], rr[:, r2, 0, :], ALU.mult, [Ab[0].buf(), rr.buf((r2, 0))], [o1.buf(r2)])
        tt_op("dve", oo[:, r2, :], Ab[1][:, :], rr[:, r2, 1, :], ALU.mult, [Ab[1].buf(), rr.buf((r2, 1))], [oo.buf(r2)])
        S.op("dve", lambda e, r2=r2: e.scalar_tensor_tensor(out=oo[:, r2, :], in0=oo[:, r2, :], scalar=NEGLAM, in1=o1[:, r2, :], op0=ALU.mult, op1=ALU.add),
             reads=[oo.buf(r2), o1.buf(r2), lsm.buf()], writes=[oo.buf(r2)])
        tt_op("dve", sqb[:, r2, :], oo[:, r2, :], oo[:, r2, :], ALU.mult, [oo.buf(r2)], [sqb.buf(r2)])

    def epilogue_b(h, I, rnd):
        r2 = rnd % 2
        mm(MSb[:, :], onesb[:, :], sqb[:, r2, :], True, True, [onesb.buf(), sqb.buf(r2)], [MSb.buf()])
        S.op("act", lambda e, r2=r2: e.activation(out=rst[:, r2, :], in_=MSb[:, :], func=AF.Ln, bias=EPS, scale=1.0 / 128.0),
             reads=[MSb.buf(), cst.buf()], writes=[rst.buf(r2)])
        S.op("act", lambda e, r2=r2: e.activation(out=rst[:, r2, :], in_=rst[:, r2, :], func=AF.Exp, scale=-0.5), reads=[rst.buf(r2)], writes=[rst.buf(r2)])
        S.op("dve", lambda e, r2=r2, h=h, I=I: e.scalar_tensor_tensor(out=catT[:, 4 + h, I * 512:(I + 1) * 512], in0=oo[:, r2, :], scalar=gcol[:, 0:1], in1=rst[:, r2, :],
                                                                    op0=ALU.mult, op1=ALU.mult),
             reads=[oo.buf(r2), gcol.buf(), rst.buf(r2)], writes=catT.bufs([(4 + h, 4 * I + i) for i in range(4)]))

    def tt_op(eng, out_ap, a_ap, b_ap, op, reads, writes):
        S.op(eng, lambda e: e.tensor_tensor(out=out_ap, in0=a_ap, in1=b_ap, op=op), reads, writes)

    stb4 = (banks[0], banks[1], banks[7], banks[6])
    iters = []
    k = 0
    for h in range(4):
        for I in range(4):
            for j in range(4 * I + 4):
                for s in range(2):
                    iters.append((h, I, s, j, stb4[k % 4], k % 6))
                    k += 1
    npair = len(iters) // 2
    pending = None
    since = 0
    rnd = 0
    for p in range(npair):
        s_stage(iters[2 * p]); s_stage(iters[2 * p + 1])
        since += 1
        if pending is not None and since >= 2:
            epilogue_b(*pending); pending = None
        if p >= 1:
            pv_stage(iters[2 * p - 2]); pv_stage(iters[2 * p - 1])
            prev, it = iters[2 * p - 1], iters[2 * p]
            if (prev[0], prev[1]) != (it[0], it[1]):
                if pending is not None:
                    epilogue_b(*pending); pending = None
                epilogue_a(prev[0], prev[1], rnd)
                pending = (prev[0], prev[1], rnd); since = 0
                rnd += 1
    pv_stage(iters[-2]); pv_stage(iters[-1])
    if pending is not None:
        epilogue_b(*pending)
    epilogue_a(iters[-1][0], iters[-1][1], rnd)
    epilogue_b(iters[-1][0], iters[-1][1], rnd)
    S.free("rb", "chc", "ohs", "Lh", "gsb", "biasf", "biasb", "lqk", "lsm", "gsub", "PT", "qT", "kT", "vaug", "gcol", "rr", "o1", "oo", "sqb", "rst")
    if upto == "D":
        S.emit()
        return nc


    if str(2) in os.environ.get("BARRIERS", ""):
        S.barrier()
    I32 = mybir.dt.int32
    W2 = 2048

    def A8(name, dt_=F32):
        return S.alloc(name, [W2], dt_, 4)

    def tt_op(eng, out_ap, a_ap, b_ap, op, reads, writes):
        S.op(eng, lambda e: e.tensor_tensor(out=out_ap, in0=a_ap, in1=b_ap, op=op), reads, writes)

    def ts_op(eng, out_ap, a_ap, s1, s2, op0, op1, reads, writes):
        if s2 is None:
            S.op(eng, lambda e: e.tensor_scalar(out=out_ap, in0=a_ap, scalar1=s1, scalar2=None, op0=op0), reads, writes)
        else:
            S.op(eng, lambda e: e.tensor_scalar(out=out_ap, in0=a_ap, scalar1=s1, scalar2=s2, op0=op0, op1=op1), reads, writes)

    ki = A8("ki", I32)
    kf = A8("kf")

    def sin_from_u(u, out):
        S.op("dve", lambda e: e.tensor_copy(out=ki[:, :], in_=u[:, :]), reads=[u.buf()], writes=[ki.buf()])
        S.op("dve", lambda e: e.tensor_copy(out=kf[:, :], in_=ki[:, :]), reads=[ki.buf()], writes=[kf.buf()])
        tt_op("dve", u[:, :], u[:, :], kf[:, :], ALU.subtract, [u.buf(), kf.buf()], [u.buf()])
        S.op("dve", lambda e: e.scalar_tensor_tensor(out=u[:, :], in0=u[:, :], scalar=0.0, in1=u[:, :], op0=ALU.is_lt, op1=ALU.add),
             reads=[u.buf()], writes=[u.buf()])
        S.op("act", lambda e: e.activation(out=out[:, :], in_=u[:, :], func=AF.Sin, bias=cst[:, 5:6], scale=2 * PI),
             reads=[u.buf(), cst.buf()], writes=[out.buf()])

    lr = A8("lr"); li = A8("li"); lrdt = A8("lrdt"); ang = A8("ang")
    dtr = S.alloc("dtr", [32], F32, 4)
    S.dma("sp", "c2", lr[:, :], lam_re.partition_broadcast(128), writes=[lr.buf()])
    S.dma("sp", "c2", li[:, :], lam_im.partition_broadcast(128), writes=[li.buf()])
    S.dma("sp", "c2", dtr[:, :], log_dt.partition_broadcast(128), writes=[dtr.buf()])
    S.op("act", lambda e: e.activation(out=dtr[:, :], in_=dtr[:, :], func=AF.Exp), reads=[dtr.buf()], writes=[dtr.buf()])
    dt_b = dtr[:, :].unsqueeze(2).to_broadcast([128, 32, 64])
    v3 = lambda t: t[:, :].rearrange("p (g s) -> p g s", s=64)
    tt_op("dve", v3(lrdt), v3(lr), dt_b, ALU.mult, [lr.buf(), dtr.buf()], [lrdt.buf()])
    tt_op("dve", v3(ang), v3(li), dt_b, ALU.mult, [li.buf(), dtr.buf()], [ang.buf()])
    mg = A8("mg"); sn = A8("sn"); cs = A8("cs"); ua = A8("ua")
    lnat = S.alloc("lnat", [2, 128], F32, 4)
    S.dma("sp", "c2", lnat[0:16, 0, :], lam_re.rearrange("(j p) -> j p", p=128), writes=[lnat.buf()])
    S.dma("sp", "c2", lnat[0:16, 1, :], lam_im.rearrange("(j p) -> j p", p=128), writes=[lnat.buf()])
    f16 = S.alloc("f16", [12, 16], F32, 4)
    bkf = banks[4]
    for i in range(2):
        S.op("pe", lambda e, i=i, bkf=bkf: e.transpose(out=bkf[:, i * 16:(i + 1) * 16], in_=lnat[0:16, i, :], identity=idf[0:16, 0:16]),
             reads=[lnat.buf(), idf.buf()], writes=[bkf.buf()])
    fb = [f16.buf()]
    copy_op("dve", f16[:, 0:2, :], bkf[:, 0:32].rearrange("p (a j) -> p a j", a=2), [bkf.buf()], fb)
    for hf in range(2):
        S.op("dve", lambda e, hf=hf: e.tensor_copy(out=f16[hf * 64:(hf + 1) * 64, 2, :],
                                                  in_=dtr[hf * 64:(hf + 1) * 64, :].rearrange("p (j two) -> p j two", two=2)[:, :, hf]),
             reads=[dtr.buf()] + fb, writes=fb)
    Fk = lambda k: f16[:, k, :]

    def f_tt(o, a, b, op):
        tt_op("dve", Fk(o), Fk(a), Fk(b), op, fb, fb)

    def sin_small(k):
        S.op("dve", lambda e: e.tensor_copy(out=ki[:, 0:16], in_=Fk(k)), reads=fb, writes=[ki.buf()])
        S.op("dve", lambda e: e.tensor_copy(out=kf[:, 0:16], in_=ki[:, 0:16]), reads=[ki.buf()], writes=[kf.buf()])
        tt_op("dve", Fk(k), Fk(k), kf[:, 0:16], ALU.subtract, fb + [kf.buf()], fb)
        S.op("dve", lambda e: e.scalar_tensor_tensor(out=Fk(k), in0=Fk(k), scalar=0.0, in1=Fk(k), op0=ALU.is_lt, op1=ALU.add), reads=fb, writes=fb)
        S.op("act", lambda e: e.activation(out=Fk(k), in_=Fk(k), func=AF.Sin, bias=cst[:, 5:6], scale=2 * PI), reads=fb + [cst.buf()], writes=fb)

    f_tt(3, 0, 2, ALU.mult)
    f_tt(4, 1, 2, ALU.mult)
    S.op("act", lambda e: e.activation(out=Fk(3), in_=Fk(3), func=AF.Exp), reads=fb, writes=fb)
    ts_op("dve", Fk(5), Fk(4), 1.0 / (2 * PI), 1.5, ALU.mult, ALU.add, fb, fb)
    sin_small(5)
    ts_op("dve", Fk(6), Fk(4), 1.0 / (2 * PI), 1.75, ALU.mult, ALU.add, fb, fb)
    sin_small(6)
    f_tt(6, 3, 6, ALU.mult)
    ts_op("dve", Fk(6), Fk(6), -1.0, None, ALU.add, None, fb, fb)
    f_tt(5, 3, 5, ALU.mult)
    f_tt(7, 0, 0, ALU.mult)
    f_tt(8, 1, 1, ALU.mult)
    f_tt(8, 7, 8, ALU.add)
    S.op("dve", lambda e: e.reciprocal(out=Fk(8), in_=Fk(8)), reads=fb, writes=fb)
    f_tt(9, 6, 0, ALU.mult)
    f_tt(7, 5, 1, ALU.mult)
    f_tt(9, 9, 7, ALU.add)
    f_tt(9, 9, 8, ALU.mult)
    f_tt(10, 5, 0, ALU.mult)
    f_tt(7, 6, 1, ALU.mult)
    f_tt(10, 10, 7, ALU.subtract)
    f_tt(10, 10, 8, ALU.mult)
    if os.environ.get("S5_STOP") == "1":
        S.emit()
        return nc
    S.free("lr", "li", "dtr", "lnat")
    Wmr = A8("Wmr"); Wmi = A8("Wmi")
    S.op("act", lambda e: e.activation(out=mg[:, :], in_=lrdt[:, :], func=AF.Exp, scale=cst[:, 1:2]), reads=[lrdt.buf(), cst.buf()], writes=[mg.buf()])
    ts_op("dve", ua[:, :], ang[:, :], cst[:, 2:3], cst[:, 3:4], ALU.mult, ALU.add, [ang.buf(), cst.buf()], [ua.buf()])
    sin_from_u(ua, sn)
    ts_op("dve", ua[:, :], ang[:, :], cst[:, 2:3], cst[:, 4:5], ALU.mult, ALU.add, [ang.buf(), cst.buf()], [ua.buf()])
    sin_from_u(ua, cs)
    tt_op("dve", Wmr[:, :], mg[:, :], cs[:, :], ALU.mult, [mg.buf(), cs.buf()], [Wmr.buf()])
    S.op("dve", lambda e: e.scalar_tensor_tensor(out=Wmi[:, :], in0=mg[:, :], scalar=-1.0, in1=sn[:, :], op0=ALU.mult, op1=ALU.mult),
         reads=[mg.buf(), sn.buf()], writes=[Wmi.buf()])
    if os.environ.get("S5_STOP") == "2":
        S.emit()
        return nc
    trow = S.alloc("trow", [128], F32, 4)
    S.dma("sp", "c2", trow[:, :], c_trow, writes=[trow.buf()])
    lrdtT = A8("lrdtT"); angT = A8("angT")
    bi = 0
    for (src, dst) in ((lrdt, lrdtT), (ang, angT)):
        for q4 in range(4):
            bk = banks[bi % 4]; bi += 1
            for i in range(4):
                j = q4 * 4 + i
                S.op("pe", lambda e, bk=bk, i=i, j=j, src=src: e.transpose(out=bk[:, i * 128:(i + 1) * 128], in_=src[:, j * 128:(j + 1) * 128], identity=idf[:, :]),
                     reads=[src.buf(), idf.buf()], writes=[bk.buf()])
            copy_op("act", dst[:, q4 * 512:(q4 + 1) * 512], bk[:, :], [bk.buf()], [dst.buf()])
    S.free("lrdt", "ang")
    WpTr = A8("WpTr"); WpTi = A8("WpTi")
    trow_b = trow[:, :].unsqueeze(1).to_broadcast([128, 16, 128])
    v16 = lambda t: t[:, :].rearrange("p (j t) -> p j t", t=128)
    tt_op("dve", v16(lrdtT), v16(lrdtT), trow_b, ALU.mult, [lrdtT.buf(), trow.buf()], [lrdtT.buf()])
    S.op("act", lambda e: e.activation(out=mg[:, :], in_=lrdtT[:, :], func=AF.Exp), reads=[lrdtT.buf()], writes=[mg.buf()])
    tt_op("dve", v16(angT), v16(angT), trow_b, ALU.mult, [angT.buf(), trow.buf()], [angT.buf()])
    ts_op("dve", ua[:, :], angT[:, :], 1.0 / (2 * PI), 1.5, ALU.mult, ALU.add, [angT.buf()], [ua.buf()])
    sin_from_u(ua, sn)
    ts_op("dve", ua[:, :], angT[:, :], 1.0 / (2 * PI), 1.75, ALU.mult, ALU.add, [angT.buf()], [ua.buf()])
    sin_from_u(ua, cs)
    tt_op("dve", WpTr[:, :], mg[:, :], cs[:, :], ALU.mult, [mg.buf(), cs.buf()], [WpTr.buf()])
    tt_op("dve", WpTi[:, :], mg[:, :], sn[:, :], ALU.mult, [mg.buf(), sn.buf()], [WpTi.buf()])
    S.free("lrdtT", "angT", "ua", "ki", "kf", "mg", "trow")
    if os.environ.get("S5_STOP") == "3":
        S.emit()
        return nc
    maskB = S.alloc("maskB", [4, 128], F32, 4)
    maskC = S.alloc("maskC", [4, 128], F32, 4)
    S.dma("sp", "c2", maskB[:, :, :], c_maskB, writes=[maskB.buf()])
    S.dma("sp", "c2", maskC[:, :, :], c_maskC, writes=[maskC.buf()])
    bnat = S.alloc("bnat", [2, 16, 16], F32, 4)
    S.dma("sp", "c2", bnat[:, 0, :, :], b_re.rearrange("(j p) h -> p j h", p=128), writes=[bnat.buf()])
    S.dma("sp", "c2", bnat[:, 1, :, :], b_im.rearrange("(j p) h -> p j h", p=128), writes=[bnat.buf()])
    bbar = S.alloc("bbar", [2, 16, 16], F32, 4)
    tb4 = S.alloc("tb4", [4, 16, 16], F32, 4)
    fr_b = f16[:, 9, :].unsqueeze(2).to_broadcast([128, 16, 16])
    fi_b = f16[:, 10, :].unsqueeze(2).to_broadcast([128, 16, 16])
    rdb = [bnat.buf(), f16.buf()]
    tt_op("dve", tb4[:, 0, :, :], bnat[:, 0, :, :], fr_b, ALU.mult, rdb, [tb4.buf()])
    tt_op("dve", tb4[:, 1, :, :], bnat[:, 1, :, :], fi_b, ALU.mult, rdb, [tb4.buf()])
    tt_op("dve", tb4[:, 2, :, :], bnat[:, 0, :, :], fi_b, ALU.mult, rdb, [tb4.buf()])
    tt_op("dve", tb4[:, 3, :, :], bnat[:, 1, :, :], fr_b, ALU.mult, rdb, [tb4.buf()])
    tt_op("dve", bbar[:, 0, :, :], tb4[:, 0, :, :], tb4[:, 1, :, :], ALU.subtract, [tb4.buf()], [bbar.buf()])
    tt_op("dve", bbar[:, 1, :, :], tb4[:, 2, :, :], tb4[:, 3, :, :], ALU.add, [tb4.buf()], [bbar.buf()])
    bn8 = S.alloc("bn8", [2, 16, 8, 16], F32, 4)
    for ri in range(2):
        S.op("dve", lambda e, ri=ri: e.tensor_copy(out=bn8[:, ri, :, :, :], in_=bbar[:, ri, :, :].unsqueeze(2).to_broadcast([128, 16, 8, 16])),
             reads=[bbar.buf()], writes=[bn8.buf()])
    Bmr = S.alloc("Bmr", [4, 512], BF16, 2)
    Bmi = S.alloc("Bmi", [4, 512], BF16, 2)
    tq = S.alloc("tq", [4, 128], F32, 4)
    for j in range(16):
        ctile, jm = j // 4, j % 4
        bk = banks[j % 4]
        for ri in range(2):
            S.op("pe", lambda e, bk=bk, ri=ri, j=j: e.transpose(out=bk[:, ri * 128:(ri + 1) * 128],
                                                             in_=bn8[:, ri, j, :, :].rearrange("p c h -> p (c h)"), identity=idf[:, :]),
                 reads=[bn8.buf(), idf.buf()], writes=[bk.buf()])
        tt_op("dve", Bmr[:, ctile, jm * 128:(jm + 1) * 128], bk[:, 0:128], maskB[:, jm, :], ALU.mult, [bk.buf(), maskB.buf()], [Bmr.buf()])
        tt_op("dve", Bmi[:, ctile, jm * 128:(jm + 1) * 128], bk[:, 128:256], maskB[:, jm, :], ALU.mult, [bk.buf(), maskB.buf()], [Bmi.buf()])
    S.free("bnat", "bn8", "bbar", "tb4", "f16")
    if os.environ.get("S5_STOP") == "4":
        S.emit()
        return nc
    cnat = S.alloc("cnat", [2, 4, 64], F32, 4)
    S.dma("sp", "c2", cnat[:, 0, :, :], c_re.rearrange("(ct p) s -> p ct s", p=128), writes=[cnat.buf()])
    S.dma("sp", "c2", cnat[:, 1, :, :], c_im.rearrange("(ct p) s -> p ct s", p=128), writes=[cnat.buf()])
    cn2 = S.alloc("cn2", [2, 4, 2, 64], F32, 4)
    for ri in range(2):
        S.op("dve", lambda e, ri=ri: e.tensor_copy(out=cn2[:, ri, :, :, :], in_=cnat[:, ri, :, :].unsqueeze(2).to_broadcast([128, 4, 2, 64])),
             reads=[cnat.buf()], writes=[cn2.buf()])
    Cmr = S.alloc("Cmr", [16, 128], BF16, 2)
    Cmi = S.alloc("Cmi", [16, 128], BF16, 2)
    for ctile in range(4):
        bk = banks[ctile % 4]
        for ri in range(2):
            S.op("pe", lambda e, bk=bk, ri=ri, ctile=ctile: e.transpose(out=bk[:, ri * 128:(ri + 1) * 128],
                                                                    in_=cn2[:, ri, ctile, :, :].rearrange("p c s -> p (c s)"), identity=idf[:, :]),
                 reads=[cn2.buf(), idf.buf()], writes=[bk.buf()])
        for jm in range(4):
            j = ctile * 4 + jm
            tt_op("dve", Cmr[:, j, :], bk[:, 0:128], maskC[:, jm, :], ALU.mult, [bk.buf(), maskC.buf()], [Cmr.buf()])
            S.op("dve", lambda e, bk=bk, j=j, jm=jm: e.scalar_tensor_tensor(out=Cmi[:, j, :], in0=bk[:, 128:256], scalar=-1.0, in1=maskC[:, jm, :],
                                                                         op0=ALU.mult, op1=ALU.mult),
                 reads=[bk.buf(), maskC.buf()], writes=[Cmi.buf()])
    S.free("cnat", "cn2", "maskB", "maskC", "tq", "sn", "cs")
    if os.environ.get("S5_STOP") == "5":
        S.emit()
        return nc
    dnat = S.alloc("dnat", [2, 128], F32, 4)
    S.dma("sp", "c2", dnat[0:4, 0, :], s5_d.rearrange("(ct p) -> ct p", p=128), writes=[dnat.buf()])
    S.dma("sp", "c2", dnat[0:4, 1, :], glu_b.rearrange("(ct p) -> ct p", p=128), writes=[dnat.buf()])
    dcol = S.alloc("dcol", [2, 4], F32, 4)
    bk = banks[0]
    for i in range(2):
        S.op("pe", lambda e, i=i, bk=bk: e.transpose(out=bk[:, i * 4:(i + 1) * 4], in_=dnat[0:4, i, :], identity=idf[0:4, 0:4]),
             reads=[dnat.buf(), idf.buf()], writes=[bk.buf()])
    copy_op("dve", dcol[:, 0, :], bk[:, 0:4], [bk.buf()], [dcol.buf()])
    copy_op("dve", dcol[:, 1, :], bk[:, 4:8], [bk.buf()], [dcol.buf()])
    S.free("dnat")
    if os.environ.get("S5_STOP") == "6":
        S.emit()
        return nc
    gw = wload("gw", glu_w, 4, 512, 512, "w_s5")

    if os.environ.get("S5_STOP") == "7":
        S.emit()
        return nc
    if str(3) in os.environ.get("BARRIERS", ""):
        S.barrier()
    zb = S.alloc("zb", [2, 2, W2], BF16, 2)
    tm = S.alloc("tm", [2, 4, 512], F32, 4)
    td = S.alloc("td", [2, 4, 512], F32, 4)
    wc = S.alloc("wc", [2, 2, 512], F32, 4)
    xbf = S.alloc("xbf", [2, 16, 128], BF16, 2)
    car = S.alloc("car", [16, 2], F32, 4)
    ypre = S.alloc("ypre", [2, 4, 512], F32, 4)
    S.op("pool", lambda e: e.memset(car[:, :, :], 0.0), writes=[car.buf(g) for g in range(4)])
    gl = S.alloc("gl", [4, 512], F32, 4)
    glb = S.alloc("glb", [4, 512], BF16, 2)
    g1 = S.alloc("g1", [2, 512], F32, 4)
    bR, bI = banks[0], banks[1]
    wbk = (banks[2], banks[3])
    ybk = banks[4]
    gbk = banks[5]
    mi = 0
    di = 0
    for c in range(int(os.environ.get("S5_CHUNKS", NTT))):
        zs = c % 2
        if os.environ.get("S5_PART") == "1" and c == 0:
            pass
        for ctile in range(4):
            mm(bR[:, :], uT[:, ctile, c * 128:(c + 1) * 128], Bmr[:, ctile, :], True, True, [uT.buf((ctile, c)), Bmr.buf()], [bR.buf()])
            mm(bI[:, :], uT[:, ctile, c * 128:(c + 1) * 128], Bmi[:, ctile, :], True, True, [uT.buf((ctile, c)), Bmi.buf()], [bI.buf()])
            ms = mi % 2; mi += 1
            blk = slice(ctile * 512, (ctile + 1) * 512)
            tt_op("dve", tm[:, ms, 0, :], bR[:, :], Wmr[:, blk], ALU.mult, [bR.buf(), Wmr.buf()], [tm.buf((ms, 0))])
            tt_op("dve", tm[:, ms, 1, :], bI[:, :], Wmi[:, blk], ALU.mult, [bI.buf(), Wmi.buf()], [tm.buf((ms, 1))])
            tt_op("dve", tm[:, ms, 2, :], bR[:, :], Wmi[:, blk], ALU.mult, [bR.buf(), Wmi.buf()], [tm.buf((ms, 2))])
            tt_op("dve", tm[:, ms, 3, :], bI[:, :], Wmr[:, blk], ALU.mult, [bI.buf(), Wmr.buf()], [tm.buf((ms, 3))])
            tt_op("pool", zb[:, zs, 0, blk], tm[:, ms, 0, :], tm[:, ms, 1, :], ALU.subtract, [tm.buf((ms, 0)), tm.buf((ms, 1))], [zb.buf((zs, 0, ctile))])
            tt_op("pool", zb[:, zs, 1, blk], tm[:, ms, 2, :], tm[:, ms, 3, :], ALU.add, [tm.buf((ms, 2)), tm.buf((ms, 3))], [zb.buf((zs, 1, ctile))])
        if os.environ.get("S5_PART") == "1":
            continue
        for g4 in range(4):
            WR, WI = (banks[2], banks[3]) if g4 % 2 == 0 else (banks[6], banks[7])
            for jj in range(4):
                j = 4 * g4 + jj
                mm(WR[:, jj * 128:(jj + 1) * 128], zb[:, zs, 0, j * 128:(j + 1) * 128], trib[:, :], True, True, [zb.buf((zs, 0, j // 4)), trib.buf()], [WR.buf()])
                mm(WI[:, jj * 128:(jj + 1) * 128], zb[:, zs, 1, j * 128:(j + 1) * 128], trib[:, :], True, True, [zb.buf((zs, 1, j // 4)), trib.buf()], [WI.buf()])
            ds = di % 2; di += 1
            for jj in range(4):
                j = 4 * g4 + jj
                S.op("act", lambda e, WR=WR, ds=ds, jj=jj, j=j: e.activation(out=wc[:, ds, 0, jj * 128:(jj + 1) * 128], in_=WR[:, jj * 128:(jj + 1) * 128],
                                                                          func=AF.Identity, bias=car[:, j, 0:1], scale=1.0),
                     reads=[WR.buf(), car.buf(g4)], writes=[wc.buf((ds, 0))])
                S.op("act", lambda e, WI=WI, ds=ds, jj=jj, j=j: e.activation(out=wc[:, ds, 1, jj * 128:(jj + 1) * 128], in_=WI[:, jj * 128:(jj + 1) * 128],
                                                                          func=AF.Identity, bias=car[:, j, 1:2], scale=1.0),
                     reads=[WI.buf(), car.buf(g4)], writes=[wc.buf((ds, 1))])
            gcols = slice(g4 * 512, (g4 + 1) * 512)
            pr, pi_ = WpTr[:, gcols], WpTi[:, gcols]
            wr_, wi_ = wc[:, ds, 0, :], wc[:, ds, 1, :]
            for k, (w_, p_, wk) in enumerate(((wr_, pr, 0), (wi_, pi_, 1), (wr_, pi_, 0), (wi_, pr, 1))):
                tt_op("dve", td[:, ds, k, :], w_, p_, ALU.mult, [wc.buf((ds, wk)), WpTr.buf(), WpTi.buf()], [td.buf((ds, k))])
            xr_out = xbf[:, 0, 4 * g4:4 * g4 + 4, :].rearrange("p j t -> p (j t)")
            xi_out = xbf[:, 1, 4 * g4:4 * g4 + 4, :].rearrange("p j t -> p (j t)")
            tt_op("dve", xr_out, td[:, ds, 0, :], td[:, ds, 1, :], ALU.subtract, [td.buf((ds, 0)), td.buf((ds, 1))], xbf.bufs([(0, 4 * g4 + i) for i in range(4)]))
            tt_op("dve", xi_out, td[:, ds, 2, :], td[:, ds, 3, :], ALU.add, [td.buf((ds, 2)), td.buf((ds, 3))], xbf.bufs([(1, 4 * g4 + i) for i in range(4)]))
            l127 = lambda k, ds=ds: td[:, ds, k, :].rearrange("p (j t) -> p j t", t=128)[:, :, 127]
            tt_op("dve", car[:, 4 * g4:4 * g4 + 4, 0], l127(0), l127(1), ALU.subtract, [td.buf((ds, 0)), td.buf((ds, 1))], [car.buf(g4)])
            tt_op("dve", car[:, 4 * g4:4 * g4 + 4, 1], l127(2), l127(3), ALU.add, [td.buf((ds, 2)), td.buf((ds, 3))], [car.buf(g4)])
        if os.environ.get("S5_PART") == "2":
            continue
        ys = (c // 4) % 2
        for ctile in range(4):
            ybk = banks[4 + ctile % 2]
            ya = ybk[:, 0:128]
            n = 0
            for jm in range(4):
                j = ctile * 4 + jm
                mm(ya, Cmr[:, j, :], xbf[:, 0, j, :], n == 0, False, [Cmr.buf(), xbf.buf((0, j))], [ybk.buf()]); n += 1
                mm(ya, Cmi[:, j, :], xbf[:, 1, j, :], False, jm == 3, [Cmi.buf(), xbf.buf((1, j))], [ybk.buf()]); n += 1
            S.op("dve", lambda e, ya=ya, ctile=ctile, ys=ys, c=c: e.scalar_tensor_tensor(
                out=ypre[:, ys, ctile, (c % 4) * 128:(c % 4 + 1) * 128], in0=uT[:, ctile, c * 128:(c + 1) * 128], scalar=dcol[:, 0, ctile:ctile + 1], in1=ya,
                op0=ALU.mult, op1=ALU.add),
                reads=[uT.buf((ctile, c)), dcol.buf(), ybk.buf()], writes=[ypre.buf((ys, ctile))])
        if c % 4 == 3:
            tb = c // 4
            for ctile in range(4):
                xx = ypre[:, ys, ctile, :]
                xb_ = ypre.buf((ys, ctile))
                tt_op("dve", g1[:, 0, :], xx, xx, ALU.mult, [xb_], [g1.buf(0)])
                ts_op("dve", g1[:, 0, :], g1[:, 0, :], 0.044715, 1.0, ALU.mult, ALU.add, [g1.buf(0)], [g1.buf(0)])
                tt_op("dve", g1[:, 0, :], g1[:, 0, :], xx, ALU.mult, [g1.buf(0), xb_], [g1.buf(0)])
                S.op("act", lambda e: e.activation(out=g1[:, 1, :], in_=g1[:, 0, :], func=AF.Sigmoid, scale=1.5957691216057308), reads=[g1.buf(0)], writes=[g1.buf(1)])
                tt_op("dve", gl[:, ctile, :], xx, g1[:, 1, :], ALU.mult, [xb_, g1.buf(1)], [gl.buf(ctile)])
                copy_op("act", glb[:, ctile, :], gl[:, ctile, :], [gl.buf(ctile)], [glb.buf(ctile)])
            for cp in range(0 if os.environ.get("S5_G") != "1" else 4, 4):
                gbk = banks[4 + cp % 2]
                for ctile in range(4):
                    mm(gbk[:, :], gw[:, ctile, cp * 128:(cp + 1) * 128], glb[:, ctile, :], ctile == 0, ctile == 3, [gw.buf(0), glb.buf(ctile)], [gbk.buf()])
                if os.environ.get("S5_G") == "2":
                    continue
                S.op("act", lambda e, cp=cp, gbk=gbk: e.activation(out=g1[:, 0, :], in_=gbk[:, :], func=AF.Sigmoid, bias=dcol[:, 1, cp:cp + 1], scale=1.0),
                     reads=[gbk.buf(), dcol.buf()], writes=[g1.buf(0)])
                tt_op("dve", catT[:, cp, tb * 512:(tb + 1) * 512], gl[:, cp, :], g1[:, 0, :], ALU.mult, [gl.buf(cp), g1.buf(0)],
                      catT.bufs([(cp, 4 * tb + i) for i in range(4)]))
    S.free("Wmr", "Wmi", "WpTr", "WpTi", "Bmr", "Bmi", "Cmr", "Cmi", "dcol", "gw", "zb", "tm", "td", "wc", "xbf", "car", "ypre", "gl", "glb", "g1", "uT")
    if debug:
        S.dma("sp", "dbg", dbg["d_cat"], catT[:, :, :], reads=catT.bufs([(k, t) for k in range(8) for t in range(16)]), final=True)
    if upto == "C":
        S.emit()
        return nc


    if str(4) in os.environ.get("BARRIERS", ""):
        S.barrier()
    hs = S.alloc("hs", [16, D], F32, 4)
    hT = S.alloc("hT", [8, T], BF16, 2)
    wob = wload("wob", w_out, 8, D, 512, "w_e")
    lnp0 = load_lnp("lnp0", ln_in_g, ln_in_b)
    lnp1 = load_lnp("lnp1", ln1_g, ln1_b)
    lbt = S.alloc("lbt", [2, D], BF16, 2)
    bi = 0

    def resid_stats(tt, acc_banks):
        for nh in range(2):
            bk = acc_banks[nh]
            S.op("dve", lambda e, nh=nh, bk=bk: e.scalar_tensor_tensor(out=hs[:, tt, nh * 512:(nh + 1) * 512], in0=hs[:, tt, nh * 512:(nh + 1) * 512],
                                                                     scalar=ALPHA, in1=bk[:, :], op0=ALU.mult, op1=ALU.add),
                 reads=[hs.buf(tt), bk.buf()], writes=[hs.buf(tt)])
        return ln_stats(hs[:, tt, :], hs.buf(tt))

    def ln_finish_a(tt, sl, lnp):
        ln_apply(hs[:, tt, :], hs.buf(tt), lnp, None, None, sl)

    def ln_finish(tt, sl, lnp, do_T, split=False):
        if not split:
            ln_apply(hs[:, tt, :], hs.buf(tt), lnp, hs[:, tt, :], hs.buf(tt), sl)
        else:
            ln_apply_b(hs[:, tt, :], hs.buf(tt), lnp, hs[:, tt, :], hs.buf(tt))
        if do_T:
            s2 = tt % 2
            copy_op("act", lbt[:, s2, :], hs[:, tt, :], [hs.buf(tt)], [lbt.buf(s2)])
            transpose_to_hT(lbt[:, s2, :], lbt.buf(s2), hT, tt)

    def resid_ln(tt, acc_banks, lnp, do_T):
        sl = resid_stats(tt, acc_banks)
        ln_finish(tt, sl, lnp, do_T)

    sl_in = {}
    sl_1 = {}
    abb = S.alloc("abb", [D], BF16, 2)
    S.op("dve", lambda e: e.tensor_scalar(out=abb[0:1, :], in0=lnp0[0:1, 1, :], scalar1=ALPHA, scalar2=None, op0=ALU.mult),
         reads=[lnp0.buf()], writes=[abb.buf()])
    accs_of = {}
    for step in range(NTT + 6):
        if step >= 6:
            ln_finish(step - 6, None, lnp1, True, split=True)
        if step < NTT:
            tt = step
            S.dma("sp", "xin", hs[:, tt, :], x[tt * 128:(tt + 1) * 128, :], writes=[hs.buf(tt)])
            sl_in[tt] = ln_stats(hs[:, tt, :], hs.buf(tt))
        if 1 <= step <= NTT:
            tt = step - 1
            ln_apply(hs[:, tt, :], hs.buf(tt), lnp0, None, None, sl_in[tt])
        if 2 <= step <= NTT + 1:
            tt = step - 2
            accs = []
            for nh in range(2):
                bk = banks[bi % 4]; bi += 1
                mm(bk[:, :], onesb[0:1, :], abb[0:1, nh * 512:(nh + 1) * 512], True, False, [onesb.buf(), abb.buf()], [bk.buf()])
                for kt in range(8):
                    mm(bk[:, :], catT[:, kt, tt * 128:(tt + 1) * 128], wob[:, kt, nh * 512:(nh + 1) * 512], False, kt == 7,
                       [catT.buf((kt, tt)), wob.buf(nh)], [bk.buf()])
                accs.append(bk)
            accs_of[tt] = accs
        if 3 <= step <= NTT + 2:
            tt = step - 3
            sl_1[tt] = resid_stats(tt, accs_of[tt])
        if 4 <= step <= NTT + 3:
            tt = step - 4
            ln_finish_a(tt, sl_1[tt], lnp1)
    S.free("catT", "wob", "lnp0", "lnp1", "abb")
    if debug:
        S.dma("sp", "dbg", dbg["d_h1"], hs[:, :, :], reads=hs.bufs(range(16)), final=True)
    if upto == "E":
        S.emit()
        return nc


    if str(5) in os.environ.get("BARRIERS", ""):
        S.barrier()
    wkv = wload("wkv", ca_wkv, 8, 2 * D, 512, "w_f")
    memf = S.alloc("memf", [2, D], F32, 4)
    memb = S.alloc("memb", [2, D], BF16, 2)
    memT = S.alloc("memT", [8, 256], BF16, 2)
    for mt in range(2):
        S.dma("sp", "mem", memf[:, mt, :], mem[mt * 128:(mt + 1) * 128, :], writes=[memf.buf(mt)])
        copy_op("act", memb[:, mt, :], memf[:, mt, :], [memf.buf(mt)], [memb.buf(mt)])
        ctr["pt"] += 1
        pt = ptb[ctr["pt"] % 2]
        for k in range(8):
            S.op("pe", lambda e, k=k, pt=pt, mt=mt: e.transpose(out=pt[:, k, :], in_=memb[:, mt, k * 128:(k + 1) * 128], identity=idb[:, :]),
                 reads=[memb.buf(mt), idb.buf()], writes=[pt.buf()])
        copy_op(evac_eng(), memT[:, :, mt * 128:(mt + 1) * 128], pt[:, :, :], [pt.buf()], [memT.buf()])
    kTca = S.alloc("kTca", [8, 256], BF16, 2)
    vca = S.alloc("vca", [2, D], BF16, 2)
    for ct in range(8):
        bk = banks[bi % 4]; bi += 1
        for kt in range(8):
            mm(bk[:, 0:256], wkv[:, kt, ct * 128:(ct + 1) * 128], memT[:, kt, :], kt == 0, kt == 7, [wkv.buf(ct // 4), memT.buf()], [bk.buf()])
        copy_op(evac_eng(), kTca[:, ct, :], bk[:, 0:256], [bk.buf()], [kTca.buf()])
    for mt in range(2):
        for nh in range(2):
            bk = banks[bi % 4]; bi += 1
            for kt in range(8):
                mm(bk[:, :], memT[:, kt, mt * 128:(mt + 1) * 128], wkv[:, kt, D + nh * 512:D + (nh + 1) * 512], kt == 0, kt == 7,
                   [wkv.buf(2 + nh), memT.buf()], [bk.buf()])
            copy_op(evac_eng(), vca[:, mt, nh * 512:(nh + 1) * 512], bk[:, :], [bk.buf()], [vca.buf()])
    S.free("wkv", "memf", "memb", "memT")
    wqb = wload("wqb", ca_wq, 8, D, 512, "w_f2")
    wo2 = wload("wo2", ca_wo, 8, D, 512, "w_f2")
    lnp2 = load_lnp("lnp2", ln2_g, ln2_b)
    qTc = S.alloc("qTc", [2, 8, 512], BF16, 2)
    PTc = S.alloc("PTc", [2, 2, 512], BF16, 2)
    oTc = S.alloc("oTc", [8, 512], BF16, 2)
    rcs = S.alloc("rcs", [2, 512], F32, 4)
    pendF = None
    pendF2 = None
    fst = {"bi": 0, "sc": 0}

    def nbank():
        fst["bi"] += 1
        return banks[fst["bi"] % 4]

    def F_Q(tb):
        qs = tb % 2
        tcols = slice(tb * 512, (tb + 1) * 512)
        hbufs = hT.bufs(range(4 * tb, 4 * tb + 4))
        for ct in range(8):
            bk = nbank()
            for kt in range(8):
                mm(bk[:, :], wqb[:, kt, ct * 128:(ct + 1) * 128], hT[:, kt, tcols], kt == 0, kt == 7, [wqb.buf(ct // 4)] + hbufs, [bk.buf()])
            copy_op(evac_eng(), qTc[:, qs, ct, :], bk[:, :], [bk.buf()], [qTc.buf((qs, ct))])

    def F_HS(tb, hd):
        qs = tb % 2
        ps = hd % 2
        for mt in range(2):
            fst["sc"] += 1
            bk = banks[4 + fst["sc"] % 4]
            for i in range(2):
                ct = 2 * hd + i
                mm(bk[:, :], kTca[:, ct, mt * 128:(mt + 1) * 128], qTc[:, qs, ct, :], i == 0, i == 1, [kTca.buf(), qTc.buf((qs, ct))], [bk.buf()])
            S.op("act", lambda e, bk=bk, ps=ps, mt=mt: e.activation(out=PTc[:, ps, mt, :], in_=bk[:, :], func=AF.Exp, scale=1.0 / 16.0),
                 reads=[bk.buf()], writes=[PTc.buf((ps, mt))])

    def F_HP(tb, hd):
        ps = hd % 2
        sb = nbank()
        for mt in range(2):
            mm(sb[:, :], onesb[:, :], PTc[:, ps, mt, :], mt == 0, mt == 1, [onesb.buf(), PTc.buf((ps, mt))], [sb.buf()])
        S.op("act", lambda e, sb=sb, ps=ps: e.activation(out=rcs[:, ps, :], in_=sb[:, :], func=AF.Ln), reads=[sb.buf()], writes=[rcs.buf(ps)])
        S.op("act", lambda e, ps=ps: e.activation(out=rcs[:, ps, :], in_=rcs[:, ps, :], func=AF.Exp, scale=-1.0), reads=[rcs.buf(ps)], writes=[rcs.buf(ps)])
        for dti in range(2):
            ct = 2 * hd + dti
            bk = nbank()
            for mt in range(2):
                mm(bk[:, :], vca[:, mt, ct * 128:(ct + 1) * 128], PTc[:, ps, mt, :], mt == 0, mt == 1, [vca.buf(), PTc.buf((ps, mt))], [bk.buf()])
            tt_op("dve", oTc[:, ct, :], bk[:, :], rcs[:, ps, :], ALU.mult, [bk.buf(), rcs.buf(ps)], [oTc.buf(ct)])

    def F_W(tb):
        nonlocal_state = None
        prev_acc = None
        for tl in range(5):
            tt = 4 * tb + tl
            if tl < 4:
                if pstate["p3"] is not None:
                    ln_finish(pstate["p3"][0], None, lnp2, True, split=True)
                    pstate["p3"] = None
                accs = []
                for nh in range(2):
                    bk = nbank()
                    for kt in range(8):
                        mm(bk[:, :], oTc[:, kt, tl * 128:(tl + 1) * 128], wo2[:, kt, nh * 512:(nh + 1) * 512], kt == 0, kt == 7,
                           [oTc.buf(kt), wo2.buf(nh)], [bk.buf()])
                    accs.append(bk)
            if prev_acc is not None:
                pt_, pa_ = prev_acc
                slF = resid_stats(pt_, pa_)
                if pstate["p1"] is not None:
                    ln_finish_a(pstate["p1"][0], pstate["p1"][1], lnp2)
                if pstate["p3"] is not None:
                    ln_finish(pstate["p3"][0], None, lnp2, True, split=True)
                pstate["p3"] = pstate["p2"]
                pstate["p2"] = pstate["p1"]
                pstate["p1"] = (pt_, slF)
            prev_acc = (tt, accs) if tl < 4 else None

    pstate = {"p1": None, "p2": None, "p3": None}
    F_Q(0)
    for tb in range(4):
        F_HS(tb, 0)
        for hd in range(4):
            if hd < 3:
                F_HS(tb, hd + 1)
            F_HP(tb, hd)
        if tb < 3:
            F_Q(tb + 1)
        F_W(tb)
    if pstate["p3"] is not None:
        ln_finish(pstate["p3"][0], None, lnp2, True, split=True)
    ln_finish_a(pstate["p1"][0], pstate["p1"][1], lnp2)
    if pstate["p2"] is not None:
        ln_finish(pstate["p2"][0], None, lnp2, True, split=True)
    ln_finish(pstate["p1"][0], None, lnp2, True, split=True)
    S.free("wqb", "wo2", "lnp2", "qTc", "PTc", "oTc", "rcs", "kTca", "vca", "lbt")
    if debug:
        S.dma("sp", "dbg", dbg["d_h2"], hs[:, :, :], reads=hs.bufs(range(16)), final=True)
    if upto == "F":
        S.emit()
        return nc


    if str(6) in os.environ.get("BARRIERS", ""):
        S.barrier()
    lnp3 = load_lnp("lnp3", ln3_g, ln3_b)
    gus = S.alloc("gus", [3, 8, 2, 128], BF16, 2)
    actT = S.alloc("actT", [11, T], BF16, 2)
    sg = S.alloc("sg", [2, 512], F32, 4)
    guv = w_gu.rearrange("(kt p) n -> p kt n", p=128)
    gi = 0
    pendG = None
    pendG2 = None

    def g_finish(t_):
        ln_apply_b(hs[:, t_, :], hs.buf(t_), lnp3, hs[:, t_, :], hs.buf(t_))
        S.dma("sp", "out", out[t_ * 128:(t_ + 1) * 128, :], hs[:, t_, :], reads=[hs.buf(t_)], final=True)

    for ps_ in range(2):
        wd = S.alloc("wd", [11, D], BF16, 2)
        dv = w_dn[ps_ * 11 * 128:(ps_ + 1) * 11 * 128, :].rearrange("(j p) n -> p j n", p=128)
        for jl in range(11):
            j = ps_ * 11 + jl
            gs = gi % 3; gi += 1
            S.dma("pool", f"w_gu{gs}", gus[:, gs, :, 0, :], guv[:, :, j * 128:(j + 1) * 128], writes=[gus.buf(gs)])
            S.dma("pool", f"w_gu{gs}", gus[:, gs, :, 1, :], guv[:, :, FFN_H + j * 128:FFN_H + (j + 1) * 128], writes=[gus.buf(gs)])
            if jl < 11:
                S.dma("pool", "w_dn", wd[:, jl, :], dv[:, jl, :], writes=[wd.buf(jl)])
            for tb in range(4):
                tcols = slice(tb * 512, (tb + 1) * 512)
                hbufs = hT.bufs(range(4 * tb, 4 * tb + 4))
                bg = banks[(bi % 2) * 2]; bu_ = banks[(bi % 2) * 2 + 1]; bi += 1
                for kt in range(8):
                    mm(bg[:, :], gus[:, gs, kt, 0, :], hT[:, kt, tcols], kt == 0, kt == 7, [gus.buf(gs)] + hbufs, [bg.buf()])
                for kt in range(8):
                    mm(bu_[:, :], gus[:, gs, kt, 1, :], hT[:, kt, tcols], kt == 0, kt == 7, [gus.buf(gs)] + hbufs, [bu_.buf()])
                s2 = bi % 2
                S.op("act", lambda e, bg=bg, s2=s2: e.activation(out=sg[:, s2, :], in_=bg[:, :], func=AF.Silu), reads=[bg.buf()], writes=[sg.buf(s2)])
                tt_op("dve", actT[:, jl, tcols], sg[:, s2, :], bu_[:, :], ALU.mult, [sg.buf(s2), bu_.buf()], actT.bufs([(jl, 4 * tb + i) for i in range(4)]))
        for tt in range(NTT):
            accs = []
            for nh in range(2):
                bk = banks[4 + nh + 2 * (tt % 2)]
                for jl in range(11):
                    mm(bk[:, :], actT[:, jl, tt * 128:(tt + 1) * 128], wd[:, jl, nh * 512:(nh + 1) * 512], jl == 0, jl == 10,
                       [actT.buf((jl, tt)), wd.buf(jl)], [bk.buf()])
                accs.append(bk)
            if ps_ == 0:
                for nh in range(2):
                    bk = accs[nh]
                    S.op("dve", lambda e, nh=nh, bk=bk, tt=tt: e.scalar_tensor_tensor(out=hs[:, tt, nh * 512:(nh + 1) * 512], in0=hs[:, tt, nh * 512:(nh + 1) * 512],
                                                                                 scalar=ALPHA, in1=bk[:, :], op0=ALU.mult, op1=ALU.add),
                         reads=[hs.buf(tt), bk.buf()], writes=[hs.buf(tt)])
            else:
                for nh in range(2):
                    bk = accs[nh]
                    tt_op("dve", hs[:, tt, nh * 512:(nh + 1) * 512], hs[:, tt, nh * 512:(nh + 1) * 512], bk[:, :], ALU.add, [hs.buf(tt), bk.buf()], [hs.buf(tt)])
                slG = ln_stats(hs[:, tt, :], hs.buf(tt))
                if pendG2 is not None:
                    g_finish(pendG2[0])
                if pendG is not None:
                    ln_apply(hs[:, pendG[0], :], hs.buf(pendG[0]), lnp3, None, None, pendG[1])
                pendG2 = pendG
                pendG = (tt, slG)
        if ps_ == 1:
            if pendG2 is not None:
                g_finish(pendG2[0])
            ln_apply(hs[:, pendG[0], :], hs.buf(pendG[0]), lnp3, None, None, pendG[1])
            g_finish(pendG[0])
        S.free("wd")
    S.emit()
    return nc


_CACHE = {}


def kernel(**inputs):
    consts = host_consts()
    shared = {}
    for k, v in inputs.items():
        if k in ("x", "mem"):
            continue
        a = np.ascontiguousarray(np.asarray(v, dtype=np.float32))
        if k == "rel_bias":
            shared[k] = a
        elif a.ndim >= 2 and a.shape[0] == 1:
            a = a[0]
            if k in ("s5_lambda_re", "s5_lambda_im", "s5_d"):
                a = a.reshape(-1)
            elif k in ("s5_b_re", "s5_b_im"):
                a = a.reshape(2048, 16)
            elif k in ("s5_c_re", "s5_c_im"):
                a = a.reshape(512, 64)
            shared[k] = np.ascontiguousarray(a)
        else:
            shared[k] = a
    shared.update(consts)
    xs = np.asarray(inputs["x"], dtype=np.float32)
    ms = np.asarray(inputs["mem"], dtype=np.float32)
    if "nc" not in _CACHE:
        _CACHE["nc"] = build_program(False)
    nc = _CACHE["nc"]
    in_maps = []
    for b in range(8):
        m = dict(shared)
        m["x"] = np.ascontiguousarray(xs[b])
        m["mem"] = np.ascontiguousarray(ms[b])
        in_maps.append(m)
    res = run_bass_kernel_spmd(nc, in_maps, core_ids=list(range(8)))
    return np.stack([np.asarray(r["out"], dtype=np.float32) for r in res.results], axis=0)
```
